# Optimizing a Trainium2 kernel written in Bass

```python
import jax, jax.numpy as jnp
from jax import lax
import numpy as np

D_MODEL = 1024
BATCH = 8
SEQ = 2048
DEPTH = 2
DEC_BATCH = 128
DEC_SEQ = 8
PAST_LEN = 16384
PAGE_SIZE = 128

D_MIX = D_MODEL
RG_WIDTH = D_MIX // 2
RG_BLOCKS = 8
RG_BLOCK = RG_WIDTH // RG_BLOCKS
RG_C = 8.0
DN_HEADS = 4
DN_HEAD_DIM = (D_MIX - RG_WIDTH) // DN_HEADS
DN_WIDTH = DN_HEADS * DN_HEAD_DIM
CONV_WIDTH = 4
CONV_CH = RG_WIDTH + 3 * DN_WIDTH
CHUNK = 64
D_FF = 2816
EPS = 1e-6
IN_COLS = 2 * RG_WIDTH + 4 * DN_WIDTH + 2 * DN_HEADS

kernel_name = 'hymba_rglru_gdn_macaron_step'


def rmsnorm(x, g):
    xf = x.astype(jnp.float32)
    y = xf * lax.rsqrt(jnp.mean(xf * xf, axis=-1, keepdims=True) + EPS)
    return (y * g.astype(jnp.float32)).astype(x.dtype)


def l2norm(x):
    return x * lax.rsqrt(jnp.sum(x * x, axis=-1, keepdims=True) + EPS)


def swiglu(h, w_up, w_down):
    u = h @ w_up
    gate, up = u[..., :D_FF], u[..., D_FF:]
    return (jax.nn.silu(gate) * up) @ w_down


def causal_conv(x, buf, w):
    L = x.shape[1]
    xp = jnp.concatenate([buf.astype(x.dtype), x], axis=1)
    out = sum(xp[:, j:j + L] * w[j] for j in range(CONV_WIDTH))
    return out, xp[:, -(CONV_WIDTH - 1):]


def _lin_combine(c1, c2):
    a1, b1 = c1
    a2, b2 = c2
    return a1 * a2, a2 * b1 + b2


def rg_lru(xr, h0, w_a, b_a, w_x, b_x, lam):
    B, L, _ = xr.shape
    xb = xr.reshape(B, L, RG_BLOCKS, RG_BLOCK)
    r = jax.nn.sigmoid(jnp.einsum('blhi,hij->blhj', xb, w_a).reshape(B, L, RG_WIDTH) + b_a)
    i = jax.nn.sigmoid(jnp.einsum('blhi,hij->blhj', xb, w_x).reshape(B, L, RG_WIDTH) + b_x)
    log_a = -RG_C * r * jax.nn.softplus(-lam.astype(jnp.float32))
    a = jnp.exp(log_a)
    b = jnp.sqrt(-jnp.expm1(2.0 * log_a)) * (i * xr)
    b = b.at[:, 0].add(a[:, 0] * h0)
    _, h = lax.associative_scan(_lin_combine, (a, b), axis=1)
    return h, h[:, -1]


def gated_delta(q, k, v, g, beta, S0):
    B, L, H, Dk = q.shape
    Dv = v.shape[-1]
    C = min(CHUNK, L)
    n = -(-L // C)
    pad = n * C - L

    def blocks(t):
        t = jnp.pad(t, [(0, 0), (0, pad)] + [(0, 0)] * (t.ndim - 2))
        t = t.reshape((B, n, C) + t.shape[2:])
        return jnp.moveaxis(t, 3, 2).swapaxes(0, 1)

    qb, kb, vb, gb, bb = blocks(q), blocks(k), blocks(v), blocks(g), blocks(beta)
    G = jnp.cumsum(gb, axis=-1)
    idx = jnp.arange(C)
    incl = idx[:, None] >= idx[None, :]
    strict = idx[:, None] > idx[None, :]
    decay = jnp.exp(jnp.where(incl, G[..., :, None] - G[..., None, :], -jnp.inf))
    kbeta = kb * bb[..., None]
    A = jnp.where(strict, jnp.einsum('nbhid,nbhjd->nbhij', kbeta, kb) * decay, 0.0)
    IA = A + jnp.eye(C, dtype=A.dtype)
    rhs = jnp.concatenate([vb * bb[..., None], kbeta * jnp.exp(G)[..., None]], axis=-1)
    sol = lax.linalg.triangular_solve(IA, rhs, left_side=True, lower=True, unit_diagonal=True)
    u, w = sol[..., :Dv], sol[..., Dv:]
    att = jnp.einsum('nbhid,nbhjd->nbhij', qb, kb) * decay

    def step(S, inp):
        qc, kc, uc, wc, ac, Gc = inp
        v_new = uc - jnp.einsum('bhcd,bhde->bhce', wc, S)
        o = jnp.einsum('bhcd,bhde->bhce', qc * jnp.exp(Gc)[..., None], S) + jnp.einsum('bhij,bhje->bhie', ac, v_new)
        gl = Gc[..., -1]
        S = S * jnp.exp(gl)[..., None, None] + jnp.einsum('bhcd,bhce->bhde', kc * jnp.exp(gl[..., None] - Gc)[..., None], v_new)
        return S, o

    S, o = lax.scan(step, S0, (qb, kb, u, w, att, G))
    o = jnp.moveaxis(o.swapaxes(0, 1), 2, 3).reshape(B, n * C, H, Dv)[:, :L]
    return o, S


def mixer(h, conv_buf, rg_h0, S0, w_in, conv_w, conv_b_rg, rg_w_a, rg_b_a, rg_w_x, rg_b_x,
          rg_lambda, dn_a_log, dn_dt_bias, dn_norm_w, w_out):
    B, L, _ = h.shape
    f32 = jnp.float32
    proj = h @ w_in
    conv_out, new_buf = causal_conv(proj[..., :CONV_CH], conv_buf, conv_w)
    conv_out = conv_out.astype(f32)
    gates = proj[..., CONV_CH:CONV_CH + RG_WIDTH + DN_WIDTH].astype(f32)
    scal = proj[..., CONV_CH + RG_WIDTH + DN_WIDTH:].astype(f32)
    xr = conv_out[..., :RG_WIDTH] + conv_b_rg.astype(f32)
    rg_h, rg_last = rg_lru(xr, rg_h0.astype(f32), rg_w_a, rg_b_a, rg_w_x, rg_b_x, rg_lambda)
    rg_out = rg_h * jax.nn.gelu(gates[..., :RG_WIDTH])
    qkv = jax.nn.silu(conv_out[..., RG_WIDTH:]).reshape(B, L, 3, DN_HEADS, DN_HEAD_DIM)
    q = l2norm(qkv[:, :, 0]) * (DN_HEAD_DIM ** -0.5)
    k = l2norm(qkv[:, :, 1])
    v = qkv[:, :, 2]
    beta = jax.nn.sigmoid(scal[..., :DN_HEADS])
    g = -jnp.exp(dn_a_log.astype(f32)) * jax.nn.softplus(scal[..., DN_HEADS:] + dn_dt_bias.astype(f32))
    o, S = gated_delta(q, k, v, g, beta, S0.astype(f32))
    o = rmsnorm(o, dn_norm_w) * jax.nn.silu(gates[..., RG_WIDTH:].reshape(B, L, DN_HEADS, DN_HEAD_DIM))
    mixed = jnp.concatenate([rg_out, o.reshape(B, L, DN_WIDTH)], axis=-1).astype(h.dtype)
    return mixed @ w_out, new_buf, rg_last, S


def setup_inputs(seed: int = 0) -> dict:
    key = jax.random.key(seed)
    ks = iter(jax.random.split(key, 40))

    def nrm(shape, scale):
        return jax.random.normal(next(ks), shape, jnp.float32) * scale

    def gain(shape):
        return 1.0 + nrm(shape, 0.05)

    a0 = jax.random.uniform(next(ks), (DEPTH, RG_WIDTH), jnp.float32, 0.9, 0.999)
    p = a0 ** (1.0 / RG_C)
    rg_lambda = jnp.log(p) - jnp.log1p(-p)
    dn_a_log = jnp.log(jax.random.uniform(next(ks), (DEPTH, DN_HEADS), jnp.float32, 1.0, 16.0))
    dt = jnp.exp(jax.random.uniform(next(ks), (DEPTH, DN_HEADS), jnp.float32, np.log(1e-3), np.log(1e-1)))
    dn_dt_bias = jnp.log(jnp.expm1(dt))
    return {
        'x_prompt': nrm((BATCH, SEQ, D_MODEL), 1.0),
        'x_sample': nrm((DEC_BATCH, DEC_SEQ, D_MODEL), 1.0),
        'state_conv': nrm((DEPTH, DEC_BATCH, CONV_WIDTH - 1, CONV_CH), 1.0),
        'state_rglru': nrm((DEPTH, DEC_BATCH, RG_WIDTH), 0.5),
        'state_delta': nrm((DEPTH, DEC_BATCH, DN_HEADS, DN_HEAD_DIM, DN_HEAD_DIM), 0.1),
        'ffn1_norm_pre': gain((DEPTH, D_MODEL)),
        'ffn1_w_up': nrm((DEPTH, D_MODEL, 2 * D_FF), D_MODEL ** -0.5),
        'ffn1_w_down': nrm((DEPTH, D_FF, D_MODEL), D_FF ** -0.5),
        'ffn1_norm_post': gain((DEPTH, D_MODEL)),
        'mix_norm_pre': gain((DEPTH, D_MODEL)),
        'w_in': nrm((DEPTH, D_MODEL, IN_COLS), D_MODEL ** -0.5),
        'conv_w': nrm((DEPTH, CONV_WIDTH, CONV_CH), CONV_WIDTH ** -0.5),
        'conv_b_rg': nrm((DEPTH, RG_WIDTH), 0.02),
        'rg_w_a': nrm((DEPTH, RG_BLOCKS, RG_BLOCK, RG_BLOCK), RG_BLOCK ** -0.5),
        'rg_b_a': nrm((DEPTH, RG_WIDTH), 0.02),
        'rg_w_x': nrm((DEPTH, RG_BLOCKS, RG_BLOCK, RG_BLOCK), RG_BLOCK ** -0.5),
        'rg_b_x': nrm((DEPTH, RG_WIDTH), 0.02),
        'rg_lambda': rg_lambda,
        'dn_a_log': dn_a_log,
        'dn_dt_bias': dn_dt_bias,
        'dn_norm_w': gain((DEPTH, DN_HEAD_DIM)),
        'w_out': nrm((DEPTH, D_MIX, D_MODEL), D_MIX ** -0.5),
        'mix_norm_post': gain((DEPTH, D_MODEL)),
        'ffn2_norm_pre': gain((DEPTH, D_MODEL)),
        'ffn2_w_up': nrm((DEPTH, D_MODEL, 2 * D_FF), D_MODEL ** -0.5),
        'ffn2_w_down': nrm((DEPTH, D_FF, D_MODEL), D_FF ** -0.5),
        'ffn2_norm_post': gain((DEPTH, D_MODEL)),
        'final_norm': gain((D_MODEL,)),
    }


def reference(x_prompt, x_sample, state_conv, state_rglru, state_delta,
              ffn1_norm_pre, ffn1_w_up, ffn1_w_down, ffn1_norm_post,
              mix_norm_pre, w_in, conv_w, conv_b_rg, rg_w_a, rg_b_a, rg_w_x, rg_b_x,
              rg_lambda, dn_a_log, dn_dt_bias, dn_norm_w, w_out, mix_norm_post,
              ffn2_norm_pre, ffn2_w_up, ffn2_w_down, ffn2_norm_post, final_norm):

    def run(x, conv_st, rg_st, dn_st):
        new_conv, new_rg, new_dn = [], [], []
        for l in range(DEPTH):
            h = swiglu(rmsnorm(x, ffn1_norm_pre[l]), ffn1_w_up[l], ffn1_w_down[l])
            x = x + 0.5 * rmsnorm(h, ffn1_norm_post[l])
            h, cb, rh, S = mixer(rmsnorm(x, mix_norm_pre[l]), conv_st[l], rg_st[l], dn_st[l],
                                 w_in[l], conv_w[l], conv_b_rg[l], rg_w_a[l], rg_b_a[l],
                                 rg_w_x[l], rg_b_x[l], rg_lambda[l], dn_a_log[l],
                                 dn_dt_bias[l], dn_norm_w[l], w_out[l])
            x = x + rmsnorm(h, mix_norm_post[l])
            h = swiglu(rmsnorm(x, ffn2_norm_pre[l]), ffn2_w_up[l], ffn2_w_down[l])
            x = x + 0.5 * rmsnorm(h, ffn2_norm_post[l])
            new_conv.append(cb.astype(conv_st.dtype))
            new_rg.append(rh.astype(rg_st.dtype))
            new_dn.append(S.astype(dn_st.dtype))
        y = rmsnorm(x, final_norm)
        return y, jnp.stack(new_conv), jnp.stack(new_rg), jnp.stack(new_dn)

    dt = x_prompt.dtype
    zero_conv = jnp.zeros((DEPTH, BATCH, CONV_WIDTH - 1, CONV_CH), dt)
    zero_rg = jnp.zeros((DEPTH, BATCH, RG_WIDTH), dt)
    zero_dn = jnp.zeros((DEPTH, BATCH, DN_HEADS, DN_HEAD_DIM, DN_HEAD_DIM), dt)
    y_prompt, conv_p, rg_p, dn_p = run(x_prompt, zero_conv, zero_rg, zero_dn)
    y_sample, conv_s, rg_s, dn_s = run(x_sample, state_conv, state_rglru, state_delta)
    return (y_prompt, y_sample, conv_p, rg_p, dn_p, conv_s, rg_s, dn_s)
```

```python
import numpy as np
from contextlib import ExitStack
import concourse.bass as bass
import concourse.mybir as mybir
from concourse.bass_utils import run_bass_kernel_spmd

F32 = mybir.dt.float32
F32R = mybir.dt.float32r
BF16 = mybir.dt.bfloat16
ACTF = mybir.ActivationFunctionType
ALU = mybir.AluOpType

D = 1024
KC = 8
DFF = 2816
NFF = 22
DEPTH = 2
SEQ = 2048
NS = 16
DS = 8
INC = 3080
CONVC = 2048
EPS = 1e-6
NCORES = 8


class Op:
    __slots__ = ("eng", "fn", "deps", "sig", "count", "sem", "is_dma", "idx")

    def __init__(self, eng, fn, is_dma=False):
        self.eng = eng
        self.fn = fn
        self.deps = []
        self.sig = False
        self.count = None
        self.sem = None
        self.is_dma = is_dma
        self.idx = None


class Prog:
    ENGS = ("pe", "act", "dve", "pool", "sp")
    NDMASEM = 8

    def __init__(self, nc, same_engine_sync=True):
        self.nc = nc
        self.ops = {e: [] for e in self.ENGS}
        self.last_w = {}
        self.readers = {}
        self.same_engine_sync = same_engine_sync
        self.dma_n = {"sp": 0, "pool": 0}
        self.dma_hist = {"sp": [], "pool": []}
        self.n_ops = 0

    def _collect(self, op, reads, writes):
        deps = []
        for (n, i) in reads:
            lw = self.last_w.get(n)
            if lw:
                if i is None:
                    deps.extend(lw.values())
                else:
                    if i in lw:
                        deps.append(lw[i])
                    if None in lw:
                        deps.append(lw[None])
        for (n, i) in writes:
            lw = self.last_w.get(n)
            rd = self.readers.get(n)
            if lw:
                if i is None:
                    deps.extend(lw.values())
                else:
                    if i in lw:
                        deps.append(lw[i])
                    if None in lw:
                        deps.append(lw[None])
            if rd:
                if i is None:
                    for v in rd.values():
                        deps.extend(v)
                else:
                    deps.extend(rd.get(i, ()))
                    deps.extend(rd.get(None, ()))
        for (n, i) in writes:
            lw = self.last_w.setdefault(n, {})
            rd = self.readers.setdefault(n, {})
            if i is None:
                lw.clear()
                rd.clear()
                lw[None] = op
            else:
                lw[i] = op
                rd.pop(i, None)
        for (n, i) in reads:
            self.readers.setdefault(n, {}).setdefault(i, []).append(op)
        seen = set()
        for d in deps:
            if d is op or id(d) in seen:
                continue
            seen.add(id(d))
            if d.eng == op.eng and not d.is_dma and not op.is_dma:
                if op.eng == "pe" or not self.same_engine_sync:
                    continue
            op.deps.append(d)
            d.sig = True

    def op(self, eng, fn, reads=(), writes=()):
        o = Op(eng, fn)
        self._collect(o, list(reads), list(writes))
        self.ops[eng].append(o)
        self.n_ops += 1
        return o

    def dma(self, q, out, in_, reads=(), writes=()):
        o = Op(q, lambda e: e.dma_start(out=out, in_=in_), is_dma=True)
        n = self.dma_n[q]
        self.dma_n[q] += 1
        o.idx = n
        self._collect(o, list(reads), list(writes))
        hist = self.dma_hist[q]
        if n >= self.NDMASEM:
            o.deps.append(hist[n - self.NDMASEM])
        hist.append(o)
        o.sig = True
        self.ops[q].append(o)
        self.n_ops += 1
        return o

    def emit(self, final_wait_ops):
        nc = self.nc
        with ExitStack() as es:
            esem = {e: es.enter_context(nc.semaphore("prog_" + e)) for e in self.ENGS}
            dsem = {q: [es.enter_context(nc.semaphore("dma_%s_%d" % (q, i))) for i in range(self.NDMASEM)]
                    for q in ("sp", "pool")}
            for e in self.ENGS:
                c = 0
                for o in self.ops[e]:
                    if o.is_dma:
                        slot = o.idx % self.NDMASEM
                        o.sem = dsem[e][slot]
                        o.count = 16 * (o.idx // self.NDMASEM + 1)
                    elif o.sig:
                        c += 1
                        o.sem = esem[e]
                        o.count = c
            block = es.enter_context(nc.Block())

            def run(ename, e, extra_final=None):
                known = {}
                for o in self.ops[ename]:
                    need = {}
                    for d in o.deps:
                        key = id(d.sem)
                        if known.get(key, 0) >= d.count:
                            continue
                        if key not in need or need[key][1] < d.count:
                            need[key] = (d.sem, d.count)
                    for key, (s, v) in need.items():
                        e.wait_ge(s, v)
                        known[key] = v
                    ins = o.fn(e)
                    if o.is_dma:
                        ins.then_inc(o.sem, 16)
                    elif o.sig:
                        ins.then_inc(o.sem, 1)
                if extra_final:
                    need = {}
                    for d in extra_final:
                        key = id(d.sem)
                        if key not in need or need[key][1] < d.count:
                            need[key] = (d.sem, d.count)
                    for key, (s, v) in need.items():
                        e.wait_ge(s, v)

            @block.tensor
            def _(e):
                run("pe", e)

            @block.scalar
            def _(e):
                run("act", e)

            @block.vector
            def _(e):
                run("dve", e)

            @block.gpsimd
            def _(e):
                run("pool", e)

            @block.sync
            def _(e):
                run("sp", e, extra_final=final_wait_ops)


RGW = 512
NH = 4
HD = 128
NLV_P = 7
NLV_S = 3

C32 = {"ident": 0, "ones": 128, "tri_p": 256, "up_p": 384, "tri_s": 512, "up_s": 640}
N32 = 768
C16 = {"identb": 0, "mstrict_p": 128, "minclt_p": 256, "mstrict_s": 384, "minclt_s": 512}
for _i in range(NLV_P):
    C16["lv_p%d" % _i] = 640 + 128 * _i
C16["segcol"] = 640 + 128 * NLV_P
C16["segrow"] = C16["segcol"] + 16
N16 = C16["segrow"] + 16 * 128


def _consts():
    i = np.arange(128)[:, None]
    j = np.arange(128)[None, :]
    c32 = np.zeros((128, N32), np.float32)
    c32[:, 0:128] = np.eye(128)
    c32[:, 128:256] = 1.0
    seg = 8
    same_s = (i // seg) == (j // seg)
    c32[:, 256:384] = (i <= j)
    c32[:, 384:512] = (i > j)
    c32[:, 512:640] = (i <= j) & same_s
    c32[:, 640:768] = (i > j) & same_s
    c16 = np.zeros((128, N16), np.float32)
    c16[:, 0:128] = np.eye(128)
    c16[:, 128:256] = (i > j)
    c16[:, 256:384] = (j >= i)
    c16[:, 384:512] = (i > j) & same_s
    c16[:, 512:640] = (j >= i) & same_s
    for lv in range(NLV_P):
        b = 1 << lv
        m = ((i // (2 * b)) == (j // (2 * b))) & (((j // b) % 2) == 1) & (((i // b) % 2) == 0)
        c16[:, C16["lv_p%d" % lv]:C16["lv_p%d" % lv] + 128] = m
    c16[:, C16["segcol"]:C16["segcol"] + 16] = (np.arange(128)[:, None] // seg) == np.arange(16)[None, :]
    sr = (np.arange(16)[:, None] == (np.arange(128)[None, :] // seg)).astype(np.float32).reshape(1, 16 * 128)
    c16[:, C16["segrow"]:] = np.repeat(sr, 128, axis=0)
    return c32, c16


PL = {}
_o = 0
for _nm, _n in (("ffn1_norm_pre", 8), ("ffn1_norm_post", 8), ("mix_norm_pre", 8), ("mix_norm_post", 8),
                ("ffn2_norm_pre", 8), ("ffn2_norm_post", 8), ("final_norm", 8),
                ("conv_w", 64), ("conv_b_rg", 4), ("rg_b_a", 4), ("rg_b_x", 4), ("rg_lambda", 4),
                ("dn_a_log", 4), ("dn_dt_bias", 4), ("dn_norm_w", 1)):
    PL[_nm] = _o
    _o += _n
PP_LAYER = _o


def _pack_params(inp):
    cols = []

    def vecn(v, n):
        return np.ascontiguousarray(np.asarray(v).reshape(n, 128).T)

    for l in range(DEPTH):
        for nm in ("ffn1_norm_pre", "ffn1_norm_post", "mix_norm_pre", "mix_norm_post",
                   "ffn2_norm_pre", "ffn2_norm_post"):
            cols.append(vecn(inp[nm][l], 8))
        cols.append(vecn(inp["final_norm"], 8))
        cw = np.asarray(inp["conv_w"][l])
        cols.append(np.ascontiguousarray(cw.reshape(4, 16, 128).transpose(2, 1, 0).reshape(128, 64)))
        for nm in ("conv_b_rg", "rg_b_a", "rg_b_x", "rg_lambda"):
            cols.append(vecn(inp[nm][l], 4))
        cols.append(np.repeat(np.asarray(inp["dn_a_log"][l]).reshape(1, 4), 128, axis=0))
        cols.append(np.repeat(np.asarray(inp["dn_dt_bias"][l]).reshape(1, 4), 128, axis=0))
        cols.append(np.asarray(inp["dn_norm_w"][l]).reshape(128, 1))
    return np.ascontiguousarray(np.concatenate(cols, axis=1).astype(np.float32))


def _pack_rgw(inp):
    out = np.zeros((DEPTH, 2, 128, 4, 128), np.float32)
    for l in range(DEPTH):
        for gi, nm in enumerate(("rg_w_a", "rg_w_x")):
            w = np.asarray(inp[nm][l])
            for c in range(4):
                for hh in range(2):
                    out[l, gi, hh * 64:(hh + 1) * 64, c, hh * 64:(hh + 1) * 64] = w[2 * c + hh]
    return out


class Cfg:
    def __init__(self, **kw):
        self.groups = [(0, 640, True), (640, 768, False), (1408, 640, False)]
        self.stages = "full"
        self.same_engine_sync = True
        self.__dict__.update(kw)


def blocks_of(T):
    if T == 768:
        return [(0, 384), (384, 384)]
    if T == 640:
        return [(0, 384), (384, 256)]
    raise ValueError(T)


def build_program(cfg):
    nc = bass.Bass("TRN2", target_bir_lowering=False)
    TM = 768
    NTM = TM // 128
    dr = {}

    def din(name, shape, dt=F32):
        dr[name] = nc.dram_tensor(name, shape, dt, kind="ExternalInput").ap()

    def dout(name, shape):
        dr[name] = nc.dram_tensor(name, shape, F32, kind="ExternalOutput").ap()

    din("xp", [SEQ, D]); din("xs", [NS * DS, D])
    din("pp", [128, DEPTH * PP_LAYER]); din("c32", [128, N32]); din("c16", [128, N16])
    din("rgw_r", [DEPTH, 2, 128, 4, 128], F32R)
    din("st_conv", [DEPTH, NS * 3, CONVC]); din("st_rg", [DEPTH, NS, RGW]); din("st_dn", [DEPTH, NS, NH, HD, HD])
    for nm in ("ffn1_w_up", "ffn2_w_up"):
        din(nm, [DEPTH, D, 2 * DFF], F32R)
    for nm in ("ffn1_w_down", "ffn2_w_down"):
        din(nm, [DEPTH, DFF, D], F32R)
    din("w_in", [DEPTH, D, INC], F32R); din("w_out", [DEPTH, D, D], F32R)
    dout("yp", [SEQ, D]); dout("ys", [NS * DS, D])
    dout("ncp", [DEPTH, 3, CONVC]); dout("nrp", [DEPTH, RGW]); dout("ndp", [DEPTH, NH, HD, HD])
    if getattr(cfg, "debug", False):
        dout("dbg", [3, 128, KC, TM])
    dout("ncs", [DEPTH, NS * 3, CONVC]); dout("nrs", [DEPTH, NS, RGW]); dout("nds", [DEPTH, NS, NH, HD, HD])

    P = Prog(nc, same_engine_sync=cfg.same_engine_sync)
    es = ExitStack()
    sb = lambda name, shape, dt: es.enter_context(nc.sbuf_tensor(name, shape, dt))
    X = sb("X", [128, KC, TM], F32)
    H = sb("H", [128, KC, TM], F32R)
    Y = sb("Y", [128, KC, TM], F32R)
    YF = Y[:].bitcast(F32)
    HF = H[:].bitcast(F32)
    NFB = 4
    ACTB = sb("ACTB", [128, NFB, TM], F32R)
    NWA = 2
    WA = [sb("WA%d" % i, [128, KC, 256], F32R) for i in range(NWA)]
    WB = [sb("WB%d" % i, [128, NFB, 256], F32R) for i in range(2)]
    PPt = sb("PPt", [128, DEPTH * PP_LAYER], F32)
    C32t = sb("C32t", [128, N32], F32)
    C16t = sb("C16t", [128, N16], BF16)
    ONESR = sb("ONESR", [128, 128], F32R)
    EPST = sb("EPST", [128, 8], F32)
    TT = sb("TT", [128, 4, TM], F32)
    SQ = sb("SQ", [128, 4, 384], F32R)
    RSTD = sb("RSTD", [128, 512], F32)
    SG = [TT[:, 0, 0:384], TT[:, 1, 0:384]]
    ZG = sb("ZG", [128, 4, TM], BF16)
    QKV = sb("QKV", [128, 12, TM], BF16)
    XR = sb("XR", [128, TM], F32R)
    XP = sb("XP", [128, 3 + TM], F32)
    XPS = sb("XPS", [128, NS, 11], F32)
    CSS = sb("CSS", [128, 16, NS * 3], F32)
    HS0 = sb("HS0", [128, 4, NS], F32)
    CONVT = sb("CONVT", [128, DEPTH, 16, 3], F32)
    HRG = sb("HRG", [128, DEPTH, 4], F32)
    SST = sb("SST", [128, DEPTH, NH, HD], F32)
    RGWt = sb("RGWt", [128, 2, 4, 128], F32R)
    WSC = sb("WSC", [128, KC, 8], F32R)
    SCT = sb("SCT", [128, NTM, 8], F32)
    LAYC = sb("LAYC", [128, 16], F32)
    ONE1 = sb("ONE1", [128, 1], F32)
    dt_names_bf = ["KTOK", "VTOK", "AM", "ATT", "DT", "DINV", "N1", "BV", "BKG", "WT", "VN", "QG", "KD", "SBF"]
    DB = {n: sb(n, [128, NH, 128], BF16) for n in dt_names_bf if n not in ("VN", "SBF", "WT")}
    DB["VN"] = DB["N1"]; DB["SBF"] = DB["DINV"]; DB["WT"] = DB["AM"]
    GRW = sb("GRW", [128, NH, 128], F32)
    EGR = sb("EGR", [128, NH, 128], F32)
    TRG = sb("TRG", [128, NH, 128], F32)
    OT = sb("OT", [128, NH, 128], F32)
    SM = sb("SM", [128, 64], F32)
    CSL = XP[0:48, 0:512]
    RSL = XP[0:16, 0:512]
    PS = [es.enter_context(nc.psum_tensor("PS%d" % i, [128, 512], F32)) for i in range(8)]

    def c32(nm):
        return C32t[:, C32[nm]:C32[nm] + 128]

    def c16(nm, n=128):
        return C16t[:, C16[nm]:C16[nm] + n]

    ident = c32("ident")
    ones_f = c32("ones")
    ones_r = ONESR[:, :]
    identb = c16("identb")
    st = {"ps": 0, "wa": 0, "wb": 0, "stg": 0, "sg": 0}

    resv = set()

    def psum():
        while True:
            i = st["ps"]
            st["ps"] = (i + 1) % 8
            if i not in resv:
                return i

    def psb(pi):
        return PS[pi][:].bitcast(BF16)

    def ACT(out, in_, func, reads, writes, **kw):
        return P.op("act", lambda e: e.activation(out=out, in_=in_, func=func, **kw), reads, writes)

    def CP(eng, out, in_, reads, writes):
        if eng == "act":
            return P.op("act", lambda e: e.copy(out=out, in_=in_), reads, writes)
        return P.op(eng, lambda e: e.tensor_copy(out=out, in_=in_), reads, writes)

    def TTo(eng, out, in0, in1, op, reads, writes):
        return P.op(eng, lambda e: e.tensor_tensor(out=out, in0=in0, in1=in1, op=op), reads, writes)

    def TS(eng, out, in0, s1, s2, op0, op1, reads, writes):
        if op1 is None:
            return P.op(eng, lambda e: e.tensor_scalar(out=out, in0=in0, scalar1=s1, scalar2=None, op0=op0), reads, writes)
        return P.op(eng, lambda e: e.tensor_scalar(out=out, in0=in0, scalar1=s1, scalar2=s2, op0=op0, op1=op1), reads, writes)

    def STT(out, in0, scalar, in1, op0, op1, reads, writes):
        return P.op("dve", lambda e: e.scalar_tensor_tensor(out=out, in0=in0, scalar=scalar, in1=in1, op0=op0, op1=op1), reads, writes)

    def MM(out, lhsT, rhs, start, stop, reads, writes):
        return P.op("pe", lambda e: e.matmul(out, lhsT=lhsT, rhs=rhs, start=start, stop=stop, skip_group_check=True), reads, writes)

    def TR(out, in_, idn, reads, writes):
        return P.op("pe", lambda e: e.transpose(out=out, in_=in_, identity=idn), reads, writes)

    P.dma("sp", PPt[:], dr["pp"][:, :], writes=[("PP", None)])
    P.dma("sp", C32t[:], dr["c32"][:, :], writes=[("C32", None)])
    for i in range(0, N16, 768):
        n = min(768, N16 - i)
        P.dma("sp", TT[:, 0, 0:n], dr["c16"][:, i:i + n], writes=[("TT", 0)])
        CP("dve", C16t[:, i:i + n], TT[:, 0, 0:n], [("TT", 0)], [("C16", None)])
    CP("dve", ones_r, ones_f, [("C32", None)], [("ONESR", None)])
    EPSC = {}
    for i, (key, val) in enumerate([((D, 1.0), EPS), ((D, 0.5), 4.0 * EPS), ((128, 1.0), EPS), ((1, 1.0), EPS), ("q", 128.0 * EPS)]):
        EPSC[key] = EPST[:, i:i + 1]
        P.op("pool", lambda e, i=i, val=val: e.memset(EPST[:, i:i + 1], val), writes=[("EPST", i)])
    P.op("pool", lambda e: e.memset(ONE1[:, :], 1.0), writes=[("ONE1", None)])
    P.op("pool", lambda e: e.memset(CONVT[:].rearrange("p l c t -> p (l c t)"), 0.0), writes=[("CONVT", None)])
    P.op("pool", lambda e: e.memset(HRG[:].rearrange("p l c -> p (l c)"), 0.0), writes=[("HRG", None)])
    P.op("pool", lambda e: e.memset(SST[:].rearrange("p l h e -> p (l h e)"), 0.0), writes=[("SST", None)])

    out_ops = []

    def pcol(l, nm, j=0, n=1):
        c = l * PP_LAYER + PL[nm] + j
        return PPt[:, c:c + n]

    ngroups = len(cfg.groups)
    for gi, (p0, npr, smp) in enumerate(cfg.groups):
        T = npr + (128 if smp else 0)
        ntile = T // 128
        blks = blocks_of(T)
        last_group = (gi == ngroups - 1)

        def tiles_of(b0, bn):
            return list(range(b0 // 128, (b0 + bn + 127) // 128))

        def rk(name, kcs, b0, bn):
            if isinstance(kcs, int):
                kcs = [kcs]
            return [(name, (kc, t)) for kc in kcs for t in tiles_of(b0, bn)]

        def stg_view(si):
            return TT[:, 2 * si:2 * si + 2, :].rearrange("p a t -> p (a t)")[:, 0:D]

        def stg_keys(si):
            return [("TT", 2 * si), ("TT", 2 * si + 1)]

        for t in range(ntile):
            si = st["stg"]; st["stg"] ^= 1
            stg = stg_view(si)
            src = dr["xp"][p0 + t * 128: p0 + (t + 1) * 128, :] if t * 128 < npr else dr["xs"][:, :]
            P.dma("sp", stg, src, writes=stg_keys(si))
            for half in range(2):
                pi = psum()
                for j in range(4):
                    kc = half * 4 + j
                    TR(PS[pi][:, j * 128:(j + 1) * 128], stg[:, kc * 128:(kc + 1) * 128], ident,
                       stg_keys(si) + [("C32", None)], [("PS", pi)])
                CP("act" if half == 0 else "dve", X[:, half * 4:(half + 1) * 4, t * 128:(t + 1) * 128],
                   PS[pi][:].rearrange("p (j c) -> p j c", j=4), [("PS", pi)], [("X", (half * 4 + j, t)) for j in range(4)])

        def sumsq_rstd(SRC, srcname, b0, bn, nfeat, post_scale=1.0, nk=KC):
            pi = psum()
            for kc in range(nk):
                rd = rk(srcname, kc, b0, bn)
                if kc % 2 == 0:
                    ACT(SQ[:, kc % 4, 0:bn], SRC[:, kc, b0:b0 + bn], ACTF.Square, rd, [("SQ", kc % 4)])
                else:
                    TTo("pool", SQ[:, kc % 4, 0:bn], SRC[:, kc, b0:b0 + bn], SRC[:, kc, b0:b0 + bn], ALU.mult, rd, [("SQ", kc % 4)])
                MM(PS[pi][:, 0:bn], ones_r, SQ[:, kc % 4, 0:bn], kc == 0, kc == nk - 1,
                   [("SQ", kc % 4), ("ONESR", None)], [("PS", pi)])
            rstd_from_psum(pi, bn, nfeat, post_scale)

        def rstd_from_psum(pi, bn, nfeat, post_scale=1.0):
            ACT(RSTD[:, 0:bn], PS[pi][:, 0:bn], ACTF.Sqrt, [("PS", pi), ("EPST", None)], [("RSTD", None)],
                scale=1.0 / (nfeat * post_scale * post_scale), bias=EPSC[(nfeat, post_scale)])
            P.op("dve", lambda e: e.reciprocal(out=RSTD[:, 0:bn], in_=RSTD[:, 0:bn]), [("RSTD", None)], [("RSTD", None)])

        def norm_scale(DST, dstname, SRC, srcname, l, gname, b0, bn):
            for kc in range(KC):
                STT(DST[:, kc, b0:b0 + bn], SRC[:, kc, b0:b0 + bn], pcol(l, gname, kc), RSTD[:, 0:bn], ALU.mult, ALU.mult,
                    rk(srcname, kc, b0, bn) + [("RSTD", None), ("PP", None)], rk(dstname, kc, b0, bn))

        def prenorm_to_H(l, gname):
            for (b0, bn) in blks:
                sumsq_rstd(X, "X", b0, bn, D)
                norm_scale(H, "H", X, "X", l, gname, b0, bn)

        def postnorm_residual(SRCw, SRC, srcname, l, gname, scale):
            for (b0, bn) in blks:
                sumsq_rstd(SRC, srcname, b0, bn, D, post_scale=scale)
                norm_scale(SRCw, srcname, SRC, srcname, l, gname, b0, bn)
                for kc in range(KC):
                    TTo("pool", X[:, kc, b0:b0 + bn], X[:, kc, b0:b0 + bn], SRC[:, kc, b0:b0 + bn], ALU.add,
                        rk(srcname, kc, b0, bn) + rk("X", kc, b0, bn), rk("X", kc, b0, bn))

        def ffn(l, which):
            wup = dr["ffn%d_w_up" % which][l].rearrange("(kc p) n -> p kc n", p=128)
            wdn = dr["ffn%d_w_down" % which][l].rearrange("(c p) n -> p c n", p=128)
            prenorm_to_H(l, "ffn%d_norm_pre" % which)
            fblocks = [(0, 4), (4, 4), (8, 4), (12, 4), (16, 4), (20, 2)]
            for fbi, (c0, nch) in enumerate(fblocks):
                for cp in range(0, nch, 2):
                    wi_g = st["wa"]; st["wa"] = (st["wa"] + 1) % NWA
                    wi_u = st["wa"]; st["wa"] = (st["wa"] + 1) % NWA
                    col = (c0 + cp) * 128
                    P.dma("pool", WA[wi_g][:], wup[:, :, col:col + 256], writes=[("WA", wi_g)])
                    P.dma("pool", WA[wi_u][:], wup[:, :, DFF + col:DFF + col + 256], writes=[("WA", wi_u)])
                    for cc in range(2):
                        pg = [psum() for _ in blks]
                        for kc in range(KC):
                            for bi, (b0, bn) in enumerate(blks):
                                MM(PS[pg[bi]][:, 0:bn], WA[wi_g][:, kc, cc * 128:(cc + 1) * 128], H[:, kc, b0:b0 + bn],
                                   kc == 0, kc == KC - 1, [("WA", wi_g)] + rk("H", kc, b0, bn), [("PS", pg[bi])])
                        sgs = []
                        for bi, (b0, bn) in enumerate(blks):
                            si = st["sg"]; st["sg"] = (st["sg"] + 1) % len(SG)
                            sgs.append(si)
                            ACT(SG[si][:, 0:bn], PS[pg[bi]][:, 0:bn], ACTF.Silu, [("PS", pg[bi])], [("TT", si)])
                        pu = [psum() for _ in blks]
                        for kc in range(KC):
                            for bi, (b0, bn) in enumerate(blks):
                                MM(PS[pu[bi]][:, 0:bn], WA[wi_u][:, kc, cc * 128:(cc + 1) * 128], H[:, kc, b0:b0 + bn],
                                   kc == 0, kc == KC - 1, [("WA", wi_u)] + rk("H", kc, b0, bn), [("PS", pu[bi])])
                        for bi, (b0, bn) in enumerate(blks):
                            TTo("dve", ACTB[:, cp + cc, b0:b0 + bn], PS[pu[bi]][:, 0:bn], SG[sgs[bi]][:, 0:bn], ALU.mult,
                                [("PS", pu[bi]), ("TT", sgs[bi])], [("ACTB", (cp + cc, b0))])
                for ocp in range(4):
                    wi = st["wb"]; st["wb"] ^= 1
                    P.dma("pool", WB[wi][:, 0:nch, :], wdn[:, c0:c0 + nch, ocp * 256:(ocp + 1) * 256], writes=[("WB", wi)])
                    for o2 in range(2):
                        oc = ocp * 2 + o2
                        pd = [psum() for _ in blks]
                        for j in range(nch):
                            for bi, (b0, bn) in enumerate(blks):
                                MM(PS[pd[bi]][:, 0:bn], WB[wi][:, j, o2 * 128:(o2 + 1) * 128], ACTB[:, j, b0:b0 + bn],
                                   j == 0, j == nch - 1, [("WB", wi), ("ACTB", (j, b0))], [("PS", pd[bi])])
                        for bi, (b0, bn) in enumerate(blks):
                            if fbi == 0:
                                CP("act", Y[:, oc, b0:b0 + bn], PS[pd[bi]][:, 0:bn], [("PS", pd[bi])], rk("Y", oc, b0, bn))
                            else:
                                TTo("dve", Y[:, oc, b0:b0 + bn], PS[pd[bi]][:, 0:bn], YF[:, oc, b0:b0 + bn], ALU.add,
                                    [("PS", pd[bi])] + rk("Y", oc, b0, bn), rk("Y", oc, b0, bn))
            postnorm_residual(Y, YF, "Y", l, "ffn%d_norm_post" % which, 0.5)

        MIXR = Y
        allH = [("H", (kc, t)) for kc in range(KC) for t in range(NTM)]
        allACTB = [("ACTB", None)]

        def mixer(l):
            win = dr["w_in"][l].rearrange("(kc p) n -> p kc n", p=128)
            prenorm_to_H(l, "mix_norm_pre")
            ACT(LAYC[:, 0:4], pcol(l, "rg_lambda", 0, 4), ACTF.Exp, [("PP", None)], [("LAYC", 0)], scale=-1.0)
            ACT(LAYC[:, 0:4], LAYC[:, 0:4], ACTF.Ln, [("LAYC", 0), ("ONE1", None)], [("LAYC", 0)], bias=ONE1[:, 0:1])
            TS("dve", LAYC[:, 4:8], LAYC[:, 0:4], -16.0, None, ALU.mult, None, [("LAYC", 0)], [("LAYC", 1)])
            TS("dve", LAYC[:, 0:4], LAYC[:, 0:4], -8.0, None, ALU.mult, None, [("LAYC", 0), ("LAYC", 1)], [("LAYC", 0)])
            ACT(LAYC[:, 8:12], pcol(l, "dn_a_log", 0, 4), ACTF.Exp, [("PP", None)], [("LAYC", 2)])
            TS("dve", LAYC[:, 8:12], LAYC[:, 8:12], -1.0, None, ALU.mult, None, [("LAYC", 2)], [("LAYC", 2)])
            P.dma("pool", RGWt[:], dr["rgw_r"][l].rearrange("g p c m -> p g c m"), writes=[("RGW", None)])
            P.dma("pool", WSC[:], win[:, :, 3072:3080], writes=[("WSC", None)])
            if smp:
                for q in range(4):
                    P.dma("sp", CSL, dr["st_conv"][l][:, q * 512:(q + 1) * 512], writes=[("XP", None)])
                    pi = psum()
                    for j in range(4):
                        TR(PS[pi][:, j * 48:(j + 1) * 48], CSL[:, j * 128:(j + 1) * 128], ident[0:48, 0:48],
                           [("XP", None), ("C32", None)], [("PS", pi)])
                    CP("dve", CSS[:, q * 4:(q + 1) * 4, :], PS[pi][:, 0:192].rearrange("p (j c) -> p j c", j=4),
                       [("PS", pi)], [("CSS", q * 4 + j) for j in range(4)])
                P.dma("sp", RSL, dr["st_rg"][l][:, :], writes=[("XP", None)])
                pi = psum()
                for j in range(4):
                    TR(PS[pi][:, j * 16:(j + 1) * 16], RSL[:, j * 128:(j + 1) * 128], ident[0:16, 0:16],
                       [("XP", None), ("C32", None)], [("PS", pi)])
                CP("dve", HS0[:, :, :], PS[pi][:, 0:64].rearrange("p (j c) -> p j c", j=4), [("PS", pi)], [("HS0", None)])

            wa_state = {}

            def load_pair(colstart):
                wi = st["wa"]; st["wa"] = (st["wa"] + 1) % NWA
                P.dma("pool", WA[wi][:], win[:, :, colstart:colstart + 256], writes=[("WA", wi)])
                return wi

            def proj(wi, cc):
                pb = [psum() for _ in blks]
                for kc in range(KC):
                    for bi, (b0, bn) in enumerate(blks):
                        MM(PS[pb[bi]][:, 0:bn], WA[wi][:, kc, cc * 128:(cc + 1) * 128], H[:, kc, b0:b0 + bn],
                           kc == 0, kc == KC - 1, [("WA", wi)] + rk("H", kc, b0, bn), [("PS", pb[bi])])
                return pb

            def conv_chunk(l, ch, pb, OUT, outkey, bias_col=None):
                CP("act", XP[:, 0:3], CONVT[:, l, ch, :], [("CONVT", (l, ch))], [("XP", 0)])
                for bi, (b0, bn) in enumerate(blks):
                    n = min(bn, npr - b0)
                    if n > 0:
                        CP("act", XP[:, 3 + b0:3 + b0 + n], PS[pb[bi]][:, 0:n], [("PS", pb[bi])], [("XP", 1 + bi)])
                if smp:
                    b0, bn = blks[-1]
                    off = npr - b0
                    CP("pool", XPS[:, :, 0:3], CSS[:, ch, :].rearrange("p (s t) -> p s t", t=3), [("CSS", ch)], [("XPS", 0)])
                    CP("act", XPS[:, :, 3:11], PS[pb[-1]][:, off:off + 128].rearrange("p (s t) -> p s t", t=8),
                       [("PS", pb[-1])], [("XPS", 1)])
                xpk = [("XP", i) for i in range(1 + len(blks))]
                cw = lambda tap: pcol(l, "conv_w", ch * 4 + tap)
                if bias_col is None:
                    TS("dve", OUT[:, 0:npr], XP[:, 0:npr], cw(0), None, ALU.mult, None, xpk + [("PP", None)], [outkey])
                else:
                    TS("dve", OUT[:, 0:npr], XP[:, 0:npr], cw(0), bias_col, ALU.mult, ALU.add, xpk + [("PP", None)], [outkey])
                for tap in range(1, 4):
                    STT(OUT[:, 0:npr], XP[:, tap:tap + npr], cw(tap), OUT[:, 0:npr], ALU.mult, ALU.add,
                        xpk + [("PP", None), outkey], [outkey])
                CP("pool", CONVT[:, l, ch, :], XP[:, npr:npr + 3], xpk, [("CONVT", (l, ch))])
                if smp:
                    OS = OUT[:, npr:npr + 128].rearrange("p (s t) -> p s t", t=8)
                    xk = [("XPS", 0), ("XPS", 1)]
                    if bias_col is None:
                        TS("dve", OS, XPS[:, :, 0:8], cw(0), None, ALU.mult, None, xk + [("PP", None), outkey], [outkey])
                    else:
                        TS("dve", OS, XPS[:, :, 0:8], cw(0), bias_col, ALU.mult, ALU.add, xk + [("PP", None), outkey], [outkey])
                    for tap in range(1, 4):
                        STT(OS, XPS[:, :, tap:tap + 8], cw(tap), OS, ALU.mult, ALU.add, xk + [("PP", None), outkey], [outkey])
                    CP("pool", CSS[:, ch, :].rearrange("p (s t) -> p s t", t=3), XPS[:, :, 8:11], xk, [("CSS", ch)])

            T0 = TT[:, 0, :]; T1 = TT[:, 1, :]; T2 = TT[:, 2, :]
            XRf = XR[:].bitcast(F32)
            k0, k1, k2, k3 = ("TT", 0), ("TT", 1), ("TT", 2), ("XR", None)

            for cpair in range(2):
                wi_x = load_pair(cpair * 256)
                wi_g = load_pair(2048 + cpair * 256)
                for cc in range(2):
                    ch = cpair * 2 + cc
                    pb = proj(wi_x, cc)
                    conv_chunk(l, ch, pb, T2, k2, bias_col=pcol(l, "conv_b_rg", ch))
                    CP("pool", XR[:, 0:T], T2[:, 0:T], [k2], [k3])
                    pa = [psum() for _ in blks]
                    for bi, (b0, bn) in enumerate(blks):
                        MM(PS[pa[bi]][:, 0:bn], RGWt[:, 0, ch, :], XR[:, b0:b0 + bn], True, True, [("RGW", None), k3], [("PS", pa[bi])])
                    for bi, (b0, bn) in enumerate(blks):
                        ACT(T0[:, b0:b0 + bn], PS[pa[bi]][:, 0:bn], ACTF.Sigmoid, [("PS", pa[bi]), ("PP", None)], [k0],
                            bias=pcol(l, "rg_b_a", ch))
                    px = [psum() for _ in blks]
                    for bi, (b0, bn) in enumerate(blks):
                        MM(PS[px[bi]][:, 0:bn], RGWt[:, 1, ch, :], XR[:, b0:b0 + bn], True, True, [("RGW", None), k3], [("PS", px[bi])])
                    for bi, (b0, bn) in enumerate(blks):
                        ACT(T1[:, b0:b0 + bn], PS[px[bi]][:, 0:bn], ACTF.Sigmoid, [("PS", px[bi]), ("PP", None)], [k1],
                            bias=pcol(l, "rg_b_x", ch))
                    ACT(T2[:, 0:T], T0[:, 0:T], ACTF.Exp, [k0, ("LAYC", 1)], [k2], scale=LAYC[:, 4 + ch:5 + ch])
                    ACT(T2[:, 0:T], T2[:, 0:T], ACTF.Relu, [k2, ("ONE1", None)], [k2], scale=-1.0, bias=ONE1[:, 0:1])
                    ACT(T2[:, 0:T], T2[:, 0:T], ACTF.Sqrt, [k2], [k2])
                    ACT(T0[:, 0:T], T0[:, 0:T], ACTF.Exp, [k0, ("LAYC", 0)], [k0], scale=LAYC[:, ch:ch + 1])
                    TTo("dve", T1[:, 0:T], T1[:, 0:T], XRf[:, 0:T], ALU.mult, [k1, k3], [k1])
                    TTo("dve", T1[:, 0:T], T1[:, 0:T], T2[:, 0:T], ALU.mult, [k1, k2], [k1])
                    for bi, (b0, bn) in enumerate(blks):
                        n = min(bn, npr - b0)
                        if n <= 0:
                            continue
                        init = HRG[:, l, ch:ch + 1] if bi == 0 else T2[:, b0 - 1:b0]
                        P.op("dve", lambda e, b0=b0, n=n, init=init: e.tensor_tensor_scan(
                            out=T2[:, b0:b0 + n], data0=T0[:, b0:b0 + n], data1=T1[:, b0:b0 + n],
                            initial=init, op0=ALU.mult, op1=ALU.add),
                            [k0, k1, k2, ("HRG", (l, ch))], [k2])
                    CP("pool", HRG[:, l, ch:ch + 1], T2[:, npr - 1:npr], [k2], [("HRG", (l, ch))])
                    if smp:
                        a_first = T0[:, npr:npr + 128:8]
                        b_first = T1[:, npr:npr + 128:8]
                        TTo("dve", SM[:, 0:16], a_first, HS0[:, ch, :], ALU.mult, [k0, ("HS0", None)], [("SM", None)])
                        TTo("dve", b_first, b_first, SM[:, 0:16], ALU.add, [k1, ("SM", None)], [k1])
                        TS("dve", a_first, a_first, 0.0, None, ALU.mult, None, [k0, ("SM", None)], [k0])
                        P.op("dve", lambda e: e.tensor_tensor_scan(out=T2[:, npr:npr + 128], data0=T0[:, npr:npr + 128],
                                                                  data1=T1[:, npr:npr + 128], initial=0.0, op0=ALU.mult, op1=ALU.add),
                             [k0, k1, k2], [k2])
                        CP("pool", HS0[:, ch, :], T2[:, npr + 7:npr + 128:8], [k2, ("SM", None)], [("HS0", None)])
                    pgt = proj(wi_g, cc)
                    for bi, (b0, bn) in enumerate(blks):
                        CP("act", T0[:, b0:b0 + bn], PS[pgt[bi]][:, 0:bn], [("PS", pgt[bi]), k0], [k0])
                    TTo("pool", T1[:, 0:T], T0[:, 0:T], T0[:, 0:T], ALU.mult, [k0, k1], [k1])
                    TS("pool", T1[:, 0:T], T1[:, 0:T], 0.044715, 1.0, ALU.mult, ALU.add, [k1], [k1])
                    TTo("pool", T1[:, 0:T], T1[:, 0:T], T0[:, 0:T], ALU.mult, [k0, k1], [k1])
                    ACT(T1[:, 0:T], T1[:, 0:T], ACTF.Sigmoid, [k1], [k1], scale=2.0 * 0.7978845608028654)
                    TTo("pool", T0[:, 0:T], T0[:, 0:T], T1[:, 0:T], ALU.mult, [k0, k1], [k0])
                    for bi, (b0, bn) in enumerate(blks):
                        TTo("dve", MIXR[:, ch, b0:b0 + bn], T2[:, b0:b0 + bn], T0[:, b0:b0 + bn], ALU.mult,
                            [k0, k2], rk("Y", ch, b0, bn))

            for kind in range(3):
                for cpair in range(2):
                    wi = load_pair(512 + kind * 512 + cpair * 256)
                    for cc in range(2):
                        hh = cpair * 2 + cc
                        ch = 4 + kind * 4 + hh
                        pb = proj(wi, cc)
                        conv_chunk(l, ch, pb, T0, k0)
                        if kind == 2:
                            ACT(QKV[:, 8 + hh, 0:T], T0[:, 0:T], ACTF.Silu, [k0], [("QKV", None)])
                        else:
                            ACT(T1[:, 0:T], T0[:, 0:T], ACTF.Silu, [k0], [k1])
                            for bi, (b0, bn) in enumerate(blks):
                                ACT(SQ[:, 0, 0:bn], T1[:, b0:b0 + bn], ACTF.Square, [k1], [("SQ", 0)])
                                pi = psum()
                                MM(PS[pi][:, 0:bn], ones_r, SQ[:, 0, 0:bn], True, True, [("SQ", 0), ("ONESR", None)], [("PS", pi)])
                                if kind == 0:
                                    ACT(RSTD[:, 0:bn], PS[pi][:, 0:bn], ACTF.Sqrt, [("PS", pi), ("EPST", None)], [("RSTD", None)],
                                        scale=128.0, bias=EPSC["q"])
                                else:
                                    ACT(RSTD[:, 0:bn], PS[pi][:, 0:bn], ACTF.Sqrt, [("PS", pi), ("EPST", None)], [("RSTD", None)],
                                        scale=1.0, bias=EPSC[(1, 1.0)])
                                P.op("dve", lambda e, bn=bn: e.reciprocal(out=RSTD[:, 0:bn], in_=RSTD[:, 0:bn]), [("RSTD", None)], [("RSTD", None)])
                                TTo("dve", QKV[:, kind * 4 + hh, b0:b0 + bn], T1[:, b0:b0 + bn], RSTD[:, 0:bn], ALU.mult,
                                    [k1, ("RSTD", None)], [("QKV", None)])

            for cpair in range(2):
                wi = load_pair(2560 + cpair * 256)
                for cc in range(2):
                    hh = cpair * 2 + cc
                    pb = proj(wi, cc)
                    for bi, (b0, bn) in enumerate(blks):
                        ACT(ZG[:, hh, b0:b0 + bn], PS[pb[bi]][:, 0:bn], ACTF.Silu, [("PS", pb[bi])], [("ZG", (hh, bi))])

            pi = psum()
            for t in range(ntile):
                for kc in range(KC):
                    MM(PS[pi][:, t * 8:(t + 1) * 8], H[:, kc, t * 128:(t + 1) * 128], WSC[:, kc, :], (t == 0 and kc == 0), kc == KC - 1,
                       [("WSC", None), ("H", (kc, t))], [("PS", pi)])
            CP("dve", SCT[:, 0:ntile, :], PS[pi][:, 0:ntile * 8].rearrange("p (t c) -> p t c", c=8), [("PS", pi)], [("SCT", None)])

            for t in range(ntile):
                delta_tile(l, t, smp and t == ntile - 1)

            if getattr(cfg, "debug", False) and l == 0:
                out_ops.append(P.dma("sp", dr["dbg"][gi], YF, reads=[("Y", None)], writes=[("DBG", gi)]))
            wout = dr["w_out"][l].rearrange("(kc p) n -> p kc n", p=128)
            for ocp in range(4):
                wi = st["wa"]; st["wa"] = (st["wa"] + 1) % NWA
                P.dma("pool", WA[wi][:], wout[:, :, ocp * 256:(ocp + 1) * 256], writes=[("WA", wi)])
                for o2 in range(2):
                    oc = ocp * 2 + o2
                    pd = [psum() for _ in blks]
                    for kc in range(KC):
                        for bi, (b0, bn) in enumerate(blks):
                            MM(PS[pd[bi]][:, 0:bn], WA[wi][:, kc, o2 * 128:(o2 + 1) * 128], MIXR[:, kc, b0:b0 + bn],
                               kc == 0, kc == KC - 1, [("WA", wi)] + rk("Y", kc, b0, bn), [("PS", pd[bi])])
                    for bi, (b0, bn) in enumerate(blks):
                        CP("act", H[:, oc, b0:b0 + bn], PS[pd[bi]][:, 0:bn], [("PS", pd[bi])], rk("H", oc, b0, bn))
            postnorm_residual(H, HF, "H", l, "mix_norm_post", 1.0)

            if smp:
                for q in range(4):
                    pi = psum()
                    for j in range(4):
                        TR(PS[pi][0:48, j * 128:(j + 1) * 128], CSS[:, q * 4 + j, :], ident, [("CSS", q * 4 + j), ("C32", None)], [("PS", pi)])
                    CP("dve", CSL, PS[pi][0:48, :], [("PS", pi)], [("XP", None)])
                    out_ops.append(P.dma("sp", dr["ncs"][l][:, q * 512:(q + 1) * 512], CSL, reads=[("XP", None)], writes=[("CSLo", q)]))
                pi = psum()
                for j in range(4):
                    TR(PS[pi][0:16, j * 128:(j + 1) * 128], HS0[:, j, :], ident, [("HS0", None), ("C32", None)], [("PS", pi)])
                CP("dve", RSL, PS[pi][0:16, :], [("PS", pi)], [("XP", None)])
                out_ops.append(P.dma("sp", dr["nrs"][l][:, :], RSL, reads=[("XP", None)], writes=[("RSLo", 0)]))
            if last_group:
                for q in range(4):
                    pi = psum()
                    for j in range(4):
                        TR(PS[pi][0:3, j * 128:(j + 1) * 128], CONVT[:, l, q * 4 + j, :], ident, [("CONVT", (l, q * 4 + j)), ("C32", None)], [("PS", pi)])
                    CP("dve", CSL[0:3, :], PS[pi][0:3, :], [("PS", pi)], [("XP", None)])
                    out_ops.append(P.dma("sp", dr["ncp"][l][:, q * 512:(q + 1) * 512], CSL[0:3, :], reads=[("XP", None)], writes=[("CSLo", q)]))
                pi = psum()
                TR(PS[pi][0:4, 0:128], HRG[:, l, :], ident, [("HRG", None), ("C32", None)], [("PS", pi)])
                CP("dve", RSL[0:4, 0:128], PS[pi][0:4, 0:128], [("PS", pi)], [("XP", None)])
                out_ops.append(P.dma("sp", dr["nrp"][l].rearrange("(c p) -> c p", p=128), RSL[0:4, 0:128], reads=[("XP", None)], writes=[("RSLo", 0)]))
                out_ops.append(P.dma("sp", dr["ndp"][l].rearrange("h d e -> d h e"), SST[:, l, :, :], reads=[("SST", None)], writes=[("SSTo", l)]))

        def delta_tile(l, t, is_s):
            c0 = t * 128
            sfx = "_s" if is_s else "_p"
            nlv = NLV_S if is_s else NLV_P
            bk = lambda n: [(n, None)]
            KTOK, VTOK, AM, ATT, DTt, DINV, N1, BV, BKG, WT, VN, QG, KD, SBF = [DB[n] for n in dt_names_bf]
            qkv_r = [("QKV", None)]
            bc = lambda ap: ap.unsqueeze(2).broadcast_to([128, NH, 128])
            bm = lambda ap: ap.unsqueeze(1).broadcast_to([128, NH, 128])
            BETA = SM[:, 16:20]; GT = SM[:, 20:24]; GC = SM[:, 24:32]; EG = SM[:, 32:36]; BEG = SM[:, 36:40]; EKD = SM[:, 40:44]
            ACT(BETA, SCT[:, t, 0:4], ACTF.Sigmoid, [("SCT", None)], [("SM", 1)])
            TTo("dve", GT, SCT[:, t, 4:8], pcol(l, "dn_dt_bias", 0, 4), ALU.add, [("SCT", None), ("PP", None)], [("SM", 2)])
            ACT(GT, GT, ACTF.Exp, [("SM", 2)], [("SM", 2)])
            ACT(GT, GT, ACTF.Ln, [("SM", 2), ("ONE1", None)], [("SM", 2)], bias=ONE1[:, 0:1])
            TTo("dve", GT, GT, LAYC[:, 8:12], ALU.mult, [("SM", 2), ("LAYC", 2)], [("SM", 2)])
            TTo("dve", TRG[:, :, :], bm(c32("tri" + sfx)), bc(GT), ALU.mult, [("SM", 2), ("C32", None)], bk("TRG"))
            pgr = psum()
            MM(PS[pgr][:, :], ones_f, TRG[:].rearrange("p h c -> p (h c)"), True, True, bk("TRG") + [("C32", None)], [("PS", pgr)])
            pgc = psum()
            MM(PS[pgc][:, 0:4], c32("tri" + sfx), GT, True, True, [("SM", 2), ("C32", None)], [("PS", pgc)])
            MM(PS[pgc][:, 4:8], c32("up" + sfx), GT, False, True, [("SM", 2), ("C32", None)], [("PS", pgc)])
            CP("dve", GC, PS[pgc][:, 0:8], [("PS", pgc)], [("SM", 3)])
            ACT(EG, GC[:, 0:4], ACTF.Exp, [("SM", 3)], [("SM", 4)])
            ACT(EKD, GC[:, 4:8], ACTF.Exp, [("SM", 3)], [("SM", 5)])
            TTo("dve", BEG, BETA, EG, ALU.mult, [("SM", 1), ("SM", 4)], [("SM", 6)])
            PGR3 = PS[pgr][:].rearrange("p (h c) -> p h c", h=NH)
            TTo("dve", GRW[:, :, :], PGR3, bc(GC[:, 0:4]), ALU.subtract, [("PS", pgr), ("SM", 3)], bk("GRW"))
            GRWf = GRW[:].rearrange("p h c -> p (h c)")
            STT(GRWf, GRWf, -1.0, GRWf, ALU.mult, ALU.max, bk("GRW"), bk("GRW"))
            ACT(GRW[:], GRW[:], ACTF.Exp, bk("GRW"), bk("GRW"), scale=-1.0)
            ACT(EGR[:], PGR3, ACTF.Exp, [("PS", pgr)], bk("EGR"))
            for (src_j, DST, nm) in ((4, KTOK, "KTOK"), (8, VTOK, "VTOK")):
                pi = psum()
                for h in range(NH):
                    TR(psb(pi)[:, h * 128:(h + 1) * 128], QKV[:, src_j + h, c0:c0 + 128], identb, qkv_r + [("C16", None)], [("PS", pi)])
                CP("act", DST[:].rearrange("p h c -> p (h c)"), psb(pi)[:, 0:512], [("PS", pi)], bk(nm))
            TTo("pool", BV[:], VTOK[:], bc(BETA), ALU.mult, bk("VTOK") + [("SM", 1)], bk("BV"))
            TTo("pool", BKG[:], KTOK[:], bc(BEG), ALU.mult, bk("KTOK") + [("SM", 6)], bk("BKG"))
            TTo("pool", KD[:], KTOK[:], bc(EKD), ALU.mult, bk("KTOK") + [("SM", 5)], bk("KD"))
            TTo("dve", QG[:], QKV[:, 0:4, c0:c0 + 128], EGR[:], ALU.mult, qkv_r + bk("EGR"), bk("QG"))
            pkk = psum()
            for h in range(NH):
                MM(PS[pkk][:, h * 128:(h + 1) * 128], QKV[:, 4 + h, c0:c0 + 128], QKV[:, 4 + h, c0:c0 + 128], h == 0, True, qkv_r, [("PS", pkk)])
            pat = psum()
            for h in range(NH):
                MM(PS[pat][:, h * 128:(h + 1) * 128], QKV[:, 4 + h, c0:c0 + 128], QKV[:, h, c0:c0 + 128], h == 0, True, qkv_r, [("PS", pat)])
            TTo("pool", TRG[:], GRW[:], bm(c16("mstrict" + sfx)), ALU.mult, bk("GRW") + [("C16", None)] + bk("TRG"), bk("TRG"))
            TTo("pool", TRG[:], TRG[:], bc(BETA), ALU.mult, bk("TRG") + [("SM", 1)], bk("TRG"))
            TTo("pool", OT[:], GRW[:], bm(c16("minclt" + sfx)), ALU.mult, bk("GRW") + [("C16", None)] + bk("OT"), bk("OT"))
            TTo("dve", AM[:].rearrange("p h c -> p (h c)"), PS[pkk][:, :], TRG[:].rearrange("p h c -> p (h c)"), ALU.mult,
                [("PS", pkk)] + bk("TRG"), bk("AM"))
            TTo("dve", ATT[:].rearrange("p h c -> p (h c)"), PS[pat][:, :], OT[:].rearrange("p h c -> p (h c)"), ALU.mult,
                [("PS", pat)] + bk("OT"), bk("ATT"))
            CP("pool", DTt[:], bm(identb), [("C16", None)] + bk("DT"), bk("DT"))
            CP("pool", DINV[:], bm(identb), [("C16", None)] + bk("DINV"), bk("DINV"))
            for lv in range(nlv):
                p1 = psum()
                for h in range(NH):
                    MM(PS[p1][:, h * 128:(h + 1) * 128], AM[:, h, :], DTt[:, h, :], h == 0, True, bk("AM") + bk("DT"), [("PS", p1)])
                TTo("dve", N1[:], PS[p1][:].rearrange("p (h c) -> p h c", h=NH), bm(c16("lv_p%d" % lv)), ALU.mult,
                    [("PS", p1), ("C16", None)], bk("N1"))
                p2 = psum()
                for h in range(NH):
                    MM(PS[p2][:, h * 128:(h + 1) * 128], DINV[:, h, :], N1[:, h, :], h == 0, True, bk("DINV") + bk("N1"), [("PS", p2)])
                TTo("dve", DTt[:].rearrange("p h c -> p (h c)"), DTt[:].rearrange("p h c -> p (h c)"), PS[p2][:, :], ALU.subtract,
                    [("PS", p2)] + bk("DT"), bk("DT"))
                if lv < nlv - 1:
                    p3 = psum()
                    for h in range(NH):
                        TR(psb(p3)[:, h * 128:(h + 1) * 128], DTt[:, h, :], identb, bk("DT") + [("C16", None)], [("PS", p3)])
                    CP("act", DINV[:].rearrange("p h c -> p (h c)"), psb(p3)[:, 0:512], [("PS", p3)], bk("DINV"))
            pu = psum()
            resv.add(pu)
            for h in range(NH):
                if not is_s:
                    MM(PS[pu][:, h * 128:(h + 1) * 128], DTt[:, h, :], BV[:, h, :], h == 0, False, bk("DT") + bk("BV"), [("PS", pu)])
            pw = psum()
            for h in range(NH):
                MM(PS[pw][:, h * 128:(h + 1) * 128], BKG[:, h, :], DTt[:, h, :], h == 0, True, bk("DT") + bk("BKG"), [("PS", pw)])
            ACT(WT[:].rearrange("p h c -> p (h c)"), PS[pw][:, :], ACTF.Copy, [("PS", pw)], bk("AM"), scale=-1.0)
            po = psum()
            resv.add(po)
            if not is_s:
                CP("pool", SBF[:], SST[:, l, :, :], [("SST", None)] + bk("DINV"), bk("DINV"))
                for h in range(NH):
                    MM(PS[pu][:, h * 128:(h + 1) * 128], WT[:, h, :], SBF[:, h, :], False, True, bk("AM") + bk("DINV"), [("PS", pu)])
                CP("act", VN[:].rearrange("p h c -> p (h c)"), PS[pu][:, :], [("PS", pu)], bk("N1"))
                resv.discard(pu)
                for h in range(NH):
                    MM(PS[po][:, h * 128:(h + 1) * 128], SBF[:, h, :], QG[:, h, :], h == 0, False, bk("DINV") + bk("QG"), [("PS", po)])
                for h in range(NH):
                    MM(PS[po][:, h * 128:(h + 1) * 128], VN[:, h, :], ATT[:, h, :], False, True, bk("N1") + bk("ATT"), [("PS", po)])
                psu = psum()
                for h in range(NH):
                    MM(PS[psu][:, h * 128:(h + 1) * 128], KD[:, h, :], VN[:, h, :], h == 0, True, bk("KD") + bk("N1"), [("PS", psu)])
                for h in range(NH):
                    STT(SST[:, l, h, :], SST[:, l, h, :], EGR[:, h, 127:128], PS[psu][:, h * 128:(h + 1) * 128], ALU.mult, ALU.add,
                        [("PS", psu), ("SST", None)] + bk("EGR"), [("SST", None)])
            else:
                resv.discard(pu)
                HSQ = NS // 2
                SS0 = TT[:, 0:2, :].rearrange("p a t -> p (a t)")[:, 0:HSQ * 128].rearrange("p (s e) -> p s e", s=HSQ)
                SS0B = TT[:, 2, :].bitcast(BF16)[:, 0:HSQ * 128].rearrange("p (s e) -> p s e", s=HSQ)
                WTX = XP[:, 0:512].bitcast(BF16).rearrange("p (s c) -> p s c", s=HSQ)
                kS0 = [("TT", 0), ("TT", 1)]; kS0B = [("TT", 2)]; kWX = [("XP", None)]
                segrow = c16("segrow", NS * 128).rearrange("p (s c) -> p s c", s=NS)
                segcol = c16("segcol", NS)
                first_po = True
                for h in range(NH):
                    for hf in range(2):
                        s0 = hf * HSQ
                        P.dma("sp", SS0, dr["st_dn"][l][s0:s0 + HSQ, h, :, :].rearrange("s d e -> d s e"), reads=[], writes=kS0)
                        CP("pool", SS0B, SS0, kS0 + kS0B, kS0B)
                        TTo("dve", WTX, WT[:, h, :].unsqueeze(1).broadcast_to([128, HSQ, 128]), segrow[:, s0:s0 + HSQ, :], ALU.mult,
                            bk("AM") + [("C16", None)] + kWX, kWX)
                        pu2 = psum()
                        MM(PS[pu2][:, 0:128], DTt[:, h, :], BV[:, h, :], True, False, bk("DT") + bk("BV"), [("PS", pu2)])
                        for s_ in range(HSQ):
                            MM(PS[pu2][:, 0:128], WTX[:, s_, :], SS0B[:, s_, :], False, True, kS0B + kWX, [("PS", pu2)])
                        CP("act", VN[:, h, :], PS[pu2][:, 0:128], [("PS", pu2)] + bk("N1"), bk("N1"))
                        cb = h * 128 + hf * 64
                        MM(PS[po][:, cb:cb + 64], VN[:, h, :], ATT[:, h, hf * 64:hf * 64 + 64], first_po, False, bk("N1") + bk("ATT"), [("PS", po)])
                        first_po = False
                        for s_ in range(HSQ):
                            cs = h * 128 + (s0 + s_) * 8
                            MM(PS[po][:, cs:cs + 8], SS0B[:, s_, :], QG[:, h, (s0 + s_) * 8:(s0 + s_) * 8 + 8], False, True,
                               kS0B + bk("QG"), [("PS", po)])
                        TTo("dve", WTX, KD[:, h, :].unsqueeze(1).broadcast_to([128, HSQ, 128]),
                            segcol[:, s0:s0 + HSQ].unsqueeze(2).broadcast_to([128, HSQ, 128]), ALU.mult,
                            bk("KD") + [("C16", None)] + kWX, kWX)
                        for q in range(2):
                            psu = psum()
                            for j in range(4):
                                s_ = q * 4 + j
                                MM(PS[psu][:, j * 128:(j + 1) * 128], WTX[:, s_, :], VN[:, h, :], j == 0, True, kWX + bk("N1"), [("PS", psu)])
                            for j in range(4):
                                s_ = q * 4 + j
                                sg_ = s0 + s_
                                STT(SS0[:, s_, :], SS0[:, s_, :], EGR[:, h, sg_ * 8 + 7:sg_ * 8 + 8], PS[psu][:, j * 128:(j + 1) * 128],
                                    ALU.mult, ALU.add, [("PS", psu)] + kS0 + bk("EGR"), kS0)
                        out_ops.append(P.dma("sp", dr["nds"][l][s0:s0 + HSQ, h, :, :].rearrange("s d e -> d s e"), SS0, reads=kS0, writes=[("NDSo", h)]))
            resv.discard(po)
            CP("act", OT[:].rearrange("p h c -> p (h c)"), PS[po][:, :], [("PS", po)] + bk("OT"), bk("OT"))
            SQW = SQ[:, 0:2, :].rearrange("p a c -> p (a c)")[:, 0:512]
            ACT(SQW, PS[po][:, :], ACTF.Square, [("PS", po)], [("SQ", 0), ("SQ", 1)])
            pss = psum()
            MM(PS[pss][:, :], ones_r, SQW, True, True, [("SQ", 0), ("SQ", 1), ("ONESR", None)], [("PS", pss)])
            rstd_from_psum(pss, 512, 128, 1.0)
            STT(OT[:].rearrange("p h c -> p (h c)"), OT[:].rearrange("p h c -> p (h c)"), pcol(l, "dn_norm_w"), RSTD[:, 0:512],
                ALU.mult, ALU.mult, bk("OT") + [("RSTD", None), ("PP", None)], bk("OT"))
            TTo("dve", MIXR[:, 4:8, c0:c0 + 128], OT[:], ZG[:, :, c0:c0 + 128], ALU.mult,
                bk("OT") + [("ZG", None)], [("Y", (4 + h, t)) for h in range(NH)])

        for l in range(DEPTH):
            ffn(l, 1)
            mixer(l)
            ffn(l, 2)

        for (b0, bn) in blks:
            sumsq_rstd(X, "X", b0, bn, D)
            norm_scale(Y, "Y", X, "X", 0, "final_norm", b0, bn)
        for t in range(ntile):
            si = st["stg"]; st["stg"] ^= 1
            stg = stg_view(si)
            for half in range(2):
                pi = psum()
                for j in range(4):
                    kc = half * 4 + j
                    TR(PS[pi][:, j * 128:(j + 1) * 128], YF[:, kc, t * 128:(t + 1) * 128], ident, [("Y", (kc, t)), ("C32", None)], [("PS", pi)])
                CP("act" if half == 0 else "dve", stg[:, half * 512:(half + 1) * 512], PS[pi][:], [("PS", pi)], [stg_keys(si)[half]])
            dst = dr["yp"][p0 + t * 128: p0 + (t + 1) * 128, :] if t * 128 < npr else dr["ys"][:, :]
            out_ops.append(P.dma("sp", dst, stg, reads=stg_keys(si), writes=[("STGo", si)]))

    P.emit(out_ops)
    es.close()
    return nc, P


def make_in_maps(inp):
    pp = _pack_params(inp)
    c32, c16 = _consts()
    rgw = _pack_rgw(inp)
    maps = []
    shared = {"pp": pp, "c32": c32, "c16": c16, "rgw_r": rgw}
    for nm in ("ffn1_w_up", "ffn2_w_up", "ffn1_w_down", "ffn2_w_down", "w_in", "w_out"):
        shared[nm] = np.ascontiguousarray(inp[nm])
    for core in range(NCORES):
        sl = slice(core * NS, (core + 1) * NS)
        m = dict(shared)
        m["xp"] = np.ascontiguousarray(inp["x_prompt"][core])
        m["xs"] = np.ascontiguousarray(inp["x_sample"][sl].reshape(NS * DS, D))
        m["st_conv"] = np.ascontiguousarray(inp["state_conv"][:, sl].reshape(DEPTH, NS * 3, CONVC))
        m["st_rg"] = np.ascontiguousarray(inp["state_rglru"][:, sl])
        m["st_dn"] = np.ascontiguousarray(inp["state_delta"][:, sl])
        maps.append(m)
    return maps


def gather(r):
    y_prompt = np.stack([r[c]["yp"] for c in range(NCORES)], axis=0)
    y_sample = np.concatenate([r[c]["ys"].reshape(NS, DS, D) for c in range(NCORES)], axis=0)
    ncp = np.stack([r[c]["ncp"] for c in range(NCORES)], axis=1)
    nrp = np.stack([r[c]["nrp"] for c in range(NCORES)], axis=1)
    ndp = np.stack([r[c]["ndp"] for c in range(NCORES)], axis=1)
    ncs = np.concatenate([r[c]["ncs"].reshape(DEPTH, NS, 3, CONVC) for c in range(NCORES)], axis=1)
    nrs = np.concatenate([r[c]["nrs"] for c in range(NCORES)], axis=1)
    nds = np.concatenate([r[c]["nds"] for c in range(NCORES)], axis=1)
    return (y_prompt, y_sample, ncp, nrp, ndp, ncs, nrs, nds)


def kernel(**inp):
    inp = {k: np.asarray(v) for k, v in inp.items()}
    cfg = Cfg()
    nc, P = build_program(cfg)
    maps = make_in_maps(inp)
    res = run_bass_kernel_spmd(nc, maps, core_ids=list(range(NCORES)))
    return gather(res.results)
```

```python
import numpy as np
from contextlib import ExitStack
import concourse.bass as bass
import concourse.mybir as mybir
from concourse.bass_utils import run_bass_kernel_spmd

F32 = mybir.dt.float32
F32R = mybir.dt.float32r
BF16 = mybir.dt.bfloat16
ACTF = mybir.ActivationFunctionType
ALU = mybir.AluOpType

D = 1024
KC = 8
DFF = 2816
NFF = 22
DEPTH = 2
SEQ = 2048
NS = 16
DS = 8
INC = 3080
CONVC = 2048
EPS = 1e-6
NCORES = 8


class Op:
    __slots__ = ("eng", "fn", "deps", "sig", "count", "sem", "is_dma", "idx")

    def __init__(self, eng, fn, is_dma=False):
        self.eng = eng
        self.fn = fn
        self.deps = []
        self.sig = False
        self.count = None
        self.sem = None
        self.is_dma = is_dma
        self.idx = None


class Prog:
    ENGS = ("pe", "act", "dve", "pool", "sp")
    NDMASEM = 8

    def __init__(self, nc, same_engine_sync=True):
        self.nc = nc
        self.ops = {e: [] for e in self.ENGS}
        self.last_w = {}
        self.readers = {}
        self.same_engine_sync = same_engine_sync
        self.dma_n = {"sp": 0, "pool": 0}
        self.dma_hist = {"sp": [], "pool": []}
        self.n_ops = 0

    def _collect(self, op, reads, writes):
        deps = []
        for (n, i) in reads:
            lw = self.last_w.get(n)
            if lw:
                if i is None:
                    deps.extend(lw.values())
                else:
                    if i in lw:
                        deps.append(lw[i])
                    if None in lw:
                        deps.append(lw[None])
        for (n, i) in writes:
            lw = self.last_w.get(n)
            rd = self.readers.get(n)
            if lw:
                if i is None:
                    deps.extend(lw.values())
                else:
                    if i in lw:
                        deps.append(lw[i])
                    if None in lw:
                        deps.append(lw[None])
            if rd:
                if i is None:
                    for v in rd.values():
                        deps.extend(v)
                else:
                    deps.extend(rd.get(i, ()))
                    deps.extend(rd.get(None, ()))
        for (n, i) in writes:
            lw = self.last_w.setdefault(n, {})
            rd = self.readers.setdefault(n, {})
            if i is None:
                lw.clear()
                rd.clear()
                lw[None] = op
            else:
                lw[i] = op
                rd.pop(i, None)
        for (n, i) in reads:
            self.readers.setdefault(n, {}).setdefault(i, []).append(op)
        seen = set()
        for d in deps:
            if d is op or id(d) in seen:
                continue
            seen.add(id(d))
            if d.eng == op.eng and not d.is_dma and not op.is_dma:
                if op.eng == "pe" or not self.same_engine_sync:
                    continue
            op.deps.append(d)
            d.sig = True

    def op(self, eng, fn, reads=(), writes=()):
        o = Op(eng, fn)
        self._collect(o, list(reads), list(writes))
        self.ops[eng].append(o)
        self.n_ops += 1
        return o

    def dma(self, q, out, in_, reads=(), writes=()):
        o = Op(q, lambda e: e.dma_start(out=out, in_=in_), is_dma=True)
        n = self.dma_n[q]
        self.dma_n[q] += 1
        o.idx = n
        self._collect(o, list(reads), list(writes))
        hist = self.dma_hist[q]
        if n >= self.NDMASEM:
            o.deps.append(hist[n - self.NDMASEM])
        hist.append(o)
        o.sig = True
        self.ops[q].append(o)
        self.n_ops += 1
        return o

    def emit(self, final_wait_ops):
        nc = self.nc
        with ExitStack() as es:
            esem = {e: es.enter_context(nc.semaphore("prog_" + e)) for e in self.ENGS}
            dsem = {q: [es.enter_context(nc.semaphore("dma_%s_%d" % (q, i))) for i in range(self.NDMASEM)]
                    for q in ("sp", "pool")}
            for e in self.ENGS:
                c = 0
                for o in self.ops[e]:
                    if o.is_dma:
                        slot = o.idx % self.NDMASEM
                        o.sem = dsem[e][slot]
                        o.count = 16 * (o.idx // self.NDMASEM + 1)
                    elif o.sig:
                        c += 1
                        o.sem = esem[e]
                        o.count = c
            block = es.enter_context(nc.Block())

            def run(ename, e, extra_final=None):
                known = {}
                for o in self.ops[ename]:
                    need = {}
                    for d in o.deps:
                        key = id(d.sem)
                        if known.get(key, 0) >= d.count:
                            continue
                        if key not in need or need[key][1] < d.count:
                            need[key] = (d.sem, d.count)
                    for key, (s, v) in need.items():
                        e.wait_ge(s, v)
                        known[key] = v
                    ins = o.fn(e)
                    if o.is_dma:
                        ins.then_inc(o.sem, 16)
                    elif o.sig:
                        ins.then_inc(o.sem, 1)
                if extra_final:
                    need = {}
                    for d in extra_final:
                        key = id(d.sem)
                        if key not in need or need[key][1] < d.count:
                            need[key] = (d.sem, d.count)
                    for key, (s, v) in need.items():
                        e.wait_ge(s, v)

            @block.tensor
            def _(e):
                run("pe", e)

            @block.scalar
            def _(e):
                run("act", e)

            @block.vector
            def _(e):
                run("dve", e)

            @block.gpsimd
            def _(e):
                run("pool", e)

            @block.sync
            def _(e):
                run("sp", e, extra_final=final_wait_ops)


RGW = 512
NH = 4
HD = 128
NLV_P = 7
NLV_S = 3

C32 = {"ident": 0, "ones": 128, "tri_p": 256, "up_p": 384, "tri_s": 512, "up_s": 640}
N32 = 768
C16 = {"identb": 0, "mstrict_p": 128, "minclt_p": 256, "mstrict_s": 384, "minclt_s": 512}
for _i in range(NLV_P):
    C16["lv_p%d" % _i] = 640 + 128 * _i
C16["segcol"] = 640 + 128 * NLV_P
C16["segrow"] = C16["segcol"] + 16
N16 = C16["segrow"] + 16 * 128


def _consts():
    i = np.arange(128)[:, None]
    j = np.arange(128)[None, :]
    c32 = np.zeros((128, N32), np.float32)
    c32[:, 0:128] = np.eye(128)
    c32[:, 128:256] = 1.0
    seg = 8
    same_s = (i // seg) == (j // seg)
    c32[:, 256:384] = (i <= j)
    c32[:, 384:512] = (i > j)
    c32[:, 512:640] = (i <= j) & same_s
    c32[:, 640:768] = (i > j) & same_s
    c16 = np.zeros((128, N16), np.float32)
    c16[:, 0:128] = np.eye(128)
    c16[:, 128:256] = (i > j)
    c16[:, 256:384] = (j >= i)
    c16[:, 384:512] = (i > j) & same_s
    c16[:, 512:640] = (j >= i) & same_s
    for lv in range(NLV_P):
        b = 1 << lv
        m = ((i // (2 * b)) == (j // (2 * b))) & (((j // b) % 2) == 1) & (((i // b) % 2) == 0)
        c16[:, C16["lv_p%d" % lv]:C16["lv_p%d" % lv] + 128] = m
    c16[:, C16["segcol"]:C16["segcol"] + 16] = (np.arange(128)[:, None] // seg) == np.arange(16)[None, :]
    sr = (np.arange(16)[:, None] == (np.arange(128)[None, :] // seg)).astype(np.float32).reshape(1, 16 * 128)
    c16[:, C16["segrow"]:] = np.repeat(sr, 128, axis=0)
    return c32, c16


PL = {}
_o = 0
for _nm, _n in (("ffn1_norm_pre", 8), ("ffn1_norm_post", 8), ("mix_norm_pre", 8), ("mix_norm_post", 8),
                ("ffn2_norm_pre", 8), ("ffn2_norm_post", 8), ("final_norm", 8),
                ("conv_w", 64), ("conv_b_rg", 4), ("rg_b_a", 4), ("rg_b_x", 4), ("rg_lambda", 4),
                ("dn_a_log", 4), ("dn_dt_bias", 4), ("dn_norm_w", 1)):
    PL[_nm] = _o
    _o += _n
PP_LAYER = _o


def _pack_params(inp):
    cols = []

    def vecn(v, n):
        return np.ascontiguousarray(np.asarray(v).reshape(n, 128).T)

    for l in range(DEPTH):
        for nm in ("ffn1_norm_pre", "ffn1_norm_post", "mix_norm_pre", "mix_norm_post",
                   "ffn2_norm_pre", "ffn2_norm_post"):
            cols.append(vecn(inp[nm][l], 8))
        cols.append(vecn(inp["final_norm"], 8))
        cw = np.asarray(inp["conv_w"][l])
        cols.append(np.ascontiguousarray(cw.reshape(4, 16, 128).transpose(2, 1, 0).reshape(128, 64)))
        for nm in ("conv_b_rg", "rg_b_a", "rg_b_x", "rg_lambda"):
            cols.append(vecn(inp[nm][l], 4))
        cols.append(np.repeat(np.asarray(inp["dn_a_log"][l]).reshape(1, 4), 128, axis=0))
        cols.append(np.repeat(np.asarray(inp["dn_dt_bias"][l]).reshape(1, 4), 128, axis=0))
        cols.append(np.asarray(inp["dn_norm_w"][l]).reshape(128, 1))
    return np.ascontiguousarray(np.concatenate(cols, axis=1).astype(np.float32))


def _pack_rgw(inp):
    out = np.zeros((DEPTH, 2, 128, 4, 128), np.float32)
    for l in range(DEPTH):
        for gi, nm in enumerate(("rg_w_a", "rg_w_x")):
            w = np.asarray(inp[nm][l])
            for c in range(4):
                for hh in range(2):
                    out[l, gi, hh * 64:(hh + 1) * 64, c, hh * 64:(hh + 1) * 64] = w[2 * c + hh]
    return out


class Cfg:
    def __init__(self, **kw):
        self.groups = [(0, 640, True), (640, 768, False), (1408, 640, False)]
        self.stages = "full"
        self.same_engine_sync = True
        self.__dict__.update(kw)


def blocks_of(T):
    if T == 768:
        return [(0, 384), (384, 384)]
    if T == 640:
        return [(0, 384), (384, 256)]
    raise ValueError(T)


def build_program(cfg):
    nc = bass.Bass("TRN2", target_bir_lowering=False)
    TM = 768
    NTM = TM // 128
    dr = {}

    def din(name, shape, dt=F32):
        dr[name] = nc.dram_tensor(name, shape, dt, kind="ExternalInput").ap()

    def dout(name, shape):
        dr[name] = nc.dram_tensor(name, shape, F32, kind="ExternalOutput").ap()

    din("xp", [SEQ, D]); din("xs", [NS * DS, D])
    din("pp", [128, DEPTH * PP_LAYER]); din("c32", [128, N32]); din("c16", [128, N16])
    din("rgw_r", [DEPTH, 2, 128, 4, 128], F32R)
    din("st_conv", [DEPTH, NS * 3, CONVC]); din("st_rg", [DEPTH, NS, RGW]); din("st_dn", [DEPTH, NS, NH, HD, HD])
    for nm in ("ffn1_w_up", "ffn2_w_up"):
        din(nm, [DEPTH, D, 2 * DFF], F32R)
    for nm in ("ffn1_w_down", "ffn2_w_down"):
        din(nm, [DEPTH, DFF, D], F32R)
    din("w_in", [DEPTH, D, INC], F32R); din("w_out", [DEPTH, D, D], F32R)
    dout("yp", [SEQ, D]); dout("ys", [NS * DS, D])
    dout("ncp", [DEPTH, 3, CONVC]); dout("nrp", [DEPTH, RGW]); dout("ndp", [DEPTH, NH, HD, HD])
    if getattr(cfg, "debug", False):
        dout("dbg", [3, 128, KC, TM])
    dout("ncs", [DEPTH, NS * 3, CONVC]); dout("nrs", [DEPTH, NS, RGW]); dout("nds", [DEPTH, NS, NH, HD, HD])

    P = Prog(nc, same_engine_sync=cfg.same_engine_sync)
    es = ExitStack()
    sb = lambda name, shape, dt: es.enter_context(nc.sbuf_tensor(name, shape, dt))
    X = sb("X", [128, KC, TM], F32)
    H = sb("H", [128, KC, TM], F32R)
    Y = sb("Y", [128, KC, TM], F32R)
    YF = Y[:].bitcast(F32)
    HF = H[:].bitcast(F32)
    NFB = 4
    ACTB = sb("ACTB", [128, NFB, TM], F32R)
    NWA = 2
    WA = [sb("WA%d" % i, [128, KC, 256], F32R) for i in range(NWA)]
    WB = [sb("WB%d" % i, [128, NFB, 256], F32R) for i in range(2)]
    PPt = sb("PPt", [128, DEPTH * PP_LAYER], F32)
    C32t = sb("C32t", [128, N32], F32)
    C16t = sb("C16t", [128, N16], BF16)
    ONESR = sb("ONESR", [128, 128], F32R)
    EPST = sb("EPST", [128, 8], F32)
    TT = sb("TT", [128, 4, TM], F32)
    SQ = sb("SQ", [128, 4, 384], F32R)
    RSTD = sb("RSTD", [128, 512], F32)
    SG = [TT[:, 0, 0:384], TT[:, 1, 0:384]]
    ZG = sb("ZG", [128, 4, TM], BF16)
    QKV = sb("QKV", [128, 12, TM], BF16)
    XR = sb("XR", [128, TM], F32R)
    XP = sb("XP", [128, 3 + TM], F32)
    XPS = sb("XPS", [128, NS, 11], F32)
    CSS = sb("CSS", [128, 16, NS * 3], F32)
    HS0 = sb("HS0", [128, 4, NS], F32)
    CONVT = sb("CONVT", [128, DEPTH, 16, 3], F32)
    HRG = sb("HRG", [128, DEPTH, 4], F32)
    SST = sb("SST", [128, DEPTH, NH, HD], F32)
    RGWt = sb("RGWt", [128, 2, 4, 128], F32R)
    WSC = sb("WSC", [128, KC, 8], F32R)
    SCT = sb("SCT", [128, NTM, 8], F32)
    LAYC = sb("LAYC", [128, 16], F32)
    ONE1 = sb("ONE1", [128, 1], F32)
    dt_names_bf = ["KTOK", "VTOK", "AM", "ATT", "DT", "DINV", "N1", "BV", "BKG", "WT", "VN", "QG", "KD", "SBF"]
    DB = {n: sb(n, [128, NH, 128], BF16) for n in dt_names_bf if n not in ("VN", "SBF", "WT")}
    DB["VN"] = DB["N1"]; DB["SBF"] = DB["DINV"]; DB["WT"] = DB["AM"]
    GRW = sb("GRW", [128, NH, 128], F32)
    EGR = sb("EGR", [128, NH, 128], F32)
    TRG = sb("TRG", [128, NH, 128], F32)
    OT = sb("OT", [128, NH, 128], F32)
    SM = sb("SM", [128, 64], F32)
    CSL = XP[0:48, 0:512]
    RSL = XP[0:16, 0:512]
    PS = [es.enter_context(nc.psum_tensor("PS%d" % i, [128, 512], F32)) for i in range(8)]

    def c32(nm):
        return C32t[:, C32[nm]:C32[nm] + 128]

    def c16(nm, n=128):
        return C16t[:, C16[nm]:C16[nm] + n]

    ident = c32("ident")
    ones_f = c32("ones")
    ones_r = ONESR[:, :]
    identb = c16("identb")
    st = {"ps": 0, "wa": 0, "wb": 0, "stg": 0, "sg": 0}

    resv = set()

    def psum():
        while True:
            i = st["ps"]
            st["ps"] = (i + 1) % 8
            if i not in resv:
                return i

    def psb(pi):
        return PS[pi][:].bitcast(BF16)

    def ACT(out, in_, func, reads, writes, **kw):
        return P.op("act", lambda e: e.activation(out=out, in_=in_, func=func, **kw), reads, writes)

    def CP(eng, out, in_, reads, writes):
        if eng == "act":
            return P.op("act", lambda e: e.copy(out=out, in_=in_), reads, writes)
        return P.op(eng, lambda e: e.tensor_copy(out=out, in_=in_), reads, writes)

    def TTo(eng, out, in0, in1, op, reads, writes):
        return P.op(eng, lambda e: e.tensor_tensor(out=out, in0=in0, in1=in1, op=op), reads, writes)

    def TS(eng, out, in0, s1, s2, op0, op1, reads, writes):
        if op1 is None:
            return P.op(eng, lambda e: e.tensor_scalar(out=out, in0=in0, scalar1=s1, scalar2=None, op0=op0), reads, writes)
        return P.op(eng, lambda e: e.tensor_scalar(out=out, in0=in0, scalar1=s1, scalar2=s2, op0=op0, op1=op1), reads, writes)

    def STT(out, in0, scalar, in1, op0, op1, reads, writes):
        return P.op("dve", lambda e: e.scalar_tensor_tensor(out=out, in0=in0, scalar=scalar, in1=in1, op0=op0, op1=op1), reads, writes)

    def MM(out, lhsT, rhs, start, stop, reads, writes):
        return P.op("pe", lambda e: e.matmul(out, lhsT=lhsT, rhs=rhs, start=start, stop=stop, skip_group_check=True), reads, writes)

    def TR(out, in_, idn, reads, writes):
        return P.op("pe", lambda e: e.transpose(out=out, in_=in_, identity=idn), reads, writes)

    P.dma("sp", PPt[:], dr["pp"][:, :], writes=[("PP", None)])
    P.dma("sp", C32t[:], dr["c32"][:, :], writes=[("C32", None)])
    for i in range(0, N16, 768):
        n = min(768, N16 - i)
        P.dma("sp", TT[:, 0, 0:n], dr["c16"][:, i:i + n], writes=[("TT", 0)])
        CP("dve", C16t[:, i:i + n], TT[:, 0, 0:n], [("TT", 0)], [("C16", None)])
    CP("dve", ones_r, ones_f, [("C32", None)], [("ONESR", None)])
    EPSC = {}
    for i, (key, val) in enumerate([((D, 1.0), EPS), ((D, 0.5), 4.0 * EPS), ((128, 1.0), EPS), ((1, 1.0), EPS), ("q", 128.0 * EPS)]):
        EPSC[key] = EPST[:, i:i + 1]
        P.op("dve", lambda e, i=i, val=val: e.memset(EPST[:, i:i + 1], val), writes=[("EPST", i)])
    P.op("dve", lambda e: e.memset(ONE1[:, :], 1.0), writes=[("ONE1", None)])
    P.op("dve", lambda e: e.memset(CONVT[:].rearrange("p l c t -> p (l c t)"), 0.0), writes=[("CONVT", None)])
    P.op("dve", lambda e: e.memset(HRG[:].rearrange("p l c -> p (l c)"), 0.0), writes=[("HRG", None)])
    P.op("dve", lambda e: e.memset(SST[:].rearrange("p l h e -> p (l h e)"), 0.0), writes=[("SST", None)])

    out_ops = []

    def pcol(l, nm, j=0, n=1):
        c = l * PP_LAYER + PL[nm] + j
        return PPt[:, c:c + n]

    ngroups = len(cfg.groups)
    for gi, (p0, npr, smp) in enumerate(cfg.groups):
        T = npr + (128 if smp else 0)
        ntile = T // 128
        blks = blocks_of(T)
        last_group = (gi == ngroups - 1)

        def tiles_of(b0, bn):
            return list(range(b0 // 128, (b0 + bn + 127) // 128))

        def rk(name, kcs, b0, bn):
            if isinstance(kcs, int):
                kcs = [kcs]
            return [(name, (kc, t)) for kc in kcs for t in tiles_of(b0, bn)]

        def stg_view(si):
            return TT[:, 2 * si:2 * si + 2, :].rearrange("p a t -> p (a t)")[:, 0:D]

        def stg_keys(si):
            return [("TT", 2 * si), ("TT", 2 * si + 1)]

        for t in range(ntile):
            si = st["stg"]; st["stg"] ^= 1
            stg = stg_view(si)
            src = dr["xp"][p0 + t * 128: p0 + (t + 1) * 128, :] if t * 128 < npr else dr["xs"][:, :]
            P.dma("sp", stg, src, writes=stg_keys(si))
            for half in range(2):
                pi = psum()
                for j in range(4):
                    kc = half * 4 + j
                    TR(PS[pi][:, j * 128:(j + 1) * 128], stg[:, kc * 128:(kc + 1) * 128], ident,
                       stg_keys(si) + [("C32", None)], [("PS", pi)])
                CP("act" if half == 0 else "dve", X[:, half * 4:(half + 1) * 4, t * 128:(t + 1) * 128],
                   PS[pi][:].rearrange("p (j c) -> p j c", j=4), [("PS", pi)], [("X", (half * 4 + j, t)) for j in range(4)])

        def sumsq_rstd(SRC, srcname, b0, bn, nfeat, post_scale=1.0, nk=KC):
            pi = psum()
            for kc in range(nk):
                rd = rk(srcname, kc, b0, bn)
                ACT(SQ[:, kc % 4, 0:bn], SRC[:, kc, b0:b0 + bn], ACTF.Square, rd, [("SQ", kc % 4)])
                MM(PS[pi][:, 0:bn], ones_r, SQ[:, kc % 4, 0:bn], kc == 0, kc == nk - 1,
                   [("SQ", kc % 4), ("ONESR", None)], [("PS", pi)])
            rstd_from_psum(pi, bn, nfeat, post_scale)

        def rstd_from_psum(pi, bn, nfeat, post_scale=1.0):
            ACT(RSTD[:, 0:bn], PS[pi][:, 0:bn], ACTF.Sqrt, [("PS", pi), ("EPST", None)], [("RSTD", None)],
                scale=1.0 / (nfeat * post_scale * post_scale), bias=EPSC[(nfeat, post_scale)])
            P.op("dve", lambda e: e.reciprocal(out=RSTD[:, 0:bn], in_=RSTD[:, 0:bn]), [("RSTD", None)], [("RSTD", None)])

        def norm_scale(DST, dstname, SRC, srcname, l, gname, b0, bn):
            for kc in range(KC):
                STT(DST[:, kc, b0:b0 + bn], SRC[:, kc, b0:b0 + bn], pcol(l, gname, kc), RSTD[:, 0:bn], ALU.mult, ALU.mult,
                    rk(srcname, kc, b0, bn) + [("RSTD", None), ("PP", None)], rk(dstname, kc, b0, bn))

        def prenorm_to_H(l, gname):
            for (b0, bn) in blks:
                sumsq_rstd(X, "X", b0, bn, D)
                norm_scale(H, "H", X, "X", l, gname, b0, bn)

        def postnorm_residual(SRCw, SRC, srcname, l, gname, scale):
            for (b0, bn) in blks:
                sumsq_rstd(SRC, srcname, b0, bn, D, post_scale=scale)
                norm_scale(SRCw, srcname, SRC, srcname, l, gname, b0, bn)
                for kc in range(KC):
                    TTo("dve", X[:, kc, b0:b0 + bn], X[:, kc, b0:b0 + bn], SRC[:, kc, b0:b0 + bn], ALU.add,
                        rk(srcname, kc, b0, bn) + rk("X", kc, b0, bn), rk("X", kc, b0, bn))

        def ffn(l, which):
            wup = dr["ffn%d_w_up" % which][l].rearrange("(kc p) n -> p kc n", p=128)
            wdn = dr["ffn%d_w_down" % which][l].rearrange("(c p) n -> p c n", p=128)
            prenorm_to_H(l, "ffn%d_norm_pre" % which)
            fblocks = [(0, 4), (4, 4), (8, 4), (12, 4), (16, 4), (20, 2)]
            for fbi, (c0, nch) in enumerate(fblocks):
                for cp in range(0, nch, 2):
                    wi_g = st["wa"]; st["wa"] = (st["wa"] + 1) % NWA
                    wi_u = st["wa"]; st["wa"] = (st["wa"] + 1) % NWA
                    col = (c0 + cp) * 128
                    P.dma("pool", WA[wi_g][:], wup[:, :, col:col + 256], writes=[("WA", wi_g)])
                    P.dma("pool", WA[wi_u][:], wup[:, :, DFF + col:DFF + col + 256], writes=[("WA", wi_u)])
                    for cc in range(2):
                        pg = [psum() for _ in blks]
                        for kc in range(KC):
                            for bi, (b0, bn) in enumerate(blks):
                                MM(PS[pg[bi]][:, 0:bn], WA[wi_g][:, kc, cc * 128:(cc + 1) * 128], H[:, kc, b0:b0 + bn],
                                   kc == 0, kc == KC - 1, [("WA", wi_g)] + rk("H", kc, b0, bn), [("PS", pg[bi])])
                        sgs = []
                        for bi, (b0, bn) in enumerate(blks):
                            si = st["sg"]; st["sg"] = (st["sg"] + 1) % len(SG)
                            sgs.append(si)
                            ACT(SG[si][:, 0:bn], PS[pg[bi]][:, 0:bn], ACTF.Silu, [("PS", pg[bi])], [("TT", si)])
                        pu = [psum() for _ in blks]
                        for kc in range(KC):
                            for bi, (b0, bn) in enumerate(blks):
                                MM(PS[pu[bi]][:, 0:bn], WA[wi_u][:, kc, cc * 128:(cc + 1) * 128], H[:, kc, b0:b0 + bn],
                                   kc == 0, kc == KC - 1, [("WA", wi_u)] + rk("H", kc, b0, bn), [("PS", pu[bi])])
                        for bi, (b0, bn) in enumerate(blks):
                            TTo("dve", ACTB[:, cp + cc, b0:b0 + bn], PS[pu[bi]][:, 0:bn], SG[sgs[bi]][:, 0:bn], ALU.mult,
                                [("PS", pu[bi]), ("TT", sgs[bi])], [("ACTB", (cp + cc, b0))])
                for ocp in range(4):
                    wi = st["wb"]; st["wb"] ^= 1
                    P.dma("pool", WB[wi][:, 0:nch, :], wdn[:, c0:c0 + nch, ocp * 256:(ocp + 1) * 256], writes=[("WB", wi)])
                    for o2 in range(2):
                        oc = ocp * 2 + o2
                        pd = [psum() for _ in blks]
                        for j in range(nch):
                            for bi, (b0, bn) in enumerate(blks):
                                MM(PS[pd[bi]][:, 0:bn], WB[wi][:, j, o2 * 128:(o2 + 1) * 128], ACTB[:, j, b0:b0 + bn],
                                   j == 0, j == nch - 1, [("WB", wi), ("ACTB", (j, b0))], [("PS", pd[bi])])
                        for bi, (b0, bn) in enumerate(blks):
                            if fbi == 0:
                                CP("act", Y[:, oc, b0:b0 + bn], PS[pd[bi]][:, 0:bn], [("PS", pd[bi])], rk("Y", oc, b0, bn))
                            else:
                                TTo("dve", Y[:, oc, b0:b0 + bn], PS[pd[bi]][:, 0:bn], YF[:, oc, b0:b0 + bn], ALU.add,
                                    [("PS", pd[bi])] + rk("Y", oc, b0, bn), rk("Y", oc, b0, bn))
            postnorm_residual(Y, YF, "Y", l, "ffn%d_norm_post" % which, 0.5)

        MIXR = Y
        allH = [("H", (kc, t)) for kc in range(KC) for t in range(NTM)]
        allACTB = [("ACTB", None)]

        def mixer(l):
            win = dr["w_in"][l].rearrange("(kc p) n -> p kc n", p=128)
            prenorm_to_H(l, "mix_norm_pre")
            ACT(LAYC[:, 0:4], pcol(l, "rg_lambda", 0, 4), ACTF.Exp, [("PP", None)], [("LAYC", 0)], scale=-1.0)
            ACT(LAYC[:, 0:4], LAYC[:, 0:4], ACTF.Ln, [("LAYC", 0), ("ONE1", None)], [("LAYC", 0)], bias=ONE1[:, 0:1])
            TS("dve", LAYC[:, 4:8], LAYC[:, 0:4], -16.0, None, ALU.mult, None, [("LAYC", 0)], [("LAYC", 1)])
            TS("dve", LAYC[:, 0:4], LAYC[:, 0:4], -8.0, None, ALU.mult, None, [("LAYC", 0), ("LAYC", 1)], [("LAYC", 0)])
            ACT(LAYC[:, 8:12], pcol(l, "dn_a_log", 0, 4), ACTF.Exp, [("PP", None)], [("LAYC", 2)])
            TS("dve", LAYC[:, 8:12], LAYC[:, 8:12], -1.0, None, ALU.mult, None, [("LAYC", 2)], [("LAYC", 2)])
            P.dma("pool", RGWt[:], dr["rgw_r"][l].rearrange("g p c m -> p g c m"), writes=[("RGW", None)])
            P.dma("pool", WSC[:], win[:, :, 3072:3080], writes=[("WSC", None)])
            if smp:
                for q in range(4):
                    P.dma("sp", CSL, dr["st_conv"][l][:, q * 512:(q + 1) * 512], writes=[("XP", None)])
                    pi = psum()
                    for j in range(4):
                        TR(PS[pi][:, j * 48:(j + 1) * 48], CSL[:, j * 128:(j + 1) * 128], ident[0:48, 0:48],
                           [("XP", None), ("C32", None)], [("PS", pi)])
                    CP("dve", CSS[:, q * 4:(q + 1) * 4, :], PS[pi][:, 0:192].rearrange("p (j c) -> p j c", j=4),
                       [("PS", pi)], [("CSS", q * 4 + j) for j in range(4)])
                P.dma("sp", RSL, dr["st_rg"][l][:, :], writes=[("XP", None)])
                pi = psum()
                for j in range(4):
                    TR(PS[pi][:, j * 16:(j + 1) * 16], RSL[:, j * 128:(j + 1) * 128], ident[0:16, 0:16],
                       [("XP", None), ("C32", None)], [("PS", pi)])
                CP("dve", HS0[:, :, :], PS[pi][:, 0:64].rearrange("p (j c) -> p j c", j=4), [("PS", pi)], [("HS0", None)])

            wa_state = {}

            def load_pair(colstart):
                wi = st["wa"]; st["wa"] = (st["wa"] + 1) % NWA
                P.dma("pool", WA[wi][:], win[:, :, colstart:colstart + 256], writes=[("WA", wi)])
                return wi

            def proj(wi, cc):
                pb = [psum() for _ in blks]
                for kc in range(KC):
                    for bi, (b0, bn) in enumerate(blks):
                        MM(PS[pb[bi]][:, 0:bn], WA[wi][:, kc, cc * 128:(cc + 1) * 128], H[:, kc, b0:b0 + bn],
                           kc == 0, kc == KC - 1, [("WA", wi)] + rk("H", kc, b0, bn), [("PS", pb[bi])])
                return pb

            def conv_chunk(l, ch, pb, OUT, outkey, bias_col=None):
                CP("act", XP[:, 0:3], CONVT[:, l, ch, :], [("CONVT", (l, ch))], [("XP", 0)])
                for bi, (b0, bn) in enumerate(blks):
                    n = min(bn, npr - b0)
                    if n > 0:
                        CP("act", XP[:, 3 + b0:3 + b0 + n], PS[pb[bi]][:, 0:n], [("PS", pb[bi])], [("XP", 1 + bi)])
                if smp:
                    b0, bn = blks[-1]
                    off = npr - b0
                    CP("dve", XPS[:, :, 0:3], CSS[:, ch, :].rearrange("p (s t) -> p s t", t=3), [("CSS", ch)], [("XPS", 0)])
                    CP("act", XPS[:, :, 3:11], PS[pb[-1]][:, off:off + 128].rearrange("p (s t) -> p s t", t=8),
                       [("PS", pb[-1])], [("XPS", 1)])
                xpk = [("XP", i) for i in range(1 + len(blks))]
                cw = lambda tap: pcol(l, "conv_w", ch * 4 + tap)
                if bias_col is None:
                    TS("dve", OUT[:, 0:npr], XP[:, 0:npr], cw(0), None, ALU.mult, None, xpk + [("PP", None)], [outkey])
                else:
                    TS("dve", OUT[:, 0:npr], XP[:, 0:npr], cw(0), bias_col, ALU.mult, ALU.add, xpk + [("PP", None)], [outkey])
                for tap in range(1, 4):
                    STT(OUT[:, 0:npr], XP[:, tap:tap + npr], cw(tap), OUT[:, 0:npr], ALU.mult, ALU.add,
                        xpk + [("PP", None), outkey], [outkey])
                CP("dve", CONVT[:, l, ch, :], XP[:, npr:npr + 3], xpk, [("CONVT", (l, ch))])
                if smp:
                    OS = OUT[:, npr:npr + 128].rearrange("p (s t) -> p s t", t=8)
                    xk = [("XPS", 0), ("XPS", 1)]
                    if bias_col is None:
                        TS("dve", OS, XPS[:, :, 0:8], cw(0), None, ALU.mult, None, xk + [("PP", None), outkey], [outkey])
                    else:
                        TS("dve", OS, XPS[:, :, 0:8], cw(0), bias_col, ALU.mult, ALU.add, xk + [("PP", None), outkey], [outkey])
                    for tap in range(1, 4):
                        STT(OS, XPS[:, :, tap:tap + 8], cw(tap), OS, ALU.mult, ALU.add, xk + [("PP", None), outkey], [outkey])
                    CP("dve", CSS[:, ch, :].rearrange("p (s t) -> p s t", t=3), XPS[:, :, 8:11], xk, [("CSS", ch)])

            T0 = TT[:, 0, :]; T1 = TT[:, 1, :]; T2 = TT[:, 2, :]
            XRf = XR[:].bitcast(F32)
            k0, k1, k2, k3 = ("TT", 0), ("TT", 1), ("TT", 2), ("XR", None)

            for cpair in range(2):
                wi_x = load_pair(cpair * 256)
                wi_g = load_pair(2048 + cpair * 256)
                for cc in range(2):
                    ch = cpair * 2 + cc
                    pb = proj(wi_x, cc)
                    conv_chunk(l, ch, pb, T2, k2, bias_col=pcol(l, "conv_b_rg", ch))
                    CP("act", XR[:, 0:T], T2[:, 0:T], [k2], [k3])
                    pa = [psum() for _ in blks]
                    for bi, (b0, bn) in enumerate(blks):
                        MM(PS[pa[bi]][:, 0:bn], RGWt[:, 0, ch, :], XR[:, b0:b0 + bn], True, True, [("RGW", None), k3], [("PS", pa[bi])])
                    for bi, (b0, bn) in enumerate(blks):
                        ACT(T0[:, b0:b0 + bn], PS[pa[bi]][:, 0:bn], ACTF.Sigmoid, [("PS", pa[bi]), ("PP", None)], [k0],
                            bias=pcol(l, "rg_b_a", ch))
                    px = [psum() for _ in blks]
                    for bi, (b0, bn) in enumerate(blks):
                        MM(PS[px[bi]][:, 0:bn], RGWt[:, 1, ch, :], XR[:, b0:b0 + bn], True, True, [("RGW", None), k3], [("PS", px[bi])])
                    for bi, (b0, bn) in enumerate(blks):
                        ACT(T1[:, b0:b0 + bn], PS[px[bi]][:, 0:bn], ACTF.Sigmoid, [("PS", px[bi]), ("PP", None)], [k1],
                            bias=pcol(l, "rg_b_x", ch))
                    ACT(T2[:, 0:T], T0[:, 0:T], ACTF.Exp, [k0, ("LAYC", 1)], [k2], scale=LAYC[:, 4 + ch:5 + ch])
                    ACT(T2[:, 0:T], T2[:, 0:T], ACTF.Relu, [k2, ("ONE1", None)], [k2], scale=-1.0, bias=ONE1[:, 0:1])
                    ACT(T2[:, 0:T], T2[:, 0:T], ACTF.Sqrt, [k2], [k2])
                    ACT(T0[:, 0:T], T0[:, 0:T], ACTF.Exp, [k0, ("LAYC", 0)], [k0], scale=LAYC[:, ch:ch + 1])
                    TTo("dve", T1[:, 0:T], T1[:, 0:T], XRf[:, 0:T], ALU.mult, [k1, k3], [k1])
                    TTo("dve", T1[:, 0:T], T1[:, 0:T], T2[:, 0:T], ALU.mult, [k1, k2], [k1])
                    for bi, (b0, bn) in enumerate(blks):
                        n = min(bn, npr - b0)
                        if n <= 0:
                            continue
                        init = HRG[:, l, ch:ch + 1] if bi == 0 else T2[:, b0 - 1:b0]
                        P.op("dve", lambda e, b0=b0, n=n, init=init: e.tensor_tensor_scan(
                            out=T2[:, b0:b0 + n], data0=T0[:, b0:b0 + n], data1=T1[:, b0:b0 + n],
                            initial=init, op0=ALU.mult, op1=ALU.add),
                            [k0, k1, k2, ("HRG", (l, ch))], [k2])
                    CP("dve", HRG[:, l, ch:ch + 1], T2[:, npr - 1:npr], [k2], [("HRG", (l, ch))])
                    if smp:
                        a_first = T0[:, npr:npr + 128:8]
                        b_first = T1[:, npr:npr + 128:8]
                        TTo("dve", SM[:, 0:16], a_first, HS0[:, ch, :], ALU.mult, [k0, ("HS0", None)], [("SM", None)])
                        TTo("dve", b_first, b_first, SM[:, 0:16], ALU.add, [k1, ("SM", None)], [k1])
                        TS("dve", a_first, a_first, 0.0, None, ALU.mult, None, [k0, ("SM", None)], [k0])
                        P.op("dve", lambda e: e.tensor_tensor_scan(out=T2[:, npr:npr + 128], data0=T0[:, npr:npr + 128],
                                                                  data1=T1[:, npr:npr + 128], initial=0.0, op0=ALU.mult, op1=ALU.add),
                             [k0, k1, k2], [k2])
                        CP("dve", HS0[:, ch, :], T2[:, npr + 7:npr + 128:8], [k2, ("SM", None)], [("HS0", None)])
                    pgt = proj(wi_g, cc)
                    for bi, (b0, bn) in enumerate(blks):
                        CP("act", T0[:, b0:b0 + bn], PS[pgt[bi]][:, 0:bn], [("PS", pgt[bi]), k0], [k0])
                    ACT(T1[:, 0:T], T0[:, 0:T], ACTF.Square, [k0, k1], [k1])
                    TS("dve", T1[:, 0:T], T1[:, 0:T], 0.044715, 1.0, ALU.mult, ALU.add, [k1], [k1])
                    TTo("dve", T1[:, 0:T], T1[:, 0:T], T0[:, 0:T], ALU.mult, [k0, k1], [k1])
                    ACT(T1[:, 0:T], T1[:, 0:T], ACTF.Sigmoid, [k1], [k1], scale=2.0 * 0.7978845608028654)
                    TTo("dve", T0[:, 0:T], T0[:, 0:T], T1[:, 0:T], ALU.mult, [k0, k1], [k0])
                    for bi, (b0, bn) in enumerate(blks):
                        TTo("dve", MIXR[:, ch, b0:b0 + bn], T2[:, b0:b0 + bn], T0[:, b0:b0 + bn], ALU.mult,
                            [k0, k2], rk("Y", ch, b0, bn))

            for kind in range(3):
                for cpair in range(2):
                    wi = load_pair(512 + kind * 512 + cpair * 256)
                    for cc in range(2):
                        hh = cpair * 2 + cc
                        ch = 4 + kind * 4 + hh
                        pb = proj(wi, cc)
                        conv_chunk(l, ch, pb, T0, k0)
                        if kind == 2:
                            ACT(QKV[:, 8 + hh, 0:T], T0[:, 0:T], ACTF.Silu, [k0], [("QKV", None)])
                        else:
                            ACT(T1[:, 0:T], T0[:, 0:T], ACTF.Silu, [k0], [k1])
                            for bi, (b0, bn) in enumerate(blks):
                                ACT(SQ[:, 0, 0:bn], T1[:, b0:b0 + bn], ACTF.Square, [k1], [("SQ", 0)])
                                pi = psum()
                                MM(PS[pi][:, 0:bn], ones_r, SQ[:, 0, 0:bn], True, True, [("SQ", 0), ("ONESR", None)], [("PS", pi)])
                                if kind == 0:
                                    ACT(RSTD[:, 0:bn], PS[pi][:, 0:bn], ACTF.Sqrt, [("PS", pi), ("EPST", None)], [("RSTD", None)],
                                        scale=128.0, bias=EPSC["q"])
                                else:
                                    ACT(RSTD[:, 0:bn], PS[pi][:, 0:bn], ACTF.Sqrt, [("PS", pi), ("EPST", None)], [("RSTD", None)],
                                        scale=1.0, bias=EPSC[(1, 1.0)])
                                P.op("dve", lambda e, bn=bn: e.reciprocal(out=RSTD[:, 0:bn], in_=RSTD[:, 0:bn]), [("RSTD", None)], [("RSTD", None)])
                                TTo("dve", QKV[:, kind * 4 + hh, b0:b0 + bn], T1[:, b0:b0 + bn], RSTD[:, 0:bn], ALU.mult,
                                    [k1, ("RSTD", None)], [("QKV", None)])

            for cpair in range(2):
                wi = load_pair(2560 + cpair * 256)
                for cc in range(2):
                    hh = cpair * 2 + cc
                    pb = proj(wi, cc)
                    for bi, (b0, bn) in enumerate(blks):
                        ACT(ZG[:, hh, b0:b0 + bn], PS[pb[bi]][:, 0:bn], ACTF.Silu, [("PS", pb[bi])], [("ZG", (hh, bi))])

            pi = psum()
            for t in range(ntile):
                for kc in range(KC):
                    MM(PS[pi][:, t * 8:(t + 1) * 8], H[:, kc, t * 128:(t + 1) * 128], WSC[:, kc, :], (t == 0 and kc == 0), kc == KC - 1,
                       [("WSC", None), ("H", (kc, t))], [("PS", pi)])
            CP("dve", SCT[:, 0:ntile, :], PS[pi][:, 0:ntile * 8].rearrange("p (t c) -> p t c", c=8), [("PS", pi)], [("SCT", None)])

            for t in range(ntile):
                delta_tile(l, t, smp and t == ntile - 1)

            if getattr(cfg, "debug", False) and l == 0:
                out_ops.append(P.dma("sp", dr["dbg"][gi], YF, reads=[("Y", None)], writes=[("DBG", gi)]))
            wout = dr["w_out"][l].rearrange("(kc p) n -> p kc n", p=128)
            for ocp in range(4):
                wi = st["wa"]; st["wa"] = (st["wa"] + 1) % NWA
                P.dma("pool", WA[wi][:], wout[:, :, ocp * 256:(ocp + 1) * 256], writes=[("WA", wi)])
                for o2 in range(2):
                    oc = ocp * 2 + o2
                    pd = [psum() for _ in blks]
                    for kc in range(KC):
                        for bi, (b0, bn) in enumerate(blks):
                            MM(PS[pd[bi]][:, 0:bn], WA[wi][:, kc, o2 * 128:(o2 + 1) * 128], MIXR[:, kc, b0:b0 + bn],
                               kc == 0, kc == KC - 1, [("WA", wi)] + rk("Y", kc, b0, bn), [("PS", pd[bi])])
                    for bi, (b0, bn) in enumerate(blks):
                        CP("act", H[:, oc, b0:b0 + bn], PS[pd[bi]][:, 0:bn], [("PS", pd[bi])], rk("H", oc, b0, bn))
            postnorm_residual(H, HF, "H", l, "mix_norm_post", 1.0)

            if smp:
                for q in range(4):
                    pi = psum()
                    for j in range(4):
                        TR(PS[pi][0:48, j * 128:(j + 1) * 128], CSS[:, q * 4 + j, :], ident, [("CSS", q * 4 + j), ("C32", None)], [("PS", pi)])
                    CP("dve", CSL, PS[pi][0:48, :], [("PS", pi)], [("XP", None)])
                    out_ops.append(P.dma("sp", dr["ncs"][l][:, q * 512:(q + 1) * 512], CSL, reads=[("XP", None)], writes=[("CSLo", q)]))
                pi = psum()
                for j in range(4):
                    TR(PS[pi][0:16, j * 128:(j + 1) * 128], HS0[:, j, :], ident, [("HS0", None), ("C32", None)], [("PS", pi)])
                CP("dve", RSL, PS[pi][0:16, :], [("PS", pi)], [("XP", None)])
                out_ops.append(P.dma("sp", dr["nrs"][l][:, :], RSL, reads=[("XP", None)], writes=[("RSLo", 0)]))
            if last_group:
                for q in range(4):
                    pi = psum()
                    for j in range(4):
                        TR(PS[pi][0:3, j * 128:(j + 1) * 128], CONVT[:, l, q * 4 + j, :], ident, [("CONVT", (l, q * 4 + j)), ("C32", None)], [("PS", pi)])
                    CP("dve", CSL[0:3, :], PS[pi][0:3, :], [("PS", pi)], [("XP", None)])
                    out_ops.append(P.dma("sp", dr["ncp"][l][:, q * 512:(q + 1) * 512], CSL[0:3, :], reads=[("XP", None)], writes=[("CSLo", q)]))
                pi = psum()
                TR(PS[pi][0:4, 0:128], HRG[:, l, :], ident, [("HRG", None), ("C32", None)], [("PS", pi)])
                CP("dve", RSL[0:4, 0:128], PS[pi][0:4, 0:128], [("PS", pi)], [("XP", None)])
                out_ops.append(P.dma("sp", dr["nrp"][l].rearrange("(c p) -> c p", p=128), RSL[0:4, 0:128], reads=[("XP", None)], writes=[("RSLo", 0)]))
                out_ops.append(P.dma("sp", dr["ndp"][l].rearrange("h d e -> d h e"), SST[:, l, :, :], reads=[("SST", None)], writes=[("SSTo", l)]))

        def delta_tile(l, t, is_s):
            c0 = t * 128
            sfx = "_s" if is_s else "_p"
            nlv = NLV_S if is_s else NLV_P
            bk = lambda n: [(n, None)]
            KTOK, VTOK, AM, ATT, DTt, DINV, N1, BV, BKG, WT, VN, QG, KD, SBF = [DB[n] for n in dt_names_bf]
            qkv_r = [("QKV", None)]
            bc = lambda ap: ap.unsqueeze(2).broadcast_to([128, NH, 128])
            bm = lambda ap: ap.unsqueeze(1).broadcast_to([128, NH, 128])
            BETA = SM[:, 16:20]; GT = SM[:, 20:24]; GC = SM[:, 24:32]; EG = SM[:, 32:36]; BEG = SM[:, 36:40]; EKD = SM[:, 40:44]
            ACT(BETA, SCT[:, t, 0:4], ACTF.Sigmoid, [("SCT", None)], [("SM", 1)])
            TTo("dve", GT, SCT[:, t, 4:8], pcol(l, "dn_dt_bias", 0, 4), ALU.add, [("SCT", None), ("PP", None)], [("SM", 2)])
            ACT(GT, GT, ACTF.Exp, [("SM", 2)], [("SM", 2)])
            ACT(GT, GT, ACTF.Ln, [("SM", 2), ("ONE1", None)], [("SM", 2)], bias=ONE1[:, 0:1])
            TTo("dve", GT, GT, LAYC[:, 8:12], ALU.mult, [("SM", 2), ("LAYC", 2)], [("SM", 2)])
            TTo("dve", TRG[:, :, :], bm(c32("tri" + sfx)), bc(GT), ALU.mult, [("SM", 2), ("C32", None)], bk("TRG"))
            pgr = psum()
            MM(PS[pgr][:, :], ones_f, TRG[:].rearrange("p h c -> p (h c)"), True, True, bk("TRG") + [("C32", None)], [("PS", pgr)])
            pgc = psum()
            MM(PS[pgc][:, 0:4], c32("tri" + sfx), GT, True, True, [("SM", 2), ("C32", None)], [("PS", pgc)])
            MM(PS[pgc][:, 4:8], c32("up" + sfx), GT, False, True, [("SM", 2), ("C32", None)], [("PS", pgc)])
            CP("dve", GC, PS[pgc][:, 0:8], [("PS", pgc)], [("SM", 3)])
            ACT(EG, GC[:, 0:4], ACTF.Exp, [("SM", 3)], [("SM", 4)])
            ACT(EKD, GC[:, 4:8], ACTF.Exp, [("SM", 3)], [("SM", 5)])
            TTo("dve", BEG, BETA, EG, ALU.mult, [("SM", 1), ("SM", 4)], [("SM", 6)])
            PGR3 = PS[pgr][:].rearrange("p (h c) -> p h c", h=NH)
            TTo("dve", GRW[:, :, :], PGR3, bc(GC[:, 0:4]), ALU.subtract, [("PS", pgr), ("SM", 3)], bk("GRW"))
            GRWf = GRW[:].rearrange("p h c -> p (h c)")
            STT(GRWf, GRWf, -1.0, GRWf, ALU.mult, ALU.max, bk("GRW"), bk("GRW"))
            ACT(GRW[:], GRW[:], ACTF.Exp, bk("GRW"), bk("GRW"), scale=-1.0)
            ACT(EGR[:], PGR3, ACTF.Exp, [("PS", pgr)], bk("EGR"))
            for (src_j, DST, nm) in ((4, KTOK, "KTOK"), (8, VTOK, "VTOK")):
                pi = psum()
                for h in range(NH):
                    TR(psb(pi)[:, h * 128:(h + 1) * 128], QKV[:, src_j + h, c0:c0 + 128], identb, qkv_r + [("C16", None)], [("PS", pi)])
                CP("act", DST[:].rearrange("p h c -> p (h c)"), psb(pi)[:, 0:512], [("PS", pi)], bk(nm))
            TTo("dve", BV[:], VTOK[:], bc(BETA), ALU.mult, bk("VTOK") + [("SM", 1)], bk("BV"))
            TTo("dve", BKG[:], KTOK[:], bc(BEG), ALU.mult, bk("KTOK") + [("SM", 6)], bk("BKG"))
            TTo("dve", KD[:], KTOK[:], bc(EKD), ALU.mult, bk("KTOK") + [("SM", 5)], bk("KD"))
            TTo("dve", QG[:], QKV[:, 0:4, c0:c0 + 128], EGR[:], ALU.mult, qkv_r + bk("EGR"), bk("QG"))
            pkk = psum()
            for h in range(NH):
                MM(PS[pkk][:, h * 128:(h + 1) * 128], QKV[:, 4 + h, c0:c0 + 128], QKV[:, 4 + h, c0:c0 + 128], h == 0, True, qkv_r, [("PS", pkk)])
            pat = psum()
            for h in range(NH):
                MM(PS[pat][:, h * 128:(h + 1) * 128], QKV[:, 4 + h, c0:c0 + 128], QKV[:, h, c0:c0 + 128], h == 0, True, qkv_r, [("PS", pat)])
            TTo("dve", TRG[:], GRW[:], bm(c16("mstrict" + sfx)), ALU.mult, bk("GRW") + [("C16", None)] + bk("TRG"), bk("TRG"))
            TTo("dve", TRG[:], TRG[:], bc(BETA), ALU.mult, bk("TRG") + [("SM", 1)], bk("TRG"))
            TTo("dve", OT[:], GRW[:], bm(c16("minclt" + sfx)), ALU.mult, bk("GRW") + [("C16", None)] + bk("OT"), bk("OT"))
            TTo("dve", AM[:].rearrange("p h c -> p (h c)"), PS[pkk][:, :], TRG[:].rearrange("p h c -> p (h c)"), ALU.mult,
                [("PS", pkk)] + bk("TRG"), bk("AM"))
            TTo("dve", ATT[:].rearrange("p h c -> p (h c)"), PS[pat][:, :], OT[:].rearrange("p h c -> p (h c)"), ALU.mult,
                [("PS", pat)] + bk("OT"), bk("ATT"))
            CP("act", DTt[:], bm(identb), [("C16", None)] + bk("DT"), bk("DT"))
            CP("act", DINV[:], bm(identb), [("C16", None)] + bk("DINV"), bk("DINV"))
            for lv in range(nlv):
                p1 = psum()
                for h in range(NH):
                    MM(PS[p1][:, h * 128:(h + 1) * 128], AM[:, h, :], DTt[:, h, :], h == 0, True, bk("AM") + bk("DT"), [("PS", p1)])
                TTo("dve", N1[:], PS[p1][:].rearrange("p (h c) -> p h c", h=NH), bm(c16("lv_p%d" % lv)), ALU.mult,
                    [("PS", p1), ("C16", None)], bk("N1"))
                p2 = psum()
                for h in range(NH):
                    MM(PS[p2][:, h * 128:(h + 1) * 128], DINV[:, h, :], N1[:, h, :], h == 0, True, bk("DINV") + bk("N1"), [("PS", p2)])
                TTo("dve", DTt[:].rearrange("p h c -> p (h c)"), DTt[:].rearrange("p h c -> p (h c)"), PS[p2][:, :], ALU.subtract,
                    [("PS", p2)] + bk("DT"), bk("DT"))
                if lv < nlv - 1:
                    p3 = psum()
                    for h in range(NH):
                        TR(psb(p3)[:, h * 128:(h + 1) * 128], DTt[:, h, :], identb, bk("DT") + [("C16", None)], [("PS", p3)])
                    CP("act", DINV[:].rearrange("p h c -> p (h c)"), psb(p3)[:, 0:512], [("PS", p3)], bk("DINV"))
            pu = psum()
            resv.add(pu)
            for h in range(NH):
                if not is_s:
                    MM(PS[pu][:, h * 128:(h + 1) * 128], DTt[:, h, :], BV[:, h, :], h == 0, False, bk("DT") + bk("BV"), [("PS", pu)])
            pw = psum()
            for h in range(NH):
                MM(PS[pw][:, h * 128:(h + 1) * 128], BKG[:, h, :], DTt[:, h, :], h == 0, True, bk("DT") + bk("BKG"), [("PS", pw)])
            ACT(WT[:].rearrange("p h c -> p (h c)"), PS[pw][:, :], ACTF.Copy, [("PS", pw)], bk("AM"), scale=-1.0)
            po = psum()
            resv.add(po)
            if not is_s:
                CP("act", SBF[:], SST[:, l, :, :], [("SST", None)] + bk("DINV"), bk("DINV"))
                for h in range(NH):
                    MM(PS[pu][:, h * 128:(h + 1) * 128], WT[:, h, :], SBF[:, h, :], False, True, bk("AM") + bk("DINV"), [("PS", pu)])
                CP("act", VN[:].rearrange("p h c -> p (h c)"), PS[pu][:, :], [("PS", pu)], bk("N1"))
                resv.discard(pu)
                for h in range(NH):
                    MM(PS[po][:, h * 128:(h + 1) * 128], SBF[:, h, :], QG[:, h, :], h == 0, False, bk("DINV") + bk("QG"), [("PS", po)])
                for h in range(NH):
                    MM(PS[po][:, h * 128:(h + 1) * 128], VN[:, h, :], ATT[:, h, :], False, True, bk("N1") + bk("ATT"), [("PS", po)])
                psu = psum()
                for h in range(NH):
                    MM(PS[psu][:, h * 128:(h + 1) * 128], KD[:, h, :], VN[:, h, :], h == 0, True, bk("KD") + bk("N1"), [("PS", psu)])
                for h in range(NH):
                    STT(SST[:, l, h, :], SST[:, l, h, :], EGR[:, h, 127:128], PS[psu][:, h * 128:(h + 1) * 128], ALU.mult, ALU.add,
                        [("PS", psu), ("SST", None)] + bk("EGR"), [("SST", None)])
            else:
                resv.discard(pu)
                HSQ = NS // 2
                SS0 = TT[:, 0:2, :].rearrange("p a t -> p (a t)")[:, 0:HSQ * 128].rearrange("p (s e) -> p s e", s=HSQ)
                SS0B = TT[:, 2, :].bitcast(BF16)[:, 0:HSQ * 128].rearrange("p (s e) -> p s e", s=HSQ)
                WTX = XP[:, 0:512].bitcast(BF16).rearrange("p (s c) -> p s c", s=HSQ)
                kS0 = [("TT", 0), ("TT", 1)]; kS0B = [("TT", 2)]; kWX = [("XP", None)]
                segrow = c16("segrow", NS * 128).rearrange("p (s c) -> p s c", s=NS)
                segcol = c16("segcol", NS)
                first_po = True
                for h in range(NH):
                    for hf in range(2):
                        s0 = hf * HSQ
                        P.dma("sp", SS0, dr["st_dn"][l][s0:s0 + HSQ, h, :, :].rearrange("s d e -> d s e"), reads=[], writes=kS0)
                        CP("act", SS0B, SS0, kS0 + kS0B, kS0B)
                        TTo("dve", WTX, WT[:, h, :].unsqueeze(1).broadcast_to([128, HSQ, 128]), segrow[:, s0:s0 + HSQ, :], ALU.mult,
                            bk("AM") + [("C16", None)] + kWX, kWX)
                        pu2 = psum()
                        MM(PS[pu2][:, 0:128], DTt[:, h, :], BV[:, h, :], True, False, bk("DT") + bk("BV"), [("PS", pu2)])
                        for s_ in range(HSQ):
                            MM(PS[pu2][:, 0:128], WTX[:, s_, :], SS0B[:, s_, :], False, True, kS0B + kWX, [("PS", pu2)])
                        CP("act", VN[:, h, :], PS[pu2][:, 0:128], [("PS", pu2)] + bk("N1"), bk("N1"))
                        cb = h * 128 + hf * 64
                        MM(PS[po][:, cb:cb + 64], VN[:, h, :], ATT[:, h, hf * 64:hf * 64 + 64], first_po, False, bk("N1") + bk("ATT"), [("PS", po)])
                        first_po = False
                        for s_ in range(HSQ):
                            cs = h * 128 + (s0 + s_) * 8
                            MM(PS[po][:, cs:cs + 8], SS0B[:, s_, :], QG[:, h, (s0 + s_) * 8:(s0 + s_) * 8 + 8], False, True,
                               kS0B + bk("QG"), [("PS", po)])
                        TTo("dve", WTX, KD[:, h, :].unsqueeze(1).broadcast_to([128, HSQ, 128]),
                            segcol[:, s0:s0 + HSQ].unsqueeze(2).broadcast_to([128, HSQ, 128]), ALU.mult,
                            bk("KD") + [("C16", None)] + kWX, kWX)
                        for q in range(2):
                            psu = psum()
                            for j in range(4):
                                s_ = q * 4 + j
                                MM(PS[psu][:, j * 128:(j + 1) * 128], WTX[:, s_, :], VN[:, h, :], j == 0, True, kWX + bk("N1"), [("PS", psu)])
                            for j in range(4):
                                s_ = q * 4 + j
                                sg_ = s0 + s_
                                STT(SS0[:, s_, :], SS0[:, s_, :], EGR[:, h, sg_ * 8 + 7:sg_ * 8 + 8], PS[psu][:, j * 128:(j + 1) * 128],
                                    ALU.mult, ALU.add, [("PS", psu)] + kS0 + bk("EGR"), kS0)
                        out_ops.append(P.dma("sp", dr["nds"][l][s0:s0 + HSQ, h, :, :].rearrange("s d e -> d s e"), SS0, reads=kS0, writes=[("NDSo", h)]))
            resv.discard(po)
            CP("act", OT[:].rearrange("p h c -> p (h c)"), PS[po][:, :], [("PS", po)] + bk("OT"), bk("OT"))
            SQW = SQ[:, 0:2, :].rearrange("p a c -> p (a c)")[:, 0:512]
            ACT(SQW, PS[po][:, :], ACTF.Square, [("PS", po)], [("SQ", 0), ("SQ", 1)])
            pss = psum()
            MM(PS[pss][:, :], ones_r, SQW, True, True, [("SQ", 0), ("SQ", 1), ("ONESR", None)], [("PS", pss)])
            rstd_from_psum(pss, 512, 128, 1.0)
            STT(OT[:].rearrange("p h c -> p (h c)"), OT[:].rearrange("p h c -> p (h c)"), pcol(l, "dn_norm_w"), RSTD[:, 0:512],
                ALU.mult, ALU.mult, bk("OT") + [("RSTD", None), ("PP", None)], bk("OT"))
            TTo("dve", MIXR[:, 4:8, c0:c0 + 128], OT[:], ZG[:, :, c0:c0 + 128], ALU.mult,
                bk("OT") + [("ZG", None)], [("Y", (4 + h, t)) for h in range(NH)])

        for l in range(DEPTH):
            ffn(l, 1)
            mixer(l)
            ffn(l, 2)

        for (b0, bn) in blks:
            sumsq_rstd(X, "X", b0, bn, D)
            norm_scale(Y, "Y", X, "X", 0, "final_norm", b0, bn)
        for t in range(ntile):
            si = st["stg"]; st["stg"] ^= 1
            stg = stg_view(si)
            for half in range(2):
                pi = psum()
                for j in range(4):
                    kc = half * 4 + j
                    TR(PS[pi][:, j * 128:(j + 1) * 128], YF[:, kc, t * 128:(t + 1) * 128], ident, [("Y", (kc, t)), ("C32", None)], [("PS", pi)])
                CP("act" if half == 0 else "dve", stg[:, half * 512:(half + 1) * 512], PS[pi][:], [("PS", pi)], [stg_keys(si)[half]])
            dst = dr["yp"][p0 + t * 128: p0 + (t + 1) * 128, :] if t * 128 < npr else dr["ys"][:, :]
            out_ops.append(P.dma("sp", dst, stg, reads=stg_keys(si), writes=[("STGo", si)]))

    P.emit(out_ops)
    es.close()
    return nc, P


def make_in_maps(inp):
    pp = _pack_params(inp)
    c32, c16 = _consts()
    rgw = _pack_rgw(inp)
    maps = []
    shared = {"pp": pp, "c32": c32, "c16": c16, "rgw_r": rgw}
    for nm in ("ffn1_w_up", "ffn2_w_up", "ffn1_w_down", "ffn2_w_down", "w_in", "w_out"):
        shared[nm] = np.ascontiguousarray(inp[nm])
    for core in range(NCORES):
        sl = slice(core * NS, (core + 1) * NS)
        m = dict(shared)
        m["xp"] = np.ascontiguousarray(inp["x_prompt"][core])
        m["xs"] = np.ascontiguousarray(inp["x_sample"][sl].reshape(NS * DS, D))
        m["st_conv"] = np.ascontiguousarray(inp["state_conv"][:, sl].reshape(DEPTH, NS * 3, CONVC))
        m["st_rg"] = np.ascontiguousarray(inp["state_rglru"][:, sl])
        m["st_dn"] = np.ascontiguousarray(inp["state_delta"][:, sl])
        maps.append(m)
    return maps


def gather(r):
    y_prompt = np.stack([r[c]["yp"] for c in range(NCORES)], axis=0)
    y_sample = np.concatenate([r[c]["ys"].reshape(NS, DS, D) for c in range(NCORES)], axis=0)
    ncp = np.stack([r[c]["ncp"] for c in range(NCORES)], axis=1)
    nrp = np.stack([r[c]["nrp"] for c in range(NCORES)], axis=1)
    ndp = np.stack([r[c]["ndp"] for c in range(NCORES)], axis=1)
    ncs = np.concatenate([r[c]["ncs"].reshape(DEPTH, NS, 3, CONVC) for c in range(NCORES)], axis=1)
    nrs = np.concatenate([r[c]["nrs"] for c in range(NCORES)], axis=1)
    nds = np.concatenate([r[c]["nds"] for c in range(NCORES)], axis=1)
    return (y_prompt, y_sample, ncp, nrp, ndp, ncs, nrs, nds)


def kernel(**inp):
    inp = {k: np.asarray(v) for k, v in inp.items()}
    cfg = Cfg()
    nc, P = build_program(cfg)
    maps = make_in_maps(inp)
    res = run_bass_kernel_spmd(nc, maps, core_ids=list(range(NCORES)))
    return gather(res.results)
```

```python
import numpy as np
from contextlib import ExitStack
import concourse.bass as bass
import concourse.mybir as mybir
from concourse.bass_utils import run_bass_kernel_spmd

F32 = mybir.dt.float32
F32R = mybir.dt.float32r
BF16 = mybir.dt.bfloat16
ACTF = mybir.ActivationFunctionType
ALU = mybir.AluOpType

D = 1024
KC = 8
DFF = 2816
NFF = 22
DEPTH = 2
SEQ = 2048
NS = 16
DS = 8
INC = 3080
CONVC = 2048
EPS = 1e-6
NCORES = 8


class Op:
    __slots__ = ("eng", "fn", "deps", "sig", "count", "sem", "is_dma", "idx")

    def __init__(self, eng, fn, is_dma=False):
        self.eng = eng
        self.fn = fn
        self.deps = []
        self.sig = False
        self.count = None
        self.sem = None
        self.is_dma = is_dma
        self.idx = None


class Prog:
    ENGS = ("pe", "act", "dve", "pool", "sp")
    NDMASEM = 8

    def __init__(self, nc, same_engine_sync=True):
        self.nc = nc
        self.ops = {e: [] for e in self.ENGS}
        self.last_w = {}
        self.readers = {}
        self.same_engine_sync = same_engine_sync
        self.dma_n = {"sp": 0, "pool": 0}
        self.dma_hist = {"sp": [], "pool": []}
        self.n_ops = 0

    def _collect(self, op, reads, writes):
        deps = []
        for (n, i) in reads:
            lw = self.last_w.get(n)
            if lw:
                if i is None:
                    deps.extend(lw.values())
                else:
                    if i in lw:
                        deps.append(lw[i])
                    if None in lw:
                        deps.append(lw[None])
        for (n, i) in writes:
            lw = self.last_w.get(n)
            rd = self.readers.get(n)
            if lw:
                if i is None:
                    deps.extend(lw.values())
                else:
                    if i in lw:
                        deps.append(lw[i])
                    if None in lw:
                        deps.append(lw[None])
            if rd:
                if i is None:
                    for v in rd.values():
                        deps.extend(v)
                else:
                    deps.extend(rd.get(i, ()))
                    deps.extend(rd.get(None, ()))
        for (n, i) in writes:
            lw = self.last_w.setdefault(n, {})
            rd = self.readers.setdefault(n, {})
            if i is None:
                lw.clear()
                rd.clear()
                lw[None] = op
            else:
                lw[i] = op
                rd.pop(i, None)
        for (n, i) in reads:
            self.readers.setdefault(n, {}).setdefault(i, []).append(op)
        seen = set()
        for d in deps:
            if d is op or id(d) in seen:
                continue
            seen.add(id(d))
            if d.eng == op.eng and not d.is_dma and not op.is_dma:
                if op.eng == "pe" or not self.same_engine_sync:
                    continue
            op.deps.append(d)
            d.sig = True

    def op(self, eng, fn, reads=(), writes=()):
        o = Op(eng, fn)
        self._collect(o, list(reads), list(writes))
        self.ops[eng].append(o)
        self.n_ops += 1
        return o

    def dma(self, q, out, in_, reads=(), writes=()):
        o = Op(q, lambda e: e.dma_start(out=out, in_=in_), is_dma=True)
        n = self.dma_n[q]
        self.dma_n[q] += 1
        o.idx = n
        self._collect(o, list(reads), list(writes))
        hist = self.dma_hist[q]
        if n >= self.NDMASEM:
            o.deps.append(hist[n - self.NDMASEM])
        hist.append(o)
        o.sig = True
        self.ops[q].append(o)
        self.n_ops += 1
        return o

    def emit(self, final_wait_ops):
        nc = self.nc
        with ExitStack() as es:
            esem = {e: es.enter_context(nc.semaphore("prog_" + e)) for e in self.ENGS}
            dsem = {q: [es.enter_context(nc.semaphore("dma_%s_%d" % (q, i))) for i in range(self.NDMASEM)]
                    for q in ("sp", "pool")}
            for e in self.ENGS:
                c = 0
                for o in self.ops[e]:
                    if o.is_dma:
                        slot = o.idx % self.NDMASEM
                        o.sem = dsem[e][slot]
                        o.count = 16 * (o.idx // self.NDMASEM + 1)
                    elif o.sig:
                        c += 1
                        o.sem = esem[e]
                        o.count = c
            block = es.enter_context(nc.Block())

            def run(ename, e, extra_final=None):
                known = {}
                for o in self.ops[ename]:
                    need = {}
                    for d in o.deps:
                        key = id(d.sem)
                        if known.get(key, 0) >= d.count:
                            continue
                        if key not in need or need[key][1] < d.count:
                            need[key] = (d.sem, d.count)
                    for key, (s, v) in need.items():
                        e.wait_ge(s, v)
                        known[key] = v
                    ins = o.fn(e)
                    if o.is_dma:
                        ins.then_inc(o.sem, 16)
                    elif o.sig:
                        ins.then_inc(o.sem, 1)
                if extra_final:
                    need = {}
                    for d in extra_final:
                        key = id(d.sem)
                        if key not in need or need[key][1] < d.count:
                            need[key] = (d.sem, d.count)
                    for key, (s, v) in need.items():
                        e.wait_ge(s, v)

            @block.tensor
            def _(e):
                run("pe", e)

            @block.scalar
            def _(e):
                run("act", e)

            @block.vector
            def _(e):
                run("dve", e)

            @block.gpsimd
            def _(e):
                run("pool", e)

            @block.sync
            def _(e):
                run("sp", e, extra_final=final_wait_ops)


RGW = 512
NH = 4
HD = 128
NLV_P = 7
NLV_S = 3

C32 = {"ident": 0, "ones": 128, "tri_p": 256, "up_p": 384, "tri_s": 512, "up_s": 640}
N32 = 768
C16 = {"identb": 0, "mstrict_p": 128, "minclt_p": 256, "mstrict_s": 384, "minclt_s": 512}
for _i in range(NLV_P):
    C16["lv_p%d" % _i] = 640 + 128 * _i
C16["segcol"] = 640 + 128 * NLV_P
C16["segrow"] = C16["segcol"] + 16
N16 = C16["segrow"] + 16 * 128


def _consts():
    i = np.arange(128)[:, None]
    j = np.arange(128)[None, :]
    c32 = np.zeros((128, N32), np.float32)
    c32[:, 0:128] = np.eye(128)
    c32[:, 128:256] = 1.0
    seg = 8
    same_s = (i // seg) == (j // seg)
    c32[:, 256:384] = (i <= j)
    c32[:, 384:512] = (i > j)
    c32[:, 512:640] = (i <= j) & same_s
    c32[:, 640:768] = (i > j) & same_s
    c16 = np.zeros((128, N16), np.float32)
    c16[:, 0:128] = np.eye(128)
    c16[:, 128:256] = (i > j)
    c16[:, 256:384] = (j >= i)
    c16[:, 384:512] = (i > j) & same_s
    c16[:, 512:640] = (j >= i) & same_s
    for lv in range(NLV_P):
        b = 1 << lv
        m = ((i // (2 * b)) == (j // (2 * b))) & (((j // b) % 2) == 1) & (((i // b) % 2) == 0)
        c16[:, C16["lv_p%d" % lv]:C16["lv_p%d" % lv] + 128] = m
    c16[:, C16["segcol"]:C16["segcol"] + 16] = (np.arange(128)[:, None] // seg) == np.arange(16)[None, :]
    sr = (np.arange(16)[:, None] == (np.arange(128)[None, :] // seg)).astype(np.float32).reshape(1, 16 * 128)
    c16[:, C16["segrow"]:] = np.repeat(sr, 128, axis=0)
    return c32, c16


PL = {}
_o = 0
for _nm, _n in (("ffn1_norm_pre", 8), ("ffn1_norm_post", 8), ("mix_norm_pre", 8), ("mix_norm_post", 8),
                ("ffn2_norm_pre", 8), ("ffn2_norm_post", 8), ("final_norm", 8),
                ("conv_w", 64), ("conv_b_rg", 4), ("rg_b_a", 4), ("rg_b_x", 4), ("rg_lambda", 4),
                ("dn_a_log", 4), ("dn_dt_bias", 4), ("dn_norm_w", 1)):
    PL[_nm] = _o
    _o += _n
PP_LAYER = _o


def _pack_params(inp):
    cols = []

    def vecn(v, n):
        return np.ascontiguousarray(np.asarray(v).reshape(n, 128).T)

    for l in range(DEPTH):
        for nm in ("ffn1_norm_pre", "ffn1_norm_post", "mix_norm_pre", "mix_norm_post",
                   "ffn2_norm_pre", "ffn2_norm_post"):
            cols.append(vecn(inp[nm][l], 8))
        cols.append(vecn(inp["final_norm"], 8))
        cw = np.asarray(inp["conv_w"][l])
        cols.append(np.ascontiguousarray(cw.reshape(4, 16, 128).transpose(2, 1, 0).reshape(128, 64)))
        for nm in ("conv_b_rg", "rg_b_a", "rg_b_x", "rg_lambda"):
            cols.append(vecn(inp[nm][l], 4))
        cols.append(np.repeat(np.asarray(inp["dn_a_log"][l]).reshape(1, 4), 128, axis=0))
        cols.append(np.repeat(np.asarray(inp["dn_dt_bias"][l]).reshape(1, 4), 128, axis=0))
        cols.append(np.asarray(inp["dn_norm_w"][l]).reshape(128, 1))
    return np.ascontiguousarray(np.concatenate(cols, axis=1).astype(np.float32))


def _pack_rgw(inp):
    out = np.zeros((DEPTH, 2, 128, 4, 128), np.float32)
    for l in range(DEPTH):
        for gi, nm in enumerate(("rg_w_a", "rg_w_x")):
            w = np.asarray(inp[nm][l])
            for c in range(4):
                for hh in range(2):
                    out[l, gi, hh * 64:(hh + 1) * 64, c, hh * 64:(hh + 1) * 64] = w[2 * c + hh]
    return out


class Cfg:
    def __init__(self, **kw):
        self.groups = [(0, 640, True), (640, 768, False), (1408, 640, False)]
        self.stages = "full"
        self.same_engine_sync = True
        self.__dict__.update(kw)


def blocks_of(T):
    if T == 768:
        return [(0, 384), (384, 384)]
    if T == 640:
        return [(0, 384), (384, 256)]
    raise ValueError(T)


def build_program(cfg):
    nc = bass.Bass("TRN2", target_bir_lowering=False)
    TM = 768
    NTM = TM // 128
    dr = {}

    def din(name, shape, dt=F32):
        dr[name] = nc.dram_tensor(name, shape, dt, kind="ExternalInput").ap()

    def dout(name, shape):
        dr[name] = nc.dram_tensor(name, shape, F32, kind="ExternalOutput").ap()

    din("xp", [SEQ, D]); din("xs", [NS * DS, D])
    din("pp", [128, DEPTH * PP_LAYER]); din("c32", [128, N32]); din("c16", [128, N16])
    din("rgw_r", [DEPTH, 2, 128, 4, 128], F32R)
    din("st_conv", [DEPTH, NS * 3, CONVC]); din("st_rg", [DEPTH, NS, RGW]); din("st_dn", [DEPTH, NS, NH, HD, HD])
    for nm in ("ffn1_w_up", "ffn2_w_up"):
        din(nm, [DEPTH, D, 2 * DFF], F32R)
    for nm in ("ffn1_w_down", "ffn2_w_down"):
        din(nm, [DEPTH, DFF, D], F32R)
    din("w_in", [DEPTH, D, INC], F32R); din("w_out", [DEPTH, D, D], F32R)
    dout("yp", [SEQ, D]); dout("ys", [NS * DS, D])
    dout("ncp", [DEPTH, 3, CONVC]); dout("nrp", [DEPTH, RGW]); dout("ndp", [DEPTH, NH, HD, HD])
    if getattr(cfg, "debug", False):
        dout("dbg", [3, 128, KC, TM])
    dout("ncs", [DEPTH, NS * 3, CONVC]); dout("nrs", [DEPTH, NS, RGW]); dout("nds", [DEPTH, NS, NH, HD, HD])

    P = Prog(nc, same_engine_sync=cfg.same_engine_sync)
    es = ExitStack()
    sb = lambda name, shape, dt: es.enter_context(nc.sbuf_tensor(name, shape, dt))
    X = sb("X", [128, KC, TM], F32)
    H = sb("H", [128, KC, TM], F32R)
    Y = sb("Y", [128, KC, TM], F32R)
    YF = Y[:].bitcast(F32)
    HF = H[:].bitcast(F32)
    NFB = 4
    ACTB = sb("ACTB", [128, NFB, TM], F32R)
    NWA = 2
    WA = [sb("WA%d" % i, [128, KC, 256], F32R) for i in range(NWA)]
    WB = [sb("WB%d" % i, [128, NFB, 256], F32R) for i in range(2)]
    PPt = sb("PPt", [128, DEPTH * PP_LAYER], F32)
    C32t = sb("C32t", [128, N32], F32)
    C16t = sb("C16t", [128, N16], BF16)
    ONESR = sb("ONESR", [128, 128], F32R)
    EPST = sb("EPST", [128, 8], F32)
    TT = sb("TT", [128, 4, TM], F32)
    SQ = sb("SQ", [128, 4, 384], F32R)
    RSTD = sb("RSTD", [128, 512], F32)
    SG = [TT[:, 0, 0:384], TT[:, 1, 0:384]]
    ZG = sb("ZG", [128, 4, TM], BF16)
    QKV = sb("QKV", [128, 12, TM], BF16)
    XR = sb("XR", [128, TM], F32R)
    XP = sb("XP", [128, 3 + TM], F32R)
    XPS = sb("XPS", [128, NS, 11], F32R)
    XPf = XP[:].bitcast(F32)
    XPSf = XPS[:].bitcast(F32)
    DG = sb("DG", [128, 4, 128], F32R)
    CSS = sb("CSS", [128, 16, NS * 3], F32)
    HS0 = sb("HS0", [128, 4, NS], F32)
    CONVT = sb("CONVT", [128, DEPTH, 16, 3], F32)
    HRG = sb("HRG", [128, DEPTH, 4], F32)
    SST = sb("SST", [128, DEPTH, NH, HD], F32)
    RGWt = sb("RGWt", [128, 2, 4, 128], F32R)
    WSC = sb("WSC", [128, KC, 8], F32R)
    SCT = sb("SCT", [128, NTM, 8], F32)
    LAYC = sb("LAYC", [128, 16], F32)
    ONE1 = sb("ONE1", [128, 1], F32)
    dt_names_bf = ["KTOK", "VTOK", "AM", "ATT", "DT", "DINV", "N1", "BV", "BKG", "WT", "VN", "QG", "KD", "SBF"]
    DB = {n: sb(n, [128, NH, 128], BF16) for n in dt_names_bf if n not in ("VN", "SBF", "WT")}
    DB["VN"] = DB["N1"]; DB["SBF"] = DB["DINV"]; DB["WT"] = DB["AM"]
    GRW = sb("GRW", [128, NH, 128], F32)
    EGR = sb("EGR", [128, NH, 128], F32)
    TRG = sb("TRG", [128, NH, 128], F32)
    OT = sb("OT", [128, NH, 128], F32)
    SM = sb("SM", [128, 64], F32)
    CSL = TT[0:48, 3, 0:512]
    RSL = TT[0:16, 3, 0:512]
    PS = [es.enter_context(nc.psum_tensor("PS%d" % i, [128, 512], F32)) for i in range(8)]

    def c32(nm):
        return C32t[:, C32[nm]:C32[nm] + 128]

    def c16(nm, n=128):
        return C16t[:, C16[nm]:C16[nm] + n]

    ident = c32("ident")
    ones_f = c32("ones")
    ones_r = ONESR[:, :]
    identb = c16("identb")
    st = {"ps": 0, "wa": 0, "wb": 0, "stg": 0, "sg": 0}

    resv = set()

    def psum():
        while True:
            i = st["ps"]
            st["ps"] = (i + 1) % 8
            if i not in resv:
                return i

    def psb(pi):
        return PS[pi][:].bitcast(BF16)

    def ACT(out, in_, func, reads, writes, **kw):
        return P.op("act", lambda e: e.activation(out=out, in_=in_, func=func, **kw), reads, writes)

    def CP(eng, out, in_, reads, writes):
        if eng == "act":
            return P.op("act", lambda e: e.copy(out=out, in_=in_), reads, writes)
        return P.op(eng, lambda e: e.tensor_copy(out=out, in_=in_), reads, writes)

    def TTo(eng, out, in0, in1, op, reads, writes):
        return P.op(eng, lambda e: e.tensor_tensor(out=out, in0=in0, in1=in1, op=op), reads, writes)

    def TS(eng, out, in0, s1, s2, op0, op1, reads, writes):
        if op1 is None:
            return P.op(eng, lambda e: e.tensor_scalar(out=out, in0=in0, scalar1=s1, scalar2=None, op0=op0), reads, writes)
        return P.op(eng, lambda e: e.tensor_scalar(out=out, in0=in0, scalar1=s1, scalar2=s2, op0=op0, op1=op1), reads, writes)

    def STT(out, in0, scalar, in1, op0, op1, reads, writes):
        return P.op("dve", lambda e: e.scalar_tensor_tensor(out=out, in0=in0, scalar=scalar, in1=in1, op0=op0, op1=op1), reads, writes)

    def MM(out, lhsT, rhs, start, stop, reads, writes):
        return P.op("pe", lambda e: e.matmul(out, lhsT=lhsT, rhs=rhs, start=start, stop=stop, skip_group_check=True), reads, writes)

    def TR(out, in_, idn, reads, writes):
        return P.op("pe", lambda e: e.transpose(out=out, in_=in_, identity=idn), reads, writes)

    P.dma("sp", PPt[:], dr["pp"][:, :], writes=[("PP", None)])
    P.dma("sp", C32t[:], dr["c32"][:, :], writes=[("C32", None)])
    for i in range(0, N16, 768):
        n = min(768, N16 - i)
        P.dma("sp", TT[:, 0, 0:n], dr["c16"][:, i:i + n], writes=[("TT", 0)])
        CP("dve", C16t[:, i:i + n], TT[:, 0, 0:n], [("TT", 0)], [("C16", None)])
    CP("dve", ones_r, ones_f, [("C32", None)], [("ONESR", None)])
    EPSC = {}
    for i, (key, val) in enumerate([((D, 1.0), EPS), ((D, 0.5), 4.0 * EPS), ((128, 1.0), EPS), ((1, 1.0), EPS), ("q", 128.0 * EPS)]):
        EPSC[key] = EPST[:, i:i + 1]
        P.op("dve", lambda e, i=i, val=val: e.memset(EPST[:, i:i + 1], val), writes=[("EPST", i)])
    P.op("dve", lambda e: e.memset(ONE1[:, :], 1.0), writes=[("ONE1", None)])
    P.op("dve", lambda e: e.memset(CONVT[:].rearrange("p l c t -> p (l c t)"), 0.0), writes=[("CONVT", None)])
    P.op("dve", lambda e: e.memset(HRG[:].rearrange("p l c -> p (l c)"), 0.0), writes=[("HRG", None)])
    P.op("dve", lambda e: e.memset(SST[:].rearrange("p l h e -> p (l h e)"), 0.0), writes=[("SST", None)])

    out_ops = []

    def pcol(l, nm, j=0, n=1):
        c = l * PP_LAYER + PL[nm] + j
        return PPt[:, c:c + n]

    ngroups = len(cfg.groups)
    for gi, (p0, npr, smp) in enumerate(cfg.groups):
        T = npr + (128 if smp else 0)
        ntile = T // 128
        blks = blocks_of(T)
        last_group = (gi == ngroups - 1)

        def tiles_of(b0, bn):
            return list(range(b0 // 128, (b0 + bn + 127) // 128))

        def rk(name, kcs, b0, bn):
            if isinstance(kcs, int):
                kcs = [kcs]
            return [(name, (kc, t)) for kc in kcs for t in tiles_of(b0, bn)]

        def stg_view(si):
            return TT[:, 2 * si:2 * si + 2, :].rearrange("p a t -> p (a t)")[:, 0:D]

        def stg_keys(si):
            return [("TT", 2 * si), ("TT", 2 * si + 1)]

        for t in range(ntile):
            si = st["stg"]; st["stg"] ^= 1
            stg = stg_view(si)
            src = dr["xp"][p0 + t * 128: p0 + (t + 1) * 128, :] if t * 128 < npr else dr["xs"][:, :]
            P.dma("sp", stg, src, writes=stg_keys(si))
            for half in range(2):
                pi = psum()
                for j in range(4):
                    kc = half * 4 + j
                    TR(PS[pi][:, j * 128:(j + 1) * 128], stg[:, kc * 128:(kc + 1) * 128], ident,
                       stg_keys(si) + [("C32", None)], [("PS", pi)])
                CP("act" if half == 0 else "dve", X[:, half * 4:(half + 1) * 4, t * 128:(t + 1) * 128],
                   PS[pi][:].rearrange("p (j c) -> p j c", j=4), [("PS", pi)], [("X", (half * 4 + j, t)) for j in range(4)])

        def sumsq_rstd(SRC, srcname, b0, bn, nfeat, post_scale=1.0, nk=KC):
            pi = psum()
            for kc in range(nk):
                rd = rk(srcname, kc, b0, bn)
                ACT(SQ[:, kc % 4, 0:bn], SRC[:, kc, b0:b0 + bn], ACTF.Square, rd, [("SQ", kc % 4)])
                MM(PS[pi][:, 0:bn], ones_r, SQ[:, kc % 4, 0:bn], kc == 0, kc == nk - 1,
                   [("SQ", kc % 4), ("ONESR", None)], [("PS", pi)])
            rstd_from_psum(pi, bn, nfeat, post_scale)

        def rstd_from_psum(pi, bn, nfeat, post_scale=1.0):
            ACT(RSTD[:, 0:bn], PS[pi][:, 0:bn], ACTF.Sqrt, [("PS", pi), ("EPST", None)], [("RSTD", None)],
                scale=1.0 / (nfeat * post_scale * post_scale), bias=EPSC[(nfeat, post_scale)])
            P.op("dve", lambda e: e.reciprocal(out=RSTD[:, 0:bn], in_=RSTD[:, 0:bn]), [("RSTD", None)], [("RSTD", None)])

        def norm_scale(DST, dstname, SRC, srcname, l, gname, b0, bn):
            for kc in range(KC):
                STT(DST[:, kc, b0:b0 + bn], SRC[:, kc, b0:b0 + bn], pcol(l, gname, kc), RSTD[:, 0:bn], ALU.mult, ALU.mult,
                    rk(srcname, kc, b0, bn) + [("RSTD", None), ("PP", None)], rk(dstname, kc, b0, bn))

        def prenorm_to_H(l, gname):
            for (b0, bn) in blks:
                sumsq_rstd(X, "X", b0, bn, D)
                norm_scale(H, "H", X, "X", l, gname, b0, bn)

        def postnorm_residual(SRCw, SRC, srcname, l, gname, scale):
            for (b0, bn) in blks:
                sumsq_rstd(SRC, srcname, b0, bn, D, post_scale=scale)
                norm_scale(SRCw, srcname, SRC, srcname, l, gname, b0, bn)
                for kc in range(KC):
                    TTo("dve", X[:, kc, b0:b0 + bn], X[:, kc, b0:b0 + bn], SRC[:, kc, b0:b0 + bn], ALU.add,
                        rk(srcname, kc, b0, bn) + rk("X", kc, b0, bn), rk("X", kc, b0, bn))

        def ffn(l, which):
            wup = dr["ffn%d_w_up" % which][l].rearrange("(kc p) n -> p kc n", p=128)
            wdn = dr["ffn%d_w_down" % which][l].rearrange("(c p) n -> p c n", p=128)
            prenorm_to_H(l, "ffn%d_norm_pre" % which)
            fblocks = [(0, 4), (4, 4), (8, 4), (12, 4), (16, 4), (20, 2)]
            for fbi, (c0, nch) in enumerate(fblocks):
                for cp in range(0, nch, 2):
                    wi_g = st["wa"]; st["wa"] = (st["wa"] + 1) % NWA
                    wi_u = st["wa"]; st["wa"] = (st["wa"] + 1) % NWA
                    col = (c0 + cp) * 128
                    P.dma("pool", WA[wi_g][:], wup[:, :, col:col + 256], writes=[("WA", wi_g)])
                    P.dma("pool", WA[wi_u][:], wup[:, :, DFF + col:DFF + col + 256], writes=[("WA", wi_u)])
                    for cc in range(2):
                        pg = [psum() for _ in blks]
                        for kc in range(KC):
                            for bi, (b0, bn) in enumerate(blks):
                                MM(PS[pg[bi]][:, 0:bn], WA[wi_g][:, kc, cc * 128:(cc + 1) * 128], H[:, kc, b0:b0 + bn],
                                   kc == 0, kc == KC - 1, [("WA", wi_g)] + rk("H", kc, b0, bn), [("PS", pg[bi])])
                        sgs = []
                        for bi, (b0, bn) in enumerate(blks):
                            si = st["sg"]; st["sg"] = (st["sg"] + 1) % len(SG)
                            sgs.append(si)
                            ACT(SG[si][:, 0:bn], PS[pg[bi]][:, 0:bn], ACTF.Silu, [("PS", pg[bi])], [("TT", si)])
                        pu = [psum() for _ in blks]
                        for kc in range(KC):
                            for bi, (b0, bn) in enumerate(blks):
                                MM(PS[pu[bi]][:, 0:bn], WA[wi_u][:, kc, cc * 128:(cc + 1) * 128], H[:, kc, b0:b0 + bn],
                                   kc == 0, kc == KC - 1, [("WA", wi_u)] + rk("H", kc, b0, bn), [("PS", pu[bi])])
                        for bi, (b0, bn) in enumerate(blks):
                            TTo("dve", ACTB[:, cp + cc, b0:b0 + bn], PS[pu[bi]][:, 0:bn], SG[sgs[bi]][:, 0:bn], ALU.mult,
                                [("PS", pu[bi]), ("TT", sgs[bi])], [("ACTB", (cp + cc, b0))])
                for ocp in range(4):
                    wi = st["wb"]; st["wb"] ^= 1
                    P.dma("pool", WB[wi][:, 0:nch, :], wdn[:, c0:c0 + nch, ocp * 256:(ocp + 1) * 256], writes=[("WB", wi)])
                    for o2 in range(2):
                        oc = ocp * 2 + o2
                        pd = [psum() for _ in blks]
                        for j in range(nch):
                            for bi, (b0, bn) in enumerate(blks):
                                MM(PS[pd[bi]][:, 0:bn], WB[wi][:, j, o2 * 128:(o2 + 1) * 128], ACTB[:, j, b0:b0 + bn],
                                   j == 0, j == nch - 1, [("WB", wi), ("ACTB", (j, b0))], [("PS", pd[bi])])
                        for bi, (b0, bn) in enumerate(blks):
                            if fbi == 0:
                                CP("act", Y[:, oc, b0:b0 + bn], PS[pd[bi]][:, 0:bn], [("PS", pd[bi])], rk("Y", oc, b0, bn))
                            else:
                                TTo("dve", Y[:, oc, b0:b0 + bn], PS[pd[bi]][:, 0:bn], YF[:, oc, b0:b0 + bn], ALU.add,
                                    [("PS", pd[bi])] + rk("Y", oc, b0, bn), rk("Y", oc, b0, bn))
            postnorm_residual(Y, YF, "Y", l, "ffn%d_norm_post" % which, 0.5)

        MIXR = Y
        allH = [("H", (kc, t)) for kc in range(KC) for t in range(NTM)]
        allACTB = [("ACTB", None)]

        def mixer(l):
            win = dr["w_in"][l].rearrange("(kc p) n -> p kc n", p=128)
            prenorm_to_H(l, "mix_norm_pre")
            ACT(LAYC[:, 0:4], pcol(l, "rg_lambda", 0, 4), ACTF.Exp, [("PP", None)], [("LAYC", 0)], scale=-1.0)
            ACT(LAYC[:, 0:4], LAYC[:, 0:4], ACTF.Ln, [("LAYC", 0), ("ONE1", None)], [("LAYC", 0)], bias=ONE1[:, 0:1])
            TS("dve", LAYC[:, 4:8], LAYC[:, 0:4], -16.0, None, ALU.mult, None, [("LAYC", 0)], [("LAYC", 1)])
            TS("dve", LAYC[:, 0:4], LAYC[:, 0:4], -8.0, None, ALU.mult, None, [("LAYC", 0), ("LAYC", 1)], [("LAYC", 0)])
            ACT(LAYC[:, 8:12], pcol(l, "dn_a_log", 0, 4), ACTF.Exp, [("PP", None)], [("LAYC", 2)])
            TS("dve", LAYC[:, 8:12], LAYC[:, 8:12], -1.0, None, ALU.mult, None, [("LAYC", 2)], [("LAYC", 2)])
            P.dma("pool", RGWt[:], dr["rgw_r"][l].rearrange("g p c m -> p g c m"), writes=[("RGW", None)])
            P.dma("pool", WSC[:], win[:, :, 3072:3080], writes=[("WSC", None)])
            if smp:
                for q in range(4):
                    P.dma("sp", CSL, dr["st_conv"][l][:, q * 512:(q + 1) * 512], writes=[("TT", 3)])
                    pi = psum()
                    for j in range(4):
                        TR(PS[pi][:, j * 48:(j + 1) * 48], CSL[:, j * 128:(j + 1) * 128], ident[0:48, 0:48],
                           [("TT", 3), ("C32", None)], [("PS", pi)])
                    CP("dve", CSS[:, q * 4:(q + 1) * 4, :], PS[pi][:, 0:192].rearrange("p (j c) -> p j c", j=4),
                       [("PS", pi)], [("CSS", q * 4 + j) for j in range(4)])
                P.dma("sp", RSL, dr["st_rg"][l][:, :], writes=[("TT", 3)])
                pi = psum()
                for j in range(4):
                    TR(PS[pi][:, j * 16:(j + 1) * 16], RSL[:, j * 128:(j + 1) * 128], ident[0:16, 0:16],
                       [("TT", 3), ("C32", None)], [("PS", pi)])
                CP("dve", HS0[:, :, :], PS[pi][:, 0:64].rearrange("p (j c) -> p j c", j=4), [("PS", pi)], [("HS0", None)])

            wa_state = {}

            def load_pair(colstart):
                wi = st["wa"]; st["wa"] = (st["wa"] + 1) % NWA
                P.dma("pool", WA[wi][:], win[:, :, colstart:colstart + 256], writes=[("WA", wi)])
                return wi

            def proj(wi, cc):
                pb = [psum() for _ in blks]
                for kc in range(KC):
                    for bi, (b0, bn) in enumerate(blks):
                        MM(PS[pb[bi]][:, 0:bn], WA[wi][:, kc, cc * 128:(cc + 1) * 128], H[:, kc, b0:b0 + bn],
                           kc == 0, kc == KC - 1, [("WA", wi)] + rk("H", kc, b0, bn), [("PS", pb[bi])])
                return pb

            def conv_chunk(l, ch, pb):
                CP("act", XP[:, 0:3], CONVT[:, l, ch, :], [("CONVT", (l, ch))], [("XP", 0)])
                for bi, (b0, bn) in enumerate(blks):
                    n = min(bn, npr - b0)
                    if n > 0:
                        CP("act", XP[:, 3 + b0:3 + b0 + n], PS[pb[bi]][:, 0:n], [("PS", pb[bi])], [("XP", 1 + bi)])
                if smp:
                    b0, bn = blks[-1]
                    off = npr - b0
                    CP("dve", XPS[:, :, 0:3], CSS[:, ch, :].rearrange("p (s t) -> p s t", t=3), [("CSS", ch)], [("XPS", 0)])
                    CP("act", XPS[:, :, 3:11], PS[pb[-1]][:, off:off + 128].rearrange("p (s t) -> p s t", t=8),
                       [("PS", pb[-1])], [("XPS", 1)])
                for tap in range(4):
                    TS("dve", DG[:, tap, :], ident, pcol(l, "conv_w", ch * 4 + tap), None, ALU.mult, None,
                       [("C32", None), ("PP", None)], [("DG", tap)])
                xpk = [("XP", i) for i in range(1 + len(blks))]
                pc = [psum() for _ in blks]
                for bi, (b0, bn) in enumerate(blks):
                    n = min(bn, npr - b0)
                    for tap in range(4):
                        MM(PS[pc[bi]][:, 0:n], DG[:, tap, :], XP[:, b0 + tap:b0 + tap + n], tap == 0, tap == 3,
                           xpk + [("DG", tap)], [("PS", pc[bi])])
                CP("dve", CONVT[:, l, ch, :], XPf[:, npr:npr + 3], xpk, [("CONVT", (l, ch))])
                if smp:
                    b0, bn = blks[-1]
                    off = npr - b0
                    for tap in range(4):
                        MM(PS[pc[-1]][:, off:off + 128].rearrange("p (s t) -> p s t", t=8), DG[:, tap, :], XPS[:, :, tap:tap + 8],
                           False, tap == 3, [("XPS", 0), ("XPS", 1), ("DG", tap)], [("PS", pc[-1])])
                    CP("dve", CSS[:, ch, :].rearrange("p (s t) -> p s t", t=3), XPSf[:, :, 8:11], [("XPS", 0), ("XPS", 1)], [("CSS", ch)])
                return pc

            T0 = TT[:, 0, :]; T1 = TT[:, 1, :]; T2 = TT[:, 2, :]
            XRf = XR[:].bitcast(F32)
            k0, k1, k2, k3 = ("TT", 0), ("TT", 1), ("TT", 2), ("XR", None)

            for cpair in range(2):
                wi_x = load_pair(cpair * 256)
                wi_g = load_pair(2048 + cpair * 256)
                for cc in range(2):
                    ch = cpair * 2 + cc
                    pb = proj(wi_x, cc)
                    pc = conv_chunk(l, ch, pb)
                    for bi, (b0, bn) in enumerate(blks):
                        ACT(XR[:, b0:b0 + bn], PS[pc[bi]][:, 0:bn], ACTF.Identity, [("PS", pc[bi]), ("PP", None), k3], [k3],
                            bias=pcol(l, "conv_b_rg", ch))
                    pa = [psum() for _ in blks]
                    for bi, (b0, bn) in enumerate(blks):
                        MM(PS[pa[bi]][:, 0:bn], RGWt[:, 0, ch, :], XR[:, b0:b0 + bn], True, True, [("RGW", None), k3], [("PS", pa[bi])])
                    for bi, (b0, bn) in enumerate(blks):
                        ACT(T0[:, b0:b0 + bn], PS[pa[bi]][:, 0:bn], ACTF.Sigmoid, [("PS", pa[bi]), ("PP", None)], [k0],
                            bias=pcol(l, "rg_b_a", ch))
                    px = [psum() for _ in blks]
                    for bi, (b0, bn) in enumerate(blks):
                        MM(PS[px[bi]][:, 0:bn], RGWt[:, 1, ch, :], XR[:, b0:b0 + bn], True, True, [("RGW", None), k3], [("PS", px[bi])])
                    for bi, (b0, bn) in enumerate(blks):
                        ACT(T1[:, b0:b0 + bn], PS[px[bi]][:, 0:bn], ACTF.Sigmoid, [("PS", px[bi]), ("PP", None)], [k1],
                            bias=pcol(l, "rg_b_x", ch))
                    ACT(T2[:, 0:T], T0[:, 0:T], ACTF.Exp, [k0, ("LAYC", 1)], [k2], scale=LAYC[:, 4 + ch:5 + ch])
                    ACT(T2[:, 0:T], T2[:, 0:T], ACTF.Relu, [k2, ("ONE1", None)], [k2], scale=-1.0, bias=ONE1[:, 0:1])
                    ACT(T2[:, 0:T], T2[:, 0:T], ACTF.Sqrt, [k2], [k2])
                    ACT(T0[:, 0:T], T0[:, 0:T], ACTF.Exp, [k0, ("LAYC", 0)], [k0], scale=LAYC[:, ch:ch + 1])
                    TTo("dve", T1[:, 0:T], T1[:, 0:T], XRf[:, 0:T], ALU.mult, [k1, k3], [k1])
                    TTo("dve", T1[:, 0:T], T1[:, 0:T], T2[:, 0:T], ALU.mult, [k1, k2], [k1])
                    for bi, (b0, bn) in enumerate(blks):
                        n = min(bn, npr - b0)
                        if n <= 0:
                            continue
                        init = HRG[:, l, ch:ch + 1] if bi == 0 else T2[:, b0 - 1:b0]
                        P.op("dve", lambda e, b0=b0, n=n, init=init: e.tensor_tensor_scan(
                            out=T2[:, b0:b0 + n], data0=T0[:, b0:b0 + n], data1=T1[:, b0:b0 + n],
                            initial=init, op0=ALU.mult, op1=ALU.add),
                            [k0, k1, k2, ("HRG", (l, ch))], [k2])
                    CP("dve", HRG[:, l, ch:ch + 1], T2[:, npr - 1:npr], [k2], [("HRG", (l, ch))])
                    if smp:
                        a_first = T0[:, npr:npr + 128:8]
                        b_first = T1[:, npr:npr + 128:8]
                        TTo("dve", SM[:, 0:16], a_first, HS0[:, ch, :], ALU.mult, [k0, ("HS0", None)], [("SM", None)])
                        TTo("dve", b_first, b_first, SM[:, 0:16], ALU.add, [k1, ("SM", None)], [k1])
                        TS("dve", a_first, a_first, 0.0, None, ALU.mult, None, [k0, ("SM", None)], [k0])
                        P.op("dve", lambda e: e.tensor_tensor_scan(out=T2[:, npr:npr + 128], data0=T0[:, npr:npr + 128],
                                                                  data1=T1[:, npr:npr + 128], initial=0.0, op0=ALU.mult, op1=ALU.add),
                             [k0, k1, k2], [k2])
                        CP("dve", HS0[:, ch, :], T2[:, npr + 7:npr + 128:8], [k2, ("SM", None)], [("HS0", None)])
                    pgt = proj(wi_g, cc)
                    for bi, (b0, bn) in enumerate(blks):
                        CP("act", T0[:, b0:b0 + bn], PS[pgt[bi]][:, 0:bn], [("PS", pgt[bi]), k0], [k0])
                    ACT(T1[:, 0:T], T0[:, 0:T], ACTF.Square, [k0, k1], [k1])
                    TS("dve", T1[:, 0:T], T1[:, 0:T], 0.044715, 1.0, ALU.mult, ALU.add, [k1], [k1])
                    TTo("dve", T1[:, 0:T], T1[:, 0:T], T0[:, 0:T], ALU.mult, [k0, k1], [k1])
                    ACT(T1[:, 0:T], T1[:, 0:T], ACTF.Sigmoid, [k1], [k1], scale=2.0 * 0.7978845608028654)
                    TTo("dve", T0[:, 0:T], T0[:, 0:T], T1[:, 0:T], ALU.mult, [k0, k1], [k0])
                    for bi, (b0, bn) in enumerate(blks):
                        TTo("dve", MIXR[:, ch, b0:b0 + bn], T2[:, b0:b0 + bn], T0[:, b0:b0 + bn], ALU.mult,
                            [k0, k2], rk("Y", ch, b0, bn))

            for kind in range(3):
                for cpair in range(2):
                    wi = load_pair(512 + kind * 512 + cpair * 256)
                    for cc in range(2):
                        hh = cpair * 2 + cc
                        ch = 4 + kind * 4 + hh
                        pb = proj(wi, cc)
                        pc = conv_chunk(l, ch, pb)
                        if kind == 2:
                            for bi, (b0, bn) in enumerate(blks):
                                ACT(QKV[:, 8 + hh, b0:b0 + bn], PS[pc[bi]][:, 0:bn], ACTF.Silu, [("PS", pc[bi])], [("QKV", None)])
                        else:
                            for bi, (b0, bn) in enumerate(blks):
                                ACT(T1[:, b0:b0 + bn], PS[pc[bi]][:, 0:bn], ACTF.Silu, [("PS", pc[bi]), k1], [k1])
                            for bi, (b0, bn) in enumerate(blks):
                                ACT(SQ[:, 0, 0:bn], T1[:, b0:b0 + bn], ACTF.Square, [k1], [("SQ", 0)])
                                pi = psum()
                                MM(PS[pi][:, 0:bn], ones_r, SQ[:, 0, 0:bn], True, True, [("SQ", 0), ("ONESR", None)], [("PS", pi)])
                                if kind == 0:
                                    ACT(RSTD[:, 0:bn], PS[pi][:, 0:bn], ACTF.Sqrt, [("PS", pi), ("EPST", None)], [("RSTD", None)],
                                        scale=128.0, bias=EPSC["q"])
                                else:
                                    ACT(RSTD[:, 0:bn], PS[pi][:, 0:bn], ACTF.Sqrt, [("PS", pi), ("EPST", None)], [("RSTD", None)],
                                        scale=1.0, bias=EPSC[(1, 1.0)])
                                P.op("dve", lambda e, bn=bn: e.reciprocal(out=RSTD[:, 0:bn], in_=RSTD[:, 0:bn]), [("RSTD", None)], [("RSTD", None)])
                                TTo("dve", QKV[:, kind * 4 + hh, b0:b0 + bn], T1[:, b0:b0 + bn], RSTD[:, 0:bn], ALU.mult,
                                    [k1, ("RSTD", None)], [("QKV", None)])

            for cpair in range(2):
                wi = load_pair(2560 + cpair * 256)
                for cc in range(2):
                    hh = cpair * 2 + cc
                    pb = proj(wi, cc)
                    for bi, (b0, bn) in enumerate(blks):
                        ACT(ZG[:, hh, b0:b0 + bn], PS[pb[bi]][:, 0:bn], ACTF.Silu, [("PS", pb[bi])], [("ZG", (hh, bi))])

            pi = psum()
            for t in range(ntile):
                for kc in range(KC):
                    MM(PS[pi][:, t * 8:(t + 1) * 8], H[:, kc, t * 128:(t + 1) * 128], WSC[:, kc, :], (t == 0 and kc == 0), kc == KC - 1,
                       [("WSC", None), ("H", (kc, t))], [("PS", pi)])
            CP("dve", SCT[:, 0:ntile, :], PS[pi][:, 0:ntile * 8].rearrange("p (t c) -> p t c", c=8), [("PS", pi)], [("SCT", None)])

            for t in range(ntile):
                delta_tile(l, t, smp and t == ntile - 1)

            if getattr(cfg, "debug", False) and l == 0:
                out_ops.append(P.dma("sp", dr["dbg"][gi], YF, reads=[("Y", None)], writes=[("DBG", gi)]))
            wout = dr["w_out"][l].rearrange("(kc p) n -> p kc n", p=128)
            for ocp in range(4):
                wi = st["wa"]; st["wa"] = (st["wa"] + 1) % NWA
                P.dma("pool", WA[wi][:], wout[:, :, ocp * 256:(ocp + 1) * 256], writes=[("WA", wi)])
                for o2 in range(2):
                    oc = ocp * 2 + o2
                    pd = [psum() for _ in blks]
                    for kc in range(KC):
                        for bi, (b0, bn) in enumerate(blks):
                            MM(PS[pd[bi]][:, 0:bn], WA[wi][:, kc, o2 * 128:(o2 + 1) * 128], MIXR[:, kc, b0:b0 + bn],
                               kc == 0, kc == KC - 1, [("WA", wi)] + rk("Y", kc, b0, bn), [("PS", pd[bi])])
                    for bi, (b0, bn) in enumerate(blks):
                        CP("act", H[:, oc, b0:b0 + bn], PS[pd[bi]][:, 0:bn], [("PS", pd[bi])], rk("H", oc, b0, bn))
            postnorm_residual(H, HF, "H", l, "mix_norm_post", 1.0)

            if smp:
                for q in range(4):
                    pi = psum()
                    for j in range(4):
                        TR(PS[pi][0:48, j * 128:(j + 1) * 128], CSS[:, q * 4 + j, :], ident, [("CSS", q * 4 + j), ("C32", None)], [("PS", pi)])
                    CP("dve", CSL, PS[pi][0:48, :], [("PS", pi)], [("TT", 3)])
                    out_ops.append(P.dma("sp", dr["ncs"][l][:, q * 512:(q + 1) * 512], CSL, reads=[("TT", 3)], writes=[("CSLo", q)]))
                pi = psum()
                for j in range(4):
                    TR(PS[pi][0:16, j * 128:(j + 1) * 128], HS0[:, j, :], ident, [("HS0", None), ("C32", None)], [("PS", pi)])
                CP("dve", RSL, PS[pi][0:16, :], [("PS", pi)], [("TT", 3)])
                out_ops.append(P.dma("sp", dr["nrs"][l][:, :], RSL, reads=[("TT", 3)], writes=[("RSLo", 0)]))
            if last_group:
                for q in range(4):
                    pi = psum()
                    for j in range(4):
                        TR(PS[pi][0:3, j * 128:(j + 1) * 128], CONVT[:, l, q * 4 + j, :], ident, [("CONVT", (l, q * 4 + j)), ("C32", None)], [("PS", pi)])
                    CP("dve", CSL[0:3, :], PS[pi][0:3, :], [("PS", pi)], [("TT", 3)])
                    out_ops.append(P.dma("sp", dr["ncp"][l][:, q * 512:(q + 1) * 512], CSL[0:3, :], reads=[("TT", 3)], writes=[("CSLo", q)]))
                pi = psum()
                TR(PS[pi][0:4, 0:128], HRG[:, l, :], ident, [("HRG", None), ("C32", None)], [("PS", pi)])
                CP("dve", RSL[0:4, 0:128], PS[pi][0:4, 0:128], [("PS", pi)], [("TT", 3)])
                out_ops.append(P.dma("sp", dr["nrp"][l].rearrange("(c p) -> c p", p=128), RSL[0:4, 0:128], reads=[("TT", 3)], writes=[("RSLo", 0)]))
                out_ops.append(P.dma("sp", dr["ndp"][l].rearrange("h d e -> d h e"), SST[:, l, :, :], reads=[("SST", None)], writes=[("SSTo", l)]))

        def delta_tile(l, t, is_s):
            c0 = t * 128
            sfx = "_s" if is_s else "_p"
            nlv = NLV_S if is_s else NLV_P
            bk = lambda n: [(n, None)]
            KTOK, VTOK, AM, ATT, DTt, DINV, N1, BV, BKG, WT, VN, QG, KD, SBF = [DB[n] for n in dt_names_bf]
            qkv_r = [("QKV", None)]
            bc = lambda ap: ap.unsqueeze(2).broadcast_to([128, NH, 128])
            bm = lambda ap: ap.unsqueeze(1).broadcast_to([128, NH, 128])
            BETA = SM[:, 16:20]; GT = SM[:, 20:24]; GC = SM[:, 24:32]; EG = SM[:, 32:36]; BEG = SM[:, 36:40]; EKD = SM[:, 40:44]
            ACT(BETA, SCT[:, t, 0:4], ACTF.Sigmoid, [("SCT", None)], [("SM", 1)])
            TTo("dve", GT, SCT[:, t, 4:8], pcol(l, "dn_dt_bias", 0, 4), ALU.add, [("SCT", None), ("PP", None)], [("SM", 2)])
            ACT(GT, GT, ACTF.Exp, [("SM", 2)], [("SM", 2)])
            ACT(GT, GT, ACTF.Ln, [("SM", 2), ("ONE1", None)], [("SM", 2)], bias=ONE1[:, 0:1])
            TTo("dve", GT, GT, LAYC[:, 8:12], ALU.mult, [("SM", 2), ("LAYC", 2)], [("SM", 2)])
            TTo("dve", TRG[:, :, :], bm(c32("tri" + sfx)), bc(GT), ALU.mult, [("SM", 2), ("C32", None)], bk("TRG"))
            pgr = psum()
            MM(PS[pgr][:, :], ones_f, TRG[:].rearrange("p h c -> p (h c)"), True, True, bk("TRG") + [("C32", None)], [("PS", pgr)])
            pgc = psum()
            MM(PS[pgc][:, 0:4], c32("tri" + sfx), GT, True, True, [("SM", 2), ("C32", None)], [("PS", pgc)])
            MM(PS[pgc][:, 4:8], c32("up" + sfx), GT, False, True, [("SM", 2), ("C32", None)], [("PS", pgc)])
            CP("dve", GC, PS[pgc][:, 0:8], [("PS", pgc)], [("SM", 3)])
            ACT(EG, GC[:, 0:4], ACTF.Exp, [("SM", 3)], [("SM", 4)])
            ACT(EKD, GC[:, 4:8], ACTF.Exp, [("SM", 3)], [("SM", 5)])
            TTo("dve", BEG, BETA, EG, ALU.mult, [("SM", 1), ("SM", 4)], [("SM", 6)])
            PGR3 = PS[pgr][:].rearrange("p (h c) -> p h c", h=NH)
            TTo("dve", GRW[:, :, :], PGR3, bc(GC[:, 0:4]), ALU.subtract, [("PS", pgr), ("SM", 3)], bk("GRW"))
            GRWf = GRW[:].rearrange("p h c -> p (h c)")
            STT(GRWf, GRWf, -1.0, GRWf, ALU.mult, ALU.max, bk("GRW"), bk("GRW"))
            ACT(GRW[:], GRW[:], ACTF.Exp, bk("GRW"), bk("GRW"), scale=-1.0)
            ACT(EGR[:], PGR3, ACTF.Exp, [("PS", pgr)], bk("EGR"))
            for (src_j, DST, nm) in ((4, KTOK, "KTOK"), (8, VTOK, "VTOK")):
                pi = psum()
                for h in range(NH):
                    TR(psb(pi)[:, h * 128:(h + 1) * 128], QKV[:, src_j + h, c0:c0 + 128], identb, qkv_r + [("C16", None)], [("PS", pi)])
                CP("act", DST[:].rearrange("p h c -> p (h c)"), psb(pi)[:, 0:512], [("PS", pi)], bk(nm))
            TTo("dve", BV[:], VTOK[:], bc(BETA), ALU.mult, bk("VTOK") + [("SM", 1)], bk("BV"))
            TTo("dve", BKG[:], KTOK[:], bc(BEG), ALU.mult, bk("KTOK") + [("SM", 6)], bk("BKG"))
            TTo("dve", KD[:], KTOK[:], bc(EKD), ALU.mult, bk("KTOK") + [("SM", 5)], bk("KD"))
            TTo("dve", QG[:], QKV[:, 0:4, c0:c0 + 128], EGR[:], ALU.mult, qkv_r + bk("EGR"), bk("QG"))
            pkk = psum()
            for h in range(NH):
                MM(PS[pkk][:, h * 128:(h + 1) * 128], QKV[:, 4 + h, c0:c0 + 128], QKV[:, 4 + h, c0:c0 + 128], h == 0, True, qkv_r, [("PS", pkk)])
            pat = psum()
            for h in range(NH):
                MM(PS[pat][:, h * 128:(h + 1) * 128], QKV[:, 4 + h, c0:c0 + 128], QKV[:, h, c0:c0 + 128], h == 0, True, qkv_r, [("PS", pat)])
            TTo("dve", TRG[:], GRW[:], bm(c16("mstrict" + sfx)), ALU.mult, bk("GRW") + [("C16", None)] + bk("TRG"), bk("TRG"))
            TTo("dve", TRG[:], TRG[:], bc(BETA), ALU.mult, bk("TRG") + [("SM", 1)], bk("TRG"))
            TTo("dve", OT[:], GRW[:], bm(c16("minclt" + sfx)), ALU.mult, bk("GRW") + [("C16", None)] + bk("OT"), bk("OT"))
            TTo("dve", AM[:].rearrange("p h c -> p (h c)"), PS[pkk][:, :], TRG[:].rearrange("p h c -> p (h c)"), ALU.mult,
                [("PS", pkk)] + bk("TRG"), bk("AM"))
            TTo("dve", ATT[:].rearrange("p h c -> p (h c)"), PS[pat][:, :], OT[:].rearrange("p h c -> p (h c)"), ALU.mult,
                [("PS", pat)] + bk("OT"), bk("ATT"))
            CP("act", DTt[:], bm(identb), [("C16", None)] + bk("DT"), bk("DT"))
            CP("act", DINV[:], bm(identb), [("C16", None)] + bk("DINV"), bk("DINV"))
            for lv in range(nlv):
                p1 = psum()
                for h in range(NH):
                    MM(PS[p1][:, h * 128:(h + 1) * 128], AM[:, h, :], DTt[:, h, :], h == 0, True, bk("AM") + bk("DT"), [("PS", p1)])
                TTo("dve", N1[:], PS[p1][:].rearrange("p (h c) -> p h c", h=NH), bm(c16("lv_p%d" % lv)), ALU.mult,
                    [("PS", p1), ("C16", None)], bk("N1"))
                p2 = psum()
                for h in range(NH):
                    MM(PS[p2][:, h * 128:(h + 1) * 128], DINV[:, h, :], N1[:, h, :], h == 0, True, bk("DINV") + bk("N1"), [("PS", p2)])
                TTo("dve", DTt[:].rearrange("p h c -> p (h c)"), DTt[:].rearrange("p h c -> p (h c)"), PS[p2][:, :], ALU.subtract,
                    [("PS", p2)] + bk("DT"), bk("DT"))
                if lv < nlv - 1:
                    p3 = psum()
                    for h in range(NH):
                        TR(psb(p3)[:, h * 128:(h + 1) * 128], DTt[:, h, :], identb, bk("DT") + [("C16", None)], [("PS", p3)])
                    CP("act", DINV[:].rearrange("p h c -> p (h c)"), psb(p3)[:, 0:512], [("PS", p3)], bk("DINV"))
            pu = psum()
            resv.add(pu)
            for h in range(NH):
                if not is_s:
                    MM(PS[pu][:, h * 128:(h + 1) * 128], DTt[:, h, :], BV[:, h, :], h == 0, False, bk("DT") + bk("BV"), [("PS", pu)])
            pw = psum()
            for h in range(NH):
                MM(PS[pw][:, h * 128:(h + 1) * 128], BKG[:, h, :], DTt[:, h, :], h == 0, True, bk("DT") + bk("BKG"), [("PS", pw)])
            ACT(WT[:].rearrange("p h c -> p (h c)"), PS[pw][:, :], ACTF.Copy, [("PS", pw)], bk("AM"), scale=-1.0)
            po = psum()
            resv.add(po)
            if not is_s:
                CP("act", SBF[:], SST[:, l, :, :], [("SST", None)] + bk("DINV"), bk("DINV"))
                for h in range(NH):
                    MM(PS[pu][:, h * 128:(h + 1) * 128], WT[:, h, :], SBF[:, h, :], False, True, bk("AM") + bk("DINV"), [("PS", pu)])
                CP("act", VN[:].rearrange("p h c -> p (h c)"), PS[pu][:, :], [("PS", pu)], bk("N1"))
                resv.discard(pu)
                for h in range(NH):
                    MM(PS[po][:, h * 128:(h + 1) * 128], SBF[:, h, :], QG[:, h, :], h == 0, False, bk("DINV") + bk("QG"), [("PS", po)])
                for h in range(NH):
                    MM(PS[po][:, h * 128:(h + 1) * 128], VN[:, h, :], ATT[:, h, :], False, True, bk("N1") + bk("ATT"), [("PS", po)])
                psu = psum()
                for h in range(NH):
                    MM(PS[psu][:, h * 128:(h + 1) * 128], KD[:, h, :], VN[:, h, :], h == 0, True, bk("KD") + bk("N1"), [("PS", psu)])
                for h in range(NH):
                    STT(SST[:, l, h, :], SST[:, l, h, :], EGR[:, h, 127:128], PS[psu][:, h * 128:(h + 1) * 128], ALU.mult, ALU.add,
                        [("PS", psu), ("SST", None)] + bk("EGR"), [("SST", None)])
            else:
                resv.discard(pu)
                HSQ = NS // 2
                SS0 = TT[:, 0:2, :].rearrange("p a t -> p (a t)")[:, 0:HSQ * 128].rearrange("p (s e) -> p s e", s=HSQ)
                SS0B = TT[:, 2, :].bitcast(BF16)[:, 0:HSQ * 128].rearrange("p (s e) -> p s e", s=HSQ)
                WTX = TT[:, 3, 0:512].bitcast(BF16).rearrange("p (s c) -> p s c", s=HSQ)
                kS0 = [("TT", 0), ("TT", 1)]; kS0B = [("TT", 2)]; kWX = [("TT", 3)]
                segrow = c16("segrow", NS * 128).rearrange("p (s c) -> p s c", s=NS)
                segcol = c16("segcol", NS)
                first_po = True
                for h in range(NH):
                    for hf in range(2):
                        s0 = hf * HSQ
                        P.dma("sp", SS0, dr["st_dn"][l][s0:s0 + HSQ, h, :, :].rearrange("s d e -> d s e"), reads=[], writes=kS0)
                        CP("act", SS0B, SS0, kS0 + kS0B, kS0B)
                        TTo("dve", WTX, WT[:, h, :].unsqueeze(1).broadcast_to([128, HSQ, 128]), segrow[:, s0:s0 + HSQ, :], ALU.mult,
                            bk("AM") + [("C16", None)] + kWX, kWX)
                        pu2 = psum()
                        MM(PS[pu2][:, 0:128], DTt[:, h, :], BV[:, h, :], True, False, bk("DT") + bk("BV"), [("PS", pu2)])
                        for s_ in range(HSQ):
                            MM(PS[pu2][:, 0:128], WTX[:, s_, :], SS0B[:, s_, :], False, True, kS0B + kWX, [("PS", pu2)])
                        CP("act", VN[:, h, :], PS[pu2][:, 0:128], [("PS", pu2)] + bk("N1"), bk("N1"))
                        cb = h * 128 + hf * 64
                        MM(PS[po][:, cb:cb + 64], VN[:, h, :], ATT[:, h, hf * 64:hf * 64 + 64], first_po, False, bk("N1") + bk("ATT"), [("PS", po)])
                        first_po = False
                        for s_ in range(HSQ):
                            cs = h * 128 + (s0 + s_) * 8
                            MM(PS[po][:, cs:cs + 8], SS0B[:, s_, :], QG[:, h, (s0 + s_) * 8:(s0 + s_) * 8 + 8], False, True,
                               kS0B + bk("QG"), [("PS", po)])
                        TTo("dve", WTX, KD[:, h, :].unsqueeze(1).broadcast_to([128, HSQ, 128]),
                            segcol[:, s0:s0 + HSQ].unsqueeze(2).broadcast_to([128, HSQ, 128]), ALU.mult,
                            bk("KD") + [("C16", None)] + kWX, kWX)
                        for q in range(2):
                            psu = psum()
                            for j in range(4):
                                s_ = q * 4 + j
                                MM(PS[psu][:, j * 128:(j + 1) * 128], WTX[:, s_, :], VN[:, h, :], j == 0, True, kWX + bk("N1"), [("PS", psu)])
                            for j in range(4):
                                s_ = q * 4 + j
                                sg_ = s0 + s_
                                STT(SS0[:, s_, :], SS0[:, s_, :], EGR[:, h, sg_ * 8 + 7:sg_ * 8 + 8], PS[psu][:, j * 128:(j + 1) * 128],
                                    ALU.mult, ALU.add, [("PS", psu)] + kS0 + bk("EGR"), kS0)
                        out_ops.append(P.dma("sp", dr["nds"][l][s0:s0 + HSQ, h, :, :].rearrange("s d e -> d s e"), SS0, reads=kS0, writes=[("NDSo", h)]))
            resv.discard(po)
            CP("act", OT[:].rearrange("p h c -> p (h c)"), PS[po][:, :], [("PS", po)] + bk("OT"), bk("OT"))
            SQW = SQ[:, 0:2, :].rearrange("p a c -> p (a c)")[:, 0:512]
            ACT(SQW, PS[po][:, :], ACTF.Square, [("PS", po)], [("SQ", 0), ("SQ", 1)])
            pss = psum()
            MM(PS[pss][:, :], ones_r, SQW, True, True, [("SQ", 0), ("SQ", 1), ("ONESR", None)], [("PS", pss)])
            rstd_from_psum(pss, 512, 128, 1.0)
            STT(OT[:].rearrange("p h c -> p (h c)"), OT[:].rearrange("p h c -> p (h c)"), pcol(l, "dn_norm_w"), RSTD[:, 0:512],
                ALU.mult, ALU.mult, bk("OT") + [("RSTD", None), ("PP", None)], bk("OT"))
            TTo("dve", MIXR[:, 4:8, c0:c0 + 128], OT[:], ZG[:, :, c0:c0 + 128], ALU.mult,
                bk("OT") + [("ZG", None)], [("Y", (4 + h, t)) for h in range(NH)])

        for l in range(DEPTH):
            ffn(l, 1)
            mixer(l)
            ffn(l, 2)

        for (b0, bn) in blks:
            sumsq_rstd(X, "X", b0, bn, D)
            norm_scale(Y, "Y", X, "X", 0, "final_norm", b0, bn)
        for t in range(ntile):
            si = st["stg"]; st["stg"] ^= 1
            stg = stg_view(si)
            for half in range(2):
                pi = psum()
                for j in range(4):
                    kc = half * 4 + j
                    TR(PS[pi][:, j * 128:(j + 1) * 128], YF[:, kc, t * 128:(t + 1) * 128], ident, [("Y", (kc, t)), ("C32", None)], [("PS", pi)])
                CP("act" if half == 0 else "dve", stg[:, half * 512:(half + 1) * 512], PS[pi][:], [("PS", pi)], [stg_keys(si)[half]])
            dst = dr["yp"][p0 + t * 128: p0 + (t + 1) * 128, :] if t * 128 < npr else dr["ys"][:, :]
            out_ops.append(P.dma("sp", dst, stg, reads=stg_keys(si), writes=[("STGo", si)]))

    P.emit(out_ops)
    es.close()
    return nc, P


def make_in_maps(inp):
    pp = _pack_params(inp)
    c32, c16 = _consts()
    rgw = _pack_rgw(inp)
    maps = []
    shared = {"pp": pp, "c32": c32, "c16": c16, "rgw_r": rgw}
    for nm in ("ffn1_w_up", "ffn2_w_up", "ffn1_w_down", "ffn2_w_down", "w_in", "w_out"):
        shared[nm] = np.ascontiguousarray(inp[nm])
    for core in range(NCORES):
        sl = slice(core * NS, (core + 1) * NS)
        m = dict(shared)
        m["xp"] = np.ascontiguousarray(inp["x_prompt"][core])
        m["xs"] = np.ascontiguousarray(inp["x_sample"][sl].reshape(NS * DS, D))
        m["st_conv"] = np.ascontiguousarray(inp["state_conv"][:, sl].reshape(DEPTH, NS * 3, CONVC))
        m["st_rg"] = np.ascontiguousarray(inp["state_rglru"][:, sl])
        m["st_dn"] = np.ascontiguousarray(inp["state_delta"][:, sl])
        maps.append(m)
    return maps


def gather(r):
    y_prompt = np.stack([r[c]["yp"] for c in range(NCORES)], axis=0)
    y_sample = np.concatenate([r[c]["ys"].reshape(NS, DS, D) for c in range(NCORES)], axis=0)
    ncp = np.stack([r[c]["ncp"] for c in range(NCORES)], axis=1)
    nrp = np.stack([r[c]["nrp"] for c in range(NCORES)], axis=1)
    ndp = np.stack([r[c]["ndp"] for c in range(NCORES)], axis=1)
    ncs = np.concatenate([r[c]["ncs"].reshape(DEPTH, NS, 3, CONVC) for c in range(NCORES)], axis=1)
    nrs = np.concatenate([r[c]["nrs"] for c in range(NCORES)], axis=1)
    nds = np.concatenate([r[c]["nds"] for c in range(NCORES)], axis=1)
    return (y_prompt, y_sample, ncp, nrp, ndp, ncs, nrs, nds)


def kernel(**inp):
    inp = {k: np.asarray(v) for k, v in inp.items()}
    cfg = Cfg()
    nc, P = build_program(cfg)
    maps = make_in_maps(inp)
    res = run_bass_kernel_spmd(nc, maps, core_ids=list(range(NCORES)))
    return gather(res.results)
```

```python
import numpy as np
from contextlib import ExitStack
import concourse.bass as bass
import concourse.mybir as mybir
from concourse.bass_utils import run_bass_kernel_spmd

F32 = mybir.dt.float32
F32R = mybir.dt.float32r
BF16 = mybir.dt.bfloat16
ACTF = mybir.ActivationFunctionType
ALU = mybir.AluOpType

D = 1024
KC = 8
DFF = 2816
NFF = 22
DEPTH = 2
SEQ = 2048
NS = 16
DS = 8
INC = 3080
CONVC = 2048
EPS = 1e-6
NCORES = 8


class Op:
    __slots__ = ("eng", "fn", "deps", "sig", "count", "sem", "is_dma", "idx")

    def __init__(self, eng, fn, is_dma=False):
        self.eng = eng
        self.fn = fn
        self.deps = []
        self.sig = False
        self.count = None
        self.sem = None
        self.is_dma = is_dma
        self.idx = None


class Prog:
    ENGS = ("pe", "act", "dve", "pool", "sp")
    NDMASEM = 8

    def __init__(self, nc, same_engine_sync=True):
        self.nc = nc
        self.ops = {e: [] for e in self.ENGS}
        self.last_w = {}
        self.readers = {}
        self.same_engine_sync = same_engine_sync
        self.dma_n = {"sp": 0, "pool": 0}
        self.dma_hist = {"sp": [], "pool": []}
        self.n_ops = 0

    def _collect(self, op, reads, writes):
        deps = []
        for (n, i) in reads:
            lw = self.last_w.get(n)
            if lw:
                if i is None:
                    deps.extend(lw.values())
                else:
                    if i in lw:
                        deps.append(lw[i])
                    if None in lw:
                        deps.append(lw[None])
        for (n, i) in writes:
            lw = self.last_w.get(n)
            rd = self.readers.get(n)
            if lw:
                if i is None:
                    deps.extend(lw.values())
                else:
                    if i in lw:
                        deps.append(lw[i])
                    if None in lw:
                        deps.append(lw[None])
            if rd:
                if i is None:
                    for v in rd.values():
                        deps.extend(v)
                else:
                    deps.extend(rd.get(i, ()))
                    deps.extend(rd.get(None, ()))
        for (n, i) in writes:
            lw = self.last_w.setdefault(n, {})
            rd = self.readers.setdefault(n, {})
            if i is None:
                lw.clear()
                rd.clear()
                lw[None] = op
            else:
                lw[i] = op
                rd.pop(i, None)
        for (n, i) in reads:
            self.readers.setdefault(n, {}).setdefault(i, []).append(op)
        seen = set()
        for d in deps:
            if d is op or id(d) in seen:
                continue
            seen.add(id(d))
            if d.eng == op.eng and not d.is_dma and not op.is_dma:
                if op.eng == "pe" or not self.same_engine_sync:
                    continue
            op.deps.append(d)
            d.sig = True

    def op(self, eng, fn, reads=(), writes=()):
        o = Op(eng, fn)
        self._collect(o, list(reads), list(writes))
        self.ops[eng].append(o)
        self.n_ops += 1
        return o

    def dma(self, q, out, in_, reads=(), writes=()):
        o = Op(q, lambda e: e.dma_start(out=out, in_=in_), is_dma=True)
        n = self.dma_n[q]
        self.dma_n[q] += 1
        o.idx = n
        self._collect(o, list(reads), list(writes))
        hist = self.dma_hist[q]
        if n >= self.NDMASEM:
            o.deps.append(hist[n - self.NDMASEM])
        hist.append(o)
        o.sig = True
        self.ops[q].append(o)
        self.n_ops += 1
        return o

    def emit(self, final_wait_ops):
        nc = self.nc
        with ExitStack() as es:
            esem = {e: es.enter_context(nc.semaphore("prog_" + e)) for e in self.ENGS}
            dsem = {q: [es.enter_context(nc.semaphore("dma_%s_%d" % (q, i))) for i in range(self.NDMASEM)]
                    for q in ("sp", "pool")}
            for e in self.ENGS:
                c = 0
                for o in self.ops[e]:
                    if o.is_dma:
                        slot = o.idx % self.NDMASEM
                        o.sem = dsem[e][slot]
                        o.count = 16 * (o.idx // self.NDMASEM + 1)
                    elif o.sig:
                        c += 1
                        o.sem = esem[e]
                        o.count = c
            block = es.enter_context(nc.Block())

            def run(ename, e, extra_final=None):
                known = {}
                for o in self.ops[ename]:
                    need = {}
                    for d in o.deps:
                        key = id(d.sem)
                        if known.get(key, 0) >= d.count:
                            continue
                        if key not in need or need[key][1] < d.count:
                            need[key] = (d.sem, d.count)
                    for key, (s, v) in need.items():
                        e.wait_ge(s, v)
                        known[key] = v
                    ins = o.fn(e)
                    if o.is_dma:
                        ins.then_inc(o.sem, 16)
                    elif o.sig:
                        ins.then_inc(o.sem, 1)
                if extra_final:
                    need = {}
                    for d in extra_final:
                        key = id(d.sem)
                        if key not in need or need[key][1] < d.count:
                            need[key] = (d.sem, d.count)
                    for key, (s, v) in need.items():
                        e.wait_ge(s, v)

            @block.tensor
            def _(e):
                run("pe", e)

            @block.scalar
            def _(e):
                run("act", e)

            @block.vector
            def _(e):
                run("dve", e)

            @block.gpsimd
            def _(e):
                run("pool", e)

            @block.sync
            def _(e):
                run("sp", e, extra_final=final_wait_ops)


RGW = 512
NH = 4
HD = 128
NLV_P = 7
NLV_S = 3

C32 = {"ident": 0, "ones": 128, "tri_p": 256, "up_p": 384, "tri_s": 512, "up_s": 640}
N32 = 768
C16 = {"identb": 0, "mstrict_p": 128, "minclt_p": 256, "mstrict_s": 384, "minclt_s": 512}
for _i in range(NLV_P):
    C16["lv_p%d" % _i] = 640 + 128 * _i
C16["segcol"] = 640 + 128 * NLV_P
C16["segrow"] = C16["segcol"] + 16
N16 = C16["segrow"] + 16 * 128


def _consts():
    i = np.arange(128)[:, None]
    j = np.arange(128)[None, :]
    c32 = np.zeros((128, N32), np.float32)
    c32[:, 0:128] = np.eye(128)
    c32[:, 128:256] = 1.0
    seg = 8
    same_s = (i // seg) == (j // seg)
    c32[:, 256:384] = (i <= j)
    c32[:, 384:512] = (i > j)
    c32[:, 512:640] = (i <= j) & same_s
    c32[:, 640:768] = (i > j) & same_s
    c16 = np.zeros((128, N16), np.float32)
    c16[:, 0:128] = np.eye(128)
    c16[:, 128:256] = (i > j)
    c16[:, 256:384] = (j >= i)
    c16[:, 384:512] = (i > j) & same_s
    c16[:, 512:640] = (j >= i) & same_s
    for lv in range(NLV_P):
        b = 1 << lv
        m = ((i // (2 * b)) == (j // (2 * b))) & (((j // b) % 2) == 1) & (((i // b) % 2) == 0)
        c16[:, C16["lv_p%d" % lv]:C16["lv_p%d" % lv] + 128] = m
    c16[:, C16["segcol"]:C16["segcol"] + 16] = (np.arange(128)[:, None] // seg) == np.arange(16)[None, :]
    sr = (np.arange(16)[:, None] == (np.arange(128)[None, :] // seg)).astype(np.float32).reshape(1, 16 * 128)
    c16[:, C16["segrow"]:] = np.repeat(sr, 128, axis=0)
    return c32, c16


PL = {}
_o = 0
for _nm, _n in (("ffn1_norm_pre", 8), ("ffn1_norm_post", 8), ("mix_norm_pre", 8), ("mix_norm_post", 8),
                ("ffn2_norm_pre", 8), ("ffn2_norm_post", 8), ("final_norm", 8),
                ("conv_w", 64), ("conv_b_rg", 4), ("rg_b_a", 4), ("rg_b_x", 4), ("rg_lambda", 4),
                ("dn_a_log", 4), ("dn_dt_bias", 4), ("dn_norm_w", 1)):
    PL[_nm] = _o
    _o += _n
PP_LAYER = _o


def _pack_params(inp):
    cols = []

    def vecn(v, n):
        return np.ascontiguousarray(np.asarray(v).reshape(n, 128).T)

    for l in range(DEPTH):
        for nm in ("ffn1_norm_pre", "ffn1_norm_post", "mix_norm_pre", "mix_norm_post",
                   "ffn2_norm_pre", "ffn2_norm_post"):
            cols.append(vecn(inp[nm][l], 8))
        cols.append(vecn(inp["final_norm"], 8))
        cw = np.asarray(inp["conv_w"][l])
        cols.append(np.ascontiguousarray(cw.reshape(4, 16, 128).transpose(2, 1, 0).reshape(128, 64)))
        for nm in ("conv_b_rg", "rg_b_a", "rg_b_x", "rg_lambda"):
            cols.append(vecn(inp[nm][l], 4))
        cols.append(np.repeat(np.asarray(inp["dn_a_log"][l]).reshape(1, 4), 128, axis=0))
        cols.append(np.repeat(np.asarray(inp["dn_dt_bias"][l]).reshape(1, 4), 128, axis=0))
        cols.append(np.asarray(inp["dn_norm_w"][l]).reshape(128, 1))
    return np.ascontiguousarray(np.concatenate(cols, axis=1).astype(np.float32))


def _pack_rgw(inp):
    out = np.zeros((DEPTH, 2, 128, 4, 128), np.float32)
    for l in range(DEPTH):
        for gi, nm in enumerate(("rg_w_a", "rg_w_x")):
            w = np.asarray(inp[nm][l])
            for c in range(4):
                for hh in range(2):
                    out[l, gi, hh * 64:(hh + 1) * 64, c, hh * 64:(hh + 1) * 64] = w[2 * c + hh]
    return out


class Cfg:
    def __init__(self, **kw):
        self.groups = [(0, 640, True), (640, 768, False), (1408, 640, False)]
        self.stages = "full"
        self.same_engine_sync = True
        self.__dict__.update(kw)


def blocks_of(T):
    if T == 768:
        return [(0, 384), (384, 384)]
    if T == 640:
        return [(0, 384), (384, 256)]
    raise ValueError(T)


def build_program(cfg):
    nc = bass.Bass("TRN2", target_bir_lowering=False)
    TM = 768
    NTM = TM // 128
    dr = {}

    def din(name, shape, dt=F32):
        dr[name] = nc.dram_tensor(name, shape, dt, kind="ExternalInput").ap()

    def dout(name, shape):
        dr[name] = nc.dram_tensor(name, shape, F32, kind="ExternalOutput").ap()

    din("xp", [SEQ, D]); din("xs", [NS * DS, D])
    din("pp", [128, DEPTH * PP_LAYER]); din("c32", [128, N32]); din("c16", [128, N16])
    din("rgw_r", [DEPTH, 2, 128, 4, 128], F32R)
    din("st_conv", [DEPTH, NS * 3, CONVC]); din("st_rg", [DEPTH, NS, RGW]); din("st_dn", [DEPTH, NS, NH, HD, HD])
    for nm in ("ffn1_w_up", "ffn2_w_up"):
        din(nm, [DEPTH, D, 2 * DFF], F32R)
    for nm in ("ffn1_w_down", "ffn2_w_down"):
        din(nm, [DEPTH, DFF, D], F32R)
    din("w_in", [DEPTH, D, INC], F32R); din("w_out", [DEPTH, D, D], F32R)
    dout("yp", [SEQ, D]); dout("ys", [NS * DS, D])
    dout("ncp", [DEPTH, 3, CONVC]); dout("nrp", [DEPTH, RGW]); dout("ndp", [DEPTH, NH, HD, HD])
    if getattr(cfg, "debug", False):
        dout("dbg", [3, 128, KC, TM])
    dout("ncs", [DEPTH, NS * 3, CONVC]); dout("nrs", [DEPTH, NS, RGW]); dout("nds", [DEPTH, NS, NH, HD, HD])

    P = Prog(nc, same_engine_sync=cfg.same_engine_sync)
    es = ExitStack()
    sb = lambda name, shape, dt: es.enter_context(nc.sbuf_tensor(name, shape, dt))
    X = sb("X", [128, KC, TM], F32)
    H = sb("H", [128, KC, TM], F32R)
    Y = sb("Y", [128, KC, TM], F32R)
    YF = Y[:].bitcast(F32)
    HF = H[:].bitcast(F32)
    NFB = 4
    ACTB = sb("ACTB", [128, NFB, TM], F32R)
    NWA = 2
    WA = [sb("WA%d" % i, [128, KC, 256], F32R) for i in range(NWA)]
    WB = [sb("WB%d" % i, [128, NFB, 256], F32R) for i in range(2)]
    PPt = sb("PPt", [128, DEPTH * PP_LAYER], F32)
    C32t = sb("C32t", [128, N32], F32)
    C16t = sb("C16t", [128, N16], BF16)
    ONESR = sb("ONESR", [128, 128], F32R)
    EPST = sb("EPST", [128, 8], F32)
    TT = sb("TT", [128, 4, TM], F32)
    SQ = sb("SQ", [128, 4, 384], F32R)
    RSTD = sb("RSTD", [128, 512], F32)
    SG = [TT[:, 0, 0:384], TT[:, 1, 0:384]]
    ZG = sb("ZG", [128, 4, TM], BF16)
    QKV = sb("QKV", [128, 12, TM], BF16)
    XR = sb("XR", [128, TM], F32R)
    XP = sb("XP", [128, 3 + TM], F32R)
    XPS = sb("XPS", [128, NS, 11], F32R)
    XPf = XP[:].bitcast(F32)
    XPSf = XPS[:].bitcast(F32)
    DG = sb("DG", [128, 4, 128], F32R)
    CSS = sb("CSS", [128, 16, NS * 3], F32)
    HS0 = sb("HS0", [128, 4, NS], F32)
    CONVT = sb("CONVT", [128, DEPTH, 16, 3], F32)
    HRG = sb("HRG", [128, DEPTH, 4], F32)
    SST = sb("SST", [128, DEPTH, NH, HD], F32)
    RGWt = sb("RGWt", [128, 2, 4, 128], F32R)
    WSC = sb("WSC", [128, KC, 8], F32R)
    SCT = sb("SCT", [128, NTM, 8], F32)
    LAYC = sb("LAYC", [128, 16], F32)
    ONE1 = sb("ONE1", [128, 1], F32)
    dt_names_bf = ["KTOK", "VTOK", "AM", "ATT", "DT", "DINV", "N1", "BV", "BKG", "WT", "VN", "QG", "KD", "SBF"]
    DB = {n: sb(n, [128, NH, 128], BF16) for n in dt_names_bf if n not in ("VN", "SBF", "WT")}
    DB["VN"] = DB["N1"]; DB["SBF"] = DB["DINV"]; DB["WT"] = DB["AM"]
    GRW = sb("GRW", [128, NH, 128], F32)
    EGR = sb("EGR", [128, NH, 128], F32)
    TRG = sb("TRG", [128, NH, 128], F32)
    OT = sb("OT", [128, NH, 128], F32)
    SM = sb("SM", [128, 64], F32)
    CSL = TT[0:48, 3, 0:512]
    RSL = TT[0:16, 3, 0:512]
    PS = [es.enter_context(nc.psum_tensor("PS%d" % i, [128, 512], F32)) for i in range(8)]

    def c32(nm):
        return C32t[:, C32[nm]:C32[nm] + 128]

    def c16(nm, n=128):
        return C16t[:, C16[nm]:C16[nm] + n]

    ident = c32("ident")
    ones_f = c32("ones")
    ones_r = ONESR[:, :]
    identb = c16("identb")
    st = {"ps": 0, "wa": 0, "wb": 0, "stg": 0, "sg": 0}

    resv = set()

    def psum():
        while True:
            i = st["ps"]
            st["ps"] = (i + 1) % 8
            if i not in resv:
                return i

    def psb(pi):
        return PS[pi][:].bitcast(BF16)

    def ACT(out, in_, func, reads, writes, **kw):
        return P.op("act", lambda e: e.activation(out=out, in_=in_, func=func, **kw), reads, writes)

    def CP(eng, out, in_, reads, writes):
        if eng == "act":
            return P.op("act", lambda e: e.copy(out=out, in_=in_), reads, writes)
        return P.op(eng, lambda e: e.tensor_copy(out=out, in_=in_), reads, writes)

    def TTo(eng, out, in0, in1, op, reads, writes):
        return P.op(eng, lambda e: e.tensor_tensor(out=out, in0=in0, in1=in1, op=op), reads, writes)

    def TS(eng, out, in0, s1, s2, op0, op1, reads, writes):
        if op1 is None:
            return P.op(eng, lambda e: e.tensor_scalar(out=out, in0=in0, scalar1=s1, scalar2=None, op0=op0), reads, writes)
        return P.op(eng, lambda e: e.tensor_scalar(out=out, in0=in0, scalar1=s1, scalar2=s2, op0=op0, op1=op1), reads, writes)

    def STT(out, in0, scalar, in1, op0, op1, reads, writes):
        return P.op("dve", lambda e: e.scalar_tensor_tensor(out=out, in0=in0, scalar=scalar, in1=in1, op0=op0, op1=op1), reads, writes)

    def MM(out, lhsT, rhs, start, stop, reads, writes):
        return P.op("pe", lambda e: e.matmul(out, lhsT=lhsT, rhs=rhs, start=start, stop=stop, skip_group_check=True), reads, writes)

    def TR(out, in_, idn, reads, writes):
        return P.op("pe", lambda e: e.transpose(out=out, in_=in_, identity=idn), reads, writes)

    P.dma("sp", PPt[:], dr["pp"][:, :], writes=[("PP", None)])
    P.dma("sp", C32t[:], dr["c32"][:, :], writes=[("C32", None)])
    for i in range(0, N16, 768):
        n = min(768, N16 - i)
        P.dma("sp", TT[:, 0, 0:n], dr["c16"][:, i:i + n], writes=[("TT", 0)])
        CP("dve", C16t[:, i:i + n], TT[:, 0, 0:n], [("TT", 0)], [("C16", None)])
    CP("dve", ones_r, ones_f, [("C32", None)], [("ONESR", None)])
    EPSC = {}
    for i, (key, val) in enumerate([((D, 1.0), EPS), ((D, 0.5), 4.0 * EPS), ((128, 1.0), EPS), ((1, 1.0), EPS), ("q", 128.0 * EPS)]):
        EPSC[key] = EPST[:, i:i + 1]
        P.op("dve", lambda e, i=i, val=val: e.memset(EPST[:, i:i + 1], val), writes=[("EPST", i)])
    P.op("dve", lambda e: e.memset(ONE1[:, :], 1.0), writes=[("ONE1", None)])
    P.op("dve", lambda e: e.memset(CONVT[:].rearrange("p l c t -> p (l c t)"), 0.0), writes=[("CONVT", None)])
    P.op("dve", lambda e: e.memset(HRG[:].rearrange("p l c -> p (l c)"), 0.0), writes=[("HRG", None)])
    P.op("dve", lambda e: e.memset(SST[:].rearrange("p l h e -> p (l h e)"), 0.0), writes=[("SST", None)])

    out_ops = []

    def pcol(l, nm, j=0, n=1):
        c = l * PP_LAYER + PL[nm] + j
        return PPt[:, c:c + n]

    ngroups = len(cfg.groups)
    for gi, (p0, npr, smp) in enumerate(cfg.groups):
        T = npr + (128 if smp else 0)
        ntile = T // 128
        blks = blocks_of(T)
        last_group = (gi == ngroups - 1)

        def tiles_of(b0, bn):
            return list(range(b0 // 128, (b0 + bn + 127) // 128))

        def rk(name, kcs, b0, bn):
            if isinstance(kcs, int):
                kcs = [kcs]
            return [(name, (kc, t)) for kc in kcs for t in tiles_of(b0, bn)]

        def stg_view(si):
            return TT[:, 2 * si:2 * si + 2, :].rearrange("p a t -> p (a t)")[:, 0:D]

        def stg_keys(si):
            return [("TT", 2 * si), ("TT", 2 * si + 1)]

        for t in range(ntile):
            si = st["stg"]; st["stg"] ^= 1
            stg = stg_view(si)
            src = dr["xp"][p0 + t * 128: p0 + (t + 1) * 128, :] if t * 128 < npr else dr["xs"][:, :]
            P.dma("sp", stg, src, writes=stg_keys(si))
            for half in range(2):
                pi = psum()
                for j in range(4):
                    kc = half * 4 + j
                    TR(PS[pi][:, j * 128:(j + 1) * 128], stg[:, kc * 128:(kc + 1) * 128], ident,
                       stg_keys(si) + [("C32", None)], [("PS", pi)])
                CP("act" if half == 0 else "dve", X[:, half * 4:(half + 1) * 4, t * 128:(t + 1) * 128],
                   PS[pi][:].rearrange("p (j c) -> p j c", j=4), [("PS", pi)], [("X", (half * 4 + j, t)) for j in range(4)])

        def sumsq_rstd(SRC, srcname, b0, bn, nfeat, post_scale=1.0, nk=KC):
            pi = psum()
            for kc in range(nk):
                rd = rk(srcname, kc, b0, bn)
                ACT(SQ[:, kc % 4, 0:bn], SRC[:, kc, b0:b0 + bn], ACTF.Square, rd, [("SQ", kc % 4)])
                MM(PS[pi][:, 0:bn], ones_r, SQ[:, kc % 4, 0:bn], kc == 0, kc == nk - 1,
                   [("SQ", kc % 4), ("ONESR", None)], [("PS", pi)])
            rstd_from_psum(pi, bn, nfeat, post_scale)

        def rstd_from_psum(pi, bn, nfeat, post_scale=1.0):
            ACT(RSTD[:, 0:bn], PS[pi][:, 0:bn], ACTF.Ln, [("PS", pi), ("EPST", None)], [("RSTD", None)],
                scale=1.0 / (nfeat * post_scale * post_scale), bias=EPSC[(nfeat, post_scale)])
            ACT(RSTD[:, 0:bn], RSTD[:, 0:bn], ACTF.Exp, [("RSTD", None)], [("RSTD", None)], scale=-0.5)

        def norm_scale(DST, dstname, SRC, srcname, l, gname, b0, bn):
            for kc in range(KC):
                STT(DST[:, kc, b0:b0 + bn], SRC[:, kc, b0:b0 + bn], pcol(l, gname, kc), RSTD[:, 0:bn], ALU.mult, ALU.mult,
                    rk(srcname, kc, b0, bn) + [("RSTD", None), ("PP", None)], rk(dstname, kc, b0, bn))

        def prenorm_to_H(l, gname):
            for (b0, bn) in blks:
                sumsq_rstd(X, "X", b0, bn, D)
                norm_scale(H, "H", X, "X", l, gname, b0, bn)

        def postnorm_residual(SRCw, SRC, srcname, l, gname, scale):
            for (b0, bn) in blks:
                sumsq_rstd(SRC, srcname, b0, bn, D, post_scale=scale)
                norm_scale(SRCw, srcname, SRC, srcname, l, gname, b0, bn)
                for kc in range(KC):
                    TTo("dve", X[:, kc, b0:b0 + bn], X[:, kc, b0:b0 + bn], SRC[:, kc, b0:b0 + bn], ALU.add,
                        rk(srcname, kc, b0, bn) + rk("X", kc, b0, bn), rk("X", kc, b0, bn))

        def ffn(l, which):
            wup = dr["ffn%d_w_up" % which][l].rearrange("(kc p) n -> p kc n", p=128)
            wdn = dr["ffn%d_w_down" % which][l].rearrange("(c p) n -> p c n", p=128)
            prenorm_to_H(l, "ffn%d_norm_pre" % which)
            fblocks = [(0, 4), (4, 4), (8, 4), (12, 4), (16, 4), (20, 2)]
            for fbi, (c0, nch) in enumerate(fblocks):
                for cp in range(0, nch, 2):
                    wi_g = st["wa"]; st["wa"] = (st["wa"] + 1) % NWA
                    wi_u = st["wa"]; st["wa"] = (st["wa"] + 1) % NWA
                    col = (c0 + cp) * 128
                    P.dma("pool", WA[wi_g][:], wup[:, :, col:col + 256], writes=[("WA", wi_g)])
                    P.dma("pool", WA[wi_u][:], wup[:, :, DFF + col:DFF + col + 256], writes=[("WA", wi_u)])
                    for cc in range(2):
                        pg = [psum() for _ in blks]
                        for kc in range(KC):
                            for bi, (b0, bn) in enumerate(blks):
                                MM(PS[pg[bi]][:, 0:bn], WA[wi_g][:, kc, cc * 128:(cc + 1) * 128], H[:, kc, b0:b0 + bn],
                                   kc == 0, kc == KC - 1, [("WA", wi_g)] + rk("H", kc, b0, bn), [("PS", pg[bi])])
                        sgs = []
                        for bi, (b0, bn) in enumerate(blks):
                            si = st["sg"]; st["sg"] = (st["sg"] + 1) % len(SG)
                            sgs.append(si)
                            ACT(SG[si][:, 0:bn], PS[pg[bi]][:, 0:bn], ACTF.Silu, [("PS", pg[bi])], [("TT", si)])
                        pu = [psum() for _ in blks]
                        for kc in range(KC):
                            for bi, (b0, bn) in enumerate(blks):
                                MM(PS[pu[bi]][:, 0:bn], WA[wi_u][:, kc, cc * 128:(cc + 1) * 128], H[:, kc, b0:b0 + bn],
                                   kc == 0, kc == KC - 1, [("WA", wi_u)] + rk("H", kc, b0, bn), [("PS", pu[bi])])
                        for bi, (b0, bn) in enumerate(blks):
                            TTo("dve", ACTB[:, cp + cc, b0:b0 + bn], PS[pu[bi]][:, 0:bn], SG[sgs[bi]][:, 0:bn], ALU.mult,
                                [("PS", pu[bi]), ("TT", sgs[bi])], [("ACTB", (cp + cc, b0))])
                for ocp in range(4):
                    wi = st["wb"]; st["wb"] ^= 1
                    P.dma("pool", WB[wi][:, 0:nch, :], wdn[:, c0:c0 + nch, ocp * 256:(ocp + 1) * 256], writes=[("WB", wi)])
                    for o2 in range(2):
                        oc = ocp * 2 + o2
                        pd = [psum() for _ in blks]
                        for j in range(nch):
                            for bi, (b0, bn) in enumerate(blks):
                                MM(PS[pd[bi]][:, 0:bn], WB[wi][:, j, o2 * 128:(o2 + 1) * 128], ACTB[:, j, b0:b0 + bn],
                                   j == 0, j == nch - 1, [("WB", wi), ("ACTB", (j, b0))], [("PS", pd[bi])])
                        for bi, (b0, bn) in enumerate(blks):
                            if fbi == 0:
                                CP("act", Y[:, oc, b0:b0 + bn], PS[pd[bi]][:, 0:bn], [("PS", pd[bi])], rk("Y", oc, b0, bn))
                            else:
                                TTo("dve", Y[:, oc, b0:b0 + bn], PS[pd[bi]][:, 0:bn], YF[:, oc, b0:b0 + bn], ALU.add,
                                    [("PS", pd[bi])] + rk("Y", oc, b0, bn), rk("Y", oc, b0, bn))
            postnorm_residual(Y, YF, "Y", l, "ffn%d_norm_post" % which, 0.5)

        MIXR = Y
        allH = [("H", (kc, t)) for kc in range(KC) for t in range(NTM)]
        allACTB = [("ACTB", None)]

        def mixer(l):
            win = dr["w_in"][l].rearrange("(kc p) n -> p kc n", p=128)
            prenorm_to_H(l, "mix_norm_pre")
            ACT(LAYC[:, 0:4], pcol(l, "rg_lambda", 0, 4), ACTF.Exp, [("PP", None)], [("LAYC", 0)], scale=-1.0)
            ACT(LAYC[:, 0:4], LAYC[:, 0:4], ACTF.Ln, [("LAYC", 0), ("ONE1", None)], [("LAYC", 0)], bias=ONE1[:, 0:1])
            TS("dve", LAYC[:, 4:8], LAYC[:, 0:4], -16.0, None, ALU.mult, None, [("LAYC", 0)], [("LAYC", 1)])
            TS("dve", LAYC[:, 0:4], LAYC[:, 0:4], -8.0, None, ALU.mult, None, [("LAYC", 0), ("LAYC", 1)], [("LAYC", 0)])
            ACT(LAYC[:, 8:12], pcol(l, "dn_a_log", 0, 4), ACTF.Exp, [("PP", None)], [("LAYC", 2)])
            TS("dve", LAYC[:, 8:12], LAYC[:, 8:12], -1.0, None, ALU.mult, None, [("LAYC", 2)], [("LAYC", 2)])
            P.dma("pool", RGWt[:], dr["rgw_r"][l].rearrange("g p c m -> p g c m"), writes=[("RGW", None)])
            P.dma("pool", WSC[:], win[:, :, 3072:3080], writes=[("WSC", None)])
            if smp:
                for q in range(4):
                    P.dma("sp", CSL, dr["st_conv"][l][:, q * 512:(q + 1) * 512], writes=[("TT", 3)])
                    pi = psum()
                    for j in range(4):
                        TR(PS[pi][:, j * 48:(j + 1) * 48], CSL[:, j * 128:(j + 1) * 128], ident[0:48, 0:48],
                           [("TT", 3), ("C32", None)], [("PS", pi)])
                    CP("dve", CSS[:, q * 4:(q + 1) * 4, :], PS[pi][:, 0:192].rearrange("p (j c) -> p j c", j=4),
                       [("PS", pi)], [("CSS", q * 4 + j) for j in range(4)])
                P.dma("sp", RSL, dr["st_rg"][l][:, :], writes=[("TT", 3)])
                pi = psum()
                for j in range(4):
                    TR(PS[pi][:, j * 16:(j + 1) * 16], RSL[:, j * 128:(j + 1) * 128], ident[0:16, 0:16],
                       [("TT", 3), ("C32", None)], [("PS", pi)])
                CP("dve", HS0[:, :, :], PS[pi][:, 0:64].rearrange("p (j c) -> p j c", j=4), [("PS", pi)], [("HS0", None)])

            wa_state = {}

            def load_pair(colstart):
                wi = st["wa"]; st["wa"] = (st["wa"] + 1) % NWA
                P.dma("pool", WA[wi][:], win[:, :, colstart:colstart + 256], writes=[("WA", wi)])
                return wi

            def proj(wi, cc):
                pb = [psum() for _ in blks]
                for kc in range(KC):
                    for bi, (b0, bn) in enumerate(blks):
                        MM(PS[pb[bi]][:, 0:bn], WA[wi][:, kc, cc * 128:(cc + 1) * 128], H[:, kc, b0:b0 + bn],
                           kc == 0, kc == KC - 1, [("WA", wi)] + rk("H", kc, b0, bn), [("PS", pb[bi])])
                return pb

            def conv_chunk(l, ch, pb):
                CP("act", XP[:, 0:3], CONVT[:, l, ch, :], [("CONVT", (l, ch))], [("XP", 0)])
                for bi, (b0, bn) in enumerate(blks):
                    n = min(bn, npr - b0)
                    if n > 0:
                        CP("act", XP[:, 3 + b0:3 + b0 + n], PS[pb[bi]][:, 0:n], [("PS", pb[bi])], [("XP", 1 + bi)])
                if smp:
                    b0, bn = blks[-1]
                    off = npr - b0
                    CP("dve", XPS[:, :, 0:3], CSS[:, ch, :].rearrange("p (s t) -> p s t", t=3), [("CSS", ch)], [("XPS", 0)])
                    CP("act", XPS[:, :, 3:11], PS[pb[-1]][:, off:off + 128].rearrange("p (s t) -> p s t", t=8),
                       [("PS", pb[-1])], [("XPS", 1)])
                for tap in range(4):
                    TS("dve", DG[:, tap, :], ident, pcol(l, "conv_w", ch * 4 + tap), None, ALU.mult, None,
                       [("C32", None), ("PP", None)], [("DG", tap)])
                xpk = [("XP", i) for i in range(1 + len(blks))]
                pc = [psum() for _ in blks]
                for bi, (b0, bn) in enumerate(blks):
                    n = min(bn, npr - b0)
                    for tap in range(4):
                        MM(PS[pc[bi]][:, 0:n], DG[:, tap, :], XP[:, b0 + tap:b0 + tap + n], tap == 0, tap == 3,
                           xpk + [("DG", tap)], [("PS", pc[bi])])
                CP("dve", CONVT[:, l, ch, :], XPf[:, npr:npr + 3], xpk, [("CONVT", (l, ch))])
                if smp:
                    b0, bn = blks[-1]
                    off = npr - b0
                    for tap in range(4):
                        MM(PS[pc[-1]][:, off:off + 128].rearrange("p (s t) -> p s t", t=8), DG[:, tap, :], XPS[:, :, tap:tap + 8],
                           False, tap == 3, [("XPS", 0), ("XPS", 1), ("DG", tap)], [("PS", pc[-1])])
                    CP("dve", CSS[:, ch, :].rearrange("p (s t) -> p s t", t=3), XPSf[:, :, 8:11], [("XPS", 0), ("XPS", 1)], [("CSS", ch)])
                return pc

            T0 = TT[:, 0, :]; T1 = TT[:, 1, :]; T2 = TT[:, 2, :]
            XRf = XR[:].bitcast(F32)
            k0, k1, k2, k3 = ("TT", 0), ("TT", 1), ("TT", 2), ("XR", None)

            for cpair in range(2):
                wi_x = load_pair(cpair * 256)
                wi_g = load_pair(2048 + cpair * 256)
                for cc in range(2):
                    ch = cpair * 2 + cc
                    pb = proj(wi_x, cc)
                    pc = conv_chunk(l, ch, pb)
                    for bi, (b0, bn) in enumerate(blks):
                        ACT(XR[:, b0:b0 + bn], PS[pc[bi]][:, 0:bn], ACTF.Identity, [("PS", pc[bi]), ("PP", None), k3], [k3],
                            bias=pcol(l, "conv_b_rg", ch))
                    pa = [psum() for _ in blks]
                    for bi, (b0, bn) in enumerate(blks):
                        MM(PS[pa[bi]][:, 0:bn], RGWt[:, 0, ch, :], XR[:, b0:b0 + bn], True, True, [("RGW", None), k3], [("PS", pa[bi])])
                    for bi, (b0, bn) in enumerate(blks):
                        ACT(T0[:, b0:b0 + bn], PS[pa[bi]][:, 0:bn], ACTF.Sigmoid, [("PS", pa[bi]), ("PP", None)], [k0],
                            bias=pcol(l, "rg_b_a", ch))
                    px = [psum() for _ in blks]
                    for bi, (b0, bn) in enumerate(blks):
                        MM(PS[px[bi]][:, 0:bn], RGWt[:, 1, ch, :], XR[:, b0:b0 + bn], True, True, [("RGW", None), k3], [("PS", px[bi])])
                    for bi, (b0, bn) in enumerate(blks):
                        ACT(T1[:, b0:b0 + bn], PS[px[bi]][:, 0:bn], ACTF.Sigmoid, [("PS", px[bi]), ("PP", None)], [k1],
                            bias=pcol(l, "rg_b_x", ch))
                    ACT(T2[:, 0:T], T0[:, 0:T], ACTF.Exp, [k0, ("LAYC", 1)], [k2], scale=LAYC[:, 4 + ch:5 + ch])
                    TS("dve", T2[:, 0:T], T2[:, 0:T], -1.0, 1.0, ALU.mult, ALU.add, [k2], [k2])
                    TS("dve", T2[:, 0:T], T2[:, 0:T], 1e-30, None, ALU.max, None, [k2], [k2])
                    ACT(T2[:, 0:T], T2[:, 0:T], ACTF.Ln, [k2], [k2])
                    ACT(T2[:, 0:T], T2[:, 0:T], ACTF.Exp, [k2], [k2], scale=0.5)
                    ACT(T0[:, 0:T], T0[:, 0:T], ACTF.Exp, [k0, ("LAYC", 0)], [k0], scale=LAYC[:, ch:ch + 1])
                    TTo("dve", T1[:, 0:T], T1[:, 0:T], XRf[:, 0:T], ALU.mult, [k1, k3], [k1])
                    TTo("dve", T1[:, 0:T], T1[:, 0:T], T2[:, 0:T], ALU.mult, [k1, k2], [k1])
                    for bi, (b0, bn) in enumerate(blks):
                        n = min(bn, npr - b0)
                        if n <= 0:
                            continue
                        init = HRG[:, l, ch:ch + 1] if bi == 0 else T2[:, b0 - 1:b0]
                        P.op("dve", lambda e, b0=b0, n=n, init=init: e.tensor_tensor_scan(
                            out=T2[:, b0:b0 + n], data0=T0[:, b0:b0 + n], data1=T1[:, b0:b0 + n],
                            initial=init, op0=ALU.mult, op1=ALU.add),
                            [k0, k1, k2, ("HRG", (l, ch))], [k2])
                    CP("dve", HRG[:, l, ch:ch + 1], T2[:, npr - 1:npr], [k2], [("HRG", (l, ch))])
                    if smp:
                        a_first = T0[:, npr:npr + 128:8]
                        b_first = T1[:, npr:npr + 128:8]
                        TTo("dve", SM[:, 0:16], a_first, HS0[:, ch, :], ALU.mult, [k0, ("HS0", None)], [("SM", None)])
                        TTo("dve", b_first, b_first, SM[:, 0:16], ALU.add, [k1, ("SM", None)], [k1])
                        TS("dve", a_first, a_first, 0.0, None, ALU.mult, None, [k0, ("SM", None)], [k0])
                        P.op("dve", lambda e: e.tensor_tensor_scan(out=T2[:, npr:npr + 128], data0=T0[:, npr:npr + 128],
                                                                  data1=T1[:, npr:npr + 128], initial=0.0, op0=ALU.mult, op1=ALU.add),
                             [k0, k1, k2], [k2])
                        CP("dve", HS0[:, ch, :], T2[:, npr + 7:npr + 128:8], [k2, ("SM", None)], [("HS0", None)])
                    pgt = proj(wi_g, cc)
                    for bi, (b0, bn) in enumerate(blks):
                        CP("act", T0[:, b0:b0 + bn], PS[pgt[bi]][:, 0:bn], [("PS", pgt[bi]), k0], [k0])
                    ACT(T1[:, 0:T], T0[:, 0:T], ACTF.Square, [k0, k1], [k1])
                    TS("dve", T1[:, 0:T], T1[:, 0:T], 0.044715, 1.0, ALU.mult, ALU.add, [k1], [k1])
                    TTo("dve", T1[:, 0:T], T1[:, 0:T], T0[:, 0:T], ALU.mult, [k0, k1], [k1])
                    ACT(T1[:, 0:T], T1[:, 0:T], ACTF.Sigmoid, [k1], [k1], scale=2.0 * 0.7978845608028654)
                    TTo("dve", T0[:, 0:T], T0[:, 0:T], T1[:, 0:T], ALU.mult, [k0, k1], [k0])
                    for bi, (b0, bn) in enumerate(blks):
                        TTo("dve", MIXR[:, ch, b0:b0 + bn], T2[:, b0:b0 + bn], T0[:, b0:b0 + bn], ALU.mult,
                            [k0, k2], rk("Y", ch, b0, bn))

            for kind in range(3):
                for cpair in range(2):
                    wi = load_pair(512 + kind * 512 + cpair * 256)
                    for cc in range(2):
                        hh = cpair * 2 + cc
                        ch = 4 + kind * 4 + hh
                        pb = proj(wi, cc)
                        pc = conv_chunk(l, ch, pb)
                        if kind == 2:
                            for bi, (b0, bn) in enumerate(blks):
                                ACT(QKV[:, 8 + hh, b0:b0 + bn], PS[pc[bi]][:, 0:bn], ACTF.Silu, [("PS", pc[bi])], [("QKV", None)])
                        else:
                            for bi, (b0, bn) in enumerate(blks):
                                ACT(T1[:, b0:b0 + bn], PS[pc[bi]][:, 0:bn], ACTF.Silu, [("PS", pc[bi]), k1], [k1])
                            for bi, (b0, bn) in enumerate(blks):
                                ACT(SQ[:, 0, 0:bn], T1[:, b0:b0 + bn], ACTF.Square, [k1], [("SQ", 0)])
                                pi = psum()
                                MM(PS[pi][:, 0:bn], ones_r, SQ[:, 0, 0:bn], True, True, [("SQ", 0), ("ONESR", None)], [("PS", pi)])
                                if kind == 0:
                                    ACT(RSTD[:, 0:bn], PS[pi][:, 0:bn], ACTF.Ln, [("PS", pi), ("EPST", None)], [("RSTD", None)],
                                        scale=128.0, bias=EPSC["q"])
                                else:
                                    ACT(RSTD[:, 0:bn], PS[pi][:, 0:bn], ACTF.Ln, [("PS", pi), ("EPST", None)], [("RSTD", None)],
                                        scale=1.0, bias=EPSC[(1, 1.0)])
                                ACT(RSTD[:, 0:bn], RSTD[:, 0:bn], ACTF.Exp, [("RSTD", None)], [("RSTD", None)], scale=-0.5)
                                TTo("dve", QKV[:, kind * 4 + hh, b0:b0 + bn], T1[:, b0:b0 + bn], RSTD[:, 0:bn], ALU.mult,
                                    [k1, ("RSTD", None)], [("QKV", None)])

            for cpair in range(2):
                wi = load_pair(2560 + cpair * 256)
                for cc in range(2):
                    hh = cpair * 2 + cc
                    pb = proj(wi, cc)
                    for bi, (b0, bn) in enumerate(blks):
                        ACT(ZG[:, hh, b0:b0 + bn], PS[pb[bi]][:, 0:bn], ACTF.Silu, [("PS", pb[bi])], [("ZG", (hh, bi))])

            pi = psum()
            for t in range(ntile):
                for kc in range(KC):
                    MM(PS[pi][:, t * 8:(t + 1) * 8], H[:, kc, t * 128:(t + 1) * 128], WSC[:, kc, :], (t == 0 and kc == 0), kc == KC - 1,
                       [("WSC", None), ("H", (kc, t))], [("PS", pi)])
            CP("dve", SCT[:, 0:ntile, :], PS[pi][:, 0:ntile * 8].rearrange("p (t c) -> p t c", c=8), [("PS", pi)], [("SCT", None)])

            for t in range(ntile):
                delta_tile(l, t, smp and t == ntile - 1)

            if getattr(cfg, "debug", False) and l == 0:
                out_ops.append(P.dma("sp", dr["dbg"][gi], YF, reads=[("Y", None)], writes=[("DBG", gi)]))
            wout = dr["w_out"][l].rearrange("(kc p) n -> p kc n", p=128)
            for ocp in range(4):
                wi = st["wa"]; st["wa"] = (st["wa"] + 1) % NWA
                P.dma("pool", WA[wi][:], wout[:, :, ocp * 256:(ocp + 1) * 256], writes=[("WA", wi)])
                for o2 in range(2):
                    oc = ocp * 2 + o2
                    pd = [psum() for _ in blks]
                    for kc in range(KC):
                        for bi, (b0, bn) in enumerate(blks):
                            MM(PS[pd[bi]][:, 0:bn], WA[wi][:, kc, o2 * 128:(o2 + 1) * 128], MIXR[:, kc, b0:b0 + bn],
                               kc == 0, kc == KC - 1, [("WA", wi)] + rk("Y", kc, b0, bn), [("PS", pd[bi])])
                    for bi, (b0, bn) in enumerate(blks):
                        CP("act", H[:, oc, b0:b0 + bn], PS[pd[bi]][:, 0:bn], [("PS", pd[bi])], rk("H", oc, b0, bn))
            postnorm_residual(H, HF, "H", l, "mix_norm_post", 1.0)

            if smp:
                for q in range(4):
                    pi = psum()
                    for j in range(4):
                        TR(PS[pi][0:48, j * 128:(j + 1) * 128], CSS[:, q * 4 + j, :], ident, [("CSS", q * 4 + j), ("C32", None)], [("PS", pi)])
                    CP("dve", CSL, PS[pi][0:48, :], [("PS", pi)], [("TT", 3)])
                    out_ops.append(P.dma("sp", dr["ncs"][l][:, q * 512:(q + 1) * 512], CSL, reads=[("TT", 3)], writes=[("CSLo", q)]))
                pi = psum()
                for j in range(4):
                    TR(PS[pi][0:16, j * 128:(j + 1) * 128], HS0[:, j, :], ident, [("HS0", None), ("C32", None)], [("PS", pi)])
                CP("dve", RSL, PS[pi][0:16, :], [("PS", pi)], [("TT", 3)])
                out_ops.append(P.dma("sp", dr["nrs"][l][:, :], RSL, reads=[("TT", 3)], writes=[("RSLo", 0)]))
            if last_group:
                for q in range(4):
                    pi = psum()
                    for j in range(4):
                        TR(PS[pi][0:3, j * 128:(j + 1) * 128], CONVT[:, l, q * 4 + j, :], ident, [("CONVT", (l, q * 4 + j)), ("C32", None)], [("PS", pi)])
                    CP("dve", CSL[0:3, :], PS[pi][0:3, :], [("PS", pi)], [("TT", 3)])
                    out_ops.append(P.dma("sp", dr["ncp"][l][:, q * 512:(q + 1) * 512], CSL[0:3, :], reads=[("TT", 3)], writes=[("CSLo", q)]))
                pi = psum()
                TR(PS[pi][0:4, 0:128], HRG[:, l, :], ident, [("HRG", None), ("C32", None)], [("PS", pi)])
                CP("dve", RSL[0:4, 0:128], PS[pi][0:4, 0:128], [("PS", pi)], [("TT", 3)])
                out_ops.append(P.dma("sp", dr["nrp"][l].rearrange("(c p) -> c p", p=128), RSL[0:4, 0:128], reads=[("TT", 3)], writes=[("RSLo", 0)]))
                out_ops.append(P.dma("sp", dr["ndp"][l].rearrange("h d e -> d h e"), SST[:, l, :, :], reads=[("SST", None)], writes=[("SSTo", l)]))

        def delta_prelude(l, t, is_s):
            sfx = "_s" if is_s else "_p"
            BETA = SM[:, 16:20]; GT = SM[:, 20:24]; GC = SM[:, 24:32]; EG = SM[:, 32:36]; BEG = SM[:, 36:40]; EKD = SM[:, 40:44]
            ACT(BETA, SCT[:, t, 0:4], ACTF.Sigmoid, [("SCT", None)], [("SM", 1)])
            TTo("dve", GT, SCT[:, t, 4:8], pcol(l, "dn_dt_bias", 0, 4), ALU.add, [("SCT", None), ("PP", None)], [("SM", 2)])
            ACT(GT, GT, ACTF.Exp, [("SM", 2)], [("SM", 2)])
            ACT(GT, GT, ACTF.Ln, [("SM", 2), ("ONE1", None)], [("SM", 2)], bias=ONE1[:, 0:1])
            TTo("dve", GT, GT, LAYC[:, 8:12], ALU.mult, [("SM", 2), ("LAYC", 2)], [("SM", 2)])
            pgc = psum()
            MM(PS[pgc][:, 0:4], c32("tri" + sfx), GT, True, True, [("SM", 2), ("C32", None)], [("PS", pgc)])
            MM(PS[pgc][:, 4:8], c32("up" + sfx), GT, False, True, [("SM", 2), ("C32", None)], [("PS", pgc)])
            bc4 = lambda ap: ap.unsqueeze(2).broadcast_to([128, NH, 128])
            bm4 = lambda ap: ap.unsqueeze(1).broadcast_to([128, NH, 128])
            TTo("dve", TRG[:, :, :], bm4(c32("tri" + sfx)), bc4(GT), ALU.mult, [("SM", 2), ("C32", None), ("TRG", 0), ("TRG", 1)], [("TRG", 0), ("TRG", 1)])
            pgr = psum()
            resv.add(pgr)
            st["pgr"] = pgr
            st["pgr_users"] = 2
            MM(PS[pgr][:, :], ones_f, TRG[:].rearrange("p h c -> p (h c)"), True, True, [("TRG", 0), ("TRG", 1), ("C32", None)], [("PS", pgr)])
            CP("dve", GC, PS[pgc][:, 0:8], [("PS", pgc), ("PS", pgr)], [("SM", 3)])
            ACT(EG, GC[:, 0:4], ACTF.Exp, [("SM", 3)], [("SM", 4)])
            ACT(EKD, GC[:, 4:8], ACTF.Exp, [("SM", 3)], [("SM", 5)])
            TTo("dve", BEG, BETA, EG, ALU.mult, [("SM", 1), ("SM", 4)], [("SM", 6)])
            PGR4 = PS[pgr][:].rearrange("p (h c) -> p h c", h=NH)
            gk = [("GRW", 0), ("GRW", 1)]; ek = [("EGR", 0), ("EGR", 1)]
            TTo("dve", GRW[:, :, :], PGR4, GC[:, 0:4].unsqueeze(2).broadcast_to([128, NH, 128]), ALU.subtract, [("PS", pgr), ("SM", 3)] + gk, gk)
            GRWf = GRW[:].rearrange("p h c -> p (h c)")
            STT(GRWf, GRWf, -1.0, GRWf, ALU.mult, ALU.max, gk, gk)
            ACT(GRW[:], GRW[:], ACTF.Exp, gk, gk, scale=-1.0)
            ACT(EGR[:], PGR4, ACTF.Exp, [("PS", pgr)] + ek, ek)
            resv.discard(pgr)

        def delta_pair(l, t, is_s, hg):
            c0 = t * 128
            sfx = "_s" if is_s else "_p"
            nlv = NLV_S if is_s else NLV_P
            HS = slice(2 * hg, 2 * hg + 2)
            heads = (2 * hg, 2 * hg + 1)
            bk = lambda n: [(n, hg)]
            KTOK, VTOK, AM, ATT, DTt, DINV, N1, BV, BKG, WT, VN, QG, KD, SBF = [DB[n][:, HS, :] for n in dt_names_bf]
            f2 = lambda ap: ap.rearrange("p h c -> p (h c)")
            qkv_r = [("QKV", None)]
            bc = lambda ap: ap.unsqueeze(2).broadcast_to([128, 2, 128])
            bm = lambda ap: ap.unsqueeze(1).broadcast_to([128, 2, 128])
            BETA = SM[:, 16 + 2 * hg:18 + 2 * hg]; GT = SM[:, 20 + 2 * hg:22 + 2 * hg]; GC0 = SM[:, 24 + 2 * hg:26 + 2 * hg]
            BEG = SM[:, 36 + 2 * hg:38 + 2 * hg]; EKD = SM[:, 40 + 2 * hg:42 + 2 * hg]
            GRWp = GRW[:, HS, :]; EGRp = EGR[:, HS, :]; TRGp = TRG[:, HS, :]; OTp = OT[:, HS, :]
            if getattr(cfg, 'delta_stop', 99) == 1:
                resv.clear()
                return
            pgr = st["pgr"]
            ptk = psum()
            for i, h in enumerate(heads):
                TR(psb(ptk)[:, i * 128:(i + 1) * 128], QKV[:, 4 + h, c0:c0 + 128], identb, qkv_r + [("C16", None)], [("PS", ptk)])
            for i, h in enumerate(heads):
                TR(psb(ptk)[:, 256 + i * 128:256 + (i + 1) * 128], QKV[:, 8 + h, c0:c0 + 128], identb, qkv_r + [("C16", None)], [("PS", ptk)])
            pkk = psum()
            for i, h in enumerate(heads):
                MM(PS[pkk][:, i * 128:(i + 1) * 128], QKV[:, 4 + h, c0:c0 + 128], QKV[:, 4 + h, c0:c0 + 128], i == 0, True, qkv_r, [("PS", pkk)])
            for i, h in enumerate(heads):
                MM(PS[pkk][:, 256 + i * 128:256 + (i + 1) * 128], QKV[:, 4 + h, c0:c0 + 128], QKV[:, h, c0:c0 + 128], False, True, qkv_r,
                   [("PS", pkk), ("PGD", hg)])
            yield
            if getattr(cfg, 'delta_stop', 99) == 2:
                resv.clear()
                return
            PGR3 = PS[pgr][:, hg * 256:(hg + 1) * 256].rearrange("p (h c) -> p h c", h=2)

            if getattr(cfg, 'delta_stop', 99) == 25:
                resv.clear()
                return
            CP("act", f2(KTOK), psb(ptk)[:, 0:256], [("PS", ptk)] + bk("KTOK"), bk("KTOK"))
            CP("act", f2(VTOK), psb(ptk)[:, 256:512], [("PS", ptk)] + bk("VTOK"), bk("VTOK"))
            yield
            if getattr(cfg, 'delta_stop', 99) == 3:
                resv.clear()
                return
            TTo("dve", BV, VTOK, bc(BETA), ALU.mult, bk("VTOK") + [("SM", 1)] + bk("BV"), bk("BV"))
            TTo("dve", BKG, KTOK, bc(BEG), ALU.mult, bk("KTOK") + [("SM", 6)] + bk("BKG"), bk("BKG"))
            TTo("dve", KD, KTOK, bc(EKD), ALU.mult, bk("KTOK") + [("SM", 5)] + bk("KD"), bk("KD"))
            TTo("dve", QG, QKV[:, HS, c0:c0 + 128], EGRp, ALU.mult, qkv_r + bk("EGR") + bk("QG"), bk("QG"))
            CP("act", DTt, bm(identb), [("C16", None)] + bk("DT"), bk("DT"))
            CP("act", DINV, bm(identb), [("C16", None)] + bk("DINV"), bk("DINV"))
            yield
            if getattr(cfg, 'delta_stop', 99) == 4:
                resv.clear()
                return
            TTo("dve", TRGp, GRWp, bm(c16("mstrict" + sfx)), ALU.mult, bk("GRW") + [("C16", None)] + bk("TRG"), bk("TRG"))
            TTo("dve", TRGp, TRGp, bc(BETA), ALU.mult, bk("TRG") + [("SM", 1)], bk("TRG"))
            TTo("dve", OTp, GRWp, bm(c16("minclt" + sfx)), ALU.mult, bk("GRW") + [("C16", None)] + bk("OT"), bk("OT"))
            TTo("dve", f2(AM), PS[pkk][:, 0:256], f2(TRGp), ALU.mult, [("PS", pkk)] + bk("TRG") + bk("AM"), bk("AM"))
            TTo("dve", f2(ATT), PS[pkk][:, 256:512], f2(OTp), ALU.mult, [("PS", pkk)] + bk("OT") + bk("ATT"), bk("ATT"))
            yield
            if getattr(cfg, 'delta_stop', 99) == 5:
                resv.clear()
                return
            for lv in range(nlv):
                p1 = psum()
                for i in range(2):
                    MM(PS[p1][:, i * 128:(i + 1) * 128], AM[:, i, :], DTt[:, i, :], i == 0, True, bk("AM") + bk("DT"), [("PS", p1)])
                yield
                TTo("dve", N1, PS[p1][:, 0:256].rearrange("p (h c) -> p h c", h=2), bm(c16("lv_p%d" % lv)), ALU.mult,
                    [("PS", p1), ("C16", None)] + bk("N1"), bk("N1"))
                yield
                p2 = psum()
                for i in range(2):
                    MM(PS[p2][:, i * 128:(i + 1) * 128], DINV[:, i, :], N1[:, i, :], i == 0, True, bk("DINV") + bk("N1"), [("PS", p2)])
                yield
                TTo("dve", f2(DTt), f2(DTt), PS[p2][:, 0:256], ALU.subtract, [("PS", p2)] + bk("DT"), bk("DT"))
                yield
                if lv < nlv - 1:
                    p3 = psum()
                    for i in range(2):
                        TR(psb(p3)[:, i * 128:(i + 1) * 128], DTt[:, i, :], identb, bk("DT") + [("C16", None)], [("PS", p3)])
                    yield
                    CP("act", f2(DINV), psb(p3)[:, 0:256], [("PS", p3)] + bk("DINV"), bk("DINV"))
                    yield
            if getattr(cfg, 'delta_stop', 99) == 6:
                resv.clear()
                return
            pu = psum()
            resv.add(pu)
            if not is_s:
                for i in range(2):
                    MM(PS[pu][:, i * 128:(i + 1) * 128], DTt[:, i, :], BV[:, i, :], i == 0, False, bk("DT") + bk("BV"), [("PS", pu)])
            pw = psum()
            for i in range(2):
                MM(PS[pw][:, i * 128:(i + 1) * 128], BKG[:, i, :], DTt[:, i, :], i == 0, True, bk("DT") + bk("BKG"), [("PS", pw)])
            yield
            ACT(f2(WT), PS[pw][:, 0:256], ACTF.Copy, [("PS", pw)] + bk("AM"), bk("AM"), scale=-1.0)
            po = psum()
            resv.add(po)
            if getattr(cfg, 'delta_stop', 99) == 7:
                resv.clear()
                return
            if not is_s:
                CP("act", SBF, SST[:, l, HS, :], [("SST", hg)] + bk("DINV"), bk("DINV"))
                yield
                for i in range(2):
                    MM(PS[pu][:, i * 128:(i + 1) * 128], WT[:, i, :], SBF[:, i, :], False, True, bk("AM") + bk("DINV"), [("PS", pu)])
                yield
                CP("act", f2(VN), PS[pu][:, 0:256], [("PS", pu)] + bk("N1"), bk("N1"))
                resv.discard(pu)
                yield
                for i in range(2):
                    MM(PS[po][:, i * 128:(i + 1) * 128], SBF[:, i, :], QG[:, i, :], i == 0, False, bk("DINV") + bk("QG"), [("PS", po)])
                for i in range(2):
                    MM(PS[po][:, i * 128:(i + 1) * 128], VN[:, i, :], ATT[:, i, :], False, True, bk("N1") + bk("ATT"), [("PS", po)])
                psu = psum()
                for i in range(2):
                    MM(PS[psu][:, i * 128:(i + 1) * 128], KD[:, i, :], VN[:, i, :], i == 0, True, bk("KD") + bk("N1"), [("PS", psu)])
                yield
                for i, h in enumerate(heads):
                    STT(SST[:, l, h, :], SST[:, l, h, :], EGR[:, h, 127:128], PS[psu][:, i * 128:(i + 1) * 128], ALU.mult, ALU.add,
                        [("PS", psu), ("SST", hg)] + bk("EGR"), [("SST", hg)])
            else:
                resv.discard(pu)
                HSQ = NS // 2
                SS0 = TT[:, 0:2, :].rearrange("p a t -> p (a t)")[:, 0:HSQ * 128].rearrange("p (s e) -> p s e", s=HSQ)
                SS0B = TT[:, 2, :].bitcast(BF16)[:, 0:HSQ * 128].rearrange("p (s e) -> p s e", s=HSQ)
                WTX = TT[:, 3, 0:512].bitcast(BF16).rearrange("p (s c) -> p s c", s=HSQ)
                kS0 = [("TT", 0), ("TT", 1)]; kS0B = [("TT", 2)]; kWX = [("TT", 3)]
                segrow = c16("segrow", NS * 128).rearrange("p (s c) -> p s c", s=NS)
                segcol = c16("segcol", NS)
                first_po = True
                for i, h in enumerate(heads):
                    for hf in range(2):
                        s0 = hf * HSQ
                        P.dma("sp", SS0, dr["st_dn"][l][s0:s0 + HSQ, h, :, :].rearrange("s d e -> d s e"), reads=[], writes=kS0)
                        CP("act", SS0B, SS0, kS0 + kS0B, kS0B)
                        TTo("dve", WTX, WT[:, i, :].unsqueeze(1).broadcast_to([128, HSQ, 128]), segrow[:, s0:s0 + HSQ, :], ALU.mult,
                            bk("AM") + [("C16", None)] + kWX, kWX)
                        pu2 = psum()
                        MM(PS[pu2][:, 0:128], DTt[:, i, :], BV[:, i, :], True, False, bk("DT") + bk("BV"), [("PS", pu2)])
                        for s_ in range(HSQ):
                            MM(PS[pu2][:, 0:128], WTX[:, s_, :], SS0B[:, s_, :], False, True, kS0B + kWX, [("PS", pu2)])
                        CP("act", VN[:, i, :], PS[pu2][:, 0:128], [("PS", pu2)] + bk("N1"), bk("N1"))
                        cb = i * 128 + hf * 64
                        MM(PS[po][:, cb:cb + 64], VN[:, i, :], ATT[:, i, hf * 64:hf * 64 + 64], first_po, False, bk("N1") + bk("ATT"), [("PS", po)])
                        first_po = False
                        for s_ in range(HSQ):
                            cs = i * 128 + (s0 + s_) * 8
                            MM(PS[po][:, cs:cs + 8], SS0B[:, s_, :], QG[:, i, (s0 + s_) * 8:(s0 + s_) * 8 + 8], False, True,
                               kS0B + bk("QG"), [("PS", po)])
                        TTo("dve", WTX, KD[:, i, :].unsqueeze(1).broadcast_to([128, HSQ, 128]),
                            segcol[:, s0:s0 + HSQ].unsqueeze(2).broadcast_to([128, HSQ, 128]), ALU.mult,
                            bk("KD") + [("C16", None)] + kWX, kWX)
                        for q in range(2):
                            psu = psum()
                            for j in range(4):
                                s_ = q * 4 + j
                                MM(PS[psu][:, j * 128:(j + 1) * 128], WTX[:, s_, :], VN[:, i, :], j == 0, True, kWX + bk("N1"), [("PS", psu)])
                            for j in range(4):
                                s_ = q * 4 + j
                                sg_ = s0 + s_
                                STT(SS0[:, s_, :], SS0[:, s_, :], EGR[:, h, sg_ * 8 + 7:sg_ * 8 + 8], PS[psu][:, j * 128:(j + 1) * 128],
                                    ALU.mult, ALU.add, [("PS", psu)] + kS0 + bk("EGR"), kS0)
                        out_ops.append(P.dma("sp", dr["nds"][l][s0:s0 + HSQ, h, :, :].rearrange("s d e -> d s e"), SS0, reads=kS0, writes=[("NDSo", h)]))
                        yield
            if getattr(cfg, 'delta_stop', 99) == 8:
                resv.clear()
                return
            resv.discard(po)
            CP("act", f2(OTp), PS[po][:, 0:256], [("PS", po)] + bk("OT"), bk("OT"))
            SQW = SQ[:, 2 * hg, 0:256]
            ACT(SQW, PS[po][:, 0:256], ACTF.Square, [("PS", po)], [("SQ", 2 * hg)])
            yield
            pss = psum()
            MM(PS[pss][:, 0:256], ones_r, SQW, True, True, [("SQ", 2 * hg), ("ONESR", None)], [("PS", pss)])
            yield
            RS = RSTD[:, hg * 256:(hg + 1) * 256]
            ACT(RS, PS[pss][:, 0:256], ACTF.Ln, [("PS", pss), ("EPST", None)], [("RSTD", hg)], scale=1.0 / 128.0, bias=EPSC[(128, 1.0)])
            ACT(RS, RS, ACTF.Exp, [("RSTD", hg)], [("RSTD", hg)], scale=-0.5)
            STT(f2(OTp), f2(OTp), pcol(l, "dn_norm_w"), RS, ALU.mult, ALU.mult, bk("OT") + [("RSTD", hg), ("PP", None)], bk("OT"))
            TTo("dve", MIXR[:, 4 + 2 * hg:6 + 2 * hg, c0:c0 + 128], OTp, ZG[:, HS, c0:c0 + 128], ALU.mult,
                bk("OT") + [("ZG", None)], [("Y", (4 + h, t)) for h in heads])

        def delta_tile(l, t, is_s):
            delta_prelude(l, t, is_s)
            gens = [delta_pair(l, t, is_s, 0), delta_pair(l, t, is_s, 1)]
            if getattr(cfg, "seq_pairs", False):
                for g in gens:
                    for _ in g:
                        pass
                return
            alive = [True, True]
            while any(alive):
                for gi_, g in enumerate(gens):
                    if alive[gi_]:
                        try:
                            next(g)
                        except StopIteration:
                            alive[gi_] = False

        for l in range(DEPTH):
            ffn(l, 1)
            mixer(l)
            ffn(l, 2)

        for (b0, bn) in blks:
            sumsq_rstd(X, "X", b0, bn, D)
            norm_scale(Y, "Y", X, "X", 0, "final_norm", b0, bn)
        for t in range(ntile):
            si = st["stg"]; st["stg"] ^= 1
            stg = stg_view(si)
            for half in range(2):
                pi = psum()
                for j in range(4):
                    kc = half * 4 + j
                    TR(PS[pi][:, j * 128:(j + 1) * 128], YF[:, kc, t * 128:(t + 1) * 128], ident, [("Y", (kc, t)), ("C32", None)], [("PS", pi)])
                CP("act" if half == 0 else "dve", stg[:, half * 512:(half + 1) * 512], PS[pi][:], [("PS", pi)], [stg_keys(si)[half]])
            dst = dr["yp"][p0 + t * 128: p0 + (t + 1) * 128, :] if t * 128 < npr else dr["ys"][:, :]
            out_ops.append(P.dma("sp", dst, stg, reads=stg_keys(si), writes=[("STGo", si)]))

    P.emit(out_ops)
    es.close()
    return nc, P


def make_in_maps(inp):
    pp = _pack_params(inp)
    c32, c16 = _consts()
    rgw = _pack_rgw(inp)
    maps = []
    shared = {"pp": pp, "c32": c32, "c16": c16, "rgw_r": rgw}
    for nm in ("ffn1_w_up", "ffn2_w_up", "ffn1_w_down", "ffn2_w_down", "w_in", "w_out"):
        shared[nm] = np.ascontiguousarray(inp[nm])
    for core in range(NCORES):
        sl = slice(core * NS, (core + 1) * NS)
        m = dict(shared)
        m["xp"] = np.ascontiguousarray(inp["x_prompt"][core])
        m["xs"] = np.ascontiguousarray(inp["x_sample"][sl].reshape(NS * DS, D))
        m["st_conv"] = np.ascontiguousarray(inp["state_conv"][:, sl].reshape(DEPTH, NS * 3, CONVC))
        m["st_rg"] = np.ascontiguousarray(inp["state_rglru"][:, sl])
        m["st_dn"] = np.ascontiguousarray(inp["state_delta"][:, sl])
        maps.append(m)
    return maps


def gather(r):
    y_prompt = np.stack([r[c]["yp"] for c in range(NCORES)], axis=0)
    y_sample = np.concatenate([r[c]["ys"].reshape(NS, DS, D) for c in range(NCORES)], axis=0)
    ncp = np.stack([r[c]["ncp"] for c in range(NCORES)], axis=1)
    nrp = np.stack([r[c]["nrp"] for c in range(NCORES)], axis=1)
    ndp = np.stack([r[c]["ndp"] for c in range(NCORES)], axis=1)
    ncs = np.concatenate([r[c]["ncs"].reshape(DEPTH, NS, 3, CONVC) for c in range(NCORES)], axis=1)
    nrs = np.concatenate([r[c]["nrs"] for c in range(NCORES)], axis=1)
    nds = np.concatenate([r[c]["nds"] for c in range(NCORES)], axis=1)
    return (y_prompt, y_sample, ncp, nrp, ndp, ncs, nrs, nds)


def kernel(**inp):
    inp = {k: np.asarray(v) for k, v in inp.items()}
    cfg = Cfg()
    nc, P = build_program(cfg)
    maps = make_in_maps(inp)
    res = run_bass_kernel_spmd(nc, maps, core_ids=list(range(NCORES)))
    return gather(res.results)
```

```python
import numpy as np
from contextlib import ExitStack
import concourse.bass as bass
import concourse.mybir as mybir
from concourse.bass_utils import run_bass_kernel_spmd

F32 = mybir.dt.float32
F32R = mybir.dt.float32r
BF16 = mybir.dt.bfloat16
ACTF = mybir.ActivationFunctionType
ALU = mybir.AluOpType

D = 1024
KC = 8
DFF = 2816
NFF = 22
DEPTH = 2
SEQ = 2048
NS = 16
DS = 8
INC = 3080
CONVC = 2048
EPS = 1e-6
NCORES = 8


class Op:
    __slots__ = ("eng", "fn", "deps", "sig", "count", "sem", "is_dma", "idx")

    def __init__(self, eng, fn, is_dma=False):
        self.eng = eng
        self.fn = fn
        self.deps = []
        self.sig = False
        self.count = None
        self.sem = None
        self.is_dma = is_dma
        self.idx = None


class Prog:
    ENGS = ("pe", "act", "dve", "pool", "sp")
    NDMASEM = 8

    def __init__(self, nc, same_engine_sync=True):
        self.nc = nc
        self.ops = {e: [] for e in self.ENGS}
        self.last_w = {}
        self.readers = {}
        self.same_engine_sync = same_engine_sync
        self.dma_n = {"sp": 0, "pool": 0}
        self.dma_hist = {"sp": [], "pool": []}
        self.n_ops = 0

    def _collect(self, op, reads, writes):
        deps = []
        for (n, i) in reads:
            lw = self.last_w.get(n)
            if lw:
                if i is None:
                    deps.extend(lw.values())
                else:
                    if i in lw:
                        deps.append(lw[i])
                    if None in lw:
                        deps.append(lw[None])
        for (n, i) in writes:
            lw = self.last_w.get(n)
            rd = self.readers.get(n)
            if lw:
                if i is None:
                    deps.extend(lw.values())
                else:
                    if i in lw:
                        deps.append(lw[i])
                    if None in lw:
                        deps.append(lw[None])
            if rd:
                if i is None:
                    for v in rd.values():
                        deps.extend(v)
                else:
                    deps.extend(rd.get(i, ()))
                    deps.extend(rd.get(None, ()))
        for (n, i) in writes:
            lw = self.last_w.setdefault(n, {})
            rd = self.readers.setdefault(n, {})
            if i is None:
                lw.clear()
                rd.clear()
                lw[None] = op
            else:
                lw[i] = op
                rd.pop(i, None)
        for (n, i) in reads:
            self.readers.setdefault(n, {}).setdefault(i, []).append(op)
        seen = set()
        for d in deps:
            if d is op or id(d) in seen:
                continue
            seen.add(id(d))
            if d.eng == op.eng and not d.is_dma and not op.is_dma:
                if op.eng == "pe" or not self.same_engine_sync:
                    continue
            op.deps.append(d)
            d.sig = True

    def op(self, eng, fn, reads=(), writes=()):
        o = Op(eng, fn)
        self._collect(o, list(reads), list(writes))
        self.ops[eng].append(o)
        self.n_ops += 1
        return o

    def dma(self, q, out, in_, reads=(), writes=()):
        o = Op(q, lambda e: e.dma_start(out=out, in_=in_), is_dma=True)
        n = self.dma_n[q]
        self.dma_n[q] += 1
        o.idx = n
        self._collect(o, list(reads), list(writes))
        hist = self.dma_hist[q]
        if n >= self.NDMASEM:
            o.deps.append(hist[n - self.NDMASEM])
        hist.append(o)
        o.sig = True
        self.ops[q].append(o)
        self.n_ops += 1
        return o

    def emit(self, final_wait_ops):
        nc = self.nc
        with ExitStack() as es:
            esem = {e: es.enter_context(nc.semaphore("prog_" + e)) for e in self.ENGS}
            dsem = {q: [es.enter_context(nc.semaphore("dma_%s_%d" % (q, i))) for i in range(self.NDMASEM)]
                    for q in ("sp", "pool")}
            for e in self.ENGS:
                c = 0
                for o in self.ops[e]:
                    if o.is_dma:
                        slot = o.idx % self.NDMASEM
                        o.sem = dsem[e][slot]
                        o.count = 16 * (o.idx // self.NDMASEM + 1)
                    elif o.sig:
                        c += 1
                        o.sem = esem[e]
                        o.count = c
            block = es.enter_context(nc.Block())

            def run(ename, e, extra_final=None):
                known = {}
                for o in self.ops[ename]:
                    need = {}
                    for d in o.deps:
                        key = id(d.sem)
                        if known.get(key, 0) >= d.count:
                            continue
                        if key not in need or need[key][1] < d.count:
                            need[key] = (d.sem, d.count)
                    for key, (s, v) in need.items():
                        e.wait_ge(s, v)
                        known[key] = v
                    ins = o.fn(e)
                    if o.is_dma:
                        ins.then_inc(o.sem, 16)
                    elif o.sig:
                        ins.then_inc(o.sem, 1)
                if extra_final:
                    need = {}
                    for d in extra_final:
                        key = id(d.sem)
                        if key not in need or need[key][1] < d.count:
                            need[key] = (d.sem, d.count)
                    for key, (s, v) in need.items():
                        e.wait_ge(s, v)

            @block.tensor
            def _(e):
                run("pe", e)

            @block.scalar
            def _(e):
                run("act", e)

            @block.vector
            def _(e):
                run("dve", e)

            @block.gpsimd
            def _(e):
                run("pool", e)

            @block.sync
            def _(e):
                run("sp", e, extra_final=final_wait_ops)


RGW = 512
NH = 4
HD = 128
NLV_P = 7
NLV_S = 3

C32 = {"ident": 0, "ones": 128, "tri_p": 256, "up_p": 384, "tri_s": 512, "up_s": 640}
N32 = 768
C16 = {"identb": 0, "mstrict_p": 128, "minclt_p": 256, "mstrict_s": 384, "minclt_s": 512}
for _i in range(NLV_P):
    C16["lv_p%d" % _i] = 640 + 128 * _i
C16["segcol"] = 640 + 128 * NLV_P
C16["segrow"] = C16["segcol"] + 16
N16 = C16["segrow"] + 16 * 128


def _consts():
    i = np.arange(128)[:, None]
    j = np.arange(128)[None, :]
    c32 = np.zeros((128, N32), np.float32)
    c32[:, 0:128] = np.eye(128)
    c32[:, 128:256] = 1.0
    seg = 8
    same_s = (i // seg) == (j // seg)
    c32[:, 256:384] = (i <= j)
    c32[:, 384:512] = (i > j)
    c32[:, 512:640] = (i <= j) & same_s
    c32[:, 640:768] = (i > j) & same_s
    c16 = np.zeros((128, N16), np.float32)
    c16[:, 0:128] = np.eye(128)
    c16[:, 128:256] = (i > j)
    c16[:, 256:384] = (j >= i)
    c16[:, 384:512] = (i > j) & same_s
    c16[:, 512:640] = (j >= i) & same_s
    for lv in range(NLV_P):
        b = 1 << lv
        m = ((i // (2 * b)) == (j // (2 * b))) & (((j // b) % 2) == 1) & (((i // b) % 2) == 0)
        c16[:, C16["lv_p%d" % lv]:C16["lv_p%d" % lv] + 128] = m
    c16[:, C16["segcol"]:C16["segcol"] + 16] = (np.arange(128)[:, None] // seg) == np.arange(16)[None, :]
    sr = (np.arange(16)[:, None] == (np.arange(128)[None, :] // seg)).astype(np.float32).reshape(1, 16 * 128)
    c16[:, C16["segrow"]:] = np.repeat(sr, 128, axis=0)
    return c32, c16


PL = {}
_o = 0
for _nm, _n in (("ffn1_norm_pre", 8), ("ffn1_norm_post", 8), ("mix_norm_pre", 8), ("mix_norm_post", 8),
                ("ffn2_norm_pre", 8), ("ffn2_norm_post", 8), ("final_norm", 8),
                ("conv_w", 64), ("conv_b_rg", 4), ("rg_b_a", 4), ("rg_b_x", 4), ("rg_lambda", 4),
                ("dn_a_log", 4), ("dn_dt_bias", 4), ("dn_norm_w", 1)):
    PL[_nm] = _o
    _o += _n
PP_LAYER = _o


def _pack_params(inp):
    cols = []

    def vecn(v, n):
        return np.ascontiguousarray(np.asarray(v).reshape(n, 128).T)

    for l in range(DEPTH):
        for nm in ("ffn1_norm_pre", "ffn1_norm_post", "mix_norm_pre", "mix_norm_post",
                   "ffn2_norm_pre", "ffn2_norm_post"):
            cols.append(vecn(inp[nm][l], 8))
        cols.append(vecn(inp["final_norm"], 8))
        cw = np.asarray(inp["conv_w"][l])
        cols.append(np.ascontiguousarray(cw.reshape(4, 16, 128).transpose(2, 1, 0).reshape(128, 64)))
        for nm in ("conv_b_rg", "rg_b_a", "rg_b_x", "rg_lambda"):
            cols.append(vecn(inp[nm][l], 4))
        cols.append(np.repeat(np.asarray(inp["dn_a_log"][l]).reshape(1, 4), 128, axis=0))
        cols.append(np.repeat(np.asarray(inp["dn_dt_bias"][l]).reshape(1, 4), 128, axis=0))
        cols.append(np.asarray(inp["dn_norm_w"][l]).reshape(128, 1))
    return np.ascontiguousarray(np.concatenate(cols, axis=1).astype(np.float32))


def _pack_rgw(inp):
    out = np.zeros((DEPTH, 2, 128, 4, 128), np.float32)
    for l in range(DEPTH):
        for gi, nm in enumerate(("rg_w_a", "rg_w_x")):
            w = np.asarray(inp[nm][l])
            for c in range(4):
                for hh in range(2):
                    out[l, gi, hh * 64:(hh + 1) * 64, c, hh * 64:(hh + 1) * 64] = w[2 * c + hh]
    return out


class Cfg:
    def __init__(self, **kw):
        self.groups = [(0, 640, True), (640, 768, False), (1408, 640, False)]
        self.stages = "full"
        self.same_engine_sync = True
        self.__dict__.update(kw)


def blocks_of(T):
    if T == 768:
        return [(0, 384), (384, 384)]
    if T == 640:
        return [(0, 384), (384, 256)]
    raise ValueError(T)


def build_program(cfg):
    nc = bass.Bass("TRN2", target_bir_lowering=False)
    TM = 768
    NTM = TM // 128
    dr = {}

    def din(name, shape, dt=F32):
        dr[name] = nc.dram_tensor(name, shape, dt, kind="ExternalInput").ap()

    def dout(name, shape):
        dr[name] = nc.dram_tensor(name, shape, F32, kind="ExternalOutput").ap()

    din("xp", [SEQ, D]); din("xs", [NS * DS, D])
    din("pp", [128, DEPTH * PP_LAYER]); din("c32", [128, N32]); din("c16", [128, N16])
    din("rgw_r", [DEPTH, 2, 128, 4, 128], F32R)
    din("st_conv", [DEPTH, NS * 3, CONVC]); din("st_rg", [DEPTH, NS, RGW]); din("st_dn", [DEPTH, NS, NH, HD, HD])
    for nm in ("ffn1_w_up", "ffn2_w_up"):
        din(nm, [DEPTH, D, 2 * DFF], F32R)
    for nm in ("ffn1_w_down", "ffn2_w_down"):
        din(nm, [DEPTH, DFF, D], F32R)
    din("w_in", [DEPTH, D, INC], F32R); din("w_out", [DEPTH, D, D], F32R)
    dout("yp", [SEQ, D]); dout("ys", [NS * DS, D])
    dout("ncp", [DEPTH, 3, CONVC]); dout("nrp", [DEPTH, RGW]); dout("ndp", [DEPTH, NH, HD, HD])
    if getattr(cfg, "debug", False):
        dout("dbg", [3, 128, KC, TM])
    dout("ncs", [DEPTH, NS * 3, CONVC]); dout("nrs", [DEPTH, NS, RGW]); dout("nds", [DEPTH, NS, NH, HD, HD])

    P = Prog(nc, same_engine_sync=cfg.same_engine_sync)
    es = ExitStack()
    sb = lambda name, shape, dt: es.enter_context(nc.sbuf_tensor(name, shape, dt))
    X = sb("X", [128, KC, TM], F32)
    H = sb("H", [128, KC, TM], F32R)
    Y = sb("Y", [128, KC, TM], F32R)
    YF = Y[:].bitcast(F32)
    HF = H[:].bitcast(F32)
    NFB = 4
    ACTB = sb("ACTB", [128, NFB, TM], F32R)
    NWA = 4
    WA = [sb("WA%d" % i, [128, KC, 128], F32R) for i in range(NWA)]
    NWB = 4
    WB = [sb("WB%d" % i, [128, NFB, 128], F32R) for i in range(NWB)]
    PPt = sb("PPt", [128, DEPTH * PP_LAYER], F32)
    C32t = sb("C32t", [128, N32], F32)
    C16t = sb("C16t", [128, N16], BF16)
    ONESR = sb("ONESR", [128, 128], F32R)
    EPST = sb("EPST", [128, 8], F32)
    TT = sb("TT", [128, 4, TM], F32)
    SQ = sb("SQ", [128, 4, 384], F32R)
    RSTD = sb("RSTD", [128, 512], F32)
    SG = [TT[:, 0, 0:384], TT[:, 1, 0:384]]
    ZG = sb("ZG", [128, 4, TM], BF16)
    QKV = sb("QKV", [128, 12, TM], BF16)
    XR = sb("XR", [128, TM], F32R)
    XP = sb("XP", [128, 3 + TM], F32R)
    XPS = sb("XPS", [128, NS, 11], F32R)
    XPf = XP[:].bitcast(F32)
    XPSf = XPS[:].bitcast(F32)
    DG = sb("DG", [128, 4, 128], F32R)
    CSS = sb("CSS", [128, 16, NS * 3], F32)
    HS0 = sb("HS0", [128, 4, NS], F32)
    CONVT = sb("CONVT", [128, DEPTH, 16, 3], F32)
    HRG = sb("HRG", [128, DEPTH, 4], F32)
    SST = sb("SST", [128, DEPTH, NH, HD], F32)
    RGWt = sb("RGWt", [128, 2, 4, 128], F32R)
    WSC = sb("WSC", [128, KC, 8], F32R)
    SCT = sb("SCT", [128, NTM, 8], F32)
    LAYC = sb("LAYC", [128, 16], F32)
    ONE1 = sb("ONE1", [128, 1], F32)
    dt_names_bf = ["KTOK", "VTOK", "AM", "ATT", "DT", "DINV", "N1", "BV", "BKG", "WT", "VN", "QG", "KD", "SBF"]
    DB = {n: sb(n, [128, NH, 128], BF16) for n in dt_names_bf if n not in ("VN", "SBF", "WT")}
    DB["VN"] = DB["N1"]; DB["SBF"] = DB["DINV"]; DB["WT"] = DB["AM"]
    GRW = sb("GRW", [128, NH, 128], F32)
    EGR = sb("EGR", [128, NH, 128], F32)
    TRG = sb("TRG", [128, NH, 128], F32)
    OT = sb("OT", [128, NH, 128], F32)
    SM = sb("SM", [128, 64], F32)
    CSL = TT[0:48, 3, 0:512]
    RSL = TT[0:16, 3, 0:512]
    PS = [es.enter_context(nc.psum_tensor("PS%d" % i, [128, 512], F32)) for i in range(8)]

    def c32(nm):
        return C32t[:, C32[nm]:C32[nm] + 128]

    def c16(nm, n=128):
        return C16t[:, C16[nm]:C16[nm] + n]

    ident = c32("ident")
    ones_f = c32("ones")
    ones_r = ONESR[:, :]
    identb = c16("identb")
    st = {"ps": 0, "wa": 0, "wb": 0, "stg": 0, "sg": 0}

    resv = set()

    def psum():
        while True:
            i = st["ps"]
            st["ps"] = (i + 1) % 8
            if i not in resv:
                return i

    def psb(pi):
        return PS[pi][:].bitcast(BF16)

    def ACT(out, in_, func, reads, writes, **kw):
        return P.op("act", lambda e: e.activation(out=out, in_=in_, func=func, **kw), reads, writes)

    def CP(eng, out, in_, reads, writes):
        if eng == "act":
            return P.op("act", lambda e: e.copy(out=out, in_=in_), reads, writes)
        return P.op(eng, lambda e: e.tensor_copy(out=out, in_=in_), reads, writes)

    def TTo(eng, out, in0, in1, op, reads, writes):
        return P.op(eng, lambda e: e.tensor_tensor(out=out, in0=in0, in1=in1, op=op), reads, writes)

    def TS(eng, out, in0, s1, s2, op0, op1, reads, writes):
        if op1 is None:
            return P.op(eng, lambda e: e.tensor_scalar(out=out, in0=in0, scalar1=s1, scalar2=None, op0=op0), reads, writes)
        return P.op(eng, lambda e: e.tensor_scalar(out=out, in0=in0, scalar1=s1, scalar2=s2, op0=op0, op1=op1), reads, writes)

    def STT(out, in0, scalar, in1, op0, op1, reads, writes):
        return P.op("dve", lambda e: e.scalar_tensor_tensor(out=out, in0=in0, scalar=scalar, in1=in1, op0=op0, op1=op1), reads, writes)

    def MM(out, lhsT, rhs, start, stop, reads, writes):
        return P.op("pe", lambda e: e.matmul(out, lhsT=lhsT, rhs=rhs, start=start, stop=stop, skip_group_check=True), reads, writes)

    def TR(out, in_, idn, reads, writes):
        return P.op("pe", lambda e: e.transpose(out=out, in_=in_, identity=idn), reads, writes)

    class WStream:
        def __init__(self, name, bufs):
            self.name, self.bufs, self.n = name, bufs, len(bufs)
            self.reqs = []
            self.issued = 0
            self.cur = 0

        def add(self, fn):
            self.reqs.append(fn)

        def next(self):
            i = self.cur
            self.cur += 1
            upto = min(len(self.reqs), i + self.n - 1)
            while self.issued < upto:
                k = self.issued
                slot = k % self.n
                out_ap, in_ap = self.reqs[k](self.bufs[slot])
                P.dma("pool", out_ap, in_ap, writes=[(self.name, slot)])
                self.issued += 1
            return i % self.n

    WAS = WStream("WA", WA)
    WBS = WStream("WB", WB)
    FBLOCKS = [(0, 4), (4, 4), (8, 4), (12, 4), (16, 4), (20, 2)]

    def plan_cols(src, col):
        v = src.rearrange("(kc p) n -> p kc n", p=128)
        WAS.add(lambda buf, v=v, col=col: (buf[:], v[:, :, col:col + 128]))

    def plan_ffn(l, which):
        wup = dr["ffn%d_w_up" % which][l]
        wdn = dr["ffn%d_w_down" % which][l].rearrange("(c p) n -> p c n", p=128)
        for (c0, nch) in FBLOCKS:
            for j in range(nch):
                plan_cols(wup, (c0 + j) * 128)
                plan_cols(wup, DFF + (c0 + j) * 128)
            for oc in range(8):
                WBS.add(lambda buf, c0=c0, nch=nch, oc=oc, wdn=wdn: (buf[:, 0:nch, :], wdn[:, c0:c0 + nch, oc * 128:(oc + 1) * 128]))

    def plan_mixer(l):
        win = dr["w_in"][l]
        for ch in range(4):
            plan_cols(win, ch * 128)
            plan_cols(win, 2048 + ch * 128)
        for kind in range(3):
            for hh in range(4):
                plan_cols(win, 512 + kind * 512 + hh * 128)
        for hh in range(4):
            plan_cols(win, 2560 + hh * 128)
        for oc in range(8):
            plan_cols(dr["w_out"][l], oc * 128)

    for _g in cfg.groups:
        for l in range(DEPTH):
            plan_ffn(l, 1)
            plan_mixer(l)
            plan_ffn(l, 2)

    P.dma("sp", PPt[:], dr["pp"][:, :], writes=[("PP", None)])
    P.dma("sp", C32t[:], dr["c32"][:, :], writes=[("C32", None)])
    for i in range(0, N16, 768):
        n = min(768, N16 - i)
        P.dma("sp", TT[:, 0, 0:n], dr["c16"][:, i:i + n], writes=[("TT", 0)])
        CP("dve", C16t[:, i:i + n], TT[:, 0, 0:n], [("TT", 0)], [("C16", None)])
    CP("dve", ones_r, ones_f, [("C32", None)], [("ONESR", None)])
    EPSC = {}
    for i, (key, val) in enumerate([((D, 1.0), EPS), ((D, 0.5), 4.0 * EPS), ((128, 1.0), EPS), ((1, 1.0), EPS), ("q", 128.0 * EPS)]):
        EPSC[key] = EPST[:, i:i + 1]
        P.op("dve", lambda e, i=i, val=val: e.memset(EPST[:, i:i + 1], val), writes=[("EPST", i)])
    P.op("dve", lambda e: e.memset(ONE1[:, :], 1.0), writes=[("ONE1", None)])
    P.op("dve", lambda e: e.memset(CONVT[:].rearrange("p l c t -> p (l c t)"), 0.0), writes=[("CONVT", None)])
    P.op("dve", lambda e: e.memset(HRG[:].rearrange("p l c -> p (l c)"), 0.0), writes=[("HRG", None)])
    P.op("dve", lambda e: e.memset(SST[:].rearrange("p l h e -> p (l h e)"), 0.0), writes=[("SST", None)])

    out_ops = []

    def pcol(l, nm, j=0, n=1):
        c = l * PP_LAYER + PL[nm] + j
        return PPt[:, c:c + n]

    ngroups = len(cfg.groups)
    for gi, (p0, npr, smp) in enumerate(cfg.groups):
        T = npr + (128 if smp else 0)
        ntile = T // 128
        blks = blocks_of(T)
        last_group = (gi == ngroups - 1)

        def tiles_of(b0, bn):
            return list(range(b0 // 128, (b0 + bn + 127) // 128))

        def rk(name, kcs, b0, bn):
            if isinstance(kcs, int):
                kcs = [kcs]
            return [(name, (kc, t)) for kc in kcs for t in tiles_of(b0, bn)]

        def stg_view(si):
            return TT[:, 2 * si:2 * si + 2, :].rearrange("p a t -> p (a t)")[:, 0:D]

        def stg_keys(si):
            return [("TT", 2 * si), ("TT", 2 * si + 1)]

        for t in range(ntile):
            si = st["stg"]; st["stg"] ^= 1
            stg = stg_view(si)
            src = dr["xp"][p0 + t * 128: p0 + (t + 1) * 128, :] if t * 128 < npr else dr["xs"][:, :]
            P.dma("sp", stg, src, writes=stg_keys(si))
            for half in range(2):
                pi = psum()
                for j in range(4):
                    kc = half * 4 + j
                    TR(PS[pi][:, j * 128:(j + 1) * 128], stg[:, kc * 128:(kc + 1) * 128], ident,
                       stg_keys(si) + [("C32", None)], [("PS", pi)])
                CP("act" if half == 0 else "dve", X[:, half * 4:(half + 1) * 4, t * 128:(t + 1) * 128],
                   PS[pi][:].rearrange("p (j c) -> p j c", j=4), [("PS", pi)], [("X", (half * 4 + j, t)) for j in range(4)])

        def sumsq_rstd(SRC, srcname, b0, bn, nfeat, post_scale=1.0, nk=KC):
            pi = psum()
            for kc in range(nk):
                rd = rk(srcname, kc, b0, bn)
                ACT(SQ[:, kc % 4, 0:bn], SRC[:, kc, b0:b0 + bn], ACTF.Square, rd, [("SQ", kc % 4)])
                MM(PS[pi][:, 0:bn], ones_r, SQ[:, kc % 4, 0:bn], kc == 0, kc == nk - 1,
                   [("SQ", kc % 4), ("ONESR", None)], [("PS", pi)])
            rstd_from_psum(pi, bn, nfeat, post_scale)

        def rstd_from_psum(pi, bn, nfeat, post_scale=1.0):
            ACT(RSTD[:, 0:bn], PS[pi][:, 0:bn], ACTF.Ln, [("PS", pi), ("EPST", None)], [("RSTD", None)],
                scale=1.0 / (nfeat * post_scale * post_scale), bias=EPSC[(nfeat, post_scale)])
            ACT(RSTD[:, 0:bn], RSTD[:, 0:bn], ACTF.Exp, [("RSTD", None)], [("RSTD", None)], scale=-0.5)

        def norm_scale(DST, dstname, SRC, srcname, l, gname, b0, bn):
            for kc in range(KC):
                STT(DST[:, kc, b0:b0 + bn], SRC[:, kc, b0:b0 + bn], pcol(l, gname, kc), RSTD[:, 0:bn], ALU.mult, ALU.mult,
                    rk(srcname, kc, b0, bn) + [("RSTD", None), ("PP", None)], rk(dstname, kc, b0, bn))

        def prenorm_to_H(l, gname):
            for (b0, bn) in blks:
                sumsq_rstd(X, "X", b0, bn, D)
                norm_scale(H, "H", X, "X", l, gname, b0, bn)

        def postnorm_residual(SRCw, SRC, srcname, l, gname, scale):
            for (b0, bn) in blks:
                sumsq_rstd(SRC, srcname, b0, bn, D, post_scale=scale)
                norm_scale(SRCw, srcname, SRC, srcname, l, gname, b0, bn)
                for kc in range(KC):
                    TTo("dve", X[:, kc, b0:b0 + bn], X[:, kc, b0:b0 + bn], SRC[:, kc, b0:b0 + bn], ALU.add,
                        rk(srcname, kc, b0, bn) + rk("X", kc, b0, bn), rk("X", kc, b0, bn))

        def ffn(l, which):
            prenorm_to_H(l, "ffn%d_norm_pre" % which)
            for fbi, (c0, nch) in enumerate(FBLOCKS):
                for j in range(nch):
                    wi_g = WAS.next()
                    pg = [psum() for _ in blks]
                    for kc in range(KC):
                        for bi, (b0, bn) in enumerate(blks):
                            MM(PS[pg[bi]][:, 0:bn], WA[wi_g][:, kc, :], H[:, kc, b0:b0 + bn],
                               kc == 0, kc == KC - 1, [("WA", wi_g)] + rk("H", kc, b0, bn), [("PS", pg[bi])])
                    sgs = []
                    for bi, (b0, bn) in enumerate(blks):
                        si = st["sg"]; st["sg"] = (st["sg"] + 1) % len(SG)
                        sgs.append(si)
                        ACT(SG[si][:, 0:bn], PS[pg[bi]][:, 0:bn], ACTF.Silu, [("PS", pg[bi])], [("TT", si)])
                    wi_u = WAS.next()
                    pu = [psum() for _ in blks]
                    for kc in range(KC):
                        for bi, (b0, bn) in enumerate(blks):
                            MM(PS[pu[bi]][:, 0:bn], WA[wi_u][:, kc, :], H[:, kc, b0:b0 + bn],
                               kc == 0, kc == KC - 1, [("WA", wi_u)] + rk("H", kc, b0, bn), [("PS", pu[bi])])
                    for bi, (b0, bn) in enumerate(blks):
                        TTo("dve", ACTB[:, j, b0:b0 + bn], PS[pu[bi]][:, 0:bn], SG[sgs[bi]][:, 0:bn], ALU.mult,
                            [("PS", pu[bi]), ("TT", sgs[bi])], [("ACTB", (j, b0))])
                for oc in range(8):
                    wi = WBS.next()
                    pd = [psum() for _ in blks]
                    for j in range(nch):
                        for bi, (b0, bn) in enumerate(blks):
                            MM(PS[pd[bi]][:, 0:bn], WB[wi][:, j, :], ACTB[:, j, b0:b0 + bn],
                               j == 0, j == nch - 1, [("WB", wi), ("ACTB", (j, b0))], [("PS", pd[bi])])
                    for bi, (b0, bn) in enumerate(blks):
                        if fbi == 0:
                            CP("act", Y[:, oc, b0:b0 + bn], PS[pd[bi]][:, 0:bn], [("PS", pd[bi])], rk("Y", oc, b0, bn))
                        else:
                            TTo("dve", Y[:, oc, b0:b0 + bn], PS[pd[bi]][:, 0:bn], YF[:, oc, b0:b0 + bn], ALU.add,
                                [("PS", pd[bi])] + rk("Y", oc, b0, bn), rk("Y", oc, b0, bn))
            postnorm_residual(Y, YF, "Y", l, "ffn%d_norm_post" % which, 0.5)

        MIXR = Y
        allH = [("H", (kc, t)) for kc in range(KC) for t in range(NTM)]
        allACTB = [("ACTB", None)]

        def mixer(l):
            win = dr["w_in"][l].rearrange("(kc p) n -> p kc n", p=128)
            prenorm_to_H(l, "mix_norm_pre")
            ACT(LAYC[:, 0:4], pcol(l, "rg_lambda", 0, 4), ACTF.Exp, [("PP", None)], [("LAYC", 0)], scale=-1.0)
            ACT(LAYC[:, 0:4], LAYC[:, 0:4], ACTF.Ln, [("LAYC", 0), ("ONE1", None)], [("LAYC", 0)], bias=ONE1[:, 0:1])
            TS("dve", LAYC[:, 4:8], LAYC[:, 0:4], -16.0, None, ALU.mult, None, [("LAYC", 0)], [("LAYC", 1)])
            TS("dve", LAYC[:, 0:4], LAYC[:, 0:4], -8.0, None, ALU.mult, None, [("LAYC", 0), ("LAYC", 1)], [("LAYC", 0)])
            ACT(LAYC[:, 8:12], pcol(l, "dn_a_log", 0, 4), ACTF.Exp, [("PP", None)], [("LAYC", 2)])
            TS("dve", LAYC[:, 8:12], LAYC[:, 8:12], -1.0, None, ALU.mult, None, [("LAYC", 2)], [("LAYC", 2)])
            P.dma("pool", RGWt[:], dr["rgw_r"][l].rearrange("g p c m -> p g c m"), writes=[("RGW", None)])
            P.dma("pool", WSC[:], win[:, :, 3072:3080], writes=[("WSC", None)])
            if smp:
                for q in range(4):
                    P.dma("sp", CSL, dr["st_conv"][l][:, q * 512:(q + 1) * 512], writes=[("TT", 3)])
                    pi = psum()
                    for j in range(4):
                        TR(PS[pi][:, j * 48:(j + 1) * 48], CSL[:, j * 128:(j + 1) * 128], ident[0:48, 0:48],
                           [("TT", 3), ("C32", None)], [("PS", pi)])
                    CP("dve", CSS[:, q * 4:(q + 1) * 4, :], PS[pi][:, 0:192].rearrange("p (j c) -> p j c", j=4),
                       [("PS", pi)], [("CSS", q * 4 + j) for j in range(4)])
                P.dma("sp", RSL, dr["st_rg"][l][:, :], writes=[("TT", 3)])
                pi = psum()
                for j in range(4):
                    TR(PS[pi][:, j * 16:(j + 1) * 16], RSL[:, j * 128:(j + 1) * 128], ident[0:16, 0:16],
                       [("TT", 3), ("C32", None)], [("PS", pi)])
                CP("dve", HS0[:, :, :], PS[pi][:, 0:64].rearrange("p (j c) -> p j c", j=4), [("PS", pi)], [("HS0", None)])

            wa_state = {}

            def proj():
                wi = WAS.next()
                pb = [psum() for _ in blks]
                for kc in range(KC):
                    for bi, (b0, bn) in enumerate(blks):
                        MM(PS[pb[bi]][:, 0:bn], WA[wi][:, kc, :], H[:, kc, b0:b0 + bn],
                           kc == 0, kc == KC - 1, [("WA", wi)] + rk("H", kc, b0, bn), [("PS", pb[bi])])
                return pb

            def conv_chunk(l, ch, pb):
                CP("act", XP[:, 0:3], CONVT[:, l, ch, :], [("CONVT", (l, ch))], [("XP", 0)])
                for bi, (b0, bn) in enumerate(blks):
                    n = min(bn, npr - b0)
                    if n > 0:
                        CP("act", XP[:, 3 + b0:3 + b0 + n], PS[pb[bi]][:, 0:n], [("PS", pb[bi])], [("XP", 1 + bi)])
                if smp:
                    b0, bn = blks[-1]
                    off = npr - b0
                    CP("dve", XPS[:, :, 0:3], CSS[:, ch, :].rearrange("p (s t) -> p s t", t=3), [("CSS", ch)], [("XPS", 0)])
                    CP("act", XPS[:, :, 3:11], PS[pb[-1]][:, off:off + 128].rearrange("p (s t) -> p s t", t=8),
                       [("PS", pb[-1])], [("XPS", 1)])
                for tap in range(4):
                    TS("dve", DG[:, tap, :], ident, pcol(l, "conv_w", ch * 4 + tap), None, ALU.mult, None,
                       [("C32", None), ("PP", None)], [("DG", tap)])
                xpk = [("XP", i) for i in range(1 + len(blks))]
                pc = [psum() for _ in blks]
                for bi, (b0, bn) in enumerate(blks):
                    n = min(bn, npr - b0)
                    for tap in range(4):
                        MM(PS[pc[bi]][:, 0:n], DG[:, tap, :], XP[:, b0 + tap:b0 + tap + n], tap == 0, tap == 3,
                           xpk + [("DG", tap)], [("PS", pc[bi])])
                CP("dve", CONVT[:, l, ch, :], XPf[:, npr:npr + 3], xpk, [("CONVT", (l, ch))])
                if smp:
                    b0, bn = blks[-1]
                    off = npr - b0
                    for tap in range(4):
                        MM(PS[pc[-1]][:, off:off + 128].rearrange("p (s t) -> p s t", t=8), DG[:, tap, :], XPS[:, :, tap:tap + 8],
                           False, tap == 3, [("XPS", 0), ("XPS", 1), ("DG", tap)], [("PS", pc[-1])])
                    CP("dve", CSS[:, ch, :].rearrange("p (s t) -> p s t", t=3), XPSf[:, :, 8:11], [("XPS", 0), ("XPS", 1)], [("CSS", ch)])
                return pc

            T0 = TT[:, 0, :]; T1 = TT[:, 1, :]; T2 = TT[:, 2, :]
            XRf = XR[:].bitcast(F32)
            k0, k1, k2, k3 = ("TT", 0), ("TT", 1), ("TT", 2), ("XR", None)

            for cpair in range(2):
                for cc in range(2):
                    ch = cpair * 2 + cc
                    pb = proj()
                    pc = conv_chunk(l, ch, pb)
                    for bi, (b0, bn) in enumerate(blks):
                        ACT(XR[:, b0:b0 + bn], PS[pc[bi]][:, 0:bn], ACTF.Identity, [("PS", pc[bi]), ("PP", None), k3], [k3],
                            bias=pcol(l, "conv_b_rg", ch))
                    pa = [psum() for _ in blks]
                    for bi, (b0, bn) in enumerate(blks):
                        MM(PS[pa[bi]][:, 0:bn], RGWt[:, 0, ch, :], XR[:, b0:b0 + bn], True, True, [("RGW", None), k3], [("PS", pa[bi])])
                    for bi, (b0, bn) in enumerate(blks):
                        ACT(T0[:, b0:b0 + bn], PS[pa[bi]][:, 0:bn], ACTF.Sigmoid, [("PS", pa[bi]), ("PP", None)], [k0],
                            bias=pcol(l, "rg_b_a", ch))
                    px = [psum() for _ in blks]
                    for bi, (b0, bn) in enumerate(blks):
                        MM(PS[px[bi]][:, 0:bn], RGWt[:, 1, ch, :], XR[:, b0:b0 + bn], True, True, [("RGW", None), k3], [("PS", px[bi])])
                    for bi, (b0, bn) in enumerate(blks):
                        ACT(T1[:, b0:b0 + bn], PS[px[bi]][:, 0:bn], ACTF.Sigmoid, [("PS", px[bi]), ("PP", None)], [k1],
                            bias=pcol(l, "rg_b_x", ch))
                    ACT(T2[:, 0:T], T0[:, 0:T], ACTF.Exp, [k0, ("LAYC", 1)], [k2], scale=LAYC[:, 4 + ch:5 + ch])
                    TS("dve", T2[:, 0:T], T2[:, 0:T], -1.0, 1.0, ALU.mult, ALU.add, [k2], [k2])
                    TS("dve", T2[:, 0:T], T2[:, 0:T], 1e-30, None, ALU.max, None, [k2], [k2])
                    ACT(T2[:, 0:T], T2[:, 0:T], ACTF.Ln, [k2], [k2])
                    ACT(T2[:, 0:T], T2[:, 0:T], ACTF.Exp, [k2], [k2], scale=0.5)
                    ACT(T0[:, 0:T], T0[:, 0:T], ACTF.Exp, [k0, ("LAYC", 0)], [k0], scale=LAYC[:, ch:ch + 1])
                    TTo("dve", T1[:, 0:T], T1[:, 0:T], XRf[:, 0:T], ALU.mult, [k1, k3], [k1])
                    TTo("dve", T1[:, 0:T], T1[:, 0:T], T2[:, 0:T], ALU.mult, [k1, k2], [k1])
                    for bi, (b0, bn) in enumerate(blks):
                        n = min(bn, npr - b0)
                        if n <= 0:
                            continue
                        init = HRG[:, l, ch:ch + 1] if bi == 0 else T2[:, b0 - 1:b0]
                        P.op("dve", lambda e, b0=b0, n=n, init=init: e.tensor_tensor_scan(
                            out=T2[:, b0:b0 + n], data0=T0[:, b0:b0 + n], data1=T1[:, b0:b0 + n],
                            initial=init, op0=ALU.mult, op1=ALU.add),
                            [k0, k1, k2, ("HRG", (l, ch))], [k2])
                    CP("dve", HRG[:, l, ch:ch + 1], T2[:, npr - 1:npr], [k2], [("HRG", (l, ch))])
                    if smp:
                        a_first = T0[:, npr:npr + 128:8]
                        b_first = T1[:, npr:npr + 128:8]
                        TTo("dve", SM[:, 0:16], a_first, HS0[:, ch, :], ALU.mult, [k0, ("HS0", None)], [("SM", None)])
                        TTo("dve", b_first, b_first, SM[:, 0:16], ALU.add, [k1, ("SM", None)], [k1])
                        TS("dve", a_first, a_first, 0.0, None, ALU.mult, None, [k0, ("SM", None)], [k0])
                        P.op("dve", lambda e: e.tensor_tensor_scan(out=T2[:, npr:npr + 128], data0=T0[:, npr:npr + 128],
                                                                  data1=T1[:, npr:npr + 128], initial=0.0, op0=ALU.mult, op1=ALU.add),
                             [k0, k1, k2], [k2])
                        CP("dve", HS0[:, ch, :], T2[:, npr + 7:npr + 128:8], [k2, ("SM", None)], [("HS0", None)])
                    pgt = proj()
                    for bi, (b0, bn) in enumerate(blks):
                        CP("act", T0[:, b0:b0 + bn], PS[pgt[bi]][:, 0:bn], [("PS", pgt[bi]), k0], [k0])
                    ACT(T1[:, 0:T], T0[:, 0:T], ACTF.Square, [k0, k1], [k1])
                    TS("dve", T1[:, 0:T], T1[:, 0:T], 0.044715, 1.0, ALU.mult, ALU.add, [k1], [k1])
                    TTo("dve", T1[:, 0:T], T1[:, 0:T], T0[:, 0:T], ALU.mult, [k0, k1], [k1])
                    ACT(T1[:, 0:T], T1[:, 0:T], ACTF.Sigmoid, [k1], [k1], scale=2.0 * 0.7978845608028654)
                    TTo("dve", T0[:, 0:T], T0[:, 0:T], T1[:, 0:T], ALU.mult, [k0, k1], [k0])
                    for bi, (b0, bn) in enumerate(blks):
                        TTo("dve", MIXR[:, ch, b0:b0 + bn], T2[:, b0:b0 + bn], T0[:, b0:b0 + bn], ALU.mult,
                            [k0, k2], rk("Y", ch, b0, bn))

            for kind in range(3):
                for cpair in range(2):
                    for cc in range(2):
                        hh = cpair * 2 + cc
                        ch = 4 + kind * 4 + hh
                        pb = proj()
                        pc = conv_chunk(l, ch, pb)
                        if kind == 2:
                            for bi, (b0, bn) in enumerate(blks):
                                ACT(QKV[:, 8 + hh, b0:b0 + bn], PS[pc[bi]][:, 0:bn], ACTF.Silu, [("PS", pc[bi])], [("QKV", None)])
                        else:
                            for bi, (b0, bn) in enumerate(blks):
                                ACT(T1[:, b0:b0 + bn], PS[pc[bi]][:, 0:bn], ACTF.Silu, [("PS", pc[bi]), k1], [k1])
                            for bi, (b0, bn) in enumerate(blks):
                                ACT(SQ[:, 0, 0:bn], T1[:, b0:b0 + bn], ACTF.Square, [k1], [("SQ", 0)])
                                pi = psum()
                                MM(PS[pi][:, 0:bn], ones_r, SQ[:, 0, 0:bn], True, True, [("SQ", 0), ("ONESR", None)], [("PS", pi)])
                                if kind == 0:
                                    ACT(RSTD[:, 0:bn], PS[pi][:, 0:bn], ACTF.Ln, [("PS", pi), ("EPST", None)], [("RSTD", None)],
                                        scale=128.0, bias=EPSC["q"])
                                else:
                                    ACT(RSTD[:, 0:bn], PS[pi][:, 0:bn], ACTF.Ln, [("PS", pi), ("EPST", None)], [("RSTD", None)],
                                        scale=1.0, bias=EPSC[(1, 1.0)])
                                ACT(RSTD[:, 0:bn], RSTD[:, 0:bn], ACTF.Exp, [("RSTD", None)], [("RSTD", None)], scale=-0.5)
                                TTo("dve", QKV[:, kind * 4 + hh, b0:b0 + bn], T1[:, b0:b0 + bn], RSTD[:, 0:bn], ALU.mult,
                                    [k1, ("RSTD", None)], [("QKV", None)])

            for cpair in range(2):
                for cc in range(2):
                    hh = cpair * 2 + cc
                    pb = proj()
                    for bi, (b0, bn) in enumerate(blks):
                        ACT(ZG[:, hh, b0:b0 + bn], PS[pb[bi]][:, 0:bn], ACTF.Silu, [("PS", pb[bi])], [("ZG", (hh, bi))])

            pi = psum()
            for t in range(ntile):
                for kc in range(KC):
                    MM(PS[pi][:, t * 8:(t + 1) * 8], H[:, kc, t * 128:(t + 1) * 128], WSC[:, kc, :], (t == 0 and kc == 0), kc == KC - 1,
                       [("WSC", None), ("H", (kc, t))], [("PS", pi)])
            CP("dve", SCT[:, 0:ntile, :], PS[pi][:, 0:ntile * 8].rearrange("p (t c) -> p t c", c=8), [("PS", pi)], [("SCT", None)])

            for t in range(ntile):
                delta_tile(l, t, smp and t == ntile - 1)

            if getattr(cfg, "debug", False) and l == 0:
                out_ops.append(P.dma("sp", dr["dbg"][gi], YF, reads=[("Y", None)], writes=[("DBG", gi)]))
            for ocp in range(4):
                for o2 in range(2):
                    oc = ocp * 2 + o2
                    wi = WAS.next()
                    pd = [psum() for _ in blks]
                    for kc in range(KC):
                        for bi, (b0, bn) in enumerate(blks):
                            MM(PS[pd[bi]][:, 0:bn], WA[wi][:, kc, :], MIXR[:, kc, b0:b0 + bn],
                               kc == 0, kc == KC - 1, [("WA", wi)] + rk("Y", kc, b0, bn), [("PS", pd[bi])])
                    for bi, (b0, bn) in enumerate(blks):
                        CP("act", H[:, oc, b0:b0 + bn], PS[pd[bi]][:, 0:bn], [("PS", pd[bi])], rk("H", oc, b0, bn))
            postnorm_residual(H, HF, "H", l, "mix_norm_post", 1.0)

            if smp:
                for q in range(4):
                    pi = psum()
                    for j in range(4):
                        TR(PS[pi][0:48, j * 128:(j + 1) * 128], CSS[:, q * 4 + j, :], ident, [("CSS", q * 4 + j), ("C32", None)], [("PS", pi)])
                    CP("dve", CSL, PS[pi][0:48, :], [("PS", pi)], [("TT", 3)])
                    out_ops.append(P.dma("sp", dr["ncs"][l][:, q * 512:(q + 1) * 512], CSL, reads=[("TT", 3)], writes=[("CSLo", q)]))
                pi = psum()
                for j in range(4):
                    TR(PS[pi][0:16, j * 128:(j + 1) * 128], HS0[:, j, :], ident, [("HS0", None), ("C32", None)], [("PS", pi)])
                CP("dve", RSL, PS[pi][0:16, :], [("PS", pi)], [("TT", 3)])
                out_ops.append(P.dma("sp", dr["nrs"][l][:, :], RSL, reads=[("TT", 3)], writes=[("RSLo", 0)]))
            if last_group:
                for q in range(4):
                    pi = psum()
                    for j in range(4):
                        TR(PS[pi][0:3, j * 128:(j + 1) * 128], CONVT[:, l, q * 4 + j, :], ident, [("CONVT", (l, q * 4 + j)), ("C32", None)], [("PS", pi)])
                    CP("dve", CSL[0:3, :], PS[pi][0:3, :], [("PS", pi)], [("TT", 3)])
                    out_ops.append(P.dma("sp", dr["ncp"][l][:, q * 512:(q + 1) * 512], CSL[0:3, :], reads=[("TT", 3)], writes=[("CSLo", q)]))
                pi = psum()
                TR(PS[pi][0:4, 0:128], HRG[:, l, :], ident, [("HRG", None), ("C32", None)], [("PS", pi)])
                CP("dve", RSL[0:4, 0:128], PS[pi][0:4, 0:128], [("PS", pi)], [("TT", 3)])
                out_ops.append(P.dma("sp", dr["nrp"][l].rearrange("(c p) -> c p", p=128), RSL[0:4, 0:128], reads=[("TT", 3)], writes=[("RSLo", 0)]))
                out_ops.append(P.dma("sp", dr["ndp"][l].rearrange("h d e -> d h e"), SST[:, l, :, :], reads=[("SST", None)], writes=[("SSTo", l)]))

        def delta_prelude(l, t, is_s):
            sfx = "_s" if is_s else "_p"
            BETA = SM[:, 16:20]; GT = SM[:, 20:24]; GC = SM[:, 24:32]; EG = SM[:, 32:36]; BEG = SM[:, 36:40]; EKD = SM[:, 40:44]
            ACT(BETA, SCT[:, t, 0:4], ACTF.Sigmoid, [("SCT", None)], [("SM", 1)])
            TTo("dve", GT, SCT[:, t, 4:8], pcol(l, "dn_dt_bias", 0, 4), ALU.add, [("SCT", None), ("PP", None)], [("SM", 2)])
            ACT(GT, GT, ACTF.Exp, [("SM", 2)], [("SM", 2)])
            ACT(GT, GT, ACTF.Ln, [("SM", 2), ("ONE1", None)], [("SM", 2)], bias=ONE1[:, 0:1])
            TTo("dve", GT, GT, LAYC[:, 8:12], ALU.mult, [("SM", 2), ("LAYC", 2)], [("SM", 2)])
            pgc = psum()
            MM(PS[pgc][:, 0:4], c32("tri" + sfx), GT, True, True, [("SM", 2), ("C32", None)], [("PS", pgc)])
            MM(PS[pgc][:, 4:8], c32("up" + sfx), GT, False, True, [("SM", 2), ("C32", None)], [("PS", pgc)])
            bc4 = lambda ap: ap.unsqueeze(2).broadcast_to([128, NH, 128])
            bm4 = lambda ap: ap.unsqueeze(1).broadcast_to([128, NH, 128])
            TTo("dve", TRG[:, :, :], bm4(c32("tri" + sfx)), bc4(GT), ALU.mult, [("SM", 2), ("C32", None), ("TRG", 0), ("TRG", 1)], [("TRG", 0), ("TRG", 1)])
            pgr = psum()
            resv.add(pgr)
            st["pgr"] = pgr
            st["pgr_users"] = 2
            MM(PS[pgr][:, :], ones_f, TRG[:].rearrange("p h c -> p (h c)"), True, True, [("TRG", 0), ("TRG", 1), ("C32", None)], [("PS", pgr)])
            CP("dve", GC, PS[pgc][:, 0:8], [("PS", pgc), ("PS", pgr)], [("SM", 3)])
            ACT(EG, GC[:, 0:4], ACTF.Exp, [("SM", 3)], [("SM", 4)])
            ACT(EKD, GC[:, 4:8], ACTF.Exp, [("SM", 3)], [("SM", 5)])
            TTo("dve", BEG, BETA, EG, ALU.mult, [("SM", 1), ("SM", 4)], [("SM", 6)])
            PGR4 = PS[pgr][:].rearrange("p (h c) -> p h c", h=NH)
            gk = [("GRW", 0), ("GRW", 1)]; ek = [("EGR", 0), ("EGR", 1)]
            TTo("dve", GRW[:, :, :], PGR4, GC[:, 0:4].unsqueeze(2).broadcast_to([128, NH, 128]), ALU.subtract, [("PS", pgr), ("SM", 3)] + gk, gk)
            GRWf = GRW[:].rearrange("p h c -> p (h c)")
            STT(GRWf, GRWf, -1.0, GRWf, ALU.mult, ALU.max, gk, gk)
            ACT(GRW[:], GRW[:], ACTF.Exp, gk, gk, scale=-1.0)
            ACT(EGR[:], PGR4, ACTF.Exp, [("PS", pgr)] + ek, ek)
            resv.discard(pgr)

        def delta_pair(l, t, is_s, hg):
            c0 = t * 128
            sfx = "_s" if is_s else "_p"
            nlv = NLV_S if is_s else NLV_P
            HS = slice(2 * hg, 2 * hg + 2)
            heads = (2 * hg, 2 * hg + 1)
            bk = lambda n: [(n, hg)]
            KTOK, VTOK, AM, ATT, DTt, DINV, N1, BV, BKG, WT, VN, QG, KD, SBF = [DB[n][:, HS, :] for n in dt_names_bf]
            f2 = lambda ap: ap.rearrange("p h c -> p (h c)")
            qkv_r = [("QKV", None)]
            bc = lambda ap: ap.unsqueeze(2).broadcast_to([128, 2, 128])
            bm = lambda ap: ap.unsqueeze(1).broadcast_to([128, 2, 128])
            BETA = SM[:, 16 + 2 * hg:18 + 2 * hg]; GT = SM[:, 20 + 2 * hg:22 + 2 * hg]; GC0 = SM[:, 24 + 2 * hg:26 + 2 * hg]
            BEG = SM[:, 36 + 2 * hg:38 + 2 * hg]; EKD = SM[:, 40 + 2 * hg:42 + 2 * hg]
            GRWp = GRW[:, HS, :]; EGRp = EGR[:, HS, :]; TRGp = TRG[:, HS, :]; OTp = OT[:, HS, :]
            if getattr(cfg, 'delta_stop', 99) == 1:
                resv.clear()
                return
            pgr = st["pgr"]
            ptk = psum()
            for i, h in enumerate(heads):
                TR(psb(ptk)[:, i * 128:(i + 1) * 128], QKV[:, 4 + h, c0:c0 + 128], identb, qkv_r + [("C16", None)], [("PS", ptk)])
            for i, h in enumerate(heads):
                TR(psb(ptk)[:, 256 + i * 128:256 + (i + 1) * 128], QKV[:, 8 + h, c0:c0 + 128], identb, qkv_r + [("C16", None)], [("PS", ptk)])
            pkk = psum()
            for i, h in enumerate(heads):
                MM(PS[pkk][:, i * 128:(i + 1) * 128], QKV[:, 4 + h, c0:c0 + 128], QKV[:, 4 + h, c0:c0 + 128], i == 0, True, qkv_r, [("PS", pkk)])
            for i, h in enumerate(heads):
                MM(PS[pkk][:, 256 + i * 128:256 + (i + 1) * 128], QKV[:, 4 + h, c0:c0 + 128], QKV[:, h, c0:c0 + 128], False, True, qkv_r,
                   [("PS", pkk), ("PGD", hg)])
            yield
            if getattr(cfg, 'delta_stop', 99) == 2:
                resv.clear()
                return
            PGR3 = PS[pgr][:, hg * 256:(hg + 1) * 256].rearrange("p (h c) -> p h c", h=2)

            if getattr(cfg, 'delta_stop', 99) == 25:
                resv.clear()
                return
            CP("act", f2(KTOK), psb(ptk)[:, 0:256], [("PS", ptk)] + bk("KTOK"), bk("KTOK"))
            CP("act", f2(VTOK), psb(ptk)[:, 256:512], [("PS", ptk)] + bk("VTOK"), bk("VTOK"))
            yield
            if getattr(cfg, 'delta_stop', 99) == 3:
                resv.clear()
                return
            TTo("dve", BV, VTOK, bc(BETA), ALU.mult, bk("VTOK") + [("SM", 1)] + bk("BV"), bk("BV"))
            TTo("dve", BKG, KTOK, bc(BEG), ALU.mult, bk("KTOK") + [("SM", 6)] + bk("BKG"), bk("BKG"))
            TTo("dve", KD, KTOK, bc(EKD), ALU.mult, bk("KTOK") + [("SM", 5)] + bk("KD"), bk("KD"))
            TTo("dve", QG, QKV[:, HS, c0:c0 + 128], EGRp, ALU.mult, qkv_r + bk("EGR") + bk("QG"), bk("QG"))
            CP("act", DTt, bm(identb), [("C16", None)] + bk("DT"), bk("DT"))
            CP("act", DINV, bm(identb), [("C16", None)] + bk("DINV"), bk("DINV"))
            yield
            if getattr(cfg, 'delta_stop', 99) == 4:
                resv.clear()
                return
            TTo("dve", TRGp, GRWp, bm(c16("mstrict" + sfx)), ALU.mult, bk("GRW") + [("C16", None)] + bk("TRG"), bk("TRG"))
            TTo("dve", TRGp, TRGp, bc(BETA), ALU.mult, bk("TRG") + [("SM", 1)], bk("TRG"))
            TTo("dve", OTp, GRWp, bm(c16("minclt" + sfx)), ALU.mult, bk("GRW") + [("C16", None)] + bk("OT"), bk("OT"))
            TTo("dve", f2(AM), PS[pkk][:, 0:256], f2(TRGp), ALU.mult, [("PS", pkk)] + bk("TRG") + bk("AM"), bk("AM"))
            TTo("dve", f2(ATT), PS[pkk][:, 256:512], f2(OTp), ALU.mult, [("PS", pkk)] + bk("OT") + bk("ATT"), bk("ATT"))
            yield
            if getattr(cfg, 'delta_stop', 99) == 5:
                resv.clear()
                return
            for lv in range(nlv):
                p1 = psum()
                for i in range(2):
                    MM(PS[p1][:, i * 128:(i + 1) * 128], AM[:, i, :], DTt[:, i, :], i == 0, True, bk("AM") + bk("DT"), [("PS", p1)])
                yield
                TTo("dve", N1, PS[p1][:, 0:256].rearrange("p (h c) -> p h c", h=2), bm(c16("lv_p%d" % lv)), ALU.mult,
                    [("PS", p1), ("C16", None)] + bk("N1"), bk("N1"))
                yield
                p2 = psum()
                for i in range(2):
                    MM(PS[p2][:, i * 128:(i + 1) * 128], DINV[:, i, :], N1[:, i, :], i == 0, True, bk("DINV") + bk("N1"), [("PS", p2)])
                yield
                TTo("dve", f2(DTt), f2(DTt), PS[p2][:, 0:256], ALU.subtract, [("PS", p2)] + bk("DT"), bk("DT"))
                yield
                if lv < nlv - 1:
                    p3 = psum()
                    for i in range(2):
                        TR(psb(p3)[:, i * 128:(i + 1) * 128], DTt[:, i, :], identb, bk("DT") + [("C16", None)], [("PS", p3)])
                    yield
                    CP("act", f2(DINV), psb(p3)[:, 0:256], [("PS", p3)] + bk("DINV"), bk("DINV"))
                    yield
            if getattr(cfg, 'delta_stop', 99) == 6:
                resv.clear()
                return
            pu = psum()
            resv.add(pu)
            if not is_s:
                for i in range(2):
                    MM(PS[pu][:, i * 128:(i + 1) * 128], DTt[:, i, :], BV[:, i, :], i == 0, False, bk("DT") + bk("BV"), [("PS", pu)])
            pw = psum()
            for i in range(2):
                MM(PS[pw][:, i * 128:(i + 1) * 128], BKG[:, i, :], DTt[:, i, :], i == 0, True, bk("DT") + bk("BKG"), [("PS", pw)])
            yield
            ACT(f2(WT), PS[pw][:, 0:256], ACTF.Copy, [("PS", pw)] + bk("AM"), bk("AM"), scale=-1.0)
            po = psum()
            resv.add(po)
            if getattr(cfg, 'delta_stop', 99) == 7:
                resv.clear()
                return
            if not is_s:
                CP("act", SBF, SST[:, l, HS, :], [("SST", hg)] + bk("DINV"), bk("DINV"))
                yield
                for i in range(2):
                    MM(PS[pu][:, i * 128:(i + 1) * 128], WT[:, i, :], SBF[:, i, :], False, True, bk("AM") + bk("DINV"), [("PS", pu)])
                yield
                CP("act", f2(VN), PS[pu][:, 0:256], [("PS", pu)] + bk("N1"), bk("N1"))
                resv.discard(pu)
                yield
                for i in range(2):
                    MM(PS[po][:, i * 128:(i + 1) * 128], SBF[:, i, :], QG[:, i, :], i == 0, False, bk("DINV") + bk("QG"), [("PS", po)])
                for i in range(2):
                    MM(PS[po][:, i * 128:(i + 1) * 128], VN[:, i, :], ATT[:, i, :], False, True, bk("N1") + bk("ATT"), [("PS", po)])
                psu = psum()
                for i in range(2):
                    MM(PS[psu][:, i * 128:(i + 1) * 128], KD[:, i, :], VN[:, i, :], i == 0, True, bk("KD") + bk("N1"), [("PS", psu)])
                yield
                for i, h in enumerate(heads):
                    STT(SST[:, l, h, :], SST[:, l, h, :], EGR[:, h, 127:128], PS[psu][:, i * 128:(i + 1) * 128], ALU.mult, ALU.add,
                        [("PS", psu), ("SST", hg)] + bk("EGR"), [("SST", hg)])
            else:
                resv.discard(pu)
                HSQ = NS // 2
                SS0 = TT[:, 0:2, :].rearrange("p a t -> p (a t)")[:, 0:HSQ * 128].rearrange("p (s e) -> p s e", s=HSQ)
                SS0B = TT[:, 2, :].bitcast(BF16)[:, 0:HSQ * 128].rearrange("p (s e) -> p s e", s=HSQ)
                WTX = TT[:, 3, 0:512].bitcast(BF16).rearrange("p (s c) -> p s c", s=HSQ)
                kS0 = [("TT", 0), ("TT", 1)]; kS0B = [("TT", 2)]; kWX = [("TT", 3)]
                segrow = c16("segrow", NS * 128).rearrange("p (s c) -> p s c", s=NS)
                segcol = c16("segcol", NS)
                first_po = True
                for i, h in enumerate(heads):
                    for hf in range(2):
                        s0 = hf * HSQ
                        P.dma("sp", SS0, dr["st_dn"][l][s0:s0 + HSQ, h, :, :].rearrange("s d e -> d s e"), reads=[], writes=kS0)
                        CP("act", SS0B, SS0, kS0 + kS0B, kS0B)
                        TTo("dve", WTX, WT[:, i, :].unsqueeze(1).broadcast_to([128, HSQ, 128]), segrow[:, s0:s0 + HSQ, :], ALU.mult,
                            bk("AM") + [("C16", None)] + kWX, kWX)
                        pu2 = psum()
                        MM(PS[pu2][:, 0:128], DTt[:, i, :], BV[:, i, :], True, False, bk("DT") + bk("BV"), [("PS", pu2)])
                        for s_ in range(HSQ):
                            MM(PS[pu2][:, 0:128], WTX[:, s_, :], SS0B[:, s_, :], False, True, kS0B + kWX, [("PS", pu2)])
                        CP("act", VN[:, i, :], PS[pu2][:, 0:128], [("PS", pu2)] + bk("N1"), bk("N1"))
                        cb = i * 128 + hf * 64
                        MM(PS[po][:, cb:cb + 64], VN[:, i, :], ATT[:, i, hf * 64:hf * 64 + 64], first_po, False, bk("N1") + bk("ATT"), [("PS", po)])
                        first_po = False
                        for s_ in range(HSQ):
                            cs = i * 128 + (s0 + s_) * 8
                            MM(PS[po][:, cs:cs + 8], SS0B[:, s_, :], QG[:, i, (s0 + s_) * 8:(s0 + s_) * 8 + 8], False, True,
                               kS0B + bk("QG"), [("PS", po)])
                        TTo("dve", WTX, KD[:, i, :].unsqueeze(1).broadcast_to([128, HSQ, 128]),
                            segcol[:, s0:s0 + HSQ].unsqueeze(2).broadcast_to([128, HSQ, 128]), ALU.mult,
                            bk("KD") + [("C16", None)] + kWX, kWX)
                        for q in range(2):
                            psu = psum()
                            for j in range(4):
                                s_ = q * 4 + j
                                MM(PS[psu][:, j * 128:(j + 1) * 128], WTX[:, s_, :], VN[:, i, :], j == 0, True, kWX + bk("N1"), [("PS", psu)])
                            for j in range(4):
                                s_ = q * 4 + j
                                sg_ = s0 + s_
                                STT(SS0[:, s_, :], SS0[:, s_, :], EGR[:, h, sg_ * 8 + 7:sg_ * 8 + 8], PS[psu][:, j * 128:(j + 1) * 128],
                                    ALU.mult, ALU.add, [("PS", psu)] + kS0 + bk("EGR"), kS0)
                        out_ops.append(P.dma("sp", dr["nds"][l][s0:s0 + HSQ, h, :, :].rearrange("s d e -> d s e"), SS0, reads=kS0, writes=[("NDSo", h)]))
                        yield
            if getattr(cfg, 'delta_stop', 99) == 8:
                resv.clear()
                return
            resv.discard(po)
            CP("act", f2(OTp), PS[po][:, 0:256], [("PS", po)] + bk("OT"), bk("OT"))
            SQW = SQ[:, 2 * hg, 0:256]
            ACT(SQW, PS[po][:, 0:256], ACTF.Square, [("PS", po)], [("SQ", 2 * hg)])
            yield
            pss = psum()
            MM(PS[pss][:, 0:256], ones_r, SQW, True, True, [("SQ", 2 * hg), ("ONESR", None)], [("PS", pss)])
            yield
            RS = RSTD[:, hg * 256:(hg + 1) * 256]
            ACT(RS, PS[pss][:, 0:256], ACTF.Ln, [("PS", pss), ("EPST", None)], [("RSTD", hg)], scale=1.0 / 128.0, bias=EPSC[(128, 1.0)])
            ACT(RS, RS, ACTF.Exp, [("RSTD", hg)], [("RSTD", hg)], scale=-0.5)
            STT(f2(OTp), f2(OTp), pcol(l, "dn_norm_w"), RS, ALU.mult, ALU.mult, bk("OT") + [("RSTD", hg), ("PP", None)], bk("OT"))
            TTo("dve", MIXR[:, 4 + 2 * hg:6 + 2 * hg, c0:c0 + 128], OTp, ZG[:, HS, c0:c0 + 128], ALU.mult,
                bk("OT") + [("ZG", None)], [("Y", (4 + h, t)) for h in heads])

        def delta_tile(l, t, is_s):
            delta_prelude(l, t, is_s)
            gens = [delta_pair(l, t, is_s, 0), delta_pair(l, t, is_s, 1)]
            if getattr(cfg, "seq_pairs", False):
                for g in gens:
                    for _ in g:
                        pass
                return
            alive = [True, True]
            while any(alive):
                for gi_, g in enumerate(gens):
                    if alive[gi_]:
                        try:
                            next(g)
                        except StopIteration:
                            alive[gi_] = False

        for l in range(DEPTH):
            ffn(l, 1)
            mixer(l)
            ffn(l, 2)

        for (b0, bn) in blks:
            sumsq_rstd(X, "X", b0, bn, D)
            norm_scale(Y, "Y", X, "X", 0, "final_norm", b0, bn)
        for t in range(ntile):
            si = st["stg"]; st["stg"] ^= 1
            stg = stg_view(si)
            for half in range(2):
                pi = psum()
                for j in range(4):
                    kc = half * 4 + j
                    TR(PS[pi][:, j * 128:(j + 1) * 128], YF[:, kc, t * 128:(t + 1) * 128], ident, [("Y", (kc, t)), ("C32", None)], [("PS", pi)])
                CP("act" if half == 0 else "dve", stg[:, half * 512:(half + 1) * 512], PS[pi][:], [("PS", pi)], [stg_keys(si)[half]])
            dst = dr["yp"][p0 + t * 128: p0 + (t + 1) * 128, :] if t * 128 < npr else dr["ys"][:, :]
            out_ops.append(P.dma("sp", dst, stg, reads=stg_keys(si), writes=[("STGo", si)]))

    P.emit(out_ops)
    es.close()
    return nc, P


def make_in_maps(inp):
    pp = _pack_params(inp)
    c32, c16 = _consts()
    rgw = _pack_rgw(inp)
    maps = []
    shared = {"pp": pp, "c32": c32, "c16": c16, "rgw_r": rgw}
    for nm in ("ffn1_w_up", "ffn2_w_up", "ffn1_w_down", "ffn2_w_down", "w_in", "w_out"):
        shared[nm] = np.ascontiguousarray(inp[nm])
    for core in range(NCORES):
        sl = slice(core * NS, (core + 1) * NS)
        m = dict(shared)
        m["xp"] = np.ascontiguousarray(inp["x_prompt"][core])
        m["xs"] = np.ascontiguousarray(inp["x_sample"][sl].reshape(NS * DS, D))
        m["st_conv"] = np.ascontiguousarray(inp["state_conv"][:, sl].reshape(DEPTH, NS * 3, CONVC))
        m["st_rg"] = np.ascontiguousarray(inp["state_rglru"][:, sl])
        m["st_dn"] = np.ascontiguousarray(inp["state_delta"][:, sl])
        maps.append(m)
    return maps


def gather(r):
    y_prompt = np.stack([r[c]["yp"] for c in range(NCORES)], axis=0)
    y_sample = np.concatenate([r[c]["ys"].reshape(NS, DS, D) for c in range(NCORES)], axis=0)
    ncp = np.stack([r[c]["ncp"] for c in range(NCORES)], axis=1)
    nrp = np.stack([r[c]["nrp"] for c in range(NCORES)], axis=1)
    ndp = np.stack([r[c]["ndp"] for c in range(NCORES)], axis=1)
    ncs = np.concatenate([r[c]["ncs"].reshape(DEPTH, NS, 3, CONVC) for c in range(NCORES)], axis=1)
    nrs = np.concatenate([r[c]["nrs"] for c in range(NCORES)], axis=1)
    nds = np.concatenate([r[c]["nds"] for c in range(NCORES)], axis=1)
    return (y_prompt, y_sample, ncp, nrp, ndp, ncs, nrs, nds)


def kernel(**inp):
    inp = {k: np.asarray(v) for k, v in inp.items()}
    cfg = Cfg()
    nc, P = build_program(cfg)
    maps = make_in_maps(inp)
    res = run_bass_kernel_spmd(nc, maps, core_ids=list(range(NCORES)))
    return gather(res.results)
```

```python
import numpy as np
from contextlib import ExitStack
import concourse.bass as bass
import concourse.mybir as mybir
from concourse.bass_utils import run_bass_kernel_spmd

F32 = mybir.dt.float32
F32R = mybir.dt.float32r
BF16 = mybir.dt.bfloat16
ACTF = mybir.ActivationFunctionType
ALU = mybir.AluOpType

D = 1024
KC = 8
DFF = 2816
NFF = 22
DEPTH = 2
SEQ = 2048
NS = 16
DS = 8
INC = 3080
CONVC = 2048
EPS = 1e-6
NCORES = 8


class Op:
    __slots__ = ("eng", "fn", "deps", "sig", "count", "sem", "is_dma", "idx")

    def __init__(self, eng, fn, is_dma=False):
        self.eng = eng
        self.fn = fn
        self.deps = []
        self.sig = False
        self.count = None
        self.sem = None
        self.is_dma = is_dma
        self.idx = None


class Prog:
    ENGS = ("pe", "act", "dve", "pool", "sp")
    NDMASEM = 8

    def __init__(self, nc, same_engine_sync=True):
        self.nc = nc
        self.ops = {e: [] for e in self.ENGS}
        self.last_w = {}
        self.readers = {}
        self.same_engine_sync = same_engine_sync
        self.dma_n = {"sp": 0, "pool": 0}
        self.dma_hist = {"sp": [], "pool": []}
        self.n_ops = 0

    def _collect(self, op, reads, writes):
        deps = []
        for (n, i) in reads:
            lw = self.last_w.get(n)
            if lw:
                if i is None:
                    deps.extend(lw.values())
                else:
                    if i in lw:
                        deps.append(lw[i])
                    if None in lw:
                        deps.append(lw[None])
        for (n, i) in writes:
            lw = self.last_w.get(n)
            rd = self.readers.get(n)
            if lw:
                if i is None:
                    deps.extend(lw.values())
                else:
                    if i in lw:
                        deps.append(lw[i])
                    if None in lw:
                        deps.append(lw[None])
            if rd:
                if i is None:
                    for v in rd.values():
                        deps.extend(v)
                else:
                    deps.extend(rd.get(i, ()))
                    deps.extend(rd.get(None, ()))
        for (n, i) in writes:
            lw = self.last_w.setdefault(n, {})
            rd = self.readers.setdefault(n, {})
            if i is None:
                lw.clear()
                rd.clear()
                lw[None] = op
            else:
                lw[i] = op
                rd.pop(i, None)
        for (n, i) in reads:
            self.readers.setdefault(n, {}).setdefault(i, []).append(op)
        seen = set()
        for d in deps:
            if d is op or id(d) in seen:
                continue
            seen.add(id(d))
            if d.eng == op.eng and not d.is_dma and not op.is_dma:
                if op.eng == "pe" or not self.same_engine_sync:
                    continue
            op.deps.append(d)
            d.sig = True

    def op(self, eng, fn, reads=(), writes=()):
        o = Op(eng, fn)
        self._collect(o, list(reads), list(writes))
        self.ops[eng].append(o)
        self.n_ops += 1
        return o

    def dma(self, q, out, in_, reads=(), writes=()):
        o = Op(q, lambda e: e.dma_start(out=out, in_=in_), is_dma=True)
        n = self.dma_n[q]
        self.dma_n[q] += 1
        o.idx = n
        self._collect(o, list(reads), list(writes))
        hist = self.dma_hist[q]
        if n >= self.NDMASEM:
            o.deps.append(hist[n - self.NDMASEM])
        hist.append(o)
        o.sig = True
        self.ops[q].append(o)
        self.n_ops += 1
        return o

    def emit(self, final_wait_ops):
        nc = self.nc
        with ExitStack() as es:
            esem = {e: es.enter_context(nc.semaphore("prog_" + e)) for e in self.ENGS}
            dsem = {q: [es.enter_context(nc.semaphore("dma_%s_%d" % (q, i))) for i in range(self.NDMASEM)]
                    for q in ("sp", "pool")}
            for e in self.ENGS:
                c = 0
                for o in self.ops[e]:
                    if o.is_dma:
                        slot = o.idx % self.NDMASEM
                        o.sem = dsem[e][slot]
                        o.count = 16 * (o.idx // self.NDMASEM + 1)
                    elif o.sig:
                        c += 1
                        o.sem = esem[e]
                        o.count = c
            block = es.enter_context(nc.Block())

            def run(ename, e, extra_final=None):
                known = {}
                for o in self.ops[ename]:
                    need = {}
                    for d in o.deps:
                        key = id(d.sem)
                        if known.get(key, 0) >= d.count:
                            continue
                        if key not in need or need[key][1] < d.count:
                            need[key] = (d.sem, d.count)
                    for key, (s, v) in need.items():
                        e.wait_ge(s, v)
                        known[key] = v
                    ins = o.fn(e)
                    if o.is_dma:
                        ins.then_inc(o.sem, 16)
                    elif o.sig:
                        ins.then_inc(o.sem, 1)
                if extra_final:
                    need = {}
                    for d in extra_final:
                        key = id(d.sem)
                        if key not in need or need[key][1] < d.count:
                            need[key] = (d.sem, d.count)
                    for key, (s, v) in need.items():
                        e.wait_ge(s, v)

            @block.tensor
            def _(e):
                run("pe", e)

            @block.scalar
            def _(e):
                run("act", e)

            @block.vector
            def _(e):
                run("dve", e)

            @block.gpsimd
            def _(e):
                run("pool", e)

            @block.sync
            def _(e):
                run("sp", e, extra_final=final_wait_ops)


RGW = 512
NH = 4
HD = 128
NLV_P = 7
NLV_S = 3

C32 = {"ident": 0, "ones": 128, "tri_p": 256, "up_p": 384, "tri_s": 512, "up_s": 640}
N32 = 768
C16 = {"identb": 0, "mstrict_p": 128, "minclt_p": 256, "mstrict_s": 384, "minclt_s": 512}
for _i in range(NLV_P):
    C16["lv_p%d" % _i] = 640 + 128 * _i
C16["segcol"] = 640 + 128 * NLV_P
C16["segrow"] = C16["segcol"] + 16
N16 = C16["segrow"] + 16 * 128


def _consts():
    i = np.arange(128)[:, None]
    j = np.arange(128)[None, :]
    c32 = np.zeros((128, N32), np.float32)
    c32[:, 0:128] = np.eye(128)
    c32[:, 128:256] = 1.0
    seg = 8
    same_s = (i // seg) == (j // seg)
    c32[:, 256:384] = (i <= j)
    c32[:, 384:512] = (i > j)
    c32[:, 512:640] = (i <= j) & same_s
    c32[:, 640:768] = (i > j) & same_s
    c16 = np.zeros((128, N16), np.float32)
    c16[:, 0:128] = np.eye(128)
    c16[:, 128:256] = (i > j)
    c16[:, 256:384] = (j >= i)
    c16[:, 384:512] = (i > j) & same_s
    c16[:, 512:640] = (j >= i) & same_s
    for lv in range(NLV_P):
        b = 1 << lv
        m = ((i // (2 * b)) == (j // (2 * b))) & (((j // b) % 2) == 1) & (((i // b) % 2) == 0)
        c16[:, C16["lv_p%d" % lv]:C16["lv_p%d" % lv] + 128] = m
    c16[:, C16["segcol"]:C16["segcol"] + 16] = (np.arange(128)[:, None] // seg) == np.arange(16)[None, :]
    sr = (np.arange(16)[:, None] == (np.arange(128)[None, :] // seg)).astype(np.float32).reshape(1, 16 * 128)
    c16[:, C16["segrow"]:] = np.repeat(sr, 128, axis=0)
    return c32, c16


PL = {}
_o = 0
for _nm, _n in (("ffn1_norm_pre", 8), ("ffn1_norm_post", 8), ("mix_norm_pre", 8), ("mix_norm_post", 8),
                ("ffn2_norm_pre", 8), ("ffn2_norm_post", 8), ("final_norm", 8),
                ("conv_w", 64), ("conv_b_rg", 4), ("rg_b_a", 4), ("rg_b_x", 4), ("rg_lambda", 4),
                ("dn_a_log", 4), ("dn_dt_bias", 4), ("dn_norm_w", 1)):
    PL[_nm] = _o
    _o += _n
PP_LAYER = _o


def _pack_params(inp):
    cols = []

    def vecn(v, n):
        return np.ascontiguousarray(np.asarray(v).reshape(n, 128).T)

    for l in range(DEPTH):
        for nm in ("ffn1_norm_pre", "ffn1_norm_post", "mix_norm_pre", "mix_norm_post",
                   "ffn2_norm_pre", "ffn2_norm_post"):
            cols.append(vecn(inp[nm][l], 8))
        cols.append(vecn(inp["final_norm"], 8))
        cw = np.asarray(inp["conv_w"][l])
        cols.append(np.ascontiguousarray(cw.reshape(4, 16, 128).transpose(2, 1, 0).reshape(128, 64)))
        for nm in ("conv_b_rg", "rg_b_a", "rg_b_x", "rg_lambda"):
            cols.append(vecn(inp[nm][l], 4))
        cols.append(np.repeat(np.asarray(inp["dn_a_log"][l]).reshape(1, 4), 128, axis=0))
        cols.append(np.repeat(np.asarray(inp["dn_dt_bias"][l]).reshape(1, 4), 128, axis=0))
        cols.append(np.asarray(inp["dn_norm_w"][l]).reshape(128, 1))
    return np.ascontiguousarray(np.concatenate(cols, axis=1).astype(np.float32))


def _pack_rgw(inp):
    out = np.zeros((DEPTH, 2, 128, 4, 128), np.float32)
    for l in range(DEPTH):
        for gi, nm in enumerate(("rg_w_a", "rg_w_x")):
            w = np.asarray(inp[nm][l])
            for c in range(4):
                for hh in range(2):
                    out[l, gi, hh * 64:(hh + 1) * 64, c, hh * 64:(hh + 1) * 64] = w[2 * c + hh]
    return out


class Cfg:
    def __init__(self, **kw):
        self.groups = [(0, 640, True), (640, 768, False), (1408, 640, False)]
        self.stages = "full"
        self.same_engine_sync = True
        self.__dict__.update(kw)


def blocks_of(T):
    if T == 768:
        return [(0, 384), (384, 384)]
    if T == 640:
        return [(0, 384), (384, 256)]
    raise ValueError(T)


def build_program(cfg):
    nc = bass.Bass("TRN2", target_bir_lowering=False)
    TM = 768
    NTM = TM // 128
    dr = {}

    def din(name, shape, dt=F32):
        dr[name] = nc.dram_tensor(name, shape, dt, kind="ExternalInput").ap()

    def dout(name, shape):
        dr[name] = nc.dram_tensor(name, shape, F32, kind="ExternalOutput").ap()

    din("xp", [SEQ, D]); din("xs", [NS * DS, D])
    din("pp", [128, DEPTH * PP_LAYER]); din("c32", [128, N32]); din("c16", [128, N16])
    din("rgw_r", [DEPTH, 2, 128, 4, 128], F32R)
    din("st_conv", [DEPTH, NS * 3, CONVC]); din("st_rg", [DEPTH, NS, RGW]); din("st_dn", [DEPTH, NS, NH, HD, HD])
    for nm in ("ffn1_w_up", "ffn2_w_up"):
        din(nm, [DEPTH, D, 2 * DFF], F32R)
    for nm in ("ffn1_w_down", "ffn2_w_down"):
        din(nm, [DEPTH, DFF, D], F32R)
    din("w_in", [DEPTH, D, INC], F32R); din("w_out", [DEPTH, D, D], F32R)
    dout("yp", [SEQ, D]); dout("ys", [NS * DS, D])
    dout("ncp", [DEPTH, 3, CONVC]); dout("nrp", [DEPTH, RGW]); dout("ndp", [DEPTH, NH, HD, HD])
    if getattr(cfg, "debug", False):
        dout("dbg", [3, 128, KC, TM])
    dout("ncs", [DEPTH, NS * 3, CONVC]); dout("nrs", [DEPTH, NS, RGW]); dout("nds", [DEPTH, NS, NH, HD, HD])

    P = Prog(nc, same_engine_sync=cfg.same_engine_sync)
    es = ExitStack()
    sb = lambda name, shape, dt: es.enter_context(nc.sbuf_tensor(name, shape, dt))
    X = sb("X", [128, KC, TM], F32)
    H = sb("H", [128, KC, TM], F32R)
    Y = sb("Y", [128, KC, TM], F32R)
    YF = Y[:].bitcast(F32)
    HF = H[:].bitcast(F32)
    NFB = 4
    ACTB = sb("ACTB", [128, NFB, TM], F32R)
    NWA = 4
    WA = [sb("WA%d" % i, [128, KC, 128], F32R) for i in range(NWA)]
    NWB = 4
    WB = [sb("WB%d" % i, [128, NFB, 128], F32R) for i in range(NWB)]
    PPt = sb("PPt", [128, DEPTH * PP_LAYER], F32)
    C32t = sb("C32t", [128, N32], F32)
    C16t = sb("C16t", [128, N16], BF16)
    ONESR = sb("ONESR", [128, 128], F32R)
    EPST = sb("EPST", [128, 8], F32)
    TT = sb("TT", [128, 4, TM], F32)
    SQ = sb("SQ", [128, 4, 384], F32R)
    RSTD = sb("RSTD", [128, 512], F32)
    SG = [TT[:, 0, 0:384], TT[:, 1, 0:384]]
    ZG = sb("ZG", [128, 4, TM], BF16)
    QKV = sb("QKV", [128, 12, TM], BF16)
    XR = sb("XR", [128, TM], F32R)
    XP = sb("XP", [128, 3 + TM], F32R)
    XPS = sb("XPS", [128, NS, 11], F32R)
    XPf = XP[:].bitcast(F32)
    XPSf = XPS[:].bitcast(F32)
    DG = sb("DG", [128, 4, 128], F32R)
    CSS = sb("CSS", [128, 16, NS * 3], F32)
    HS0 = sb("HS0", [128, 4, NS], F32)
    CONVT = sb("CONVT", [128, DEPTH, 16, 3], F32)
    HRG = sb("HRG", [128, DEPTH, 4], F32)
    SST = sb("SST", [128, DEPTH, NH, HD], F32)
    RGWt = sb("RGWt", [128, 2, 4, 128], F32R)
    WSC = sb("WSC", [128, KC, 8], F32R)
    SCT = sb("SCT", [128, NTM, 8], F32)
    LAYC = sb("LAYC", [128, 16], F32)
    ONE1 = sb("ONE1", [128, 1], F32)
    dt_names_bf = ["KTOK", "VTOK", "AM", "ATT", "DT", "DINV", "N1", "BV", "BKG", "WT", "VN", "QG", "KD", "SBF"]
    DB = {n: sb(n, [128, NH, 128], BF16) for n in dt_names_bf if n not in ("VN", "SBF", "WT")}
    DB["VN"] = DB["N1"]; DB["SBF"] = DB["DINV"]; DB["WT"] = DB["AM"]
    GRW = sb("GRW", [128, NH, 128], F32)
    EGR2 = sb("EGR", [128, 2, NH, 128], F32)
    TRG = sb("TRG", [128, NH, 128], F32)
    OT = sb("OT", [128, NH, 128], F32)
    SM2 = sb("SM", [128, 2, 64], F32)
    CSL = TT[0:48, 3, 0:512]
    RSL = TT[0:16, 3, 0:512]
    PS = [es.enter_context(nc.psum_tensor("PS%d" % i, [128, 512], F32)) for i in range(8)]

    def c32(nm):
        return C32t[:, C32[nm]:C32[nm] + 128]

    def c16(nm, n=128):
        return C16t[:, C16[nm]:C16[nm] + n]

    ident = c32("ident")
    ones_f = c32("ones")
    ones_r = ONESR[:, :]
    identb = c16("identb")
    st = {"ps": 0, "wa": 0, "wb": 0, "stg": 0, "sg": 0}

    resv = set()

    def psum():
        while True:
            i = st["ps"]
            st["ps"] = (i + 1) % 8
            if i not in resv:
                return i

    def psb(pi):
        return PS[pi][:].bitcast(BF16)

    def ACT(out, in_, func, reads, writes, **kw):
        return P.op("act", lambda e: e.activation(out=out, in_=in_, func=func, **kw), reads, writes)

    def CP(eng, out, in_, reads, writes):
        if eng == "act":
            return P.op("act", lambda e: e.copy(out=out, in_=in_), reads, writes)
        return P.op(eng, lambda e: e.tensor_copy(out=out, in_=in_), reads, writes)

    def TTo(eng, out, in0, in1, op, reads, writes):
        return P.op(eng, lambda e: e.tensor_tensor(out=out, in0=in0, in1=in1, op=op), reads, writes)

    def TS(eng, out, in0, s1, s2, op0, op1, reads, writes):
        if op1 is None:
            return P.op(eng, lambda e: e.tensor_scalar(out=out, in0=in0, scalar1=s1, scalar2=None, op0=op0), reads, writes)
        return P.op(eng, lambda e: e.tensor_scalar(out=out, in0=in0, scalar1=s1, scalar2=s2, op0=op0, op1=op1), reads, writes)

    def STT(out, in0, scalar, in1, op0, op1, reads, writes):
        return P.op("dve", lambda e: e.scalar_tensor_tensor(out=out, in0=in0, scalar=scalar, in1=in1, op0=op0, op1=op1), reads, writes)

    def MM(out, lhsT, rhs, start, stop, reads, writes):
        return P.op("pe", lambda e: e.matmul(out, lhsT=lhsT, rhs=rhs, start=start, stop=stop, skip_group_check=True), reads, writes)

    def TR(out, in_, idn, reads, writes):
        return P.op("pe", lambda e: e.transpose(out=out, in_=in_, identity=idn), reads, writes)

    class WStream:
        def __init__(self, name, bufs):
            self.name, self.bufs, self.n = name, bufs, len(bufs)
            self.reqs = []
            self.issued = 0
            self.cur = 0

        def add(self, fn):
            self.reqs.append(fn)

        def next(self):
            i = self.cur
            self.cur += 1
            upto = min(len(self.reqs), i + self.n - 1)
            while self.issued < upto:
                k = self.issued
                slot = k % self.n
                out_ap, in_ap = self.reqs[k](self.bufs[slot])
                P.dma("pool", out_ap, in_ap, writes=[(self.name, slot)])
                self.issued += 1
            return i % self.n

    WAS = WStream("WA", WA)
    WBS = WStream("WB", WB)
    FBLOCKS = [(0, 4), (4, 4), (8, 4), (12, 4), (16, 4), (20, 2)]

    def plan_cols(src, col):
        v = src.rearrange("(kc p) n -> p kc n", p=128)
        WAS.add(lambda buf, v=v, col=col: (buf[:], v[:, :, col:col + 128]))

    def plan_ffn(l, which):
        wup = dr["ffn%d_w_up" % which][l]
        wdn = dr["ffn%d_w_down" % which][l].rearrange("(c p) n -> p c n", p=128)
        for (c0, nch) in FBLOCKS:
            for j in range(nch):
                plan_cols(wup, (c0 + j) * 128)
                plan_cols(wup, DFF + (c0 + j) * 128)
            for oc in range(8):
                WBS.add(lambda buf, c0=c0, nch=nch, oc=oc, wdn=wdn: (buf[:, 0:nch, :], wdn[:, c0:c0 + nch, oc * 128:(oc + 1) * 128]))

    def plan_mixer(l):
        win = dr["w_in"][l]
        for ch in range(4):
            plan_cols(win, ch * 128)
            plan_cols(win, 2048 + ch * 128)
        for kind in range(3):
            for hh in range(4):
                plan_cols(win, 512 + kind * 512 + hh * 128)
        for hh in range(4):
            plan_cols(win, 2560 + hh * 128)
        for oc in range(8):
            plan_cols(dr["w_out"][l], oc * 128)

    for _g in cfg.groups:
        for l in range(DEPTH):
            plan_ffn(l, 1)
            plan_mixer(l)
            plan_ffn(l, 2)

    P.dma("sp", PPt[:], dr["pp"][:, :], writes=[("PP", None)])
    P.dma("sp", C32t[:], dr["c32"][:, :], writes=[("C32", None)])
    for i in range(0, N16, 768):
        n = min(768, N16 - i)
        P.dma("sp", TT[:, 0, 0:n], dr["c16"][:, i:i + n], writes=[("TT", 0)])
        CP("dve", C16t[:, i:i + n], TT[:, 0, 0:n], [("TT", 0)], [("C16", None)])
    CP("dve", ones_r, ones_f, [("C32", None)], [("ONESR", None)])
    EPSC = {}
    for i, (key, val) in enumerate([((D, 1.0), EPS), ((D, 0.5), 4.0 * EPS), ((128, 1.0), EPS), ((1, 1.0), EPS), ("q", 128.0 * EPS)]):
        EPSC[key] = EPST[:, i:i + 1]
        P.op("dve", lambda e, i=i, val=val: e.memset(EPST[:, i:i + 1], val), writes=[("EPST", i)])
    P.op("dve", lambda e: e.memset(ONE1[:, :], 1.0), writes=[("ONE1", None)])
    P.op("dve", lambda e: e.memset(CONVT[:].rearrange("p l c t -> p (l c t)"), 0.0), writes=[("CONVT", None)])
    P.op("dve", lambda e: e.memset(HRG[:].rearrange("p l c -> p (l c)"), 0.0), writes=[("HRG", None)])
    P.op("dve", lambda e: e.memset(SST[:].rearrange("p l h e -> p (l h e)"), 0.0), writes=[("SST", None)])

    out_ops = []

    def pcol(l, nm, j=0, n=1):
        c = l * PP_LAYER + PL[nm] + j
        return PPt[:, c:c + n]

    ngroups = len(cfg.groups)
    for gi, (p0, npr, smp) in enumerate(cfg.groups):
        T = npr + (128 if smp else 0)
        ntile = T // 128
        blks = blocks_of(T)
        last_group = (gi == ngroups - 1)

        def tiles_of(b0, bn):
            return list(range(b0 // 128, (b0 + bn + 127) // 128))

        def rk(name, kcs, b0, bn):
            if isinstance(kcs, int):
                kcs = [kcs]
            return [(name, (kc, t)) for kc in kcs for t in tiles_of(b0, bn)]

        def stg_view(si):
            return TT[:, 2 * si:2 * si + 2, :].rearrange("p a t -> p (a t)")[:, 0:D]

        def stg_keys(si):
            return [("TT", 2 * si), ("TT", 2 * si + 1)]

        for t in range(ntile):
            si = st["stg"]; st["stg"] ^= 1
            stg = stg_view(si)
            src = dr["xp"][p0 + t * 128: p0 + (t + 1) * 128, :] if t * 128 < npr else dr["xs"][:, :]
            P.dma("sp", stg, src, writes=stg_keys(si))
            for half in range(2):
                pi = psum()
                for j in range(4):
                    kc = half * 4 + j
                    TR(PS[pi][:, j * 128:(j + 1) * 128], stg[:, kc * 128:(kc + 1) * 128], ident,
                       stg_keys(si) + [("C32", None)], [("PS", pi)])
                CP("act" if half == 0 else "dve", X[:, half * 4:(half + 1) * 4, t * 128:(t + 1) * 128],
                   PS[pi][:].rearrange("p (j c) -> p j c", j=4), [("PS", pi)], [("X", (half * 4 + j, t)) for j in range(4)])

        def sumsq_rstd(SRC, srcname, b0, bn, nfeat, post_scale=1.0, nk=KC):
            pi = psum()
            for kc in range(nk):
                rd = rk(srcname, kc, b0, bn)
                ACT(SQ[:, kc % 4, 0:bn], SRC[:, kc, b0:b0 + bn], ACTF.Square, rd, [("SQ", kc % 4)])
                MM(PS[pi][:, 0:bn], ones_r, SQ[:, kc % 4, 0:bn], kc == 0, kc == nk - 1,
                   [("SQ", kc % 4), ("ONESR", None)], [("PS", pi)])
            rstd_from_psum(pi, bn, nfeat, post_scale)

        def rstd_from_psum(pi, bn, nfeat, post_scale=1.0):
            ACT(RSTD[:, 0:bn], PS[pi][:, 0:bn], ACTF.Ln, [("PS", pi), ("EPST", None)], [("RSTD", None)],
                scale=1.0 / (nfeat * post_scale * post_scale), bias=EPSC[(nfeat, post_scale)])
            ACT(RSTD[:, 0:bn], RSTD[:, 0:bn], ACTF.Exp, [("RSTD", None)], [("RSTD", None)], scale=-0.5)

        def norm_scale(DST, dstname, SRC, srcname, l, gname, b0, bn):
            for kc in range(KC):
                STT(DST[:, kc, b0:b0 + bn], SRC[:, kc, b0:b0 + bn], pcol(l, gname, kc), RSTD[:, 0:bn], ALU.mult, ALU.mult,
                    rk(srcname, kc, b0, bn) + [("RSTD", None), ("PP", None)], rk(dstname, kc, b0, bn))

        def prenorm_to_H(l, gname):
            for (b0, bn) in blks:
                sumsq_rstd(X, "X", b0, bn, D)
                norm_scale(H, "H", X, "X", l, gname, b0, bn)

        def postnorm_residual(SRCw, SRC, srcname, l, gname, scale):
            for (b0, bn) in blks:
                sumsq_rstd(SRC, srcname, b0, bn, D, post_scale=scale)
                norm_scale(SRCw, srcname, SRC, srcname, l, gname, b0, bn)
                for kc in range(KC):
                    TTo("dve", X[:, kc, b0:b0 + bn], X[:, kc, b0:b0 + bn], SRC[:, kc, b0:b0 + bn], ALU.add,
                        rk(srcname, kc, b0, bn) + rk("X", kc, b0, bn), rk("X", kc, b0, bn))

        def ffn(l, which):
            prenorm_to_H(l, "ffn%d_norm_pre" % which)
            for fbi, (c0, nch) in enumerate(FBLOCKS):
                for j in range(nch):
                    wi_g = WAS.next()
                    pg = [psum() for _ in blks]
                    for kc in range(KC):
                        for bi, (b0, bn) in enumerate(blks):
                            MM(PS[pg[bi]][:, 0:bn], WA[wi_g][:, kc, :], H[:, kc, b0:b0 + bn],
                               kc == 0, kc == KC - 1, [("WA", wi_g)] + rk("H", kc, b0, bn), [("PS", pg[bi])])
                    sgs = []
                    for bi, (b0, bn) in enumerate(blks):
                        si = st["sg"]; st["sg"] = (st["sg"] + 1) % len(SG)
                        sgs.append(si)
                        ACT(SG[si][:, 0:bn], PS[pg[bi]][:, 0:bn], ACTF.Silu, [("PS", pg[bi])], [("TT", si)])
                    wi_u = WAS.next()
                    pu = [psum() for _ in blks]
                    for kc in range(KC):
                        for bi, (b0, bn) in enumerate(blks):
                            MM(PS[pu[bi]][:, 0:bn], WA[wi_u][:, kc, :], H[:, kc, b0:b0 + bn],
                               kc == 0, kc == KC - 1, [("WA", wi_u)] + rk("H", kc, b0, bn), [("PS", pu[bi])])
                    for bi, (b0, bn) in enumerate(blks):
                        TTo("dve", ACTB[:, j, b0:b0 + bn], PS[pu[bi]][:, 0:bn], SG[sgs[bi]][:, 0:bn], ALU.mult,
                            [("PS", pu[bi]), ("TT", sgs[bi])], [("ACTB", (j, b0))])
                for oc in range(8):
                    wi = WBS.next()
                    pd = [psum() for _ in blks]
                    for j in range(nch):
                        for bi, (b0, bn) in enumerate(blks):
                            MM(PS[pd[bi]][:, 0:bn], WB[wi][:, j, :], ACTB[:, j, b0:b0 + bn],
                               j == 0, j == nch - 1, [("WB", wi), ("ACTB", (j, b0))], [("PS", pd[bi])])
                    for bi, (b0, bn) in enumerate(blks):
                        if fbi == 0:
                            CP("act", Y[:, oc, b0:b0 + bn], PS[pd[bi]][:, 0:bn], [("PS", pd[bi])], rk("Y", oc, b0, bn))
                        else:
                            TTo("dve", Y[:, oc, b0:b0 + bn], PS[pd[bi]][:, 0:bn], YF[:, oc, b0:b0 + bn], ALU.add,
                                [("PS", pd[bi])] + rk("Y", oc, b0, bn), rk("Y", oc, b0, bn))
            postnorm_residual(Y, YF, "Y", l, "ffn%d_norm_post" % which, 0.5)

        MIXR = Y
        allH = [("H", (kc, t)) for kc in range(KC) for t in range(NTM)]
        allACTB = [("ACTB", None)]

        def mixer(l):
            win = dr["w_in"][l].rearrange("(kc p) n -> p kc n", p=128)
            prenorm_to_H(l, "mix_norm_pre")
            ACT(LAYC[:, 0:4], pcol(l, "rg_lambda", 0, 4), ACTF.Exp, [("PP", None)], [("LAYC", 0)], scale=-1.0)
            ACT(LAYC[:, 0:4], LAYC[:, 0:4], ACTF.Ln, [("LAYC", 0), ("ONE1", None)], [("LAYC", 0)], bias=ONE1[:, 0:1])
            TS("dve", LAYC[:, 4:8], LAYC[:, 0:4], -16.0, None, ALU.mult, None, [("LAYC", 0)], [("LAYC", 1)])
            TS("dve", LAYC[:, 0:4], LAYC[:, 0:4], -8.0, None, ALU.mult, None, [("LAYC", 0), ("LAYC", 1)], [("LAYC", 0)])
            ACT(LAYC[:, 8:12], pcol(l, "dn_a_log", 0, 4), ACTF.Exp, [("PP", None)], [("LAYC", 2)])
            TS("dve", LAYC[:, 8:12], LAYC[:, 8:12], -1.0, None, ALU.mult, None, [("LAYC", 2)], [("LAYC", 2)])
            P.dma("pool", RGWt[:], dr["rgw_r"][l].rearrange("g p c m -> p g c m"), writes=[("RGW", None)])
            P.dma("pool", WSC[:], win[:, :, 3072:3080], writes=[("WSC", None)])
            if smp:
                for q in range(4):
                    P.dma("sp", CSL, dr["st_conv"][l][:, q * 512:(q + 1) * 512], writes=[("TT", 3)])
                    pi = psum()
                    for j in range(4):
                        TR(PS[pi][:, j * 48:(j + 1) * 48], CSL[:, j * 128:(j + 1) * 128], ident[0:48, 0:48],
                           [("TT", 3), ("C32", None)], [("PS", pi)])
                    CP("dve", CSS[:, q * 4:(q + 1) * 4, :], PS[pi][:, 0:192].rearrange("p (j c) -> p j c", j=4),
                       [("PS", pi)], [("CSS", q * 4 + j) for j in range(4)])
                P.dma("sp", RSL, dr["st_rg"][l][:, :], writes=[("TT", 3)])
                pi = psum()
                for j in range(4):
                    TR(PS[pi][:, j * 16:(j + 1) * 16], RSL[:, j * 128:(j + 1) * 128], ident[0:16, 0:16],
                       [("TT", 3), ("C32", None)], [("PS", pi)])
                CP("dve", HS0[:, :, :], PS[pi][:, 0:64].rearrange("p (j c) -> p j c", j=4), [("PS", pi)], [("HS0", None)])

            wa_state = {}

            def proj():
                wi = WAS.next()
                pb = [psum() for _ in blks]
                for kc in range(KC):
                    for bi, (b0, bn) in enumerate(blks):
                        MM(PS[pb[bi]][:, 0:bn], WA[wi][:, kc, :], H[:, kc, b0:b0 + bn],
                           kc == 0, kc == KC - 1, [("WA", wi)] + rk("H", kc, b0, bn), [("PS", pb[bi])])
                return pb

            def conv_chunk(l, ch, pb):
                CP("act", XP[:, 0:3], CONVT[:, l, ch, :], [("CONVT", (l, ch))], [("XP", 0)])
                for bi, (b0, bn) in enumerate(blks):
                    n = min(bn, npr - b0)
                    if n > 0:
                        CP("dve" if bi == 0 else "act", XP[:, 3 + b0:3 + b0 + n], PS[pb[bi]][:, 0:n], [("PS", pb[bi])], [("XP", 1 + bi)])
                if smp:
                    b0, bn = blks[-1]
                    off = npr - b0
                    CP("dve", XPS[:, :, 0:3], CSS[:, ch, :].rearrange("p (s t) -> p s t", t=3), [("CSS", ch)], [("XPS", 0)])
                    CP("act", XPS[:, :, 3:11], PS[pb[-1]][:, off:off + 128].rearrange("p (s t) -> p s t", t=8),
                       [("PS", pb[-1])], [("XPS", 1)])
                for tap in range(4):
                    TS("dve", DG[:, tap, :], ident, pcol(l, "conv_w", ch * 4 + tap), None, ALU.mult, None,
                       [("C32", None), ("PP", None)], [("DG", tap)])
                xpk = [("XP", i) for i in range(1 + len(blks))]
                pc = [psum() for _ in blks]
                for bi, (b0, bn) in enumerate(blks):
                    n = min(bn, npr - b0)
                    for tap in range(4):
                        MM(PS[pc[bi]][:, 0:n], DG[:, tap, :], XP[:, b0 + tap:b0 + tap + n], tap == 0, tap == 3,
                           xpk + [("DG", tap)], [("PS", pc[bi])])
                CP("dve", CONVT[:, l, ch, :], XPf[:, npr:npr + 3], xpk, [("CONVT", (l, ch))])
                if smp:
                    b0, bn = blks[-1]
                    off = npr - b0
                    for tap in range(4):
                        MM(PS[pc[-1]][:, off:off + 128].rearrange("p (s t) -> p s t", t=8), DG[:, tap, :], XPS[:, :, tap:tap + 8],
                           False, tap == 3, [("XPS", 0), ("XPS", 1), ("DG", tap)], [("PS", pc[-1])])
                    CP("dve", CSS[:, ch, :].rearrange("p (s t) -> p s t", t=3), XPSf[:, :, 8:11], [("XPS", 0), ("XPS", 1)], [("CSS", ch)])
                return pc

            T0 = TT[:, 0, :]; T1 = TT[:, 1, :]; T2 = TT[:, 2, :]
            XRf = XR[:].bitcast(F32)
            k0, k1, k2, k3 = ("TT", 0), ("TT", 1), ("TT", 2), ("XR", None)

            for cpair in range(2):
                for cc in range(2):
                    ch = cpair * 2 + cc
                    pb = proj()
                    pc = conv_chunk(l, ch, pb)
                    for bi, (b0, bn) in enumerate(blks):
                        ACT(XR[:, b0:b0 + bn], PS[pc[bi]][:, 0:bn], ACTF.Identity, [("PS", pc[bi]), ("PP", None), k3], [k3],
                            bias=pcol(l, "conv_b_rg", ch))
                    pa = [psum() for _ in blks]
                    for bi, (b0, bn) in enumerate(blks):
                        MM(PS[pa[bi]][:, 0:bn], RGWt[:, 0, ch, :], XR[:, b0:b0 + bn], True, True, [("RGW", None), k3], [("PS", pa[bi])])
                    for bi, (b0, bn) in enumerate(blks):
                        ACT(T0[:, b0:b0 + bn], PS[pa[bi]][:, 0:bn], ACTF.Sigmoid, [("PS", pa[bi]), ("PP", None)], [k0],
                            bias=pcol(l, "rg_b_a", ch))
                    px = [psum() for _ in blks]
                    for bi, (b0, bn) in enumerate(blks):
                        MM(PS[px[bi]][:, 0:bn], RGWt[:, 1, ch, :], XR[:, b0:b0 + bn], True, True, [("RGW", None), k3], [("PS", px[bi])])
                    for bi, (b0, bn) in enumerate(blks):
                        ACT(T1[:, b0:b0 + bn], PS[px[bi]][:, 0:bn], ACTF.Sigmoid, [("PS", px[bi]), ("PP", None)], [k1],
                            bias=pcol(l, "rg_b_x", ch))
                    ACT(T2[:, 0:T], T0[:, 0:T], ACTF.Exp, [k0, ("LAYC", 1)], [k2], scale=LAYC[:, 4 + ch:5 + ch])
                    TS("dve", T2[:, 0:T], T2[:, 0:T], -1.0, 1.0, ALU.mult, ALU.add, [k2], [k2])
                    TS("dve", T2[:, 0:T], T2[:, 0:T], 1e-30, None, ALU.max, None, [k2], [k2])
                    ACT(T2[:, 0:T], T2[:, 0:T], ACTF.Ln, [k2], [k2])
                    ACT(T2[:, 0:T], T2[:, 0:T], ACTF.Exp, [k2], [k2], scale=0.5)
                    ACT(T0[:, 0:T], T0[:, 0:T], ACTF.Exp, [k0, ("LAYC", 0)], [k0], scale=LAYC[:, ch:ch + 1])
                    TTo("dve", T1[:, 0:T], T1[:, 0:T], XRf[:, 0:T], ALU.mult, [k1, k3], [k1])
                    TTo("dve", T1[:, 0:T], T1[:, 0:T], T2[:, 0:T], ALU.mult, [k1, k2], [k1])
                    for bi, (b0, bn) in enumerate(blks):
                        n = min(bn, npr - b0)
                        if n <= 0:
                            continue
                        init = HRG[:, l, ch:ch + 1] if bi == 0 else T2[:, b0 - 1:b0]
                        P.op("dve", lambda e, b0=b0, n=n, init=init: e.tensor_tensor_scan(
                            out=T2[:, b0:b0 + n], data0=T0[:, b0:b0 + n], data1=T1[:, b0:b0 + n],
                            initial=init, op0=ALU.mult, op1=ALU.add),
                            [k0, k1, k2, ("HRG", (l, ch))], [k2])
                    CP("dve", HRG[:, l, ch:ch + 1], T2[:, npr - 1:npr], [k2], [("HRG", (l, ch))])
                    if smp:
                        a_first = T0[:, npr:npr + 128:8]
                        b_first = T1[:, npr:npr + 128:8]
                        TTo("dve", SM2[:, 0, 0:16], a_first, HS0[:, ch, :], ALU.mult, [k0, ("HS0", None)], [("SMr", None)])
                        TTo("dve", b_first, b_first, SM2[:, 0, 0:16], ALU.add, [k1, ("SMr", None)], [k1])
                        TS("dve", a_first, a_first, 0.0, None, ALU.mult, None, [k0, ("SMr", None)], [k0])
                        P.op("dve", lambda e: e.tensor_tensor_scan(out=T2[:, npr:npr + 128], data0=T0[:, npr:npr + 128],
                                                                  data1=T1[:, npr:npr + 128], initial=0.0, op0=ALU.mult, op1=ALU.add),
                             [k0, k1, k2], [k2])
                        CP("dve", HS0[:, ch, :], T2[:, npr + 7:npr + 128:8], [k2, ("SMr", None)], [("HS0", None)])
                    pgt = proj()
                    for bi, (b0, bn) in enumerate(blks):
                        CP("act", T0[:, b0:b0 + bn], PS[pgt[bi]][:, 0:bn], [("PS", pgt[bi]), k0], [k0])
                    ACT(T1[:, 0:T], T0[:, 0:T], ACTF.Square, [k0, k1], [k1])
                    TS("dve", T1[:, 0:T], T1[:, 0:T], 0.044715, 1.0, ALU.mult, ALU.add, [k1], [k1])
                    TTo("dve", T1[:, 0:T], T1[:, 0:T], T0[:, 0:T], ALU.mult, [k0, k1], [k1])
                    ACT(T1[:, 0:T], T1[:, 0:T], ACTF.Sigmoid, [k1], [k1], scale=2.0 * 0.7978845608028654)
                    TTo("dve", T0[:, 0:T], T0[:, 0:T], T1[:, 0:T], ALU.mult, [k0, k1], [k0])
                    for bi, (b0, bn) in enumerate(blks):
                        TTo("dve", MIXR[:, ch, b0:b0 + bn], T2[:, b0:b0 + bn], T0[:, b0:b0 + bn], ALU.mult,
                            [k0, k2], rk("Y", ch, b0, bn))

            TB = [(T0, k0), (T1, k1)]

            def qk_front(kind, hh, slot):
                ch = 4 + kind * 4 + hh
                Tb, kb = TB[slot]
                pb = proj()
                pc = conv_chunk(l, ch, pb)
                for bi, (b0, bn) in enumerate(blks):
                    ACT(Tb[:, b0:b0 + bn], PS[pc[bi]][:, 0:bn], ACTF.Silu, [("PS", pc[bi]), kb], [kb])
                for bi, (b0, bn) in enumerate(blks):
                    sq = slot * 2 + bi
                    TTo("dve", SQ[:, sq, 0:bn], Tb[:, b0:b0 + bn], Tb[:, b0:b0 + bn], ALU.mult, [kb], [("SQ", sq)])

            def qk_tail(kind, hh, slot):
                Tb, kb = TB[slot]
                for bi, (b0, bn) in enumerate(blks):
                    sq = slot * 2 + bi
                    pi = psum()
                    MM(PS[pi][:, 0:bn], ones_r, SQ[:, sq, 0:bn], True, True, [("SQ", sq), ("ONESR", None)], [("PS", pi)])
                    if kind == 0:
                        ACT(RSTD[:, 0:bn], PS[pi][:, 0:bn], ACTF.Ln, [("PS", pi), ("EPST", None)], [("RSTD", None)],
                            scale=128.0, bias=EPSC["q"])
                    else:
                        ACT(RSTD[:, 0:bn], PS[pi][:, 0:bn], ACTF.Ln, [("PS", pi), ("EPST", None)], [("RSTD", None)],
                            scale=1.0, bias=EPSC[(1, 1.0)])
                    ACT(RSTD[:, 0:bn], RSTD[:, 0:bn], ACTF.Exp, [("RSTD", None)], [("RSTD", None)], scale=-0.5)
                    TTo("dve", QKV[:, kind * 4 + hh, b0:b0 + bn], Tb[:, b0:b0 + bn], RSTD[:, 0:bn], ALU.mult,
                        [kb, ("RSTD", None)], [("QKV", None)])

            for kind in range(2):
                for hp in range(2):
                    qk_front(kind, 2 * hp, 0)
                    qk_front(kind, 2 * hp + 1, 1)
                    qk_tail(kind, 2 * hp, 0)
                    qk_tail(kind, 2 * hp + 1, 1)
            for hh in range(4):
                pb = proj()
                pc = conv_chunk(l, 12 + hh, pb)
                for bi, (b0, bn) in enumerate(blks):
                    ACT(QKV[:, 8 + hh, b0:b0 + bn], PS[pc[bi]][:, 0:bn], ACTF.Silu, [("PS", pc[bi])], [("QKV", None)])

            for cpair in range(2):
                for cc in range(2):
                    hh = cpair * 2 + cc
                    pb = proj()
                    for bi, (b0, bn) in enumerate(blks):
                        ACT(ZG[:, hh, b0:b0 + bn], PS[pb[bi]][:, 0:bn], ACTF.Silu, [("PS", pb[bi])], [("ZG", (hh, bi))])

            pi = psum()
            for t in range(ntile):
                for kc in range(KC):
                    MM(PS[pi][:, t * 8:(t + 1) * 8], H[:, kc, t * 128:(t + 1) * 128], WSC[:, kc, :], (t == 0 and kc == 0), kc == KC - 1,
                       [("WSC", None), ("H", (kc, t))], [("PS", pi)])
            CP("dve", SCT[:, 0:ntile, :], PS[pi][:, 0:ntile * 8].rearrange("p (t c) -> p t c", c=8), [("PS", pi)], [("SCT", None)])

            for t in range(ntile):
                nxt = (t + 1, smp and t + 1 == ntile - 1) if t + 1 < ntile else None
                delta_tile(l, t, smp and t == ntile - 1, nxt=nxt, first=(t == 0))

            if getattr(cfg, "debug", False) and l == 0:
                out_ops.append(P.dma("sp", dr["dbg"][gi], YF, reads=[("Y", None)], writes=[("DBG", gi)]))
            for ocp in range(4):
                for o2 in range(2):
                    oc = ocp * 2 + o2
                    wi = WAS.next()
                    pd = [psum() for _ in blks]
                    for kc in range(KC):
                        for bi, (b0, bn) in enumerate(blks):
                            MM(PS[pd[bi]][:, 0:bn], WA[wi][:, kc, :], MIXR[:, kc, b0:b0 + bn],
                               kc == 0, kc == KC - 1, [("WA", wi)] + rk("Y", kc, b0, bn), [("PS", pd[bi])])
                    for bi, (b0, bn) in enumerate(blks):
                        CP("act", H[:, oc, b0:b0 + bn], PS[pd[bi]][:, 0:bn], [("PS", pd[bi])], rk("H", oc, b0, bn))
            postnorm_residual(H, HF, "H", l, "mix_norm_post", 1.0)

            if smp:
                for q in range(4):
                    pi = psum()
                    for j in range(4):
                        TR(PS[pi][0:48, j * 128:(j + 1) * 128], CSS[:, q * 4 + j, :], ident, [("CSS", q * 4 + j), ("C32", None)], [("PS", pi)])
                    CP("dve", CSL, PS[pi][0:48, :], [("PS", pi)], [("TT", 3)])
                    out_ops.append(P.dma("sp", dr["ncs"][l][:, q * 512:(q + 1) * 512], CSL, reads=[("TT", 3)], writes=[("CSLo", q)]))
                pi = psum()
                for j in range(4):
                    TR(PS[pi][0:16, j * 128:(j + 1) * 128], HS0[:, j, :], ident, [("HS0", None), ("C32", None)], [("PS", pi)])
                CP("dve", RSL, PS[pi][0:16, :], [("PS", pi)], [("TT", 3)])
                out_ops.append(P.dma("sp", dr["nrs"][l][:, :], RSL, reads=[("TT", 3)], writes=[("RSLo", 0)]))
            if last_group:
                for q in range(4):
                    pi = psum()
                    for j in range(4):
                        TR(PS[pi][0:3, j * 128:(j + 1) * 128], CONVT[:, l, q * 4 + j, :], ident, [("CONVT", (l, q * 4 + j)), ("C32", None)], [("PS", pi)])
                    CP("dve", CSL[0:3, :], PS[pi][0:3, :], [("PS", pi)], [("TT", 3)])
                    out_ops.append(P.dma("sp", dr["ncp"][l][:, q * 512:(q + 1) * 512], CSL[0:3, :], reads=[("TT", 3)], writes=[("CSLo", q)]))
                pi = psum()
                TR(PS[pi][0:4, 0:128], HRG[:, l, :], ident, [("HRG", None), ("C32", None)], [("PS", pi)])
                CP("dve", RSL[0:4, 0:128], PS[pi][0:4, 0:128], [("PS", pi)], [("TT", 3)])
                out_ops.append(P.dma("sp", dr["nrp"][l].rearrange("(c p) -> c p", p=128), RSL[0:4, 0:128], reads=[("TT", 3)], writes=[("RSLo", 0)]))
                out_ops.append(P.dma("sp", dr["ndp"][l].rearrange("h d e -> d h e"), SST[:, l, :, :], reads=[("SST", None)], writes=[("SSTo", l)]))

        def delta_prelude(l, t, is_s):
            sfx = "_s" if is_s else "_p"
            par = t % 2
            SM = SM2[:, par, :]
            EGR = EGR2[:, par]
            BETA = SM[:, 16:20]; GT = SM[:, 20:24]; GC = SM[:, 24:32]; EG = SM[:, 32:36]; BEG = SM[:, 36:40]; EKD = SM[:, 40:44]
            ACT(BETA, SCT[:, t, 0:4], ACTF.Sigmoid, [("SCT", None)], [("SM", (par, 1))])
            TTo("dve", GT, SCT[:, t, 4:8], pcol(l, "dn_dt_bias", 0, 4), ALU.add, [("SCT", None), ("PP", None)], [("SM", (par, 2))])
            ACT(GT, GT, ACTF.Exp, [("SM", (par, 2))], [("SM", (par, 2))])
            ACT(GT, GT, ACTF.Ln, [("SM", (par, 2)), ("ONE1", None)], [("SM", (par, 2))], bias=ONE1[:, 0:1])
            TTo("dve", GT, GT, LAYC[:, 8:12], ALU.mult, [("SM", (par, 2)), ("LAYC", 2)], [("SM", (par, 2))])
            pgc = psum()
            MM(PS[pgc][:, 0:4], c32("tri" + sfx), GT, True, True, [("SM", (par, 2)), ("C32", None)], [("PS", pgc)])
            MM(PS[pgc][:, 4:8], c32("up" + sfx), GT, False, True, [("SM", (par, 2)), ("C32", None)], [("PS", pgc)])
            bc4 = lambda ap: ap.unsqueeze(2).broadcast_to([128, NH, 128])
            bm4 = lambda ap: ap.unsqueeze(1).broadcast_to([128, NH, 128])
            TTo("dve", TRG[:, :, :], bm4(c32("tri" + sfx)), bc4(GT), ALU.mult, [("SM", (par, 2)), ("C32", None), ("TRG", 0), ("TRG", 1)], [("TRG", 0), ("TRG", 1)])
            pgr = psum()
            resv.add(pgr)
            st["pgr"] = pgr
            st["pgr_users"] = 2
            MM(PS[pgr][:, :], ones_f, TRG[:].rearrange("p h c -> p (h c)"), True, True, [("TRG", 0), ("TRG", 1), ("C32", None)], [("PS", pgr)])
            CP("dve", GC, PS[pgc][:, 0:8], [("PS", pgc), ("PS", pgr)], [("SM", (par, 3))])
            ACT(EG, GC[:, 0:4], ACTF.Exp, [("SM", (par, 3))], [("SM", (par, 4))])
            ACT(EKD, GC[:, 4:8], ACTF.Exp, [("SM", (par, 3))], [("SM", (par, 5))])
            TTo("dve", BEG, BETA, EG, ALU.mult, [("SM", (par, 1)), ("SM", (par, 4))], [("SM", (par, 6))])
            PGR4 = PS[pgr][:].rearrange("p (h c) -> p h c", h=NH)
            gk = [("GRW", 0), ("GRW", 1)]; ek = [("EGR", (par, 0)), ("EGR", (par, 1))]
            TTo("dve", GRW[:, :, :], PGR4, GC[:, 0:4].unsqueeze(2).broadcast_to([128, NH, 128]), ALU.subtract, [("PS", pgr), ("SM", (par, 3))] + gk, gk)
            GRWf = GRW[:].rearrange("p h c -> p (h c)")
            STT(GRWf, GRWf, -1.0, GRWf, ALU.mult, ALU.max, gk, gk)
            ACT(GRW[:], GRW[:], ACTF.Exp, gk, gk, scale=-1.0)
            ACT(EGR[:], PGR4, ACTF.Exp, [("PS", pgr)] + ek, ek)
            resv.discard(pgr)

        def delta_pair(l, t, is_s, hg):
            c0 = t * 128
            par = t % 2
            SM = SM2[:, par, :]
            EGR = EGR2[:, par]
            sfx = "_s" if is_s else "_p"
            nlv = NLV_S if is_s else NLV_P
            HS = slice(2 * hg, 2 * hg + 2)
            heads = (2 * hg, 2 * hg + 1)
            bk = lambda n: [(n, hg)]
            KTOK, VTOK, AM, ATT, DTt, DINV, N1, BV, BKG, WT, VN, QG, KD, SBF = [DB[n][:, HS, :] for n in dt_names_bf]
            f2 = lambda ap: ap.rearrange("p h c -> p (h c)")
            qkv_r = [("QKV", None)]
            bc = lambda ap: ap.unsqueeze(2).broadcast_to([128, 2, 128])
            bm = lambda ap: ap.unsqueeze(1).broadcast_to([128, 2, 128])
            BETA = SM[:, 16 + 2 * hg:18 + 2 * hg]; GT = SM[:, 20 + 2 * hg:22 + 2 * hg]; GC0 = SM[:, 24 + 2 * hg:26 + 2 * hg]
            BEG = SM[:, 36 + 2 * hg:38 + 2 * hg]; EKD = SM[:, 40 + 2 * hg:42 + 2 * hg]
            GRWp = GRW[:, HS, :]; EGRp = EGR[:, HS, :]; TRGp = TRG[:, HS, :]; OTp = OT[:, HS, :]
            if getattr(cfg, 'delta_stop', 99) == 1:
                resv.clear()
                return
            pgr = st["pgr"]
            ptk = psum()
            for i, h in enumerate(heads):
                TR(psb(ptk)[:, i * 128:(i + 1) * 128], QKV[:, 4 + h, c0:c0 + 128], identb, qkv_r + [("C16", None)], [("PS", ptk)])
            for i, h in enumerate(heads):
                TR(psb(ptk)[:, 256 + i * 128:256 + (i + 1) * 128], QKV[:, 8 + h, c0:c0 + 128], identb, qkv_r + [("C16", None)], [("PS", ptk)])
            pkk = psum()
            for i, h in enumerate(heads):
                MM(PS[pkk][:, i * 128:(i + 1) * 128], QKV[:, 4 + h, c0:c0 + 128], QKV[:, 4 + h, c0:c0 + 128], i == 0, True, qkv_r, [("PS", pkk)])
            for i, h in enumerate(heads):
                MM(PS[pkk][:, 256 + i * 128:256 + (i + 1) * 128], QKV[:, 4 + h, c0:c0 + 128], QKV[:, h, c0:c0 + 128], False, True, qkv_r,
                   [("PS", pkk), ("PGD", hg)])
            yield
            if getattr(cfg, 'delta_stop', 99) == 2:
                resv.clear()
                return
            PGR3 = PS[pgr][:, hg * 256:(hg + 1) * 256].rearrange("p (h c) -> p h c", h=2)

            if getattr(cfg, 'delta_stop', 99) == 25:
                resv.clear()
                return
            CP("act", f2(KTOK), psb(ptk)[:, 0:256], [("PS", ptk)] + bk("KTOK"), bk("KTOK"))
            CP("act", f2(VTOK), psb(ptk)[:, 256:512], [("PS", ptk)] + bk("VTOK"), bk("VTOK"))
            yield
            if getattr(cfg, 'delta_stop', 99) == 3:
                resv.clear()
                return
            TTo("dve", BV, VTOK, bc(BETA), ALU.mult, bk("VTOK") + [("SM", (par, 1))] + bk("BV"), bk("BV"))
            TTo("dve", BKG, KTOK, bc(BEG), ALU.mult, bk("KTOK") + [("SM", (par, 6))] + bk("BKG"), bk("BKG"))
            TTo("dve", KD, KTOK, bc(EKD), ALU.mult, bk("KTOK") + [("SM", (par, 5))] + bk("KD"), bk("KD"))
            TTo("dve", QG, QKV[:, HS, c0:c0 + 128], EGRp, ALU.mult, qkv_r + [("EGR", (par, hg))] + bk("QG"), bk("QG"))
            CP("act", DTt, bm(identb), [("C16", None)] + bk("DT"), bk("DT"))
            CP("act", DINV, bm(identb), [("C16", None)] + bk("DINV"), bk("DINV"))
            yield
            if getattr(cfg, 'delta_stop', 99) == 4:
                resv.clear()
                return
            TTo("dve", TRGp, GRWp, bm(c16("mstrict" + sfx)), ALU.mult, bk("GRW") + [("C16", None)] + bk("TRG"), bk("TRG"))
            TTo("dve", TRGp, TRGp, bc(BETA), ALU.mult, bk("TRG") + [("SM", (par, 1))], bk("TRG"))
            TTo("dve", OTp, GRWp, bm(c16("minclt" + sfx)), ALU.mult, bk("GRW") + [("C16", None)] + bk("OT"), bk("OT"))
            TTo("dve", f2(AM), PS[pkk][:, 0:256], f2(TRGp), ALU.mult, [("PS", pkk)] + bk("TRG") + bk("AM"), bk("AM"))
            TTo("dve", f2(ATT), PS[pkk][:, 256:512], f2(OTp), ALU.mult, [("PS", pkk)] + bk("OT") + bk("ATT"), bk("ATT"))
            yield
            if getattr(cfg, 'delta_stop', 99) == 5:
                resv.clear()
                return
            for lv in range(nlv):
                p1 = psum()
                for i in range(2):
                    MM(PS[p1][:, i * 128:(i + 1) * 128], AM[:, i, :], DTt[:, i, :], i == 0, True, bk("AM") + bk("DT"), [("PS", p1)])
                yield
                TTo("dve", N1, PS[p1][:, 0:256].rearrange("p (h c) -> p h c", h=2), bm(c16("lv_p%d" % lv)), ALU.mult,
                    [("PS", p1), ("C16", None)] + bk("N1"), bk("N1"))
                yield
                p2 = psum()
                for i in range(2):
                    MM(PS[p2][:, i * 128:(i + 1) * 128], DINV[:, i, :], N1[:, i, :], i == 0, True, bk("DINV") + bk("N1"), [("PS", p2)])
                yield
                TTo("dve", f2(DTt), f2(DTt), PS[p2][:, 0:256], ALU.subtract, [("PS", p2)] + bk("DT"), bk("DT"))
                yield
                if lv < nlv - 1:
                    p3 = psum()
                    for i in range(2):
                        TR(psb(p3)[:, i * 128:(i + 1) * 128], DTt[:, i, :], identb, bk("DT") + [("C16", None)], [("PS", p3)])
                    yield
                    CP("act", f2(DINV), psb(p3)[:, 0:256], [("PS", p3)] + bk("DINV"), bk("DINV"))
                    yield
            if getattr(cfg, 'delta_stop', 99) == 6:
                resv.clear()
                return
            pu = psum()
            resv.add(pu)
            if not is_s:
                for i in range(2):
                    MM(PS[pu][:, i * 128:(i + 1) * 128], DTt[:, i, :], BV[:, i, :], i == 0, False, bk("DT") + bk("BV"), [("PS", pu)])
            pw = psum()
            for i in range(2):
                MM(PS[pw][:, i * 128:(i + 1) * 128], BKG[:, i, :], DTt[:, i, :], i == 0, True, bk("DT") + bk("BKG"), [("PS", pw)])
            yield
            ACT(f2(WT), PS[pw][:, 0:256], ACTF.Copy, [("PS", pw)] + bk("AM"), bk("AM"), scale=-1.0)
            po = psum()
            resv.add(po)
            if getattr(cfg, 'delta_stop', 99) == 7:
                resv.clear()
                return
            if not is_s:
                CP("act", SBF, SST[:, l, HS, :], [("SST", hg)] + bk("DINV"), bk("DINV"))
                yield
                for i in range(2):
                    MM(PS[pu][:, i * 128:(i + 1) * 128], WT[:, i, :], SBF[:, i, :], False, True, bk("AM") + bk("DINV"), [("PS", pu)])
                yield
                CP("act", f2(VN), PS[pu][:, 0:256], [("PS", pu)] + bk("N1"), bk("N1"))
                resv.discard(pu)
                yield
                for i in range(2):
                    MM(PS[po][:, i * 128:(i + 1) * 128], SBF[:, i, :], QG[:, i, :], i == 0, False, bk("DINV") + bk("QG"), [("PS", po)])
                for i in range(2):
                    MM(PS[po][:, i * 128:(i + 1) * 128], VN[:, i, :], ATT[:, i, :], False, True, bk("N1") + bk("ATT"), [("PS", po)])
                psu = psum()
                for i in range(2):
                    MM(PS[psu][:, i * 128:(i + 1) * 128], KD[:, i, :], VN[:, i, :], i == 0, True, bk("KD") + bk("N1"), [("PS", psu)])
                yield
                for i, h in enumerate(heads):
                    STT(SST[:, l, h, :], SST[:, l, h, :], EGR[:, h, 127:128], PS[psu][:, i * 128:(i + 1) * 128], ALU.mult, ALU.add,
                        [("PS", psu), ("SST", hg)] + [("EGR", (par, hg))], [("SST", hg)])
            else:
                resv.discard(pu)
                HSQ = NS // 2
                SS0 = TT[:, 0:2, :].rearrange("p a t -> p (a t)")[:, 0:HSQ * 128].rearrange("p (s e) -> p s e", s=HSQ)
                SS0B = TT[:, 2, :].bitcast(BF16)[:, 0:HSQ * 128].rearrange("p (s e) -> p s e", s=HSQ)
                WTX = TT[:, 3, 0:512].bitcast(BF16).rearrange("p (s c) -> p s c", s=HSQ)
                kS0 = [("TT", 0), ("TT", 1)]; kS0B = [("TT", 2)]; kWX = [("TT", 3)]
                segrow = c16("segrow", NS * 128).rearrange("p (s c) -> p s c", s=NS)
                segcol = c16("segcol", NS)
                first_po = True
                for i, h in enumerate(heads):
                    for hf in range(2):
                        s0 = hf * HSQ
                        P.dma("sp", SS0, dr["st_dn"][l][s0:s0 + HSQ, h, :, :].rearrange("s d e -> d s e"), reads=[], writes=kS0)
                        CP("act", SS0B, SS0, kS0 + kS0B, kS0B)
                        TTo("dve", WTX, WT[:, i, :].unsqueeze(1).broadcast_to([128, HSQ, 128]), segrow[:, s0:s0 + HSQ, :], ALU.mult,
                            bk("AM") + [("C16", None)] + kWX, kWX)
                        pu2 = psum()
                        MM(PS[pu2][:, 0:128], DTt[:, i, :], BV[:, i, :], True, False, bk("DT") + bk("BV"), [("PS", pu2)])
                        for s_ in range(HSQ):
                            MM(PS[pu2][:, 0:128], WTX[:, s_, :], SS0B[:, s_, :], False, True, kS0B + kWX, [("PS", pu2)])
                        CP("act", VN[:, i, :], PS[pu2][:, 0:128], [("PS", pu2)] + bk("N1"), bk("N1"))
                        cb = i * 128 + hf * 64
                        MM(PS[po][:, cb:cb + 64], VN[:, i, :], ATT[:, i, hf * 64:hf * 64 + 64], first_po, False, bk("N1") + bk("ATT"), [("PS", po)])
                        first_po = False
                        for s_ in range(HSQ):
                            cs = i * 128 + (s0 + s_) * 8
                            MM(PS[po][:, cs:cs + 8], SS0B[:, s_, :], QG[:, i, (s0 + s_) * 8:(s0 + s_) * 8 + 8], False, True,
                               kS0B + bk("QG"), [("PS", po)])
                        TTo("dve", WTX, KD[:, i, :].unsqueeze(1).broadcast_to([128, HSQ, 128]),
                            segcol[:, s0:s0 + HSQ].unsqueeze(2).broadcast_to([128, HSQ, 128]), ALU.mult,
                            bk("KD") + [("C16", None)] + kWX, kWX)
                        for q in range(2):
                            psu = psum()
                            for j in range(4):
                                s_ = q * 4 + j
                                MM(PS[psu][:, j * 128:(j + 1) * 128], WTX[:, s_, :], VN[:, i, :], j == 0, True, kWX + bk("N1"), [("PS", psu)])
                            for j in range(4):
                                s_ = q * 4 + j
                                sg_ = s0 + s_
                                STT(SS0[:, s_, :], SS0[:, s_, :], EGR[:, h, sg_ * 8 + 7:sg_ * 8 + 8], PS[psu][:, j * 128:(j + 1) * 128],
                                    ALU.mult, ALU.add, [("PS", psu)] + kS0 + [("EGR", (par, hg))], kS0)
                        out_ops.append(P.dma("sp", dr["nds"][l][s0:s0 + HSQ, h, :, :].rearrange("s d e -> d s e"), SS0, reads=kS0, writes=[("NDSo", h)]))
                        yield
            if getattr(cfg, 'delta_stop', 99) == 8:
                resv.clear()
                return
            resv.discard(po)
            CP("act", f2(OTp), PS[po][:, 0:256], [("PS", po)] + bk("OT"), bk("OT"))
            SQW = SQ[:, 2 * hg, 0:256]
            ACT(SQW, PS[po][:, 0:256], ACTF.Square, [("PS", po)], [("SQ", 2 * hg)])
            yield
            pss = psum()
            MM(PS[pss][:, 0:256], ones_r, SQW, True, True, [("SQ", 2 * hg), ("ONESR", None)], [("PS", pss)])
            yield
            RS = RSTD[:, hg * 256:(hg + 1) * 256]
            ACT(RS, PS[pss][:, 0:256], ACTF.Ln, [("PS", pss), ("EPST", None)], [("RSTD", hg)], scale=1.0 / 128.0, bias=EPSC[(128, 1.0)])
            ACT(RS, RS, ACTF.Exp, [("RSTD", hg)], [("RSTD", hg)], scale=-0.5)
            STT(f2(OTp), f2(OTp), pcol(l, "dn_norm_w"), RS, ALU.mult, ALU.mult, bk("OT") + [("RSTD", hg), ("PP", None)], bk("OT"))
            TTo("dve", MIXR[:, 4 + 2 * hg:6 + 2 * hg, c0:c0 + 128], OTp, ZG[:, HS, c0:c0 + 128], ALU.mult,
                bk("OT") + [("ZG", None)], [("Y", (4 + h, t)) for h in heads])

        def delta_tile(l, t, is_s, nxt=None, first=False):
            if first:
                delta_prelude(l, t, is_s)
            gens = [delta_pair(l, t, is_s, 0), delta_pair(l, t, is_s, 1)]
            alive = [True, True]
            steps = [0, 0]
            pre_done = nxt is None
            while any(alive):
                for gi_, g in enumerate(gens):
                    if alive[gi_]:
                        try:
                            next(g)
                            steps[gi_] += 1
                        except StopIteration:
                            alive[gi_] = False
                if not pre_done and min(steps) >= 4:
                    delta_prelude(l, nxt[0], nxt[1])
                    pre_done = True
            if not pre_done:
                delta_prelude(l, nxt[0], nxt[1])

        for l in range(DEPTH):
            ffn(l, 1)
            mixer(l)
            ffn(l, 2)

        for (b0, bn) in blks:
            sumsq_rstd(X, "X", b0, bn, D)
            norm_scale(Y, "Y", X, "X", 0, "final_norm", b0, bn)
        for t in range(ntile):
            si = st["stg"]; st["stg"] ^= 1
            stg = stg_view(si)
            for half in range(2):
                pi = psum()
                for j in range(4):
                    kc = half * 4 + j
                    TR(PS[pi][:, j * 128:(j + 1) * 128], YF[:, kc, t * 128:(t + 1) * 128], ident, [("Y", (kc, t)), ("C32", None)], [("PS", pi)])
                CP("act" if half == 0 else "dve", stg[:, half * 512:(half + 1) * 512], PS[pi][:], [("PS", pi)], [stg_keys(si)[half]])
            dst = dr["yp"][p0 + t * 128: p0 + (t + 1) * 128, :] if t * 128 < npr else dr["ys"][:, :]
            out_ops.append(P.dma("sp", dst, stg, reads=stg_keys(si), writes=[("STGo", si)]))

    P.emit(out_ops)
    es.close()
    return nc, P


def make_in_maps(inp):
    pp = _pack_params(inp)
    c32, c16 = _consts()
    rgw = _pack_rgw(inp)
    maps = []
    shared = {"pp": pp, "c32": c32, "c16": c16, "rgw_r": rgw}
    for nm in ("ffn1_w_up", "ffn2_w_up", "ffn1_w_down", "ffn2_w_down", "w_in", "w_out"):
        shared[nm] = np.ascontiguousarray(inp[nm])
    for core in range(NCORES):
        sl = slice(core * NS, (core + 1) * NS)
        m = dict(shared)
        m["xp"] = np.ascontiguousarray(inp["x_prompt"][core])
        m["xs"] = np.ascontiguousarray(inp["x_sample"][sl].reshape(NS * DS, D))
        m["st_conv"] = np.ascontiguousarray(inp["state_conv"][:, sl].reshape(DEPTH, NS * 3, CONVC))
        m["st_rg"] = np.ascontiguousarray(inp["state_rglru"][:, sl])
        m["st_dn"] = np.ascontiguousarray(inp["state_delta"][:, sl])
        maps.append(m)
    return maps


def gather(r):
    y_prompt = np.stack([r[c]["yp"] for c in range(NCORES)], axis=0)
    y_sample = np.concatenate([r[c]["ys"].reshape(NS, DS, D) for c in range(NCORES)], axis=0)
    ncp = np.stack([r[c]["ncp"] for c in range(NCORES)], axis=1)
    nrp = np.stack([r[c]["nrp"] for c in range(NCORES)], axis=1)
    ndp = np.stack([r[c]["ndp"] for c in range(NCORES)], axis=1)
    ncs = np.concatenate([r[c]["ncs"].reshape(DEPTH, NS, 3, CONVC) for c in range(NCORES)], axis=1)
    nrs = np.concatenate([r[c]["nrs"] for c in range(NCORES)], axis=1)
    nds = np.concatenate([r[c]["nds"] for c in range(NCORES)], axis=1)
    return (y_prompt, y_sample, ncp, nrp, ndp, ncs, nrs, nds)


def kernel(**inp):
    inp = {k: np.asarray(v) for k, v in inp.items()}
    cfg = Cfg()
    nc, P = build_program(cfg)
    maps = make_in_maps(inp)
    res = run_bass_kernel_spmd(nc, maps, core_ids=list(range(NCORES)))
    return gather(res.results)
```

```python
import numpy as np
from contextlib import ExitStack
import concourse.bass as bass
import concourse.mybir as mybir
from concourse.bass_utils import run_bass_kernel_spmd

F32 = mybir.dt.float32
F32R = mybir.dt.float32r
BF16 = mybir.dt.bfloat16
ACTF = mybir.ActivationFunctionType
ALU = mybir.AluOpType

D = 1024
KC = 8
DFF = 2816
NFF = 22
DEPTH = 2
SEQ = 2048
NS = 16
DS = 8
INC = 3080
CONVC = 2048
EPS = 1e-6
NCORES = 8


class Op:
    __slots__ = ("eng", "fn", "deps", "sig", "count", "sem", "is_dma", "idx")

    def __init__(self, eng, fn, is_dma=False):
        self.eng = eng
        self.fn = fn
        self.deps = []
        self.sig = False
        self.count = None
        self.sem = None
        self.is_dma = is_dma
        self.idx = None


class Prog:
    ENGS = ("pe", "act", "dve", "pool", "sp")
    NDMASEM = 8

    def __init__(self, nc, same_engine_sync=True):
        self.nc = nc
        self.ops = {e: [] for e in self.ENGS}
        self.last_w = {}
        self.readers = {}
        self.same_engine_sync = same_engine_sync
        self.dma_n = {"sp": 0, "pool": 0}
        self.dma_hist = {"sp": [], "pool": []}
        self.n_ops = 0

    def _collect(self, op, reads, writes):
        deps = []
        for (n, i) in reads:
            lw = self.last_w.get(n)
            if lw:
                if i is None:
                    deps.extend(lw.values())
                else:
                    if i in lw:
                        deps.append(lw[i])
                    if None in lw:
                        deps.append(lw[None])
        for (n, i) in writes:
            lw = self.last_w.get(n)
            rd = self.readers.get(n)
            if lw:
                if i is None:
                    deps.extend(lw.values())
                else:
                    if i in lw:
                        deps.append(lw[i])
                    if None in lw:
                        deps.append(lw[None])
            if rd:
                if i is None:
                    for v in rd.values():
                        deps.extend(v)
                else:
                    deps.extend(rd.get(i, ()))
                    deps.extend(rd.get(None, ()))
        for (n, i) in writes:
            lw = self.last_w.setdefault(n, {})
            rd = self.readers.setdefault(n, {})
            if i is None:
                lw.clear()
                rd.clear()
                lw[None] = op
            else:
                lw[i] = op
                rd.pop(i, None)
        for (n, i) in reads:
            self.readers.setdefault(n, {}).setdefault(i, []).append(op)
        seen = set()
        for d in deps:
            if d is op or id(d) in seen:
                continue
            seen.add(id(d))
            if d.eng == op.eng and not d.is_dma and not op.is_dma:
                if op.eng == "pe" or not self.same_engine_sync:
                    continue
            op.deps.append(d)
            d.sig = True

    def op(self, eng, fn, reads=(), writes=()):
        o = Op(eng, fn)
        self._collect(o, list(reads), list(writes))
        self.ops[eng].append(o)
        self.n_ops += 1
        return o

    def dma(self, q, out, in_, reads=(), writes=()):
        o = Op(q, lambda e: e.dma_start(out=out, in_=in_), is_dma=True)
        n = self.dma_n[q]
        self.dma_n[q] += 1
        o.idx = n
        self._collect(o, list(reads), list(writes))
        hist = self.dma_hist[q]
        if n >= self.NDMASEM:
            o.deps.append(hist[n - self.NDMASEM])
        hist.append(o)
        o.sig = True
        self.ops[q].append(o)
        self.n_ops += 1
        return o

    def emit(self, final_wait_ops):
        nc = self.nc
        with ExitStack() as es:
            esem = {e: es.enter_context(nc.semaphore("prog_" + e)) for e in self.ENGS}
            dsem = {q: [es.enter_context(nc.semaphore("dma_%s_%d" % (q, i))) for i in range(self.NDMASEM)]
                    for q in ("sp", "pool")}
            for e in self.ENGS:
                c = 0
                for o in self.ops[e]:
                    if o.is_dma:
                        slot = o.idx % self.NDMASEM
                        o.sem = dsem[e][slot]
                        o.count = 16 * (o.idx // self.NDMASEM + 1)
                    elif o.sig:
                        c += 1
                        o.sem = esem[e]
                        o.count = c
            block = es.enter_context(nc.Block())

            def run(ename, e, extra_final=None):
                known = {}
                for o in self.ops[ename]:
                    need = {}
                    for d in o.deps:
                        key = id(d.sem)
                        if known.get(key, 0) >= d.count:
                            continue
                        if key not in need or need[key][1] < d.count:
                            need[key] = (d.sem, d.count)
                    for key, (s, v) in need.items():
                        e.wait_ge(s, v)
                        known[key] = v
                    ins = o.fn(e)
                    if o.is_dma:
                        ins.then_inc(o.sem, 16)
                    elif o.sig:
                        ins.then_inc(o.sem, 1)
                if extra_final:
                    need = {}
                    for d in extra_final:
                        key = id(d.sem)
                        if key not in need or need[key][1] < d.count:
                            need[key] = (d.sem, d.count)
                    for key, (s, v) in need.items():
                        e.wait_ge(s, v)

            @block.tensor
            def _(e):
                run("pe", e)

            @block.scalar
            def _(e):
                run("act", e)

            @block.vector
            def _(e):
                run("dve", e)

            @block.gpsimd
            def _(e):
                run("pool", e)

            @block.sync
            def _(e):
                run("sp", e, extra_final=final_wait_ops)


RGW = 512
NH = 4
HD = 128
NLV_P = 7
NLV_S = 3

C32 = {"ident": 0, "ones": 128, "tri_p": 256, "up_p": 384, "tri_s": 512, "up_s": 640}
N32 = 768
C16 = {"identb": 0, "mstrict_p": 128, "minclt_p": 256, "mstrict_s": 384, "minclt_s": 512}
for _i in range(NLV_P):
    C16["lv_p%d" % _i] = 640 + 128 * _i
C16["segcol"] = 640 + 128 * NLV_P
C16["segrow"] = C16["segcol"] + 16
N16 = C16["segrow"] + 16 * 128


def _consts():
    i = np.arange(128)[:, None]
    j = np.arange(128)[None, :]
    c32 = np.zeros((128, N32), np.float32)
    c32[:, 0:128] = np.eye(128)
    c32[:, 128:256] = 1.0
    seg = 8
    same_s = (i // seg) == (j // seg)
    c32[:, 256:384] = (i <= j)
    c32[:, 384:512] = (i > j)
    c32[:, 512:640] = (i <= j) & same_s
    c32[:, 640:768] = (i > j) & same_s
    c16 = np.zeros((128, N16), np.float32)
    c16[:, 0:128] = np.eye(128)
    c16[:, 128:256] = (i > j)
    c16[:, 256:384] = (j >= i)
    c16[:, 384:512] = (i > j) & same_s
    c16[:, 512:640] = (j >= i) & same_s
    for lv in range(NLV_P):
        b = 1 << lv
        m = ((i // (2 * b)) == (j // (2 * b))) & (((j // b) % 2) == 1) & (((i // b) % 2) == 0)
        c16[:, C16["lv_p%d" % lv]:C16["lv_p%d" % lv] + 128] = m
    c16[:, C16["segcol"]:C16["segcol"] + 16] = (np.arange(128)[:, None] // seg) == np.arange(16)[None, :]
    sr = (np.arange(16)[:, None] == (np.arange(128)[None, :] // seg)).astype(np.float32).reshape(1, 16 * 128)
    c16[:, C16["segrow"]:] = np.repeat(sr, 128, axis=0)
    return c32, c16


PL = {}
_o = 0
for _nm, _n in (("ffn1_norm_pre", 8), ("ffn1_norm_post", 8), ("mix_norm_pre", 8), ("mix_norm_post", 8),
                ("ffn2_norm_pre", 8), ("ffn2_norm_post", 8), ("final_norm", 8),
                ("conv_w", 64), ("conv_b_rg", 4), ("rg_b_a", 4), ("rg_b_x", 4), ("rg_lambda", 4),
                ("dn_a_log", 4), ("dn_dt_bias", 4), ("dn_norm_w", 1)):
    PL[_nm] = _o
    _o += _n
PP_LAYER = _o


def _pack_params(inp):
    cols = []

    def vecn(v, n):
        return np.ascontiguousarray(np.asarray(v).reshape(n, 128).T)

    for l in range(DEPTH):
        for nm in ("ffn1_norm_pre", "ffn1_norm_post", "mix_norm_pre", "mix_norm_post",
                   "ffn2_norm_pre", "ffn2_norm_post"):
            cols.append(vecn(inp[nm][l], 8))
        cols.append(vecn(inp["final_norm"], 8))
        cw = np.asarray(inp["conv_w"][l])
        cols.append(np.ascontiguousarray(cw.reshape(4, 16, 128).transpose(2, 1, 0).reshape(128, 64)))
        for nm in ("conv_b_rg", "rg_b_a", "rg_b_x", "rg_lambda"):
            cols.append(vecn(inp[nm][l], 4))
        cols.append(np.repeat(np.asarray(inp["dn_a_log"][l]).reshape(1, 4), 128, axis=0))
        cols.append(np.repeat(np.asarray(inp["dn_dt_bias"][l]).reshape(1, 4), 128, axis=0))
        cols.append(np.asarray(inp["dn_norm_w"][l]).reshape(128, 1))
    return np.ascontiguousarray(np.concatenate(cols, axis=1).astype(np.float32))


def _pack_rgw(inp):
    out = np.zeros((DEPTH, 2, 128, 4, 128), np.float32)
    for l in range(DEPTH):
        for gi, nm in enumerate(("rg_w_a", "rg_w_x")):
            w = np.asarray(inp[nm][l])
            for c in range(4):
                for hh in range(2):
                    out[l, gi, hh * 64:(hh + 1) * 64, c, hh * 64:(hh + 1) * 64] = w[2 * c + hh]
    return out


class Cfg:
    def __init__(self, **kw):
        self.groups = [(0, 640, True), (640, 768, False), (1408, 640, False)]
        self.stages = "full"
        self.same_engine_sync = True
        self.__dict__.update(kw)


def blocks_of(T):
    if T == 768:
        return [(0, 384), (384, 384)]
    if T == 640:
        return [(0, 384), (384, 256)]
    raise ValueError(T)


def build_program(cfg):
    nc = bass.Bass("TRN2", target_bir_lowering=False)
    TM = 768
    NTM = TM // 128
    dr = {}

    def din(name, shape, dt=F32):
        dr[name] = nc.dram_tensor(name, shape, dt, kind="ExternalInput").ap()

    def dout(name, shape):
        dr[name] = nc.dram_tensor(name, shape, F32, kind="ExternalOutput").ap()

    din("xp", [SEQ, D]); din("xs", [NS * DS, D])
    din("pp", [128, DEPTH * PP_LAYER]); din("c32", [128, N32]); din("c16", [128, N16])
    din("rgw_r", [DEPTH, 2, 128, 4, 128], F32R)
    din("st_conv", [DEPTH, NS * 3, CONVC]); din("st_rg", [DEPTH, NS, RGW]); din("st_dn", [DEPTH, NS, NH, HD, HD])
    for nm in ("ffn1_w_up", "ffn2_w_up"):
        din(nm, [DEPTH, D, 2 * DFF], F32R)
    for nm in ("ffn1_w_down", "ffn2_w_down"):
        din(nm, [DEPTH, DFF, D], F32R)
    din("w_in", [DEPTH, D, INC], F32R); din("w_out", [DEPTH, D, D], F32R)
    dout("yp", [SEQ, D]); dout("ys", [NS * DS, D])
    dout("ncp", [DEPTH, 3, CONVC]); dout("nrp", [DEPTH, RGW]); dout("ndp", [DEPTH, NH, HD, HD])
    if getattr(cfg, "debug", False):
        dout("dbg", [3, 128, KC, TM])
    dout("ncs", [DEPTH, NS * 3, CONVC]); dout("nrs", [DEPTH, NS, RGW]); dout("nds", [DEPTH, NS, NH, HD, HD])

    P = Prog(nc, same_engine_sync=cfg.same_engine_sync)
    es = ExitStack()
    sb = lambda name, shape, dt: es.enter_context(nc.sbuf_tensor(name, shape, dt))
    X = sb("X", [128, KC, TM], F32)
    H = sb("H", [128, KC, TM], F32R)
    Y = sb("Y", [128, KC, TM], F32R)
    YF = Y[:].bitcast(F32)
    HF = H[:].bitcast(F32)
    NFB = 4
    ACTB = sb("ACTB", [128, NFB, TM], F32R)
    NWA = 4
    WA = [sb("WA%d" % i, [128, KC, 128], F32R) for i in range(NWA)]
    NWB = 4
    WB = [sb("WB%d" % i, [128, NFB, 128], F32R) for i in range(NWB)]
    PPt = sb("PPt", [128, DEPTH * PP_LAYER], F32)
    C32t = sb("C32t", [128, N32], F32)
    C16t = sb("C16t", [128, N16], BF16)
    ONESR = sb("ONESR", [128, 128], F32R)
    EPST = sb("EPST", [128, 8], F32)
    TT = sb("TT", [128, 4, TM], F32)
    SQ = sb("SQ", [128, 4, 384], F32R)
    RSTD = sb("RSTD", [128, 512], F32)
    SG = [TT[:, 0, 0:384], TT[:, 1, 0:384]]
    ZG = sb("ZG", [128, 4, TM], BF16)
    QKV = sb("QKV", [128, 12, TM], BF16)
    XR = sb("XR", [128, TM], F32R)
    XP = sb("XP", [128, 3 + TM], F32R)
    XPS = sb("XPS", [128, NS, 11], F32R)
    XPf = XP[:].bitcast(F32)
    XPSf = XPS[:].bitcast(F32)
    DG = sb("DG", [128, 4, 128], F32R)
    CSS = sb("CSS", [128, 16, NS * 3], F32)
    HS0 = sb("HS0", [128, 4, NS], F32)
    CONVT = sb("CONVT", [128, DEPTH, 16, 3], F32)
    HRG = sb("HRG", [128, DEPTH, 4], F32)
    SST = sb("SST", [128, DEPTH, NH, HD], F32)
    RGWt = sb("RGWt", [128, 2, 4, 128], F32R)
    WSC = sb("WSC", [128, KC, 8], F32R)
    SCT = sb("SCT", [128, NTM, 8], F32)
    LAYC = sb("LAYC", [128, 16], F32)
    ONE1 = sb("ONE1", [128, 1], F32)
    FEN = sb("FEN", [128, 2], F32)
    dt_names_bf = ["KTOK", "VTOK", "AM", "ATT", "DT", "DINV", "N1", "BV", "BKG", "WT", "VN", "QG", "KD", "SBF"]
    DB = {n: sb(n, [128, NH, 128], BF16) for n in dt_names_bf if n not in ("VN", "SBF", "WT")}
    DB["VN"] = DB["N1"]; DB["SBF"] = DB["DINV"]; DB["WT"] = DB["AM"]
    GRW = sb("GRW", [128, NH, 128], F32)
    EGR2 = sb("EGR", [128, 2, NH, 128], F32)
    TRG = sb("TRG", [128, NH, 128], F32)
    OT = sb("OT", [128, NH, 128], F32)
    SM2 = sb("SM", [128, 2, 64], F32)
    CSL = TT[0:48, 3, 0:512]
    RSL = TT[0:16, 3, 0:512]
    PS = [es.enter_context(nc.psum_tensor("PS%d" % i, [128, 512], F32)) for i in range(8)]

    def c32(nm):
        return C32t[:, C32[nm]:C32[nm] + 128]

    def c16(nm, n=128):
        return C16t[:, C16[nm]:C16[nm] + n]

    ident = c32("ident")
    ones_f = c32("ones")
    ones_r = ONESR[:, :]
    identb = c16("identb")
    st = {"ps": 0, "wa": 0, "wb": 0, "stg": 0, "sg": 0}

    resv = set()

    def psum():
        while True:
            i = st["ps"]
            st["ps"] = (i + 1) % 8
            if i not in resv:
                return i

    def psb(pi):
        return PS[pi][:].bitcast(BF16)

    def ACT(out, in_, func, reads, writes, **kw):
        return P.op("act", lambda e: e.activation(out=out, in_=in_, func=func, **kw), reads, writes)

    def CP(eng, out, in_, reads, writes):
        if eng == "act":
            return P.op("act", lambda e: e.copy(out=out, in_=in_), reads, writes)
        return P.op(eng, lambda e: e.tensor_copy(out=out, in_=in_), reads, writes)

    def TTo(eng, out, in0, in1, op, reads, writes):
        return P.op(eng, lambda e: e.tensor_tensor(out=out, in0=in0, in1=in1, op=op), reads, writes)

    def TS(eng, out, in0, s1, s2, op0, op1, reads, writes):
        if op1 is None:
            return P.op(eng, lambda e: e.tensor_scalar(out=out, in0=in0, scalar1=s1, scalar2=None, op0=op0), reads, writes)
        return P.op(eng, lambda e: e.tensor_scalar(out=out, in0=in0, scalar1=s1, scalar2=s2, op0=op0, op1=op1), reads, writes)

    def STT(out, in0, scalar, in1, op0, op1, reads, writes):
        return P.op("dve", lambda e: e.scalar_tensor_tensor(out=out, in0=in0, scalar=scalar, in1=in1, op0=op0, op1=op1), reads, writes)

    def MM(out, lhsT, rhs, start, stop, reads, writes):
        return P.op("pe", lambda e: e.matmul(out, lhsT=lhsT, rhs=rhs, start=start, stop=stop, skip_group_check=True), reads, writes)

    def TR(out, in_, idn, reads, writes):
        return P.op("pe", lambda e: e.transpose(out=out, in_=in_, identity=idn), reads, writes)

    class WStream:
        def __init__(self, name, bufs):
            self.name, self.bufs, self.n = name, bufs, len(bufs)
            self.reqs = []
            self.issued = 0
            self.cur = 0

        def add(self, fn):
            self.reqs.append(fn)

        def next(self):
            i = self.cur
            self.cur += 1
            upto = min(len(self.reqs), i + self.n - 1)
            while self.issued < upto:
                k = self.issued
                slot = k % self.n
                out_ap, in_ap = self.reqs[k](self.bufs[slot])
                P.dma("pool", out_ap, in_ap, writes=[(self.name, slot)])
                self.issued += 1
            return i % self.n

    WAS = WStream("WA", WA)
    WBS = WStream("WB", WB)
    FBLOCKS = [(0, 4), (4, 4), (8, 4), (12, 4), (16, 4), (20, 2)]

    def plan_cols(src, col):
        v = src.rearrange("(kc p) n -> p kc n", p=128)
        WAS.add(lambda buf, v=v, col=col: (buf[:], v[:, :, col:col + 128]))

    def plan_ffn(l, which):
        wup = dr["ffn%d_w_up" % which][l]
        wdn = dr["ffn%d_w_down" % which][l].rearrange("(c p) n -> p c n", p=128)
        for (c0, nch) in FBLOCKS:
            for j in range(nch):
                plan_cols(wup, (c0 + j) * 128)
                plan_cols(wup, DFF + (c0 + j) * 128)
            for oc in range(8):
                WBS.add(lambda buf, c0=c0, nch=nch, oc=oc, wdn=wdn: (buf[:, 0:nch, :], wdn[:, c0:c0 + nch, oc * 128:(oc + 1) * 128]))

    def plan_mixer(l):
        win = dr["w_in"][l]
        for ch in range(4):
            plan_cols(win, ch * 128)
            plan_cols(win, 2048 + ch * 128)
        for kind in range(3):
            for hh in range(4):
                plan_cols(win, 512 + kind * 512 + hh * 128)
        for hh in range(4):
            plan_cols(win, 2560 + hh * 128)
        for oc in range(8):
            plan_cols(dr["w_out"][l], oc * 128)

    for _g in cfg.groups:
        for l in range(DEPTH):
            plan_ffn(l, 1)
            plan_mixer(l)
            plan_ffn(l, 2)

    P.dma("sp", PPt[:], dr["pp"][:, :], writes=[("PP", None)])
    P.dma("sp", C32t[:], dr["c32"][:, :], writes=[("C32", None)])
    for i in range(0, N16, 768):
        n = min(768, N16 - i)
        P.dma("sp", TT[:, 0, 0:n], dr["c16"][:, i:i + n], writes=[("TT", 0)])
        CP("dve", C16t[:, i:i + n], TT[:, 0, 0:n], [("TT", 0)], [("C16", None)])
    CP("dve", ones_r, ones_f, [("C32", None)], [("ONESR", None)])
    EPSC = {}
    for i, (key, val) in enumerate([((D, 1.0), EPS), ((D, 0.5), 4.0 * EPS), ((128, 1.0), EPS), ((1, 1.0), EPS), ("q", 128.0 * EPS)]):
        EPSC[key] = EPST[:, i:i + 1]
        P.op("dve", lambda e, i=i, val=val: e.memset(EPST[:, i:i + 1], val), writes=[("EPST", i)])
    P.op("dve", lambda e: e.memset(ONE1[:, :], 1.0), writes=[("ONE1", None)])
    P.op("dve", lambda e: e.memset(CONVT[:].rearrange("p l c t -> p (l c t)"), 0.0), writes=[("CONVT", None)])
    P.op("dve", lambda e: e.memset(HRG[:].rearrange("p l c -> p (l c)"), 0.0), writes=[("HRG", None)])
    P.op("dve", lambda e: e.memset(SST[:].rearrange("p l h e -> p (l h e)"), 0.0), writes=[("SST", None)])

    out_ops = []

    def pcol(l, nm, j=0, n=1):
        c = l * PP_LAYER + PL[nm] + j
        return PPt[:, c:c + n]

    ngroups = len(cfg.groups)
    for gi, (p0, npr, smp) in enumerate(cfg.groups):
        T = npr + (128 if smp else 0)
        ntile = T // 128
        blks = blocks_of(T)
        last_group = (gi == ngroups - 1)

        def tiles_of(b0, bn):
            return list(range(b0 // 128, (b0 + bn + 127) // 128))

        def rk(name, kcs, b0, bn):
            if isinstance(kcs, int):
                kcs = [kcs]
            return [(name, (kc, t)) for kc in kcs for t in tiles_of(b0, bn)]

        def stg_view(si):
            return TT[:, 2 * si:2 * si + 2, :].rearrange("p a t -> p (a t)")[:, 0:D]

        def stg_keys(si):
            return [("TT", 2 * si), ("TT", 2 * si + 1)]

        for t in range(ntile):
            si = st["stg"]; st["stg"] ^= 1
            stg = stg_view(si)
            src = dr["xp"][p0 + t * 128: p0 + (t + 1) * 128, :] if t * 128 < npr else dr["xs"][:, :]
            P.dma("sp", stg, src, writes=stg_keys(si))
            for half in range(2):
                pi = psum()
                for j in range(4):
                    kc = half * 4 + j
                    TR(PS[pi][:, j * 128:(j + 1) * 128], stg[:, kc * 128:(kc + 1) * 128], ident,
                       stg_keys(si) + [("C32", None)], [("PS", pi)])
                CP("act" if half == 0 else "dve", X[:, half * 4:(half + 1) * 4, t * 128:(t + 1) * 128],
                   PS[pi][:].rearrange("p (j c) -> p j c", j=4), [("PS", pi)], [("X", (half * 4 + j, t)) for j in range(4)])

        def sumsq_rstd(SRC, srcname, b0, bn, nfeat, post_scale=1.0, nk=KC):
            pi = psum()
            for kc in range(nk):
                rd = rk(srcname, kc, b0, bn)
                ACT(SQ[:, kc % 4, 0:bn], SRC[:, kc, b0:b0 + bn], ACTF.Square, rd, [("SQ", kc % 4)])
                MM(PS[pi][:, 0:bn], ones_r, SQ[:, kc % 4, 0:bn], kc == 0, kc == nk - 1,
                   [("SQ", kc % 4), ("ONESR", None)], [("PS", pi)])
            rstd_from_psum(pi, bn, nfeat, post_scale)

        def rstd_from_psum(pi, bn, nfeat, post_scale=1.0):
            ACT(RSTD[:, 0:bn], PS[pi][:, 0:bn], ACTF.Ln, [("PS", pi), ("EPST", None)], [("RSTD", None)],
                scale=1.0 / (nfeat * post_scale * post_scale), bias=EPSC[(nfeat, post_scale)])
            ACT(RSTD[:, 0:bn], RSTD[:, 0:bn], ACTF.Exp, [("RSTD", None)], [("RSTD", None)], scale=-0.5)

        def norm_scale(DST, dstname, SRC, srcname, l, gname, b0, bn):
            for kc in range(KC):
                STT(DST[:, kc, b0:b0 + bn], SRC[:, kc, b0:b0 + bn], pcol(l, gname, kc), RSTD[:, 0:bn], ALU.mult, ALU.mult,
                    rk(srcname, kc, b0, bn) + [("RSTD", None), ("PP", None)], rk(dstname, kc, b0, bn))

        def prenorm_to_H(l, gname):
            for (b0, bn) in blks:
                sumsq_rstd(X, "X", b0, bn, D)
                norm_scale(H, "H", X, "X", l, gname, b0, bn)

        def postnorm_residual(SRCw, SRC, srcname, l, gname, scale):
            for (b0, bn) in blks:
                sumsq_rstd(SRC, srcname, b0, bn, D, post_scale=scale)
                norm_scale(SRCw, srcname, SRC, srcname, l, gname, b0, bn)
                for kc in range(KC):
                    TTo("dve", X[:, kc, b0:b0 + bn], X[:, kc, b0:b0 + bn], SRC[:, kc, b0:b0 + bn], ALU.add,
                        rk(srcname, kc, b0, bn) + rk("X", kc, b0, bn), rk("X", kc, b0, bn))

        def ffn(l, which):
            prenorm_to_H(l, "ffn%d_norm_pre" % which)
            for fbi, (c0, nch) in enumerate(FBLOCKS):
                for j in range(nch):
                    wi_g = WAS.next()
                    pg = [psum() for _ in blks]
                    for kc in range(KC):
                        for bi, (b0, bn) in enumerate(blks):
                            MM(PS[pg[bi]][:, 0:bn], WA[wi_g][:, kc, :], H[:, kc, b0:b0 + bn],
                               kc == 0, kc == KC - 1, [("WA", wi_g)] + rk("H", kc, b0, bn), [("PS", pg[bi])])
                    sgs = []
                    for bi, (b0, bn) in enumerate(blks):
                        si = st["sg"]; st["sg"] = (st["sg"] + 1) % len(SG)
                        sgs.append(si)
                        ACT(SG[si][:, 0:bn], PS[pg[bi]][:, 0:bn], ACTF.Silu, [("PS", pg[bi])], [("TT", si)])
                    wi_u = WAS.next()
                    pu = [psum() for _ in blks]
                    for kc in range(KC):
                        for bi, (b0, bn) in enumerate(blks):
                            MM(PS[pu[bi]][:, 0:bn], WA[wi_u][:, kc, :], H[:, kc, b0:b0 + bn],
                               kc == 0, kc == KC - 1, [("WA", wi_u)] + rk("H", kc, b0, bn), [("PS", pu[bi])])
                    for bi, (b0, bn) in enumerate(blks):
                        TTo("dve", ACTB[:, j, b0:b0 + bn], PS[pu[bi]][:, 0:bn], SG[sgs[bi]][:, 0:bn], ALU.mult,
                            [("PS", pu[bi]), ("TT", sgs[bi])], [("ACTB", (j, b0))])
                for oc in range(8):
                    wi = WBS.next()
                    pd = [psum() for _ in blks]
                    for j in range(nch):
                        for bi, (b0, bn) in enumerate(blks):
                            MM(PS[pd[bi]][:, 0:bn], WB[wi][:, j, :], ACTB[:, j, b0:b0 + bn],
                               j == 0, j == nch - 1, [("WB", wi), ("ACTB", (j, b0))], [("PS", pd[bi])])
                    for bi, (b0, bn) in enumerate(blks):
                        if fbi == 0:
                            CP("act", Y[:, oc, b0:b0 + bn], PS[pd[bi]][:, 0:bn], [("PS", pd[bi])], rk("Y", oc, b0, bn))
                        else:
                            TTo("dve", Y[:, oc, b0:b0 + bn], PS[pd[bi]][:, 0:bn], YF[:, oc, b0:b0 + bn], ALU.add,
                                [("PS", pd[bi])] + rk("Y", oc, b0, bn), rk("Y", oc, b0, bn))
            postnorm_residual(Y, YF, "Y", l, "ffn%d_norm_post" % which, 0.5)

        MIXR = Y
        allH = [("H", (kc, t)) for kc in range(KC) for t in range(NTM)]
        allACTB = [("ACTB", None)]

        def mixer(l):
            win = dr["w_in"][l].rearrange("(kc p) n -> p kc n", p=128)
            prenorm_to_H(l, "mix_norm_pre")
            ACT(LAYC[:, 0:4], pcol(l, "rg_lambda", 0, 4), ACTF.Exp, [("PP", None)], [("LAYC", 0)], scale=-1.0)
            ACT(LAYC[:, 0:4], LAYC[:, 0:4], ACTF.Ln, [("LAYC", 0), ("ONE1", None)], [("LAYC", 0)], bias=ONE1[:, 0:1])
            TS("dve", LAYC[:, 4:8], LAYC[:, 0:4], -16.0, None, ALU.mult, None, [("LAYC", 0)], [("LAYC", 1)])
            TS("dve", LAYC[:, 0:4], LAYC[:, 0:4], -8.0, None, ALU.mult, None, [("LAYC", 0), ("LAYC", 1)], [("LAYC", 0)])
            ACT(LAYC[:, 8:12], pcol(l, "dn_a_log", 0, 4), ACTF.Exp, [("PP", None)], [("LAYC", 2)])
            TS("dve", LAYC[:, 8:12], LAYC[:, 8:12], -1.0, None, ALU.mult, None, [("LAYC", 2)], [("LAYC", 2)])
            P.dma("pool", RGWt[:], dr["rgw_r"][l].rearrange("g p c m -> p g c m"), writes=[("RGW", None)])
            P.dma("pool", WSC[:], win[:, :, 3072:3080], writes=[("WSC", None)])
            if smp:
                for q in range(4):
                    P.dma("sp", CSL, dr["st_conv"][l][:, q * 512:(q + 1) * 512], writes=[("TT", 3)])
                    pi = psum()
                    for j in range(4):
                        TR(PS[pi][:, j * 48:(j + 1) * 48], CSL[:, j * 128:(j + 1) * 128], ident[0:48, 0:48],
                           [("TT", 3), ("C32", None)], [("PS", pi)])
                    CP("dve", CSS[:, q * 4:(q + 1) * 4, :], PS[pi][:, 0:192].rearrange("p (j c) -> p j c", j=4),
                       [("PS", pi)], [("CSS", q * 4 + j) for j in range(4)])
                P.dma("sp", RSL, dr["st_rg"][l][:, :], writes=[("TT", 3)])
                pi = psum()
                for j in range(4):
                    TR(PS[pi][:, j * 16:(j + 1) * 16], RSL[:, j * 128:(j + 1) * 128], ident[0:16, 0:16],
                       [("TT", 3), ("C32", None)], [("PS", pi)])
                CP("dve", HS0[:, :, :], PS[pi][:, 0:64].rearrange("p (j c) -> p j c", j=4), [("PS", pi)], [("HS0", None)])

            wa_state = {}

            def proj():
                wi = WAS.next()
                pb = [psum() for _ in blks]
                for kc in range(KC):
                    for bi, (b0, bn) in enumerate(blks):
                        MM(PS[pb[bi]][:, 0:bn], WA[wi][:, kc, :], H[:, kc, b0:b0 + bn],
                           kc == 0, kc == KC - 1, [("WA", wi)] + rk("H", kc, b0, bn), [("PS", pb[bi])])
                return pb

            def conv_chunk(l, ch, pb):
                CP("act", XP[:, 0:3], CONVT[:, l, ch, :], [("CONVT", (l, ch))], [("XP", 0)])
                for bi, (b0, bn) in enumerate(blks):
                    n = min(bn, npr - b0)
                    if n > 0:
                        CP("dve" if bi == 0 else "act", XP[:, 3 + b0:3 + b0 + n], PS[pb[bi]][:, 0:n], [("PS", pb[bi])], [("XP", 1 + bi)])
                if smp:
                    b0, bn = blks[-1]
                    off = npr - b0
                    CP("dve", XPS[:, :, 0:3], CSS[:, ch, :].rearrange("p (s t) -> p s t", t=3), [("CSS", ch)], [("XPS", 0)])
                    CP("act", XPS[:, :, 3:11], PS[pb[-1]][:, off:off + 128].rearrange("p (s t) -> p s t", t=8),
                       [("PS", pb[-1])], [("XPS", 1)])
                for tap in range(4):
                    TS("dve", DG[:, tap, :], ident, pcol(l, "conv_w", ch * 4 + tap), None, ALU.mult, None,
                       [("C32", None), ("PP", None)], [("DG", tap)])
                xpk = [("XP", i) for i in range(1 + len(blks))]
                pc = [psum() for _ in blks]
                for bi, (b0, bn) in enumerate(blks):
                    n = min(bn, npr - b0)
                    for tap in range(4):
                        MM(PS[pc[bi]][:, 0:n], DG[:, tap, :], XP[:, b0 + tap:b0 + tap + n], tap == 0, tap == 3,
                           xpk + [("DG", tap)], [("PS", pc[bi])])
                CP("dve", CONVT[:, l, ch, :], XPf[:, npr:npr + 3], xpk, [("CONVT", (l, ch))])
                if smp:
                    b0, bn = blks[-1]
                    off = npr - b0
                    for tap in range(4):
                        MM(PS[pc[-1]][:, off:off + 128].rearrange("p (s t) -> p s t", t=8), DG[:, tap, :], XPS[:, :, tap:tap + 8],
                           False, tap == 3, [("XPS", 0), ("XPS", 1), ("DG", tap)], [("PS", pc[-1])])
                    CP("dve", CSS[:, ch, :].rearrange("p (s t) -> p s t", t=3), XPSf[:, :, 8:11], [("XPS", 0), ("XPS", 1)], [("CSS", ch)])
                return pc

            T0 = TT[:, 0, :]; T1 = TT[:, 1, :]; T2 = TT[:, 2, :]
            XRf = XR[:].bitcast(F32)
            k0, k1, k2, k3 = ("TT", 0), ("TT", 1), ("TT", 2), ("XR", None)

            nb_ = len(blks)
            allrow = [("TT", i) for i in range(4)] + [("XR", None)]
            allblk = [("TT", (i, bi)) for i in range(3) for bi in range(nb_)] + [("XR", bi) for bi in range(nb_)]
            P.op("dve", lambda e: e.memset(FEN[:, 0:1], 0.0), allrow + allblk, allrow + allblk)
            kq = lambda i, bi: ("TT", (i, bi))
            kx = lambda bi: ("XR", bi)
            for ch in range(4):
                pb = proj()
                pc = conv_chunk(l, ch, pb)
                for bi, (b0, bn) in enumerate(blks):
                    ACT(XR[:, b0:b0 + bn], PS[pc[bi]][:, 0:bn], ACTF.Identity, [("PS", pc[bi]), ("PP", None), kx(bi)], [kx(bi)],
                        bias=pcol(l, "conv_b_rg", ch))
                pa = [psum() for _ in blks]
                for bi, (b0, bn) in enumerate(blks):
                    MM(PS[pa[bi]][:, 0:bn], RGWt[:, 0, ch, :], XR[:, b0:b0 + bn], True, True, [("RGW", None), kx(bi)], [("PS", pa[bi])])
                px = [psum() for _ in blks]
                for bi, (b0, bn) in enumerate(blks):
                    MM(PS[px[bi]][:, 0:bn], RGWt[:, 1, ch, :], XR[:, b0:b0 + bn], True, True, [("RGW", None), kx(bi)], [("PS", px[bi])])
                pgt = proj()
                for p_ in pgt:
                    resv.add(p_)
                for bi, (b0, bn) in enumerate(blks):
                    ACT(T0[:, b0:b0 + bn], PS[pa[bi]][:, 0:bn], ACTF.Sigmoid, [("PS", pa[bi]), ("PP", None), kq(0, bi)], [kq(0, bi)],
                        bias=pcol(l, "rg_b_a", ch))
                for bi, (b0, bn) in enumerate(blks):
                    ACT(T1[:, b0:b0 + bn], PS[px[bi]][:, 0:bn], ACTF.Sigmoid, [("PS", px[bi]), ("PP", None), kq(1, bi)], [kq(1, bi)],
                        bias=pcol(l, "rg_b_x", ch))
                for bi, (b0, bn) in enumerate(blks):
                    sl = slice(b0, b0 + bn)
                    ACT(T2[:, sl], T0[:, sl], ACTF.Exp, [kq(0, bi), ("LAYC", 1), kq(2, bi)], [kq(2, bi)], scale=LAYC[:, 4 + ch:5 + ch])
                for bi, (b0, bn) in enumerate(blks):
                    sl = slice(b0, b0 + bn)
                    ACT(T0[:, sl], T0[:, sl], ACTF.Exp, [kq(0, bi), ("LAYC", 0)], [kq(0, bi)], scale=LAYC[:, ch:ch + 1])
                for bi, (b0, bn) in enumerate(blks):
                    sl = slice(b0, b0 + bn)
                    TS("dve", T2[:, sl], T2[:, sl], -1.0, 1.0, ALU.mult, ALU.add, [kq(2, bi)], [kq(2, bi)])
                    TS("dve", T2[:, sl], T2[:, sl], 1e-30, None, ALU.max, None, [kq(2, bi)], [kq(2, bi)])
                    TTo("dve", T1[:, sl], T1[:, sl], XRf[:, sl], ALU.mult, [kq(1, bi), kx(bi)], [kq(1, bi)])
                for bi, (b0, bn) in enumerate(blks):
                    sl = slice(b0, b0 + bn)
                    ACT(T2[:, sl], T2[:, sl], ACTF.Ln, [kq(2, bi)], [kq(2, bi)])
                    ACT(T2[:, sl], T2[:, sl], ACTF.Exp, [kq(2, bi)], [kq(2, bi)], scale=0.5)
                for bi, (b0, bn) in enumerate(blks):
                    sl = slice(b0, b0 + bn)
                    TTo("dve", T1[:, sl], T1[:, sl], T2[:, sl], ALU.mult, [kq(1, bi), kq(2, bi)], [kq(1, bi)])
                for bi, (b0, bn) in enumerate(blks):
                    n = min(bn, npr - b0)
                    if n <= 0:
                        continue
                    init = HRG[:, l, ch:ch + 1] if bi == 0 else T2[:, b0 - 1:b0]
                    extra = [("HRG", (l, ch))] if bi == 0 else [kq(2, bi - 1)]
                    P.op("dve", lambda e, b0=b0, n=n, init=init: e.tensor_tensor_scan(
                        out=T2[:, b0:b0 + n], data0=T0[:, b0:b0 + n], data1=T1[:, b0:b0 + n],
                        initial=init, op0=ALU.mult, op1=ALU.add),
                        [kq(0, bi), kq(1, bi), kq(2, bi)] + extra, [kq(2, bi)])
                lastb = max(bi for bi, (b0, bn) in enumerate(blks) if npr > b0)
                CP("dve", HRG[:, l, ch:ch + 1], T2[:, npr - 1:npr], [kq(2, lastb)], [("HRG", (l, ch))])
                if smp:
                    sb_ = nb_ - 1
                    a_first = T0[:, npr:npr + 128:8]
                    b_first = T1[:, npr:npr + 128:8]
                    TTo("dve", SM2[:, 0, 0:16], a_first, HS0[:, ch, :], ALU.mult, [kq(0, sb_), ("HS0", None)], [("SMr", None)])
                    TTo("dve", b_first, b_first, SM2[:, 0, 0:16], ALU.add, [kq(1, sb_), ("SMr", None)], [kq(1, sb_)])
                    TS("dve", a_first, a_first, 0.0, None, ALU.mult, None, [kq(0, sb_), ("SMr", None)], [kq(0, sb_)])
                    P.op("dve", lambda e: e.tensor_tensor_scan(out=T2[:, npr:npr + 128], data0=T0[:, npr:npr + 128],
                                                              data1=T1[:, npr:npr + 128], initial=0.0, op0=ALU.mult, op1=ALU.add),
                         [kq(0, sb_), kq(1, sb_), kq(2, sb_)], [kq(2, sb_)])
                    CP("dve", HS0[:, ch, :], T2[:, npr + 7:npr + 128:8], [kq(2, sb_), ("SMr", None)], [("HS0", None)])
                for bi, (b0, bn) in enumerate(blks):
                    sl = slice(b0, b0 + bn)
                    CP("act", T0[:, sl], PS[pgt[bi]][:, 0:bn], [("PS", pgt[bi]), kq(0, bi)], [kq(0, bi)])
                    ACT(T1[:, sl], PS[pgt[bi]][:, 0:bn], ACTF.Square, [("PS", pgt[bi]), kq(1, bi)], [kq(1, bi)])
                    resv.discard(pgt[bi])
                for bi, (b0, bn) in enumerate(blks):
                    sl = slice(b0, b0 + bn)
                    TS("dve", T1[:, sl], T1[:, sl], 0.044715, 1.0, ALU.mult, ALU.add, [kq(1, bi)], [kq(1, bi)])
                    TTo("dve", T1[:, sl], T1[:, sl], T0[:, sl], ALU.mult, [kq(0, bi), kq(1, bi)], [kq(1, bi)])
                for bi, (b0, bn) in enumerate(blks):
                    sl = slice(b0, b0 + bn)
                    ACT(T1[:, sl], T1[:, sl], ACTF.Sigmoid, [kq(1, bi)], [kq(1, bi)], scale=2.0 * 0.7978845608028654)
                for bi, (b0, bn) in enumerate(blks):
                    sl = slice(b0, b0 + bn)
                    TTo("dve", T0[:, sl], T0[:, sl], T1[:, sl], ALU.mult, [kq(0, bi), kq(1, bi)], [kq(0, bi)])
                    TTo("dve", MIXR[:, ch, sl], T2[:, sl], T0[:, sl], ALU.mult, [kq(0, bi), kq(2, bi)], rk("Y", ch, b0, bn))
            P.op("dve", lambda e: e.memset(FEN[:, 0:1], 0.0), allrow + allblk, allrow + allblk)

            TB = [(T0, k0), (T1, k1)]

            def qk_front(kind, hh, slot):
                ch = 4 + kind * 4 + hh
                Tb, kb = TB[slot]
                pb = proj()
                pc = conv_chunk(l, ch, pb)
                for bi, (b0, bn) in enumerate(blks):
                    ACT(Tb[:, b0:b0 + bn], PS[pc[bi]][:, 0:bn], ACTF.Silu, [("PS", pc[bi]), kb], [kb])
                for bi, (b0, bn) in enumerate(blks):
                    sq = slot * 2 + bi
                    TTo("dve", SQ[:, sq, 0:bn], Tb[:, b0:b0 + bn], Tb[:, b0:b0 + bn], ALU.mult, [kb], [("SQ", sq)])

            def qk_tail(kind, hh, slot):
                Tb, kb = TB[slot]
                for bi, (b0, bn) in enumerate(blks):
                    sq = slot * 2 + bi
                    pi = psum()
                    MM(PS[pi][:, 0:bn], ones_r, SQ[:, sq, 0:bn], True, True, [("SQ", sq), ("ONESR", None)], [("PS", pi)])
                    if kind == 0:
                        ACT(RSTD[:, 0:bn], PS[pi][:, 0:bn], ACTF.Ln, [("PS", pi), ("EPST", None)], [("RSTD", None)],
                            scale=128.0, bias=EPSC["q"])
                    else:
                        ACT(RSTD[:, 0:bn], PS[pi][:, 0:bn], ACTF.Ln, [("PS", pi), ("EPST", None)], [("RSTD", None)],
                            scale=1.0, bias=EPSC[(1, 1.0)])
                    ACT(RSTD[:, 0:bn], RSTD[:, 0:bn], ACTF.Exp, [("RSTD", None)], [("RSTD", None)], scale=-0.5)
                    TTo("dve", QKV[:, kind * 4 + hh, b0:b0 + bn], Tb[:, b0:b0 + bn], RSTD[:, 0:bn], ALU.mult,
                        [kb, ("RSTD", None)], [("QKV", None)])

            for kind in range(2):
                for hp in range(2):
                    qk_front(kind, 2 * hp, 0)
                    qk_front(kind, 2 * hp + 1, 1)
                    qk_tail(kind, 2 * hp, 0)
                    qk_tail(kind, 2 * hp + 1, 1)
            for hh in range(4):
                pb = proj()
                pc = conv_chunk(l, 12 + hh, pb)
                for bi, (b0, bn) in enumerate(blks):
                    ACT(QKV[:, 8 + hh, b0:b0 + bn], PS[pc[bi]][:, 0:bn], ACTF.Silu, [("PS", pc[bi])], [("QKV", None)])

            for cpair in range(2):
                for cc in range(2):
                    hh = cpair * 2 + cc
                    pb = proj()
                    for bi, (b0, bn) in enumerate(blks):
                        ACT(ZG[:, hh, b0:b0 + bn], PS[pb[bi]][:, 0:bn], ACTF.Silu, [("PS", pb[bi])], [("ZG", (hh, bi))])

            pi = psum()
            for t in range(ntile):
                for kc in range(KC):
                    MM(PS[pi][:, t * 8:(t + 1) * 8], H[:, kc, t * 128:(t + 1) * 128], WSC[:, kc, :], (t == 0 and kc == 0), kc == KC - 1,
                       [("WSC", None), ("H", (kc, t))], [("PS", pi)])
            CP("dve", SCT[:, 0:ntile, :], PS[pi][:, 0:ntile * 8].rearrange("p (t c) -> p t c", c=8), [("PS", pi)], [("SCT", None)])

            for t in range(ntile):
                nxt = (t + 1, smp and t + 1 == ntile - 1) if t + 1 < ntile else None
                delta_tile(l, t, smp and t == ntile - 1, nxt=nxt, first=(t == 0))

            if getattr(cfg, "debug", False) and l == 0:
                out_ops.append(P.dma("sp", dr["dbg"][gi], YF, reads=[("Y", None)], writes=[("DBG", gi)]))
            for ocp in range(4):
                for o2 in range(2):
                    oc = ocp * 2 + o2
                    wi = WAS.next()
                    pd = [psum() for _ in blks]
                    for kc in range(KC):
                        for bi, (b0, bn) in enumerate(blks):
                            MM(PS[pd[bi]][:, 0:bn], WA[wi][:, kc, :], MIXR[:, kc, b0:b0 + bn],
                               kc == 0, kc == KC - 1, [("WA", wi)] + rk("Y", kc, b0, bn), [("PS", pd[bi])])
                    for bi, (b0, bn) in enumerate(blks):
                        CP("act", H[:, oc, b0:b0 + bn], PS[pd[bi]][:, 0:bn], [("PS", pd[bi])], rk("H", oc, b0, bn))
            postnorm_residual(H, HF, "H", l, "mix_norm_post", 1.0)

            if smp:
                for q in range(4):
                    pi = psum()
                    for j in range(4):
                        TR(PS[pi][0:48, j * 128:(j + 1) * 128], CSS[:, q * 4 + j, :], ident, [("CSS", q * 4 + j), ("C32", None)], [("PS", pi)])
                    CP("dve", CSL, PS[pi][0:48, :], [("PS", pi)], [("TT", 3)])
                    out_ops.append(P.dma("sp", dr["ncs"][l][:, q * 512:(q + 1) * 512], CSL, reads=[("TT", 3)], writes=[("CSLo", q)]))
                pi = psum()
                for j in range(4):
                    TR(PS[pi][0:16, j * 128:(j + 1) * 128], HS0[:, j, :], ident, [("HS0", None), ("C32", None)], [("PS", pi)])
                CP("dve", RSL, PS[pi][0:16, :], [("PS", pi)], [("TT", 3)])
                out_ops.append(P.dma("sp", dr["nrs"][l][:, :], RSL, reads=[("TT", 3)], writes=[("RSLo", 0)]))
            if last_group:
                for q in range(4):
                    pi = psum()
                    for j in range(4):
                        TR(PS[pi][0:3, j * 128:(j + 1) * 128], CONVT[:, l, q * 4 + j, :], ident, [("CONVT", (l, q * 4 + j)), ("C32", None)], [("PS", pi)])
                    CP("dve", CSL[0:3, :], PS[pi][0:3, :], [("PS", pi)], [("TT", 3)])
                    out_ops.append(P.dma("sp", dr["ncp"][l][:, q * 512:(q + 1) * 512], CSL[0:3, :], reads=[("TT", 3)], writes=[("CSLo", q)]))
                pi = psum()
                TR(PS[pi][0:4, 0:128], HRG[:, l, :], ident, [("HRG", None), ("C32", None)], [("PS", pi)])
                CP("dve", RSL[0:4, 0:128], PS[pi][0:4, 0:128], [("PS", pi)], [("TT", 3)])
                out_ops.append(P.dma("sp", dr["nrp"][l].rearrange("(c p) -> c p", p=128), RSL[0:4, 0:128], reads=[("TT", 3)], writes=[("RSLo", 0)]))
                out_ops.append(P.dma("sp", dr["ndp"][l].rearrange("h d e -> d h e"), SST[:, l, :, :], reads=[("SST", None)], writes=[("SSTo", l)]))

        def delta_prelude(l, t, is_s):
            sfx = "_s" if is_s else "_p"
            par = t % 2
            SM = SM2[:, par, :]
            EGR = EGR2[:, par]
            BETA = SM[:, 16:20]; GT = SM[:, 20:24]; GC = SM[:, 24:32]; EG = SM[:, 32:36]; BEG = SM[:, 36:40]; EKD = SM[:, 40:44]
            ACT(BETA, SCT[:, t, 0:4], ACTF.Sigmoid, [("SCT", None)], [("SM", (par, 1))])
            TTo("dve", GT, SCT[:, t, 4:8], pcol(l, "dn_dt_bias", 0, 4), ALU.add, [("SCT", None), ("PP", None)], [("SM", (par, 2))])
            ACT(GT, GT, ACTF.Exp, [("SM", (par, 2))], [("SM", (par, 2))])
            ACT(GT, GT, ACTF.Ln, [("SM", (par, 2)), ("ONE1", None)], [("SM", (par, 2))], bias=ONE1[:, 0:1])
            TTo("dve", GT, GT, LAYC[:, 8:12], ALU.mult, [("SM", (par, 2)), ("LAYC", 2)], [("SM", (par, 2))])
            pgc = psum()
            MM(PS[pgc][:, 0:4], c32("tri" + sfx), GT, True, True, [("SM", (par, 2)), ("C32", None)], [("PS", pgc)])
            MM(PS[pgc][:, 4:8], c32("up" + sfx), GT, False, True, [("SM", (par, 2)), ("C32", None)], [("PS", pgc)])
            bc4 = lambda ap: ap.unsqueeze(2).broadcast_to([128, NH, 128])
            bm4 = lambda ap: ap.unsqueeze(1).broadcast_to([128, NH, 128])
            TTo("dve", TRG[:, :, :], bm4(c32("tri" + sfx)), bc4(GT), ALU.mult, [("SM", (par, 2)), ("C32", None), ("TRG", 0), ("TRG", 1)], [("TRG", 0), ("TRG", 1)])
            pgr = psum()
            resv.add(pgr)
            st["pgr"] = pgr
            st["pgr_users"] = 2
            MM(PS[pgr][:, :], ones_f, TRG[:].rearrange("p h c -> p (h c)"), True, True, [("TRG", 0), ("TRG", 1), ("C32", None)], [("PS", pgr)])
            CP("dve", GC, PS[pgc][:, 0:8], [("PS", pgc), ("PS", pgr)], [("SM", (par, 3))])
            ACT(EG, GC[:, 0:4], ACTF.Exp, [("SM", (par, 3))], [("SM", (par, 4))])
            ACT(EKD, GC[:, 4:8], ACTF.Exp, [("SM", (par, 3))], [("SM", (par, 5))])
            TTo("dve", BEG, BETA, EG, ALU.mult, [("SM", (par, 1)), ("SM", (par, 4))], [("SM", (par, 6))])
            PGR4 = PS[pgr][:].rearrange("p (h c) -> p h c", h=NH)
            gk = [("GRW", 0), ("GRW", 1)]; ek = [("EGR", (par, 0)), ("EGR", (par, 1))]
            TTo("dve", GRW[:, :, :], PGR4, GC[:, 0:4].unsqueeze(2).broadcast_to([128, NH, 128]), ALU.subtract, [("PS", pgr), ("SM", (par, 3))] + gk, gk)
            GRWf = GRW[:].rearrange("p h c -> p (h c)")
            STT(GRWf, GRWf, -1.0, GRWf, ALU.mult, ALU.max, gk, gk)
            ACT(GRW[:], GRW[:], ACTF.Exp, gk, gk, scale=-1.0)
            ACT(EGR[:], PGR4, ACTF.Exp, [("PS", pgr)] + ek, ek)
            resv.discard(pgr)

        def delta_pair(l, t, is_s, hg):
            c0 = t * 128
            par = t % 2
            SM = SM2[:, par, :]
            EGR = EGR2[:, par]
            sfx = "_s" if is_s else "_p"
            nlv = NLV_S if is_s else NLV_P
            HS = slice(2 * hg, 2 * hg + 2)
            heads = (2 * hg, 2 * hg + 1)
            bk = lambda n: [(n, hg)]
            KTOK, VTOK, AM, ATT, DTt, DINV, N1, BV, BKG, WT, VN, QG, KD, SBF = [DB[n][:, HS, :] for n in dt_names_bf]
            f2 = lambda ap: ap.rearrange("p h c -> p (h c)")
            qkv_r = [("QKV", None)]
            bc = lambda ap: ap.unsqueeze(2).broadcast_to([128, 2, 128])
            bm = lambda ap: ap.unsqueeze(1).broadcast_to([128, 2, 128])
            BETA = SM[:, 16 + 2 * hg:18 + 2 * hg]; GT = SM[:, 20 + 2 * hg:22 + 2 * hg]; GC0 = SM[:, 24 + 2 * hg:26 + 2 * hg]
            BEG = SM[:, 36 + 2 * hg:38 + 2 * hg]; EKD = SM[:, 40 + 2 * hg:42 + 2 * hg]
            GRWp = GRW[:, HS, :]; EGRp = EGR[:, HS, :]; TRGp = TRG[:, HS, :]; OTp = OT[:, HS, :]
            if getattr(cfg, 'delta_stop', 99) == 1:
                resv.clear()
                return
            pgr = st["pgr"]
            ptk = psum()
            for i, h in enumerate(heads):
                TR(psb(ptk)[:, i * 128:(i + 1) * 128], QKV[:, 4 + h, c0:c0 + 128], identb, qkv_r + [("C16", None)], [("PS", ptk)])
            for i, h in enumerate(heads):
                TR(psb(ptk)[:, 256 + i * 128:256 + (i + 1) * 128], QKV[:, 8 + h, c0:c0 + 128], identb, qkv_r + [("C16", None)], [("PS", ptk)])
            pkk = psum()
            for i, h in enumerate(heads):
                MM(PS[pkk][:, i * 128:(i + 1) * 128], QKV[:, 4 + h, c0:c0 + 128], QKV[:, 4 + h, c0:c0 + 128], i == 0, True, qkv_r, [("PS", pkk)])
            for i, h in enumerate(heads):
                MM(PS[pkk][:, 256 + i * 128:256 + (i + 1) * 128], QKV[:, 4 + h, c0:c0 + 128], QKV[:, h, c0:c0 + 128], False, True, qkv_r,
                   [("PS", pkk), ("PGD", hg)])
            yield
            if getattr(cfg, 'delta_stop', 99) == 2:
                resv.clear()
                return
            PGR3 = PS[pgr][:, hg * 256:(hg + 1) * 256].rearrange("p (h c) -> p h c", h=2)

            if getattr(cfg, 'delta_stop', 99) == 25:
                resv.clear()
                return
            CP("act", f2(KTOK), psb(ptk)[:, 0:256], [("PS", ptk)] + bk("KTOK"), bk("KTOK"))
            CP("act", f2(VTOK), psb(ptk)[:, 256:512], [("PS", ptk)] + bk("VTOK"), bk("VTOK"))
            yield
            if getattr(cfg, 'delta_stop', 99) == 3:
                resv.clear()
                return
            TTo("dve", BV, VTOK, bc(BETA), ALU.mult, bk("VTOK") + [("SM", (par, 1))] + bk("BV"), bk("BV"))
            TTo("dve", BKG, KTOK, bc(BEG), ALU.mult, bk("KTOK") + [("SM", (par, 6))] + bk("BKG"), bk("BKG"))
            TTo("dve", KD, KTOK, bc(EKD), ALU.mult, bk("KTOK") + [("SM", (par, 5))] + bk("KD"), bk("KD"))
            TTo("dve", QG, QKV[:, HS, c0:c0 + 128], EGRp, ALU.mult, qkv_r + [("EGR", (par, hg))] + bk("QG"), bk("QG"))
            CP("act", DTt, bm(identb), [("C16", None)] + bk("DT"), bk("DT"))
            CP("act", DINV, bm(identb), [("C16", None)] + bk("DINV"), bk("DINV"))
            yield
            if getattr(cfg, 'delta_stop', 99) == 4:
                resv.clear()
                return
            TTo("dve", TRGp, GRWp, bm(c16("mstrict" + sfx)), ALU.mult, bk("GRW") + [("C16", None)] + bk("TRG"), bk("TRG"))
            TTo("dve", TRGp, TRGp, bc(BETA), ALU.mult, bk("TRG") + [("SM", (par, 1))], bk("TRG"))
            TTo("dve", OTp, GRWp, bm(c16("minclt" + sfx)), ALU.mult, bk("GRW") + [("C16", None)] + bk("OT"), bk("OT"))
            TTo("dve", f2(AM), PS[pkk][:, 0:256], f2(TRGp), ALU.mult, [("PS", pkk)] + bk("TRG") + bk("AM"), bk("AM"))
            TTo("dve", f2(ATT), PS[pkk][:, 256:512], f2(OTp), ALU.mult, [("PS", pkk)] + bk("OT") + bk("ATT"), bk("ATT"))
            yield
            if getattr(cfg, 'delta_stop', 99) == 5:
                resv.clear()
                return
            for lv in range(nlv):
                p1 = psum()
                for i in range(2):
                    MM(PS[p1][:, i * 128:(i + 1) * 128], AM[:, i, :], DTt[:, i, :], i == 0, True, bk("AM") + bk("DT"), [("PS", p1)])
                yield
                TTo("dve", N1, PS[p1][:, 0:256].rearrange("p (h c) -> p h c", h=2), bm(c16("lv_p%d" % lv)), ALU.mult,
                    [("PS", p1), ("C16", None)] + bk("N1"), bk("N1"))
                yield
                p2 = psum()
                for i in range(2):
                    MM(PS[p2][:, i * 128:(i + 1) * 128], DINV[:, i, :], N1[:, i, :], i == 0, True, bk("DINV") + bk("N1"), [("PS", p2)])
                yield
                TTo("dve", f2(DTt), f2(DTt), PS[p2][:, 0:256], ALU.subtract, [("PS", p2)] + bk("DT"), bk("DT"))
                yield
                if lv < nlv - 1:
                    p3 = psum()
                    for i in range(2):
                        TR(psb(p3)[:, i * 128:(i + 1) * 128], DTt[:, i, :], identb, bk("DT") + [("C16", None)], [("PS", p3)])
                    yield
                    CP("act", f2(DINV), psb(p3)[:, 0:256], [("PS", p3)] + bk("DINV"), bk("DINV"))
                    yield
            if getattr(cfg, 'delta_stop', 99) == 6:
                resv.clear()
                return
            pu = psum()
            resv.add(pu)
            if not is_s:
                for i in range(2):
                    MM(PS[pu][:, i * 128:(i + 1) * 128], DTt[:, i, :], BV[:, i, :], i == 0, False, bk("DT") + bk("BV"), [("PS", pu)])
            pw = psum()
            for i in range(2):
                MM(PS[pw][:, i * 128:(i + 1) * 128], BKG[:, i, :], DTt[:, i, :], i == 0, True, bk("DT") + bk("BKG"), [("PS", pw)])
            yield
            ACT(f2(WT), PS[pw][:, 0:256], ACTF.Copy, [("PS", pw)] + bk("AM"), bk("AM"), scale=-1.0)
            po = psum()
            resv.add(po)
            if getattr(cfg, 'delta_stop', 99) == 7:
                resv.clear()
                return
            if not is_s:
                CP("act", SBF, SST[:, l, HS, :], [("SST", hg)] + bk("DINV"), bk("DINV"))
                yield
                for i in range(2):
                    MM(PS[pu][:, i * 128:(i + 1) * 128], WT[:, i, :], SBF[:, i, :], False, True, bk("AM") + bk("DINV"), [("PS", pu)])
                yield
                CP("act", f2(VN), PS[pu][:, 0:256], [("PS", pu)] + bk("N1"), bk("N1"))
                resv.discard(pu)
                yield
                for i in range(2):
                    MM(PS[po][:, i * 128:(i + 1) * 128], SBF[:, i, :], QG[:, i, :], i == 0, False, bk("DINV") + bk("QG"), [("PS", po)])
                for i in range(2):
                    MM(PS[po][:, i * 128:(i + 1) * 128], VN[:, i, :], ATT[:, i, :], False, True, bk("N1") + bk("ATT"), [("PS", po)])
                psu = psum()
                for i in range(2):
                    MM(PS[psu][:, i * 128:(i + 1) * 128], KD[:, i, :], VN[:, i, :], i == 0, True, bk("KD") + bk("N1"), [("PS", psu)])
                yield
                for i, h in enumerate(heads):
                    STT(SST[:, l, h, :], SST[:, l, h, :], EGR[:, h, 127:128], PS[psu][:, i * 128:(i + 1) * 128], ALU.mult, ALU.add,
                        [("PS", psu), ("SST", hg)] + [("EGR", (par, hg))], [("SST", hg)])
            else:
                resv.discard(pu)
                HSQ = NS // 2
                SS0 = TT[:, 0:2, :].rearrange("p a t -> p (a t)")[:, 0:HSQ * 128].rearrange("p (s e) -> p s e", s=HSQ)
                SS0B = TT[:, 2, :].bitcast(BF16)[:, 0:HSQ * 128].rearrange("p (s e) -> p s e", s=HSQ)
                WTX = TT[:, 3, 0:512].bitcast(BF16).rearrange("p (s c) -> p s c", s=HSQ)
                kS0 = [("TT", 0), ("TT", 1)]; kS0B = [("TT", 2)]; kWX = [("TT", 3)]
                segrow = c16("segrow", NS * 128).rearrange("p (s c) -> p s c", s=NS)
                segcol = c16("segcol", NS)
                first_po = True
                for i, h in enumerate(heads):
                    for hf in range(2):
                        s0 = hf * HSQ
                        P.dma("sp", SS0, dr["st_dn"][l][s0:s0 + HSQ, h, :, :].rearrange("s d e -> d s e"), reads=[], writes=kS0)
                        CP("act", SS0B, SS0, kS0 + kS0B, kS0B)
                        TTo("dve", WTX, WT[:, i, :].unsqueeze(1).broadcast_to([128, HSQ, 128]), segrow[:, s0:s0 + HSQ, :], ALU.mult,
                            bk("AM") + [("C16", None)] + kWX, kWX)
                        pu2 = psum()
                        MM(PS[pu2][:, 0:128], DTt[:, i, :], BV[:, i, :], True, False, bk("DT") + bk("BV"), [("PS", pu2)])
                        for s_ in range(HSQ):
                            MM(PS[pu2][:, 0:128], WTX[:, s_, :], SS0B[:, s_, :], False, True, kS0B + kWX, [("PS", pu2)])
                        CP("act", VN[:, i, :], PS[pu2][:, 0:128], [("PS", pu2)] + bk("N1"), bk("N1"))
                        cb = i * 128 + hf * 64
                        MM(PS[po][:, cb:cb + 64], VN[:, i, :], ATT[:, i, hf * 64:hf * 64 + 64], first_po, False, bk("N1") + bk("ATT"), [("PS", po)])
                        first_po = False
                        for s_ in range(HSQ):
                            cs = i * 128 + (s0 + s_) * 8
                            MM(PS[po][:, cs:cs + 8], SS0B[:, s_, :], QG[:, i, (s0 + s_) * 8:(s0 + s_) * 8 + 8], False, True,
                               kS0B + bk("QG"), [("PS", po)])
                        TTo("dve", WTX, KD[:, i, :].unsqueeze(1).broadcast_to([128, HSQ, 128]),
                            segcol[:, s0:s0 + HSQ].unsqueeze(2).broadcast_to([128, HSQ, 128]), ALU.mult,
                            bk("KD") + [("C16", None)] + kWX, kWX)
                        for q in range(2):
                            psu = psum()
                            for j in range(4):
                                s_ = q * 4 + j
                                MM(PS[psu][:, j * 128:(j + 1) * 128], WTX[:, s_, :], VN[:, i, :], j == 0, True, kWX + bk("N1"), [("PS", psu)])
                            for j in range(4):
                                s_ = q * 4 + j
                                sg_ = s0 + s_
                                STT(SS0[:, s_, :], SS0[:, s_, :], EGR[:, h, sg_ * 8 + 7:sg_ * 8 + 8], PS[psu][:, j * 128:(j + 1) * 128],
                                    ALU.mult, ALU.add, [("PS", psu)] + kS0 + [("EGR", (par, hg))], kS0)
                        out_ops.append(P.dma("sp", dr["nds"][l][s0:s0 + HSQ, h, :, :].rearrange("s d e -> d s e"), SS0, reads=kS0, writes=[("NDSo", h)]))
                        yield
            if getattr(cfg, 'delta_stop', 99) == 8:
                resv.clear()
                return
            resv.discard(po)
            CP("act", f2(OTp), PS[po][:, 0:256], [("PS", po)] + bk("OT"), bk("OT"))
            SQW = SQ[:, 2 * hg, 0:256]
            ACT(SQW, PS[po][:, 0:256], ACTF.Square, [("PS", po)], [("SQ", 2 * hg)])
            yield
            pss = psum()
            MM(PS[pss][:, 0:256], ones_r, SQW, True, True, [("SQ", 2 * hg), ("ONESR", None)], [("PS", pss)])
            yield
            RS = RSTD[:, hg * 256:(hg + 1) * 256]
            ACT(RS, PS[pss][:, 0:256], ACTF.Ln, [("PS", pss), ("EPST", None)], [("RSTD", hg)], scale=1.0 / 128.0, bias=EPSC[(128, 1.0)])
            ACT(RS, RS, ACTF.Exp, [("RSTD", hg)], [("RSTD", hg)], scale=-0.5)
            STT(f2(OTp), f2(OTp), pcol(l, "dn_norm_w"), RS, ALU.mult, ALU.mult, bk("OT") + [("RSTD", hg), ("PP", None)], bk("OT"))
            TTo("dve", MIXR[:, 4 + 2 * hg:6 + 2 * hg, c0:c0 + 128], OTp, ZG[:, HS, c0:c0 + 128], ALU.mult,
                bk("OT") + [("ZG", None)], [("Y", (4 + h, t)) for h in heads])

        def delta_tile(l, t, is_s, nxt=None, first=False):
            if first:
                delta_prelude(l, t, is_s)
            gens = [delta_pair(l, t, is_s, 0), delta_pair(l, t, is_s, 1)]
            alive = [True, True]
            steps = [0, 0]
            pre_done = nxt is None
            while any(alive):
                for gi_, g in enumerate(gens):
                    if alive[gi_]:
                        try:
                            next(g)
                            steps[gi_] += 1
                        except StopIteration:
                            alive[gi_] = False
                if not pre_done and min(steps) >= 4:
                    delta_prelude(l, nxt[0], nxt[1])
                    pre_done = True
            if not pre_done:
                delta_prelude(l, nxt[0], nxt[1])

        for l in range(DEPTH):
            ffn(l, 1)
            mixer(l)
            ffn(l, 2)

        for (b0, bn) in blks:
            sumsq_rstd(X, "X", b0, bn, D)
            norm_scale(Y, "Y", X, "X", 0, "final_norm", b0, bn)
        for t in range(ntile):
            si = st["stg"]; st["stg"] ^= 1
            stg = stg_view(si)
            for half in range(2):
                pi = psum()
                for j in range(4):
                    kc = half * 4 + j
                    TR(PS[pi][:, j * 128:(j + 1) * 128], YF[:, kc, t * 128:(t + 1) * 128], ident, [("Y", (kc, t)), ("C32", None)], [("PS", pi)])
                CP("act" if half == 0 else "dve", stg[:, half * 512:(half + 1) * 512], PS[pi][:], [("PS", pi)], [stg_keys(si)[half]])
            dst = dr["yp"][p0 + t * 128: p0 + (t + 1) * 128, :] if t * 128 < npr else dr["ys"][:, :]
            out_ops.append(P.dma("sp", dst, stg, reads=stg_keys(si), writes=[("STGo", si)]))

    P.emit(out_ops)
    es.close()
    return nc, P


def make_in_maps(inp):
    pp = _pack_params(inp)
    c32, c16 = _consts()
    rgw = _pack_rgw(inp)
    maps = []
    shared = {"pp": pp, "c32": c32, "c16": c16, "rgw_r": rgw}
    for nm in ("ffn1_w_up", "ffn2_w_up", "ffn1_w_down", "ffn2_w_down", "w_in", "w_out"):
        shared[nm] = np.ascontiguousarray(inp[nm])
    for core in range(NCORES):
        sl = slice(core * NS, (core + 1) * NS)
        m = dict(shared)
        m["xp"] = np.ascontiguousarray(inp["x_prompt"][core])
        m["xs"] = np.ascontiguousarray(inp["x_sample"][sl].reshape(NS * DS, D))
        m["st_conv"] = np.ascontiguousarray(inp["state_conv"][:, sl].reshape(DEPTH, NS * 3, CONVC))
        m["st_rg"] = np.ascontiguousarray(inp["state_rglru"][:, sl])
        m["st_dn"] = np.ascontiguousarray(inp["state_delta"][:, sl])
        maps.append(m)
    return maps


def gather(r):
    y_prompt = np.stack([r[c]["yp"] for c in range(NCORES)], axis=0)
    y_sample = np.concatenate([r[c]["ys"].reshape(NS, DS, D) for c in range(NCORES)], axis=0)
    ncp = np.stack([r[c]["ncp"] for c in range(NCORES)], axis=1)
    nrp = np.stack([r[c]["nrp"] for c in range(NCORES)], axis=1)
    ndp = np.stack([r[c]["ndp"] for c in range(NCORES)], axis=1)
    ncs = np.concatenate([r[c]["ncs"].reshape(DEPTH, NS, 3, CONVC) for c in range(NCORES)], axis=1)
    nrs = np.concatenate([r[c]["nrs"] for c in range(NCORES)], axis=1)
    nds = np.concatenate([r[c]["nds"] for c in range(NCORES)], axis=1)
    return (y_prompt, y_sample, ncp, nrp, ndp, ncs, nrs, nds)


def kernel(**inp):
    inp = {k: np.asarray(v) for k, v in inp.items()}
    cfg = Cfg()
    nc, P = build_program(cfg)
    maps = make_in_maps(inp)
    res = run_bass_kernel_spmd(nc, maps, core_ids=list(range(NCORES)))
    return gather(res.results)
```

```python
import numpy as np
from contextlib import ExitStack
import concourse.bass as bass
import concourse.mybir as mybir
from concourse.bass_utils import run_bass_kernel_spmd

F32 = mybir.dt.float32
F32R = mybir.dt.float32r
BF16 = mybir.dt.bfloat16
ACTF = mybir.ActivationFunctionType
ALU = mybir.AluOpType

D = 1024
KC = 8
DFF = 2816
NFF = 22
DEPTH = 2
SEQ = 2048
NS = 16
DS = 8
INC = 3080
CONVC = 2048
EPS = 1e-6
NCORES = 8


class Op:
    __slots__ = ("eng", "fn", "deps", "sig", "count", "sem", "is_dma", "idx")

    def __init__(self, eng, fn, is_dma=False):
        self.eng = eng
        self.fn = fn
        self.deps = []
        self.sig = False
        self.count = None
        self.sem = None
        self.is_dma = is_dma
        self.idx = None


class Prog:
    ENGS = ("pe", "act", "dve", "pool", "sp")
    NDMASEM = 8

    def __init__(self, nc, same_engine_sync=True):
        self.nc = nc
        self.ops = {e: [] for e in self.ENGS}
        self.last_w = {}
        self.readers = {}
        self.same_engine_sync = same_engine_sync
        self.dma_n = {"sp": 0, "pool": 0}
        self.dma_hist = {"sp": [], "pool": []}
        self.n_ops = 0

    def _collect(self, op, reads, writes):
        deps = []
        for (n, i) in reads:
            lw = self.last_w.get(n)
            if lw:
                if i is None:
                    deps.extend(lw.values())
                else:
                    if i in lw:
                        deps.append(lw[i])
                    if None in lw:
                        deps.append(lw[None])
        for (n, i) in writes:
            lw = self.last_w.get(n)
            rd = self.readers.get(n)
            if lw:
                if i is None:
                    deps.extend(lw.values())
                else:
                    if i in lw:
                        deps.append(lw[i])
                    if None in lw:
                        deps.append(lw[None])
            if rd:
                if i is None:
                    for v in rd.values():
                        deps.extend(v)
                else:
                    deps.extend(rd.get(i, ()))
                    deps.extend(rd.get(None, ()))
        for (n, i) in writes:
            lw = self.last_w.setdefault(n, {})
            rd = self.readers.setdefault(n, {})
            if i is None:
                lw.clear()
                rd.clear()
                lw[None] = op
            else:
                lw[i] = op
                rd.pop(i, None)
        for (n, i) in reads:
            self.readers.setdefault(n, {}).setdefault(i, []).append(op)
        seen = set()
        for d in deps:
            if d is op or id(d) in seen:
                continue
            seen.add(id(d))
            if d.eng == op.eng and not d.is_dma and not op.is_dma:
                if op.eng == "pe" or not self.same_engine_sync:
                    continue
            op.deps.append(d)
            d.sig = True

    def op(self, eng, fn, reads=(), writes=()):
        o = Op(eng, fn)
        self._collect(o, list(reads), list(writes))
        self.ops[eng].append(o)
        self.n_ops += 1
        return o

    def dma(self, q, out, in_, reads=(), writes=()):
        o = Op(q, lambda e: e.dma_start(out=out, in_=in_), is_dma=True)
        n = self.dma_n[q]
        self.dma_n[q] += 1
        o.idx = n
        self._collect(o, list(reads), list(writes))
        hist = self.dma_hist[q]
        if n >= self.NDMASEM:
            o.deps.append(hist[n - self.NDMASEM])
        hist.append(o)
        o.sig = True
        self.ops[q].append(o)
        self.n_ops += 1
        return o

    def emit(self, final_wait_ops):
        nc = self.nc
        with ExitStack() as es:
            esem = {e: es.enter_context(nc.semaphore("prog_" + e)) for e in self.ENGS}
            dsem = {q: [es.enter_context(nc.semaphore("dma_%s_%d" % (q, i))) for i in range(self.NDMASEM)]
                    for q in ("sp", "pool")}
            for e in self.ENGS:
                c = 0
                for o in self.ops[e]:
                    if o.is_dma:
                        slot = o.idx % self.NDMASEM
                        o.sem = dsem[e][slot]
                        o.count = 16 * (o.idx // self.NDMASEM + 1)
                    elif o.sig:
                        c += 1
                        o.sem = esem[e]
                        o.count = c
            block = es.enter_context(nc.Block())

            def run(ename, e, extra_final=None):
                known = {}
                for o in self.ops[ename]:
                    need = {}
                    for d in o.deps:
                        key = id(d.sem)
                        if known.get(key, 0) >= d.count:
                            continue
                        if key not in need or need[key][1] < d.count:
                            need[key] = (d.sem, d.count)
                    for key, (s, v) in need.items():
                        e.wait_ge(s, v)
                        known[key] = v
                    ins = o.fn(e)
                    if o.is_dma:
                        ins.then_inc(o.sem, 16)
                    elif o.sig:
                        ins.then_inc(o.sem, 1)
                if extra_final:
                    need = {}
                    for d in extra_final:
                        key = id(d.sem)
                        if key not in need or need[key][1] < d.count:
                            need[key] = (d.sem, d.count)
                    for key, (s, v) in need.items():
                        e.wait_ge(s, v)

            @block.tensor
            def _(e):
                run("pe", e)

            @block.scalar
            def _(e):
                run("act", e)

            @block.vector
            def _(e):
                run("dve", e)

            @block.gpsimd
            def _(e):
                run("pool", e)

            @block.sync
            def _(e):
                run("sp", e, extra_final=final_wait_ops)


RGW = 512
NH = 4
HD = 128
NLV_P = 7
NLV_S = 3

C32 = {"ident": 0, "ones": 128, "tri_p": 256, "up_p": 384, "tri_s": 512, "up_s": 640}
N32 = 768
C16 = {"identb": 0, "mstrict_p": 128, "minclt_p": 256, "mstrict_s": 384, "minclt_s": 512}
for _i in range(NLV_P):
    C16["lv_p%d" % _i] = 640 + 128 * _i
C16["segcol"] = 640 + 128 * NLV_P
C16["segrow"] = C16["segcol"] + 16
N16 = C16["segrow"] + 16 * 128


def _consts():
    i = np.arange(128)[:, None]
    j = np.arange(128)[None, :]
    c32 = np.zeros((128, N32), np.float32)
    c32[:, 0:128] = np.eye(128)
    c32[:, 128:256] = 1.0
    seg = 8
    same_s = (i // seg) == (j // seg)
    c32[:, 256:384] = (i <= j)
    c32[:, 384:512] = (i > j)
    c32[:, 512:640] = (i <= j) & same_s
    c32[:, 640:768] = (i > j) & same_s
    c16 = np.zeros((128, N16), np.float32)
    c16[:, 0:128] = np.eye(128)
    c16[:, 128:256] = (i > j)
    c16[:, 256:384] = (j >= i)
    c16[:, 384:512] = (i > j) & same_s
    c16[:, 512:640] = (j >= i) & same_s
    for lv in range(NLV_P):
        b = 1 << lv
        m = ((i // (2 * b)) == (j // (2 * b))) & (((j // b) % 2) == 1) & (((i // b) % 2) == 0)
        c16[:, C16["lv_p%d" % lv]:C16["lv_p%d" % lv] + 128] = m
    c16[:, C16["segcol"]:C16["segcol"] + 16] = (np.arange(128)[:, None] // seg) == np.arange(16)[None, :]
    sr = (np.arange(16)[:, None] == (np.arange(128)[None, :] // seg)).astype(np.float32).reshape(1, 16 * 128)
    c16[:, C16["segrow"]:] = np.repeat(sr, 128, axis=0)
    return c32, c16


PL = {}
_o = 0
for _nm, _n in (("ffn1_norm_pre", 8), ("ffn1_norm_post", 8), ("mix_norm_pre", 8), ("mix_norm_post", 8),
                ("ffn2_norm_pre", 8), ("ffn2_norm_post", 8), ("final_norm", 8),
                ("conv_w", 64), ("conv_b_rg", 4), ("rg_b_a", 4), ("rg_b_x", 4), ("rg_lambda", 4),
                ("dn_a_log", 4), ("dn_dt_bias", 4), ("dn_norm_w", 1)):
    PL[_nm] = _o
    _o += _n
PP_LAYER = _o


def _pack_params(inp):
    cols = []

    def vecn(v, n):
        return np.ascontiguousarray(np.asarray(v).reshape(n, 128).T)

    for l in range(DEPTH):
        for nm in ("ffn1_norm_pre", "ffn1_norm_post", "mix_norm_pre", "mix_norm_post",
                   "ffn2_norm_pre", "ffn2_norm_post"):
            cols.append(vecn(inp[nm][l], 8))
        cols.append(vecn(inp["final_norm"], 8))
        cw = np.asarray(inp["conv_w"][l])
        cols.append(np.ascontiguousarray(cw.reshape(4, 16, 128).transpose(2, 1, 0).reshape(128, 64)))
        for nm in ("conv_b_rg", "rg_b_a", "rg_b_x", "rg_lambda"):
            cols.append(vecn(inp[nm][l], 4))
        cols.append(np.repeat(np.asarray(inp["dn_a_log"][l]).reshape(1, 4), 128, axis=0))
        cols.append(np.repeat(np.asarray(inp["dn_dt_bias"][l]).reshape(1, 4), 128, axis=0))
        cols.append(np.asarray(inp["dn_norm_w"][l]).reshape(128, 1))
    return np.ascontiguousarray(np.concatenate(cols, axis=1).astype(np.float32))


def _pack_rgw(inp):
    out = np.zeros((DEPTH, 2, 128, 4, 128), np.float32)
    for l in range(DEPTH):
        for gi, nm in enumerate(("rg_w_a", "rg_w_x")):
            w = np.asarray(inp[nm][l])
            for c in range(4):
                for hh in range(2):
                    out[l, gi, hh * 64:(hh + 1) * 64, c, hh * 64:(hh + 1) * 64] = w[2 * c + hh]
    return out


class Cfg:
    def __init__(self, **kw):
        self.groups = [(0, 640, True), (640, 768, False), (1408, 640, False)]
        self.stages = "full"
        self.same_engine_sync = True
        self.__dict__.update(kw)


def blocks_of(T):
    if T == 768:
        return [(0, 384), (384, 384)]
    if T == 640:
        return [(0, 384), (384, 256)]
    raise ValueError(T)


def build_program(cfg):
    nc = bass.Bass("TRN2", target_bir_lowering=False)
    TM = 768
    NTM = TM // 128
    dr = {}

    def din(name, shape, dt=F32):
        dr[name] = nc.dram_tensor(name, shape, dt, kind="ExternalInput").ap()

    def dout(name, shape):
        dr[name] = nc.dram_tensor(name, shape, F32, kind="ExternalOutput").ap()

    din("xp", [SEQ, D]); din("xs", [NS * DS, D])
    din("pp", [128, DEPTH * PP_LAYER]); din("c32", [128, N32]); din("c16", [128, N16])
    din("rgw_r", [DEPTH, 2, 128, 4, 128], F32R)
    din("st_conv", [DEPTH, NS * 3, CONVC]); din("st_rg", [DEPTH, NS, RGW]); din("st_dn", [DEPTH, NS, NH, HD, HD])
    for nm in ("ffn1_w_up", "ffn2_w_up"):
        din(nm, [DEPTH, D, 2 * DFF], F32R)
    for nm in ("ffn1_w_down", "ffn2_w_down"):
        din(nm, [DEPTH, DFF, D], F32R)
    din("w_in", [DEPTH, D, INC], F32R); din("w_out", [DEPTH, D, D], F32R)
    dout("yp", [SEQ, D]); dout("ys", [NS * DS, D])
    dout("ncp", [DEPTH, 3, CONVC]); dout("nrp", [DEPTH, RGW]); dout("ndp", [DEPTH, NH, HD, HD])
    if getattr(cfg, "debug", False):
        dout("dbg", [3, 128, KC, TM])
    dout("ncs", [DEPTH, NS * 3, CONVC]); dout("nrs", [DEPTH, NS, RGW]); dout("nds", [DEPTH, NS, NH, HD, HD])

    P = Prog(nc, same_engine_sync=cfg.same_engine_sync)
    es = ExitStack()
    sb = lambda name, shape, dt: es.enter_context(nc.sbuf_tensor(name, shape, dt))
    X = sb("X", [128, KC, TM], F32)
    H = sb("H", [128, KC, TM], F32R)
    Y = sb("Y", [128, KC, TM], F32R)
    YF = Y[:].bitcast(F32)
    HF = H[:].bitcast(F32)
    NFB = 4
    ACTB = sb("ACTB", [128, NFB, TM], F32R)
    NWA = 4
    WA = [sb("WA%d" % i, [128, KC, 128], F32R) for i in range(NWA)]
    NWB = 4
    WB = [sb("WB%d" % i, [128, NFB, 128], F32R) for i in range(NWB)]
    PPt = sb("PPt", [128, DEPTH * PP_LAYER], F32)
    C32t = sb("C32t", [128, N32], F32)
    C16t = sb("C16t", [128, N16], BF16)
    ONESR = sb("ONESR", [128, 128], F32R)
    EPST = sb("EPST", [128, 8], F32)
    TT = sb("TT", [128, 4, TM], F32)
    SQ = sb("SQ", [128, 4, 384], F32R)
    RSTD = sb("RSTD", [128, 512], F32)
    SG = [TT[:, 0, 0:384], TT[:, 1, 0:384]]
    ZG = sb("ZG", [128, 4, TM], BF16)
    QKV = sb("QKV", [128, 12, TM], BF16)
    XR = sb("XR", [128, TM], F32R)
    XP = sb("XP", [128, 3 + TM], F32R)
    XPS = sb("XPS", [128, NS, 11], F32R)
    XPf = XP[:].bitcast(F32)
    XPSf = XPS[:].bitcast(F32)
    DG = sb("DG", [128, 4, 128], F32R)
    CSS = sb("CSS", [128, 16, NS * 3], F32)
    HS0 = sb("HS0", [128, 4, NS], F32)
    CONVT = sb("CONVT", [128, DEPTH, 16, 3], F32)
    HRG = sb("HRG", [128, DEPTH, 4], F32)
    SST = sb("SST", [128, DEPTH, NH, HD], F32)
    RGWt = sb("RGWt", [128, 2, 4, 128], F32R)
    WSC = sb("WSC", [128, KC, 8], F32R)
    SCT = sb("SCT", [128, NTM, 8], F32)
    LAYC = sb("LAYC", [128, 16], F32)
    ONE1 = sb("ONE1", [128, 1], F32)
    FEN = sb("FEN", [128, 2], F32)
    dt_names_bf = ["KTOK", "VTOK", "AM", "ATT", "DT", "DINV", "N1", "BV", "BKG", "WT", "VN", "QG", "KD", "SBF"]
    DB = {n: sb(n, [128, NH, 128], BF16) for n in dt_names_bf if n not in ("VN", "SBF", "WT")}
    DB["VN"] = DB["N1"]; DB["SBF"] = DB["DINV"]; DB["WT"] = DB["AM"]
    GRW = sb("GRW", [128, NH, 128], F32)
    EGR2 = sb("EGR", [128, 2, NH, 128], F32)
    TRG = sb("TRG", [128, NH, 128], F32)
    OT = sb("OT", [128, NH, 128], F32)
    SM2 = sb("SM", [128, 2, 64], F32)
    CSL = TT[0:48, 3, 0:512]
    RSL = TT[0:16, 3, 0:512]
    PS = [es.enter_context(nc.psum_tensor("PS%d" % i, [128, 512], F32)) for i in range(8)]

    def c32(nm):
        return C32t[:, C32[nm]:C32[nm] + 128]

    def c16(nm, n=128):
        return C16t[:, C16[nm]:C16[nm] + n]

    ident = c32("ident")
    ones_f = c32("ones")
    ones_r = ONESR[:, :]
    identb = c16("identb")
    st = {"ps": 0, "wa": 0, "wb": 0, "stg": 0, "sg": 0}

    resv = set()

    def psum():
        while True:
            i = st["ps"]
            st["ps"] = (i + 1) % 8
            if i not in resv:
                return i

    def psb(pi):
        return PS[pi][:].bitcast(BF16)

    def ACT(out, in_, func, reads, writes, **kw):
        return P.op("act", lambda e: e.activation(out=out, in_=in_, func=func, **kw), reads, writes)

    def CP(eng, out, in_, reads, writes):
        if eng == "act":
            return P.op("act", lambda e: e.copy(out=out, in_=in_), reads, writes)
        return P.op(eng, lambda e: e.tensor_copy(out=out, in_=in_), reads, writes)

    def TTo(eng, out, in0, in1, op, reads, writes):
        return P.op(eng, lambda e: e.tensor_tensor(out=out, in0=in0, in1=in1, op=op), reads, writes)

    def TS(eng, out, in0, s1, s2, op0, op1, reads, writes):
        if op1 is None:
            return P.op(eng, lambda e: e.tensor_scalar(out=out, in0=in0, scalar1=s1, scalar2=None, op0=op0), reads, writes)
        return P.op(eng, lambda e: e.tensor_scalar(out=out, in0=in0, scalar1=s1, scalar2=s2, op0=op0, op1=op1), reads, writes)

    def STT(out, in0, scalar, in1, op0, op1, reads, writes):
        return P.op("dve", lambda e: e.scalar_tensor_tensor(out=out, in0=in0, scalar=scalar, in1=in1, op0=op0, op1=op1), reads, writes)

    def MM(out, lhsT, rhs, start, stop, reads, writes):
        return P.op("pe", lambda e: e.matmul(out, lhsT=lhsT, rhs=rhs, start=start, stop=stop, skip_group_check=True), reads, writes)

    def TR(out, in_, idn, reads, writes):
        return P.op("pe", lambda e: e.transpose(out=out, in_=in_, identity=idn), reads, writes)

    class WStream:
        def __init__(self, name, bufs):
            self.name, self.bufs, self.n = name, bufs, len(bufs)
            self.reqs = []
            self.issued = 0
            self.cur = 0

        def add(self, fn):
            self.reqs.append(fn)

        def next(self):
            i = self.cur
            self.cur += 1
            upto = min(len(self.reqs), i + self.n - 1)
            while self.issued < upto:
                k = self.issued
                slot = k % self.n
                out_ap, in_ap = self.reqs[k](self.bufs[slot])
                P.dma("pool", out_ap, in_ap, writes=[(self.name, slot)])
                self.issued += 1
            return i % self.n

    WAS = WStream("WA", WA)
    WBS = WStream("WB", WB)
    FBLOCKS = [(0, 4), (4, 4), (8, 4), (12, 4), (16, 4), (20, 2)]

    def plan_cols(src, col):
        v = src.rearrange("(kc p) n -> p kc n", p=128)
        WAS.add(lambda buf, v=v, col=col: (buf[:], v[:, :, col:col + 128]))

    def plan_ffn(l, which):
        wup = dr["ffn%d_w_up" % which][l]
        wdn = dr["ffn%d_w_down" % which][l].rearrange("(c p) n -> p c n", p=128)
        for (c0, nch) in FBLOCKS:
            for j in range(nch):
                plan_cols(wup, (c0 + j) * 128)
                plan_cols(wup, DFF + (c0 + j) * 128)
            for oc in range(8):
                WBS.add(lambda buf, c0=c0, nch=nch, oc=oc, wdn=wdn: (buf[:, 0:nch, :], wdn[:, c0:c0 + nch, oc * 128:(oc + 1) * 128]))

    def plan_mixer(l):
        win = dr["w_in"][l]
        for ch in range(4):
            plan_cols(win, ch * 128)
            plan_cols(win, 2048 + ch * 128)
        for kind in range(3):
            for hh in range(4):
                plan_cols(win, 512 + kind * 512 + hh * 128)
        for hh in range(4):
            plan_cols(win, 2560 + hh * 128)
        for oc in range(8):
            plan_cols(dr["w_out"][l], oc * 128)

    for _g in cfg.groups:
        for l in range(DEPTH):
            plan_ffn(l, 1)
            plan_mixer(l)
            plan_ffn(l, 2)

    P.dma("sp", PPt[:], dr["pp"][:, :], writes=[("PP", None)])
    P.dma("sp", C32t[:], dr["c32"][:, :], writes=[("C32", None)])
    for i in range(0, N16, 768):
        n = min(768, N16 - i)
        P.dma("sp", TT[:, 0, 0:n], dr["c16"][:, i:i + n], writes=[("TT", 0)])
        CP("dve", C16t[:, i:i + n], TT[:, 0, 0:n], [("TT", 0)], [("C16", None)])
    CP("dve", ones_r, ones_f, [("C32", None)], [("ONESR", None)])
    EPSC = {}
    for i, (key, val) in enumerate([((D, 1.0), EPS), ((D, 0.5), 4.0 * EPS), ((128, 1.0), EPS), ((1, 1.0), EPS), ("q", 128.0 * EPS)]):
        EPSC[key] = EPST[:, i:i + 1]
        P.op("dve", lambda e, i=i, val=val: e.memset(EPST[:, i:i + 1], val), writes=[("EPST", i)])
    P.op("dve", lambda e: e.memset(ONE1[:, :], 1.0), writes=[("ONE1", None)])
    P.op("dve", lambda e: e.memset(CONVT[:].rearrange("p l c t -> p (l c t)"), 0.0), writes=[("CONVT", None)])
    P.op("dve", lambda e: e.memset(HRG[:].rearrange("p l c -> p (l c)"), 0.0), writes=[("HRG", None)])
    P.op("dve", lambda e: e.memset(SST[:].rearrange("p l h e -> p (l h e)"), 0.0), writes=[("SST", None)])

    out_ops = []

    def pcol(l, nm, j=0, n=1):
        c = l * PP_LAYER + PL[nm] + j
        return PPt[:, c:c + n]

    ngroups = len(cfg.groups)
    for gi, (p0, npr, smp) in enumerate(cfg.groups):
        T = npr + (128 if smp else 0)
        ntile = T // 128
        blks = blocks_of(T)
        last_group = (gi == ngroups - 1)

        def tiles_of(b0, bn):
            return list(range(b0 // 128, (b0 + bn + 127) // 128))

        def rk(name, kcs, b0, bn):
            if isinstance(kcs, int):
                kcs = [kcs]
            return [(name, (kc, t)) for kc in kcs for t in tiles_of(b0, bn)]

        def stg_view(si):
            return TT[:, 2 * si:2 * si + 2, :].rearrange("p a t -> p (a t)")[:, 0:D]

        def stg_keys(si):
            return [("TT", 2 * si), ("TT", 2 * si + 1)]

        for t in range(ntile):
            si = st["stg"]; st["stg"] ^= 1
            stg = stg_view(si)
            src = dr["xp"][p0 + t * 128: p0 + (t + 1) * 128, :] if t * 128 < npr else dr["xs"][:, :]
            P.dma("sp", stg, src, writes=stg_keys(si))
            for half in range(2):
                pi = psum()
                for j in range(4):
                    kc = half * 4 + j
                    TR(PS[pi][:, j * 128:(j + 1) * 128], stg[:, kc * 128:(kc + 1) * 128], ident,
                       stg_keys(si) + [("C32", None)], [("PS", pi)])
                CP("act" if half == 0 else "dve", X[:, half * 4:(half + 1) * 4, t * 128:(t + 1) * 128],
                   PS[pi][:].rearrange("p (j c) -> p j c", j=4), [("PS", pi)], [("X", (half * 4 + j, t)) for j in range(4)])

        def sumsq_rstd(SRC, srcname, b0, bn, nfeat, post_scale=1.0, nk=KC):
            pi = psum()
            for kc in range(nk):
                rd = rk(srcname, kc, b0, bn)
                ACT(SQ[:, kc % 4, 0:bn], SRC[:, kc, b0:b0 + bn], ACTF.Square, rd, [("SQ", kc % 4)])
                MM(PS[pi][:, 0:bn], ones_r, SQ[:, kc % 4, 0:bn], kc == 0, kc == nk - 1,
                   [("SQ", kc % 4), ("ONESR", None)], [("PS", pi)])
            return rstd_from_psum(pi, bn, nfeat, post_scale)

        def rstd_from_psum(pi, bn, nfeat, post_scale=1.0):
            ACT(PS[pi][:, 0:bn], PS[pi][:, 0:bn], ACTF.Ln, [("PS", pi), ("EPST", None)], [("PS", pi)],
                scale=1.0 / (nfeat * post_scale * post_scale), bias=EPSC[(nfeat, post_scale)])
            ACT(PS[pi][:, 0:bn], PS[pi][:, 0:bn], ACTF.Exp, [("PS", pi)], [("PS", pi)], scale=-0.5)
            return pi

        def norm_scale(DST, dstname, SRC, srcname, l, gname, b0, bn, pi):
            for kc in range(KC):
                STT(DST[:, kc, b0:b0 + bn], SRC[:, kc, b0:b0 + bn], pcol(l, gname, kc), PS[pi][:, 0:bn], ALU.mult, ALU.mult,
                    rk(srcname, kc, b0, bn) + [("PS", pi), ("PP", None)], rk(dstname, kc, b0, bn))

        def prenorm_to_H(l, gname):
            for (b0, bn) in blks:
                pi = sumsq_rstd(X, "X", b0, bn, D)
                norm_scale(H, "H", X, "X", l, gname, b0, bn, pi)

        def postnorm_residual(SRCw, SRC, srcname, l, gname, scale):
            for (b0, bn) in blks:
                pi = sumsq_rstd(SRC, srcname, b0, bn, D, post_scale=scale)
                norm_scale(SRCw, srcname, SRC, srcname, l, gname, b0, bn, pi)
                for kc in range(KC):
                    TTo("dve", X[:, kc, b0:b0 + bn], X[:, kc, b0:b0 + bn], SRC[:, kc, b0:b0 + bn], ALU.add,
                        rk(srcname, kc, b0, bn) + rk("X", kc, b0, bn), rk("X", kc, b0, bn))

        def ffn(l, which):
            prenorm_to_H(l, "ffn%d_norm_pre" % which)
            for fbi, (c0, nch) in enumerate(FBLOCKS):
                for j in range(nch):
                    wi_g = WAS.next()
                    pg = [psum() for _ in blks]
                    for kc in range(KC):
                        for bi, (b0, bn) in enumerate(blks):
                            MM(PS[pg[bi]][:, 0:bn], WA[wi_g][:, kc, :], H[:, kc, b0:b0 + bn],
                               kc == 0, kc == KC - 1, [("WA", wi_g)] + rk("H", kc, b0, bn), [("PS", pg[bi])])
                    sgs = []
                    for bi, (b0, bn) in enumerate(blks):
                        si = st["sg"]; st["sg"] = (st["sg"] + 1) % len(SG)
                        sgs.append(si)
                        ACT(SG[si][:, 0:bn], PS[pg[bi]][:, 0:bn], ACTF.Silu, [("PS", pg[bi])], [("TT", si)])
                    wi_u = WAS.next()
                    pu = [psum() for _ in blks]
                    for kc in range(KC):
                        for bi, (b0, bn) in enumerate(blks):
                            MM(PS[pu[bi]][:, 0:bn], WA[wi_u][:, kc, :], H[:, kc, b0:b0 + bn],
                               kc == 0, kc == KC - 1, [("WA", wi_u)] + rk("H", kc, b0, bn), [("PS", pu[bi])])
                    for bi, (b0, bn) in enumerate(blks):
                        TTo("dve", ACTB[:, j, b0:b0 + bn], PS[pu[bi]][:, 0:bn], SG[sgs[bi]][:, 0:bn], ALU.mult,
                            [("PS", pu[bi]), ("TT", sgs[bi])], [("ACTB", (j, b0))])
                for oc in range(8):
                    wi = WBS.next()
                    pd = [psum() for _ in blks]
                    for j in range(nch):
                        for bi, (b0, bn) in enumerate(blks):
                            MM(PS[pd[bi]][:, 0:bn], WB[wi][:, j, :], ACTB[:, j, b0:b0 + bn],
                               j == 0, j == nch - 1, [("WB", wi), ("ACTB", (j, b0))], [("PS", pd[bi])])
                    for bi, (b0, bn) in enumerate(blks):
                        if fbi == 0:
                            CP("act", Y[:, oc, b0:b0 + bn], PS[pd[bi]][:, 0:bn], [("PS", pd[bi])], rk("Y", oc, b0, bn))
                        else:
                            TTo("dve", Y[:, oc, b0:b0 + bn], PS[pd[bi]][:, 0:bn], YF[:, oc, b0:b0 + bn], ALU.add,
                                [("PS", pd[bi])] + rk("Y", oc, b0, bn), rk("Y", oc, b0, bn))
            postnorm_residual(Y, YF, "Y", l, "ffn%d_norm_post" % which, 0.5)

        MIXR = Y
        allH = [("H", (kc, t)) for kc in range(KC) for t in range(NTM)]
        allACTB = [("ACTB", None)]

        def mixer(l):
            win = dr["w_in"][l].rearrange("(kc p) n -> p kc n", p=128)
            prenorm_to_H(l, "mix_norm_pre")
            ACT(LAYC[:, 0:4], pcol(l, "rg_lambda", 0, 4), ACTF.Exp, [("PP", None)], [("LAYC", 0)], scale=-1.0)
            ACT(LAYC[:, 0:4], LAYC[:, 0:4], ACTF.Ln, [("LAYC", 0), ("ONE1", None)], [("LAYC", 0)], bias=ONE1[:, 0:1])
            TS("dve", LAYC[:, 4:8], LAYC[:, 0:4], -16.0, None, ALU.mult, None, [("LAYC", 0)], [("LAYC", 1)])
            TS("dve", LAYC[:, 0:4], LAYC[:, 0:4], -8.0, None, ALU.mult, None, [("LAYC", 0), ("LAYC", 1)], [("LAYC", 0)])
            ACT(LAYC[:, 8:12], pcol(l, "dn_a_log", 0, 4), ACTF.Exp, [("PP", None)], [("LAYC", 2)])
            TS("dve", LAYC[:, 8:12], LAYC[:, 8:12], -1.0, None, ALU.mult, None, [("LAYC", 2)], [("LAYC", 2)])
            P.dma("pool", RGWt[:], dr["rgw_r"][l].rearrange("g p c m -> p g c m"), writes=[("RGW", None)])
            P.dma("pool", WSC[:], win[:, :, 3072:3080], writes=[("WSC", None)])
            if smp:
                for q in range(4):
                    P.dma("sp", CSL, dr["st_conv"][l][:, q * 512:(q + 1) * 512], writes=[("TT", 3)])
                    pi = psum()
                    for j in range(4):
                        TR(PS[pi][:, j * 48:(j + 1) * 48], CSL[:, j * 128:(j + 1) * 128], ident[0:48, 0:48],
                           [("TT", 3), ("C32", None)], [("PS", pi)])
                    CP("dve", CSS[:, q * 4:(q + 1) * 4, :], PS[pi][:, 0:192].rearrange("p (j c) -> p j c", j=4),
                       [("PS", pi)], [("CSS", q * 4 + j) for j in range(4)])
                P.dma("sp", RSL, dr["st_rg"][l][:, :], writes=[("TT", 3)])
                pi = psum()
                for j in range(4):
                    TR(PS[pi][:, j * 16:(j + 1) * 16], RSL[:, j * 128:(j + 1) * 128], ident[0:16, 0:16],
                       [("TT", 3), ("C32", None)], [("PS", pi)])
                CP("dve", HS0[:, :, :], PS[pi][:, 0:64].rearrange("p (j c) -> p j c", j=4), [("PS", pi)], [("HS0", None)])

            wa_state = {}

            def proj():
                wi = WAS.next()
                pb = [psum() for _ in blks]
                for kc in range(KC):
                    for bi, (b0, bn) in enumerate(blks):
                        MM(PS[pb[bi]][:, 0:bn], WA[wi][:, kc, :], H[:, kc, b0:b0 + bn],
                           kc == 0, kc == KC - 1, [("WA", wi)] + rk("H", kc, b0, bn), [("PS", pb[bi])])
                return pb

            def conv_chunk(l, ch, pb):
                CP("act", XP[:, 0:3], CONVT[:, l, ch, :], [("CONVT", (l, ch))], [("XP", 0)])
                for bi, (b0, bn) in enumerate(blks):
                    n = min(bn, npr - b0)
                    if n > 0:
                        CP("dve" if bi == 0 else "act", XP[:, 3 + b0:3 + b0 + n], PS[pb[bi]][:, 0:n], [("PS", pb[bi])], [("XP", 1 + bi)])
                if smp:
                    b0, bn = blks[-1]
                    off = npr - b0
                    CP("dve", XPS[:, :, 0:3], CSS[:, ch, :].rearrange("p (s t) -> p s t", t=3), [("CSS", ch)], [("XPS", 0)])
                    CP("act", XPS[:, :, 3:11], PS[pb[-1]][:, off:off + 128].rearrange("p (s t) -> p s t", t=8),
                       [("PS", pb[-1])], [("XPS", 1)])
                for tap in range(4):
                    TS("dve", DG[:, tap, :], ident, pcol(l, "conv_w", ch * 4 + tap), None, ALU.mult, None,
                       [("C32", None), ("PP", None)], [("DG", tap)])
                xpk = [("XP", i) for i in range(1 + len(blks))]
                pc = [psum() for _ in blks]
                for bi, (b0, bn) in enumerate(blks):
                    n = min(bn, npr - b0)
                    for tap in range(4):
                        MM(PS[pc[bi]][:, 0:n], DG[:, tap, :], XP[:, b0 + tap:b0 + tap + n], tap == 0, tap == 3,
                           xpk + [("DG", tap)], [("PS", pc[bi])])
                CP("dve", CONVT[:, l, ch, :], XPf[:, npr:npr + 3], xpk, [("CONVT", (l, ch))])
                if smp:
                    b0, bn = blks[-1]
                    off = npr - b0
                    for tap in range(4):
                        MM(PS[pc[-1]][:, off:off + 128].rearrange("p (s t) -> p s t", t=8), DG[:, tap, :], XPS[:, :, tap:tap + 8],
                           False, tap == 3, [("XPS", 0), ("XPS", 1), ("DG", tap)], [("PS", pc[-1])])
                    CP("dve", CSS[:, ch, :].rearrange("p (s t) -> p s t", t=3), XPSf[:, :, 8:11], [("XPS", 0), ("XPS", 1)], [("CSS", ch)])
                return pc

            T0 = TT[:, 0, :]; T1 = TT[:, 1, :]; T2 = TT[:, 2, :]
            XRf = XR[:].bitcast(F32)
            k0, k1, k2, k3 = ("TT", 0), ("TT", 1), ("TT", 2), ("XR", None)

            nb_ = len(blks)
            allrow = [("TT", i) for i in range(4)] + [("XR", None)]
            allblk = [("TT", (i, bi)) for i in range(3) for bi in range(nb_)] + [("XR", bi) for bi in range(nb_)]
            P.op("dve", lambda e: e.memset(FEN[:, 0:1], 0.0), allrow + allblk, allrow + allblk)
            kq = lambda i, bi: ("TT", (i, bi))
            kx = lambda bi: ("XR", bi)
            for ch in range(4):
                pb = proj()
                pc = conv_chunk(l, ch, pb)
                for bi, (b0, bn) in enumerate(blks):
                    ACT(XR[:, b0:b0 + bn], PS[pc[bi]][:, 0:bn], ACTF.Identity, [("PS", pc[bi]), ("PP", None), kx(bi)], [kx(bi)],
                        bias=pcol(l, "conv_b_rg", ch))
                pa = [psum() for _ in blks]
                for bi, (b0, bn) in enumerate(blks):
                    MM(PS[pa[bi]][:, 0:bn], RGWt[:, 0, ch, :], XR[:, b0:b0 + bn], True, True, [("RGW", None), kx(bi)], [("PS", pa[bi])])
                px = [psum() for _ in blks]
                for bi, (b0, bn) in enumerate(blks):
                    MM(PS[px[bi]][:, 0:bn], RGWt[:, 1, ch, :], XR[:, b0:b0 + bn], True, True, [("RGW", None), kx(bi)], [("PS", px[bi])])
                pgt = proj()
                for p_ in pgt:
                    resv.add(p_)
                for bi, (b0, bn) in enumerate(blks):
                    ACT(T0[:, b0:b0 + bn], PS[pa[bi]][:, 0:bn], ACTF.Sigmoid, [("PS", pa[bi]), ("PP", None), kq(0, bi)], [kq(0, bi)],
                        bias=pcol(l, "rg_b_a", ch))
                for bi, (b0, bn) in enumerate(blks):
                    ACT(T1[:, b0:b0 + bn], PS[px[bi]][:, 0:bn], ACTF.Sigmoid, [("PS", px[bi]), ("PP", None), kq(1, bi)], [kq(1, bi)],
                        bias=pcol(l, "rg_b_x", ch))
                for bi, (b0, bn) in enumerate(blks):
                    sl = slice(b0, b0 + bn)
                    ACT(T2[:, sl], T0[:, sl], ACTF.Exp, [kq(0, bi), ("LAYC", 1), kq(2, bi)], [kq(2, bi)], scale=LAYC[:, 4 + ch:5 + ch])
                for bi, (b0, bn) in enumerate(blks):
                    sl = slice(b0, b0 + bn)
                    ACT(T0[:, sl], T0[:, sl], ACTF.Exp, [kq(0, bi), ("LAYC", 0)], [kq(0, bi)], scale=LAYC[:, ch:ch + 1])
                for bi, (b0, bn) in enumerate(blks):
                    sl = slice(b0, b0 + bn)
                    TS("dve", T2[:, sl], T2[:, sl], -1.0, 1.0, ALU.mult, ALU.add, [kq(2, bi)], [kq(2, bi)])
                    TS("dve", T2[:, sl], T2[:, sl], 1e-30, None, ALU.max, None, [kq(2, bi)], [kq(2, bi)])
                    TTo("dve", T1[:, sl], T1[:, sl], XRf[:, sl], ALU.mult, [kq(1, bi), kx(bi)], [kq(1, bi)])
                for bi, (b0, bn) in enumerate(blks):
                    sl = slice(b0, b0 + bn)
                    ACT(T2[:, sl], T2[:, sl], ACTF.Ln, [kq(2, bi)], [kq(2, bi)])
                    ACT(T2[:, sl], T2[:, sl], ACTF.Exp, [kq(2, bi)], [kq(2, bi)], scale=0.5)
                for bi, (b0, bn) in enumerate(blks):
                    sl = slice(b0, b0 + bn)
                    TTo("dve", T1[:, sl], T1[:, sl], T2[:, sl], ALU.mult, [kq(1, bi), kq(2, bi)], [kq(1, bi)])
                for bi, (b0, bn) in enumerate(blks):
                    n = min(bn, npr - b0)
                    if n <= 0:
                        continue
                    init = HRG[:, l, ch:ch + 1] if bi == 0 else T2[:, b0 - 1:b0]
                    extra = [("HRG", (l, ch))] if bi == 0 else [kq(2, bi - 1)]
                    P.op("dve", lambda e, b0=b0, n=n, init=init: e.tensor_tensor_scan(
                        out=T2[:, b0:b0 + n], data0=T0[:, b0:b0 + n], data1=T1[:, b0:b0 + n],
                        initial=init, op0=ALU.mult, op1=ALU.add),
                        [kq(0, bi), kq(1, bi), kq(2, bi)] + extra, [kq(2, bi)])
                lastb = max(bi for bi, (b0, bn) in enumerate(blks) if npr > b0)
                CP("dve", HRG[:, l, ch:ch + 1], T2[:, npr - 1:npr], [kq(2, lastb)], [("HRG", (l, ch))])
                if smp:
                    sb_ = nb_ - 1
                    a_first = T0[:, npr:npr + 128:8]
                    b_first = T1[:, npr:npr + 128:8]
                    TTo("dve", SM2[:, 0, 0:16], a_first, HS0[:, ch, :], ALU.mult, [kq(0, sb_), ("HS0", None)], [("SMr", None)])
                    TTo("dve", b_first, b_first, SM2[:, 0, 0:16], ALU.add, [kq(1, sb_), ("SMr", None)], [kq(1, sb_)])
                    TS("dve", a_first, a_first, 0.0, None, ALU.mult, None, [kq(0, sb_), ("SMr", None)], [kq(0, sb_)])
                    P.op("dve", lambda e: e.tensor_tensor_scan(out=T2[:, npr:npr + 128], data0=T0[:, npr:npr + 128],
                                                              data1=T1[:, npr:npr + 128], initial=0.0, op0=ALU.mult, op1=ALU.add),
                         [kq(0, sb_), kq(1, sb_), kq(2, sb_)], [kq(2, sb_)])
                    CP("dve", HS0[:, ch, :], T2[:, npr + 7:npr + 128:8], [kq(2, sb_), ("SMr", None)], [("HS0", None)])
                for bi, (b0, bn) in enumerate(blks):
                    sl = slice(b0, b0 + bn)
                    CP("act", T0[:, sl], PS[pgt[bi]][:, 0:bn], [("PS", pgt[bi]), kq(0, bi)], [kq(0, bi)])
                    ACT(T1[:, sl], PS[pgt[bi]][:, 0:bn], ACTF.Square, [("PS", pgt[bi]), kq(1, bi)], [kq(1, bi)])
                    resv.discard(pgt[bi])
                for bi, (b0, bn) in enumerate(blks):
                    sl = slice(b0, b0 + bn)
                    TS("dve", T1[:, sl], T1[:, sl], 0.044715, 1.0, ALU.mult, ALU.add, [kq(1, bi)], [kq(1, bi)])
                    TTo("dve", T1[:, sl], T1[:, sl], T0[:, sl], ALU.mult, [kq(0, bi), kq(1, bi)], [kq(1, bi)])
                for bi, (b0, bn) in enumerate(blks):
                    sl = slice(b0, b0 + bn)
                    ACT(T1[:, sl], T1[:, sl], ACTF.Sigmoid, [kq(1, bi)], [kq(1, bi)], scale=2.0 * 0.7978845608028654)
                for bi, (b0, bn) in enumerate(blks):
                    sl = slice(b0, b0 + bn)
                    TTo("dve", T0[:, sl], T0[:, sl], T1[:, sl], ALU.mult, [kq(0, bi), kq(1, bi)], [kq(0, bi)])
                    TTo("dve", MIXR[:, ch, sl], T2[:, sl], T0[:, sl], ALU.mult, [kq(0, bi), kq(2, bi)], rk("Y", ch, b0, bn))
            P.op("dve", lambda e: e.memset(FEN[:, 0:1], 0.0), allrow + allblk, allrow + allblk)

            TB = [(T0, k0), (T1, k1)]

            def qk_front(kind, hh, slot):
                ch = 4 + kind * 4 + hh
                Tb, kb = TB[slot]
                pb = proj()
                pc = conv_chunk(l, ch, pb)
                for bi, (b0, bn) in enumerate(blks):
                    ACT(Tb[:, b0:b0 + bn], PS[pc[bi]][:, 0:bn], ACTF.Silu, [("PS", pc[bi]), kb], [kb])
                for bi, (b0, bn) in enumerate(blks):
                    sq = slot * 2 + bi
                    TTo("dve", SQ[:, sq, 0:bn], Tb[:, b0:b0 + bn], Tb[:, b0:b0 + bn], ALU.mult, [kb], [("SQ", sq)])

            def qk_tail(kind, hh, slot):
                Tb, kb = TB[slot]
                for bi, (b0, bn) in enumerate(blks):
                    sq = slot * 2 + bi
                    pi = psum()
                    MM(PS[pi][:, 0:bn], ones_r, SQ[:, sq, 0:bn], True, True, [("SQ", sq), ("ONESR", None)], [("PS", pi)])
                    if kind == 0:
                        ACT(PS[pi][:, 0:bn], PS[pi][:, 0:bn], ACTF.Ln, [("PS", pi), ("EPST", None)], [("PS", pi)],
                            scale=128.0, bias=EPSC["q"])
                    else:
                        ACT(PS[pi][:, 0:bn], PS[pi][:, 0:bn], ACTF.Ln, [("PS", pi), ("EPST", None)], [("PS", pi)],
                            scale=1.0, bias=EPSC[(1, 1.0)])
                    ACT(PS[pi][:, 0:bn], PS[pi][:, 0:bn], ACTF.Exp, [("PS", pi)], [("PS", pi)], scale=-0.5)
                    TTo("dve", QKV[:, kind * 4 + hh, b0:b0 + bn], Tb[:, b0:b0 + bn], PS[pi][:, 0:bn], ALU.mult,
                        [kb, ("PS", pi)], [("QKV", None)])

            for kind in range(2):
                for hp in range(2):
                    qk_front(kind, 2 * hp, 0)
                    qk_front(kind, 2 * hp + 1, 1)
                    qk_tail(kind, 2 * hp, 0)
                    qk_tail(kind, 2 * hp + 1, 1)
            for hh in range(4):
                pb = proj()
                pc = conv_chunk(l, 12 + hh, pb)
                for bi, (b0, bn) in enumerate(blks):
                    ACT(QKV[:, 8 + hh, b0:b0 + bn], PS[pc[bi]][:, 0:bn], ACTF.Silu, [("PS", pc[bi])], [("QKV", None)])

            for cpair in range(2):
                for cc in range(2):
                    hh = cpair * 2 + cc
                    pb = proj()
                    for bi, (b0, bn) in enumerate(blks):
                        ACT(ZG[:, hh, b0:b0 + bn], PS[pb[bi]][:, 0:bn], ACTF.Silu, [("PS", pb[bi])], [("ZG", (hh, bi))])

            pi = psum()
            for t in range(ntile):
                for kc in range(KC):
                    MM(PS[pi][:, t * 8:(t + 1) * 8], H[:, kc, t * 128:(t + 1) * 128], WSC[:, kc, :], (t == 0 and kc == 0), kc == KC - 1,
                       [("WSC", None), ("H", (kc, t))], [("PS", pi)])
            CP("dve", SCT[:, 0:ntile, :], PS[pi][:, 0:ntile * 8].rearrange("p (t c) -> p t c", c=8), [("PS", pi)], [("SCT", None)])

            for t in range(ntile):
                nxt = (t + 1, smp and t + 1 == ntile - 1) if t + 1 < ntile else None
                delta_tile(l, t, smp and t == ntile - 1, nxt=nxt, first=(t == 0))

            if getattr(cfg, "debug", False) and l == 0:
                out_ops.append(P.dma("sp", dr["dbg"][gi], YF, reads=[("Y", None)], writes=[("DBG", gi)]))
            for ocp in range(4):
                for o2 in range(2):
                    oc = ocp * 2 + o2
                    wi = WAS.next()
                    pd = [psum() for _ in blks]
                    for kc in range(KC):
                        for bi, (b0, bn) in enumerate(blks):
                            MM(PS[pd[bi]][:, 0:bn], WA[wi][:, kc, :], MIXR[:, kc, b0:b0 + bn],
                               kc == 0, kc == KC - 1, [("WA", wi)] + rk("Y", kc, b0, bn), [("PS", pd[bi])])
                    for bi, (b0, bn) in enumerate(blks):
                        CP("act", H[:, oc, b0:b0 + bn], PS[pd[bi]][:, 0:bn], [("PS", pd[bi])], rk("H", oc, b0, bn))
            postnorm_residual(H, HF, "H", l, "mix_norm_post", 1.0)

            if smp:
                for q in range(4):
                    pi = psum()
                    for j in range(4):
                        TR(PS[pi][0:48, j * 128:(j + 1) * 128], CSS[:, q * 4 + j, :], ident, [("CSS", q * 4 + j), ("C32", None)], [("PS", pi)])
                    CP("dve", CSL, PS[pi][0:48, :], [("PS", pi)], [("TT", 3)])
                    out_ops.append(P.dma("sp", dr["ncs"][l][:, q * 512:(q + 1) * 512], CSL, reads=[("TT", 3)], writes=[("CSLo", q)]))
                pi = psum()
                for j in range(4):
                    TR(PS[pi][0:16, j * 128:(j + 1) * 128], HS0[:, j, :], ident, [("HS0", None), ("C32", None)], [("PS", pi)])
                CP("dve", RSL, PS[pi][0:16, :], [("PS", pi)], [("TT", 3)])
                out_ops.append(P.dma("sp", dr["nrs"][l][:, :], RSL, reads=[("TT", 3)], writes=[("RSLo", 0)]))
            if last_group:
                for q in range(4):
                    pi = psum()
                    for j in range(4):
                        TR(PS[pi][0:3, j * 128:(j + 1) * 128], CONVT[:, l, q * 4 + j, :], ident, [("CONVT", (l, q * 4 + j)), ("C32", None)], [("PS", pi)])
                    CP("dve", CSL[0:3, :], PS[pi][0:3, :], [("PS", pi)], [("TT", 3)])
                    out_ops.append(P.dma("sp", dr["ncp"][l][:, q * 512:(q + 1) * 512], CSL[0:3, :], reads=[("TT", 3)], writes=[("CSLo", q)]))
                pi = psum()
                TR(PS[pi][0:4, 0:128], HRG[:, l, :], ident, [("HRG", None), ("C32", None)], [("PS", pi)])
                CP("dve", RSL[0:4, 0:128], PS[pi][0:4, 0:128], [("PS", pi)], [("TT", 3)])
                out_ops.append(P.dma("sp", dr["nrp"][l].rearrange("(c p) -> c p", p=128), RSL[0:4, 0:128], reads=[("TT", 3)], writes=[("RSLo", 0)]))
                out_ops.append(P.dma("sp", dr["ndp"][l].rearrange("h d e -> d h e"), SST[:, l, :, :], reads=[("SST", None)], writes=[("SSTo", l)]))

        def delta_prelude(l, t, is_s):
            sfx = "_s" if is_s else "_p"
            par = t % 2
            SM = SM2[:, par, :]
            EGR = EGR2[:, par]
            BETA = SM[:, 16:20]; GT = SM[:, 20:24]; GC = SM[:, 24:32]; EG = SM[:, 32:36]; BEG = SM[:, 36:40]; EKD = SM[:, 40:44]
            ACT(BETA, SCT[:, t, 0:4], ACTF.Sigmoid, [("SCT", None)], [("SM", (par, 1))])
            TTo("dve", GT, SCT[:, t, 4:8], pcol(l, "dn_dt_bias", 0, 4), ALU.add, [("SCT", None), ("PP", None)], [("SM", (par, 2))])
            ACT(GT, GT, ACTF.Exp, [("SM", (par, 2))], [("SM", (par, 2))])
            ACT(GT, GT, ACTF.Ln, [("SM", (par, 2)), ("ONE1", None)], [("SM", (par, 2))], bias=ONE1[:, 0:1])
            TTo("dve", GT, GT, LAYC[:, 8:12], ALU.mult, [("SM", (par, 2)), ("LAYC", 2)], [("SM", (par, 2))])
            pgc = psum()
            MM(PS[pgc][:, 0:4], c32("tri" + sfx), GT, True, True, [("SM", (par, 2)), ("C32", None)], [("PS", pgc)])
            MM(PS[pgc][:, 4:8], c32("up" + sfx), GT, False, True, [("SM", (par, 2)), ("C32", None)], [("PS", pgc)])
            bc4 = lambda ap: ap.unsqueeze(2).broadcast_to([128, NH, 128])
            bm4 = lambda ap: ap.unsqueeze(1).broadcast_to([128, NH, 128])
            TTo("dve", TRG[:, :, :], bm4(c32("tri" + sfx)), bc4(GT), ALU.mult, [("SM", (par, 2)), ("C32", None), ("TRG", 0), ("TRG", 1)], [("TRG", 0), ("TRG", 1)])
            pgr = psum()
            resv.add(pgr)
            st["pgr"] = pgr
            st["pgr_users"] = 2
            MM(PS[pgr][:, :], ones_f, TRG[:].rearrange("p h c -> p (h c)"), True, True, [("TRG", 0), ("TRG", 1), ("C32", None)], [("PS", pgr)])
            CP("dve", GC, PS[pgc][:, 0:8], [("PS", pgc), ("PS", pgr)], [("SM", (par, 3))])
            ACT(EG, GC[:, 0:4], ACTF.Exp, [("SM", (par, 3))], [("SM", (par, 4))])
            ACT(EKD, GC[:, 4:8], ACTF.Exp, [("SM", (par, 3))], [("SM", (par, 5))])
            TTo("dve", BEG, BETA, EG, ALU.mult, [("SM", (par, 1)), ("SM", (par, 4))], [("SM", (par, 6))])
            PGR4 = PS[pgr][:].rearrange("p (h c) -> p h c", h=NH)
            gk = [("GRW", 0), ("GRW", 1)]; ek = [("EGR", (par, 0)), ("EGR", (par, 1))]
            TTo("dve", GRW[:, :, :], PGR4, GC[:, 0:4].unsqueeze(2).broadcast_to([128, NH, 128]), ALU.subtract, [("PS", pgr), ("SM", (par, 3))] + gk, gk)
            GRWf = GRW[:].rearrange("p h c -> p (h c)")
            STT(GRWf, GRWf, -1.0, GRWf, ALU.mult, ALU.max, gk, gk)
            ACT(GRW[:], GRW[:], ACTF.Exp, gk, gk, scale=-1.0)
            ACT(EGR[:], PGR4, ACTF.Exp, [("PS", pgr)] + ek, ek)
            resv.discard(pgr)

        def delta_pair(l, t, is_s, hg):
            c0 = t * 128
            par = t % 2
            SM = SM2[:, par, :]
            EGR = EGR2[:, par]
            sfx = "_s" if is_s else "_p"
            nlv = NLV_S if is_s else NLV_P
            HS = slice(2 * hg, 2 * hg + 2)
            heads = (2 * hg, 2 * hg + 1)
            bk = lambda n: [(n, hg)]
            KTOK, VTOK, AM, ATT, DTt, DINV, N1, BV, BKG, WT, VN, QG, KD, SBF = [DB[n][:, HS, :] for n in dt_names_bf]
            f2 = lambda ap: ap.rearrange("p h c -> p (h c)")
            qkv_r = [("QKV", None)]
            bc = lambda ap: ap.unsqueeze(2).broadcast_to([128, 2, 128])
            bm = lambda ap: ap.unsqueeze(1).broadcast_to([128, 2, 128])
            BETA = SM[:, 16 + 2 * hg:18 + 2 * hg]; GT = SM[:, 20 + 2 * hg:22 + 2 * hg]; GC0 = SM[:, 24 + 2 * hg:26 + 2 * hg]
            BEG = SM[:, 36 + 2 * hg:38 + 2 * hg]; EKD = SM[:, 40 + 2 * hg:42 + 2 * hg]
            GRWp = GRW[:, HS, :]; EGRp = EGR[:, HS, :]; TRGp = TRG[:, HS, :]; OTp = OT[:, HS, :]
            if getattr(cfg, 'delta_stop', 99) == 1:
                resv.clear()
                return
            pgr = st["pgr"]
            ptk = psum()
            for i, h in enumerate(heads):
                TR(psb(ptk)[:, i * 128:(i + 1) * 128], QKV[:, 4 + h, c0:c0 + 128], identb, qkv_r + [("C16", None)], [("PS", ptk)])
            for i, h in enumerate(heads):
                TR(psb(ptk)[:, 256 + i * 128:256 + (i + 1) * 128], QKV[:, 8 + h, c0:c0 + 128], identb, qkv_r + [("C16", None)], [("PS", ptk)])
            pkk = psum()
            for i, h in enumerate(heads):
                MM(PS[pkk][:, i * 128:(i + 1) * 128], QKV[:, 4 + h, c0:c0 + 128], QKV[:, 4 + h, c0:c0 + 128], i == 0, True, qkv_r, [("PS", pkk)])
            for i, h in enumerate(heads):
                MM(PS[pkk][:, 256 + i * 128:256 + (i + 1) * 128], QKV[:, 4 + h, c0:c0 + 128], QKV[:, h, c0:c0 + 128], False, True, qkv_r,
                   [("PS", pkk), ("PGD", hg)])
            yield
            if getattr(cfg, 'delta_stop', 99) == 2:
                resv.clear()
                return
            PGR3 = PS[pgr][:, hg * 256:(hg + 1) * 256].rearrange("p (h c) -> p h c", h=2)

            if getattr(cfg, 'delta_stop', 99) == 25:
                resv.clear()
                return
            CP("act", f2(KTOK), psb(ptk)[:, 0:256], [("PS", ptk)] + bk("KTOK"), bk("KTOK"))
            CP("act", f2(VTOK), psb(ptk)[:, 256:512], [("PS", ptk)] + bk("VTOK"), bk("VTOK"))
            yield
            if getattr(cfg, 'delta_stop', 99) == 3:
                resv.clear()
                return
            TTo("dve", BV, VTOK, bc(BETA), ALU.mult, bk("VTOK") + [("SM", (par, 1))] + bk("BV"), bk("BV"))
            TTo("dve", BKG, KTOK, bc(BEG), ALU.mult, bk("KTOK") + [("SM", (par, 6))] + bk("BKG"), bk("BKG"))
            TTo("dve", KD, KTOK, bc(EKD), ALU.mult, bk("KTOK") + [("SM", (par, 5))] + bk("KD"), bk("KD"))
            TTo("dve", QG, QKV[:, HS, c0:c0 + 128], EGRp, ALU.mult, qkv_r + [("EGR", (par, hg))] + bk("QG"), bk("QG"))
            CP("act", DTt, bm(identb), [("C16", None)] + bk("DT"), bk("DT"))
            CP("act", DINV, bm(identb), [("C16", None)] + bk("DINV"), bk("DINV"))
            yield
            if getattr(cfg, 'delta_stop', 99) == 4:
                resv.clear()
                return
            TTo("dve", TRGp, GRWp, bm(c16("mstrict" + sfx)), ALU.mult, bk("GRW") + [("C16", None)] + bk("TRG"), bk("TRG"))
            TTo("dve", TRGp, TRGp, bc(BETA), ALU.mult, bk("TRG") + [("SM", (par, 1))], bk("TRG"))
            TTo("dve", OTp, GRWp, bm(c16("minclt" + sfx)), ALU.mult, bk("GRW") + [("C16", None)] + bk("OT"), bk("OT"))
            TTo("dve", f2(AM), PS[pkk][:, 0:256], f2(TRGp), ALU.mult, [("PS", pkk)] + bk("TRG") + bk("AM"), bk("AM"))
            TTo("dve", f2(ATT), PS[pkk][:, 256:512], f2(OTp), ALU.mult, [("PS", pkk)] + bk("OT") + bk("ATT"), bk("ATT"))
            yield
            if getattr(cfg, 'delta_stop', 99) == 5:
                resv.clear()
                return
            for lv in range(nlv):
                p1 = psum()
                for i in range(2):
                    MM(PS[p1][:, i * 128:(i + 1) * 128], AM[:, i, :], DTt[:, i, :], i == 0, True, bk("AM") + bk("DT"), [("PS", p1)])
                yield
                TTo("dve", N1, PS[p1][:, 0:256].rearrange("p (h c) -> p h c", h=2), bm(c16("lv_p%d" % lv)), ALU.mult,
                    [("PS", p1), ("C16", None)] + bk("N1"), bk("N1"))
                yield
                p2 = psum()
                for i in range(2):
                    MM(PS[p2][:, i * 128:(i + 1) * 128], DINV[:, i, :], N1[:, i, :], i == 0, True, bk("DINV") + bk("N1"), [("PS", p2)])
                yield
                TTo("dve", f2(DTt), f2(DTt), PS[p2][:, 0:256], ALU.subtract, [("PS", p2)] + bk("DT"), bk("DT"))
                yield
                if lv < nlv - 1:
                    p3 = psum()
                    for i in range(2):
                        TR(psb(p3)[:, i * 128:(i + 1) * 128], DTt[:, i, :], identb, bk("DT") + [("C16", None)], [("PS", p3)])
                    yield
                    CP("act", f2(DINV), psb(p3)[:, 0:256], [("PS", p3)] + bk("DINV"), bk("DINV"))
                    yield
            if getattr(cfg, 'delta_stop', 99) == 6:
                resv.clear()
                return
            pu = psum()
            resv.add(pu)
            if not is_s:
                for i in range(2):
                    MM(PS[pu][:, i * 128:(i + 1) * 128], DTt[:, i, :], BV[:, i, :], i == 0, False, bk("DT") + bk("BV"), [("PS", pu)])
            pw = psum()
            for i in range(2):
                MM(PS[pw][:, i * 128:(i + 1) * 128], BKG[:, i, :], DTt[:, i, :], i == 0, True, bk("DT") + bk("BKG"), [("PS", pw)])
            yield
            ACT(f2(WT), PS[pw][:, 0:256], ACTF.Copy, [("PS", pw)] + bk("AM"), bk("AM"), scale=-1.0)
            po = psum()
            resv.add(po)
            if getattr(cfg, 'delta_stop', 99) == 7:
                resv.clear()
                return
            if not is_s:
                CP("act", SBF, SST[:, l, HS, :], [("SST", hg)] + bk("DINV"), bk("DINV"))
                yield
                for i in range(2):
                    MM(PS[pu][:, i * 128:(i + 1) * 128], WT[:, i, :], SBF[:, i, :], False, True, bk("AM") + bk("DINV"), [("PS", pu)])
                yield
                CP("act", f2(VN), PS[pu][:, 0:256], [("PS", pu)] + bk("N1"), bk("N1"))
                resv.discard(pu)
                yield
                for i in range(2):
                    MM(PS[po][:, i * 128:(i + 1) * 128], SBF[:, i, :], QG[:, i, :], i == 0, False, bk("DINV") + bk("QG"), [("PS", po)])
                for i in range(2):
                    MM(PS[po][:, i * 128:(i + 1) * 128], VN[:, i, :], ATT[:, i, :], False, True, bk("N1") + bk("ATT"), [("PS", po)])
                psu = psum()
                for i in range(2):
                    MM(PS[psu][:, i * 128:(i + 1) * 128], KD[:, i, :], VN[:, i, :], i == 0, True, bk("KD") + bk("N1"), [("PS", psu)])
                yield
                for i, h in enumerate(heads):
                    STT(SST[:, l, h, :], SST[:, l, h, :], EGR[:, h, 127:128], PS[psu][:, i * 128:(i + 1) * 128], ALU.mult, ALU.add,
                        [("PS", psu), ("SST", hg)] + [("EGR", (par, hg))], [("SST", hg)])
            else:
                resv.discard(pu)
                HSQ = NS // 2
                SS0 = TT[:, 0:2, :].rearrange("p a t -> p (a t)")[:, 0:HSQ * 128].rearrange("p (s e) -> p s e", s=HSQ)
                SS0B = TT[:, 2, :].bitcast(BF16)[:, 0:HSQ * 128].rearrange("p (s e) -> p s e", s=HSQ)
                WTX = TT[:, 3, 0:512].bitcast(BF16).rearrange("p (s c) -> p s c", s=HSQ)
                kS0 = [("TT", 0), ("TT", 1)]; kS0B = [("TT", 2)]; kWX = [("TT", 3)]
                segrow = c16("segrow", NS * 128).rearrange("p (s c) -> p s c", s=NS)
                segcol = c16("segcol", NS)
                first_po = True
                for i, h in enumerate(heads):
                    for hf in range(2):
                        s0 = hf * HSQ
                        P.dma("sp", SS0, dr["st_dn"][l][s0:s0 + HSQ, h, :, :].rearrange("s d e -> d s e"), reads=[], writes=kS0)
                        CP("act", SS0B, SS0, kS0 + kS0B, kS0B)
                        TTo("dve", WTX, WT[:, i, :].unsqueeze(1).broadcast_to([128, HSQ, 128]), segrow[:, s0:s0 + HSQ, :], ALU.mult,
                            bk("AM") + [("C16", None)] + kWX, kWX)
                        pu2 = psum()
                        MM(PS[pu2][:, 0:128], DTt[:, i, :], BV[:, i, :], True, False, bk("DT") + bk("BV"), [("PS", pu2)])
                        for s_ in range(HSQ):
                            MM(PS[pu2][:, 0:128], WTX[:, s_, :], SS0B[:, s_, :], False, True, kS0B + kWX, [("PS", pu2)])
                        CP("act", VN[:, i, :], PS[pu2][:, 0:128], [("PS", pu2)] + bk("N1"), bk("N1"))
                        cb = i * 128 + hf * 64
                        MM(PS[po][:, cb:cb + 64], VN[:, i, :], ATT[:, i, hf * 64:hf * 64 + 64], first_po, False, bk("N1") + bk("ATT"), [("PS", po)])
                        first_po = False
                        for s_ in range(HSQ):
                            cs = i * 128 + (s0 + s_) * 8
                            MM(PS[po][:, cs:cs + 8], SS0B[:, s_, :], QG[:, i, (s0 + s_) * 8:(s0 + s_) * 8 + 8], False, True,
                               kS0B + bk("QG"), [("PS", po)])
                        TTo("dve", WTX, KD[:, i, :].unsqueeze(1).broadcast_to([128, HSQ, 128]),
                            segcol[:, s0:s0 + HSQ].unsqueeze(2).broadcast_to([128, HSQ, 128]), ALU.mult,
                            bk("KD") + [("C16", None)] + kWX, kWX)
                        for q in range(2):
                            psu = psum()
                            for j in range(4):
                                s_ = q * 4 + j
                                MM(PS[psu][:, j * 128:(j + 1) * 128], WTX[:, s_, :], VN[:, i, :], j == 0, True, kWX + bk("N1"), [("PS", psu)])
                            for j in range(4):
                                s_ = q * 4 + j
                                sg_ = s0 + s_
                                STT(SS0[:, s_, :], SS0[:, s_, :], EGR[:, h, sg_ * 8 + 7:sg_ * 8 + 8], PS[psu][:, j * 128:(j + 1) * 128],
                                    ALU.mult, ALU.add, [("PS", psu)] + kS0 + [("EGR", (par, hg))], kS0)
                        out_ops.append(P.dma("sp", dr["nds"][l][s0:s0 + HSQ, h, :, :].rearrange("s d e -> d s e"), SS0, reads=kS0, writes=[("NDSo", h)]))
                        yield
            if getattr(cfg, 'delta_stop', 99) == 8:
                resv.clear()
                return
            resv.discard(po)
            CP("act", f2(OTp), PS[po][:, 0:256], [("PS", po)] + bk("OT"), bk("OT"))
            SQW = SQ[:, 2 * hg, 0:256]
            ACT(SQW, PS[po][:, 0:256], ACTF.Square, [("PS", po)], [("SQ", 2 * hg)])
            yield
            pss = psum()
            MM(PS[pss][:, 0:256], ones_r, SQW, True, True, [("SQ", 2 * hg), ("ONESR", None)], [("PS", pss)])
            yield
            RS = RSTD[:, hg * 256:(hg + 1) * 256]
            ACT(RS, PS[pss][:, 0:256], ACTF.Ln, [("PS", pss), ("EPST", None)], [("RSTD", hg)], scale=1.0 / 128.0, bias=EPSC[(128, 1.0)])
            ACT(RS, RS, ACTF.Exp, [("RSTD", hg)], [("RSTD", hg)], scale=-0.5)
            STT(f2(OTp), f2(OTp), pcol(l, "dn_norm_w"), RS, ALU.mult, ALU.mult, bk("OT") + [("RSTD", hg), ("PP", None)], bk("OT"))
            TTo("dve", MIXR[:, 4 + 2 * hg:6 + 2 * hg, c0:c0 + 128], OTp, ZG[:, HS, c0:c0 + 128], ALU.mult,
                bk("OT") + [("ZG", None)], [("Y", (4 + h, t)) for h in heads])

        def delta_tile(l, t, is_s, nxt=None, first=False):
            if first:
                delta_prelude(l, t, is_s)
            gens = [delta_pair(l, t, is_s, 0), delta_pair(l, t, is_s, 1)]
            alive = [True, True]
            steps = [0, 0]
            pre_done = nxt is None
            while any(alive):
                for gi_, g in enumerate(gens):
                    if alive[gi_]:
                        try:
                            next(g)
                            steps[gi_] += 1
                        except StopIteration:
                            alive[gi_] = False
                if not pre_done and min(steps) >= 4:
                    delta_prelude(l, nxt[0], nxt[1])
                    pre_done = True
            if not pre_done:
                delta_prelude(l, nxt[0], nxt[1])

        for l in range(DEPTH):
            ffn(l, 1)
            mixer(l)
            ffn(l, 2)

        for (b0, bn) in blks:
            pi = sumsq_rstd(X, "X", b0, bn, D)
            norm_scale(Y, "Y", X, "X", 0, "final_norm", b0, bn, pi)
        for t in range(ntile):
            si = st["stg"]; st["stg"] ^= 1
            stg = stg_view(si)
            for half in range(2):
                pi = psum()
                for j in range(4):
                    kc = half * 4 + j
                    TR(PS[pi][:, j * 128:(j + 1) * 128], YF[:, kc, t * 128:(t + 1) * 128], ident, [("Y", (kc, t)), ("C32", None)], [("PS", pi)])
                CP("act" if half == 0 else "dve", stg[:, half * 512:(half + 1) * 512], PS[pi][:], [("PS", pi)], [stg_keys(si)[half]])
            dst = dr["yp"][p0 + t * 128: p0 + (t + 1) * 128, :] if t * 128 < npr else dr["ys"][:, :]
            out_ops.append(P.dma("sp", dst, stg, reads=stg_keys(si), writes=[("STGo", si)]))

    P.emit(out_ops)
    es.close()
    return nc, P


def make_in_maps(inp):
    pp = _pack_params(inp)
    c32, c16 = _consts()
    rgw = _pack_rgw(inp)
    maps = []
    shared = {"pp": pp, "c32": c32, "c16": c16, "rgw_r": rgw}
    for nm in ("ffn1_w_up", "ffn2_w_up", "ffn1_w_down", "ffn2_w_down", "w_in", "w_out"):
        shared[nm] = np.ascontiguousarray(inp[nm])
    for core in range(NCORES):
        sl = slice(core * NS, (core + 1) * NS)
        m = dict(shared)
        m["xp"] = np.ascontiguousarray(inp["x_prompt"][core])
        m["xs"] = np.ascontiguousarray(inp["x_sample"][sl].reshape(NS * DS, D))
        m["st_conv"] = np.ascontiguousarray(inp["state_conv"][:, sl].reshape(DEPTH, NS * 3, CONVC))
        m["st_rg"] = np.ascontiguousarray(inp["state_rglru"][:, sl])
        m["st_dn"] = np.ascontiguousarray(inp["state_delta"][:, sl])
        maps.append(m)
    return maps


def gather(r):
    y_prompt = np.stack([r[c]["yp"] for c in range(NCORES)], axis=0)
    y_sample = np.concatenate([r[c]["ys"].reshape(NS, DS, D) for c in range(NCORES)], axis=0)
    ncp = np.stack([r[c]["ncp"] for c in range(NCORES)], axis=1)
    nrp = np.stack([r[c]["nrp"] for c in range(NCORES)], axis=1)
    ndp = np.stack([r[c]["ndp"] for c in range(NCORES)], axis=1)
    ncs = np.concatenate([r[c]["ncs"].reshape(DEPTH, NS, 3, CONVC) for c in range(NCORES)], axis=1)
    nrs = np.concatenate([r[c]["nrs"] for c in range(NCORES)], axis=1)
    nds = np.concatenate([r[c]["nds"] for c in range(NCORES)], axis=1)
    return (y_prompt, y_sample, ncp, nrp, ndp, ncs, nrs, nds)


def kernel(**inp):
    inp = {k: np.asarray(v) for k, v in inp.items()}
    cfg = Cfg()
    nc, P = build_program(cfg)
    maps = make_in_maps(inp)
    res = run_bass_kernel_spmd(nc, maps, core_ids=list(range(NCORES)))
    return gather(res.results)
```

```python
import numpy as np
from contextlib import ExitStack
import concourse.bass as bass
import concourse.mybir as mybir
from concourse.bass_utils import run_bass_kernel_spmd

F32 = mybir.dt.float32
F32R = mybir.dt.float32r
BF16 = mybir.dt.bfloat16
ACTF = mybir.ActivationFunctionType
ALU = mybir.AluOpType

D = 1024
KC = 8
DFF = 2816
NFF = 22
DEPTH = 2
SEQ = 2048
NS = 16
DS = 8
INC = 3080
CONVC = 2048
EPS = 1e-6
NCORES = 8


class Op:
    __slots__ = ("eng", "fn", "deps", "sig", "count", "sem", "is_dma", "idx")

    def __init__(self, eng, fn, is_dma=False):
        self.eng = eng
        self.fn = fn
        self.deps = []
        self.sig = False
        self.count = None
        self.sem = None
        self.is_dma = is_dma
        self.idx = None


class Prog:
    ENGS = ("pe", "act", "dve", "pool", "sp")
    NDMASEM = 8

    def __init__(self, nc, same_engine_sync=True):
        self.nc = nc
        self.ops = {e: [] for e in self.ENGS}
        self.last_w = {}
        self.readers = {}
        self.same_engine_sync = same_engine_sync
        self.dma_n = {"sp": 0, "pool": 0}
        self.dma_hist = {"sp": [], "pool": []}
        self.n_ops = 0

    def _collect(self, op, reads, writes):
        deps = []
        for (n, i) in reads:
            lw = self.last_w.get(n)
            if lw:
                if i is None:
                    deps.extend(lw.values())
                else:
                    if i in lw:
                        deps.append(lw[i])
                    if None in lw:
                        deps.append(lw[None])
        for (n, i) in writes:
            lw = self.last_w.get(n)
            rd = self.readers.get(n)
            if lw:
                if i is None:
                    deps.extend(lw.values())
                else:
                    if i in lw:
                        deps.append(lw[i])
                    if None in lw:
                        deps.append(lw[None])
            if rd:
                if i is None:
                    for v in rd.values():
                        deps.extend(v)
                else:
                    deps.extend(rd.get(i, ()))
                    deps.extend(rd.get(None, ()))
        for (n, i) in writes:
            lw = self.last_w.setdefault(n, {})
            rd = self.readers.setdefault(n, {})
            if i is None:
                lw.clear()
                rd.clear()
                lw[None] = op
            else:
                lw[i] = op
                rd.pop(i, None)
        for (n, i) in reads:
            self.readers.setdefault(n, {}).setdefault(i, []).append(op)
        seen = set()
        for d in deps:
            if d is op or id(d) in seen:
                continue
            seen.add(id(d))
            if d.eng == op.eng and not d.is_dma and not op.is_dma:
                if op.eng == "pe" or not self.same_engine_sync:
                    continue
            op.deps.append(d)
            d.sig = True

    def op(self, eng, fn, reads=(), writes=()):
        o = Op(eng, fn)
        self._collect(o, list(reads), list(writes))
        self.ops[eng].append(o)
        self.n_ops += 1
        return o

    def dma(self, q, out, in_, reads=(), writes=()):
        o = Op(q, lambda e: e.dma_start(out=out, in_=in_), is_dma=True)
        n = self.dma_n[q]
        self.dma_n[q] += 1
        o.idx = n
        self._collect(o, list(reads), list(writes))
        hist = self.dma_hist[q]
        if n >= self.NDMASEM:
            o.deps.append(hist[n - self.NDMASEM])
        hist.append(o)
        o.sig = True
        self.ops[q].append(o)
        self.n_ops += 1
        return o

    def emit(self, final_wait_ops):
        nc = self.nc
        with ExitStack() as es:
            esem = {e: es.enter_context(nc.semaphore("prog_" + e)) for e in self.ENGS}
            dsem = {q: [es.enter_context(nc.semaphore("dma_%s_%d" % (q, i))) for i in range(self.NDMASEM)]
                    for q in ("sp", "pool")}
            for e in self.ENGS:
                c = 0
                for o in self.ops[e]:
                    if o.is_dma:
                        slot = o.idx % self.NDMASEM
                        o.sem = dsem[e][slot]
                        o.count = 16 * (o.idx // self.NDMASEM + 1)
                    elif o.sig:
                        c += 1
                        o.sem = esem[e]
                        o.count = c
            block = es.enter_context(nc.Block())

            def run(ename, e, extra_final=None):
                known = {}
                for o in self.ops[ename]:
                    need = {}
                    for d in o.deps:
                        key = id(d.sem)
                        if known.get(key, 0) >= d.count:
                            continue
                        if key not in need or need[key][1] < d.count:
                            need[key] = (d.sem, d.count)
                    for key, (s, v) in need.items():
                        e.wait_ge(s, v)
                        known[key] = v
                    ins = o.fn(e)
                    if o.is_dma:
                        ins.then_inc(o.sem, 16)
                    elif o.sig:
                        ins.then_inc(o.sem, 1)
                if extra_final:
                    need = {}
                    for d in extra_final:
                        key = id(d.sem)
                        if key not in need or need[key][1] < d.count:
                            need[key] = (d.sem, d.count)
                    for key, (s, v) in need.items():
                        e.wait_ge(s, v)

            @block.tensor
            def _(e):
                run("pe", e)

            @block.scalar
            def _(e):
                run("act", e)

            @block.vector
            def _(e):
                run("dve", e)

            @block.gpsimd
            def _(e):
                run("pool", e)

            @block.sync
            def _(e):
                run("sp", e, extra_final=final_wait_ops)


RGW = 512
NH = 4
HD = 128
NLV_P = 7
NLV_S = 3

C32 = {"ident": 0, "ones": 128, "tri_p": 256, "up_p": 384, "tri_s": 512, "up_s": 640}
N32 = 768
C16 = {"identb": 0, "mstrict_p": 128, "minclt_p": 256, "mstrict_s": 384, "minclt_s": 512}
for _i in range(NLV_P):
    C16["lv_p%d" % _i] = 640 + 128 * _i
C16["segcol"] = 640 + 128 * NLV_P
C16["segrow"] = C16["segcol"] + 16
N16 = C16["segrow"] + 16 * 128


def _consts():
    i = np.arange(128)[:, None]
    j = np.arange(128)[None, :]
    c32 = np.zeros((128, N32), np.float32)
    c32[:, 0:128] = np.eye(128)
    c32[:, 128:256] = 1.0
    seg = 8
    same_s = (i // seg) == (j // seg)
    c32[:, 256:384] = (i <= j)
    c32[:, 384:512] = (i > j)
    c32[:, 512:640] = (i <= j) & same_s
    c32[:, 640:768] = (i > j) & same_s
    c16 = np.zeros((128, N16), np.float32)
    c16[:, 0:128] = np.eye(128)
    c16[:, 128:256] = (i > j)
    c16[:, 256:384] = (j >= i)
    c16[:, 384:512] = (i > j) & same_s
    c16[:, 512:640] = (j >= i) & same_s
    for lv in range(NLV_P):
        b = 1 << lv
        m = ((i // (2 * b)) == (j // (2 * b))) & (((j // b) % 2) == 1) & (((i // b) % 2) == 0)
        c16[:, C16["lv_p%d" % lv]:C16["lv_p%d" % lv] + 128] = m
    c16[:, C16["segcol"]:C16["segcol"] + 16] = (np.arange(128)[:, None] // seg) == np.arange(16)[None, :]
    sr = (np.arange(16)[:, None] == (np.arange(128)[None, :] // seg)).astype(np.float32).reshape(1, 16 * 128)
    c16[:, C16["segrow"]:] = np.repeat(sr, 128, axis=0)
    return c32, c16


PL = {}
_o = 0
for _nm, _n in (("ffn1_norm_pre", 8), ("ffn1_norm_post", 8), ("mix_norm_pre", 8), ("mix_norm_post", 8),
                ("ffn2_norm_pre", 8), ("ffn2_norm_post", 8), ("final_norm", 8),
                ("conv_w", 64), ("conv_b_rg", 4), ("rg_b_a", 4), ("rg_b_x", 4), ("rg_lambda", 4),
                ("dn_a_log", 4), ("dn_dt_bias", 4), ("dn_norm_w", 1)):
    PL[_nm] = _o
    _o += _n
PP_LAYER = _o


def _pack_params(inp):
    cols = []

    def vecn(v, n):
        return np.ascontiguousarray(np.asarray(v).reshape(n, 128).T)

    for l in range(DEPTH):
        for nm in ("ffn1_norm_pre", "ffn1_norm_post", "mix_norm_pre", "mix_norm_post",
                   "ffn2_norm_pre", "ffn2_norm_post"):
            cols.append(vecn(inp[nm][l], 8))
        cols.append(vecn(inp["final_norm"], 8))
        cw = np.asarray(inp["conv_w"][l])
        cols.append(np.ascontiguousarray(cw.reshape(4, 16, 128).transpose(2, 1, 0).reshape(128, 64)))
        for nm in ("conv_b_rg", "rg_b_a", "rg_b_x", "rg_lambda"):
            cols.append(vecn(inp[nm][l], 4))
        cols.append(np.repeat(np.asarray(inp["dn_a_log"][l]).reshape(1, 4), 128, axis=0))
        cols.append(np.repeat(np.asarray(inp["dn_dt_bias"][l]).reshape(1, 4), 128, axis=0))
        cols.append(np.asarray(inp["dn_norm_w"][l]).reshape(128, 1))
    return np.ascontiguousarray(np.concatenate(cols, axis=1).astype(np.float32))


def _pack_rgw(inp):
    out = np.zeros((DEPTH, 2, 128, 4, 128), np.float32)
    for l in range(DEPTH):
        for gi, nm in enumerate(("rg_w_a", "rg_w_x")):
            w = np.asarray(inp[nm][l])
            for c in range(4):
                for hh in range(2):
                    out[l, gi, hh * 64:(hh + 1) * 64, c, hh * 64:(hh + 1) * 64] = w[2 * c + hh]
    return out


class Cfg:
    def __init__(self, **kw):
        self.groups = [(0, 640, True), (640, 768, False), (1408, 640, False)]
        self.stages = "full"
        self.same_engine_sync = True
        self.__dict__.update(kw)


def blocks_of(T):
    if T == 768:
        return [(0, 384), (384, 384)]
    if T == 640:
        return [(0, 384), (384, 256)]
    raise ValueError(T)


def build_program(cfg):
    nc = bass.Bass("TRN2", target_bir_lowering=False)
    TM = 768
    NTM = TM // 128
    dr = {}

    def din(name, shape, dt=F32):
        dr[name] = nc.dram_tensor(name, shape, dt, kind="ExternalInput").ap()

    def dout(name, shape):
        dr[name] = nc.dram_tensor(name, shape, F32, kind="ExternalOutput").ap()

    din("xp", [SEQ, D]); din("xs", [NS * DS, D])
    din("pp", [128, DEPTH * PP_LAYER]); din("c32", [128, N32]); din("c16", [128, N16])
    din("rgw_r", [DEPTH, 2, 128, 4, 128], F32R)
    din("st_conv", [DEPTH, NS * 3, CONVC]); din("st_rg", [DEPTH, NS, RGW]); din("st_dn", [DEPTH, NS, NH, HD, HD])
    for nm in ("ffn1_w_up", "ffn2_w_up"):
        din(nm, [DEPTH, D, 2 * DFF], F32R)
    for nm in ("ffn1_w_down", "ffn2_w_down"):
        din(nm, [DEPTH, DFF, D], F32R)
    din("w_in", [DEPTH, D, INC], F32R); din("w_out", [DEPTH, D, D], F32R)
    dout("yp", [SEQ, D]); dout("ys", [NS * DS, D])
    dout("ncp", [DEPTH, 3, CONVC]); dout("nrp", [DEPTH, RGW]); dout("ndp", [DEPTH, NH, HD, HD])
    if getattr(cfg, "debug", False):
        dout("dbg", [3, 128, KC, TM])
    dout("ncs", [DEPTH, NS * 3, CONVC]); dout("nrs", [DEPTH, NS, RGW]); dout("nds", [DEPTH, NS, NH, HD, HD])

    P = Prog(nc, same_engine_sync=cfg.same_engine_sync)
    es = ExitStack()
    sb = lambda name, shape, dt: es.enter_context(nc.sbuf_tensor(name, shape, dt))
    X = sb("X", [128, KC, TM], F32)
    H = sb("H", [128, KC, TM], F32R)
    Y = sb("Y", [128, KC, TM], F32R)
    YF = Y[:].bitcast(F32)
    HF = H[:].bitcast(F32)
    NFB = 4
    ACTB = sb("ACTB", [128, NFB, TM], F32R)
    NWA = 4
    WA = [sb("WA%d" % i, [128, KC, 128], F32R) for i in range(NWA)]
    NWB = 4
    WB = [sb("WB%d" % i, [128, NFB, 128], F32R) for i in range(NWB)]
    PPt = sb("PPt", [128, DEPTH * PP_LAYER], F32)
    C32t = sb("C32t", [128, N32], F32)
    C16t = sb("C16t", [128, N16], BF16)
    ONESR = sb("ONESR", [128, 128], F32R)
    EPST = sb("EPST", [128, 8], F32)
    TT = sb("TT", [128, 4, TM], F32)
    SQ = sb("SQ", [128, 4, 384], F32R)
    RSTD = sb("RSTD", [128, 512], F32)
    SG = [TT[:, 0, 0:384], TT[:, 1, 0:384]]
    ZG = sb("ZG", [128, 4, TM], BF16)
    QKV = sb("QKV", [128, 12, TM], BF16)
    XR = sb("XR", [128, TM], F32R)
    XP = sb("XP", [128, 3 + TM], F32R)
    XPS = sb("XPS", [128, NS, 11], F32R)
    XPf = XP[:].bitcast(F32)
    XPSf = XPS[:].bitcast(F32)
    DG = sb("DG", [128, 4, 128], F32R)
    CSS = sb("CSS", [128, 16, NS * 3], F32)
    HS0 = sb("HS0", [128, 4, NS], F32)
    CONVT = sb("CONVT", [128, DEPTH, 16, 3], F32)
    HRG = sb("HRG", [128, DEPTH, 4], F32)
    SST = sb("SST", [128, DEPTH, NH, HD], F32)
    RGWt = sb("RGWt", [128, 2, 4, 128], F32R)
    WSC = sb("WSC", [128, KC, 8], F32R)
    SCT = sb("SCT", [128, NTM, 8], F32)
    LAYC = sb("LAYC", [128, 16], F32)
    ONE1 = sb("ONE1", [128, 1], F32)
    FEN = sb("FEN", [128, 2], F32)
    dt_names_bf = ["KTOK", "VTOK", "AM", "ATT", "DT", "DINV", "N1", "BV", "BKG", "WT", "VN", "QG", "KD", "SBF"]
    DB = {n: sb(n, [128, NH, 128], BF16) for n in dt_names_bf if n not in ("VN", "SBF", "WT")}
    DB["VN"] = DB["N1"]; DB["SBF"] = DB["DINV"]; DB["WT"] = DB["AM"]
    TTflat = TT[:].rearrange("p a t -> p (a t)")
    DB1 = {}
    for _k, _n in enumerate([n for n in dt_names_bf if n not in ("VN", "SBF", "WT")]):
        DB1[_n] = TTflat[:, _k * 256:(_k + 1) * 256].bitcast(BF16).rearrange("p (h c) -> p h c", h=NH)
    DB1["VN"] = DB1["N1"]; DB1["SBF"] = DB1["DINV"]; DB1["WT"] = DB1["AM"]
    DBS = [DB, DB1]
    DBNAMES = [n for n in dt_names_bf if n not in ("VN", "SBF", "WT")]
    GRW = sb("GRW", [128, NH, 128], F32)
    EGR2 = sb("EGR", [128, 2, NH, 128], F32)
    TRG = sb("TRG", [128, NH, 128], F32)
    OT = sb("OT", [128, NH, 128], F32)
    SM2 = sb("SM", [128, 2, 64], F32)
    CSL = TT[0:48, 3, 0:512]
    RSL = TT[0:16, 3, 0:512]
    PS = [es.enter_context(nc.psum_tensor("PS%d" % i, [128, 512], F32)) for i in range(8)]

    def c32(nm):
        return C32t[:, C32[nm]:C32[nm] + 128]

    def c16(nm, n=128):
        return C16t[:, C16[nm]:C16[nm] + n]

    ident = c32("ident")
    ones_f = c32("ones")
    ones_r = ONESR[:, :]
    identb = c16("identb")
    st = {"ps": 0, "wa": 0, "wb": 0, "stg": 0, "sg": 0}

    resv = set()

    def psum(hold=False):
        for _ in range(16):
            i = st["ps"]
            st["ps"] = (i + 1) % 8
            if i not in resv:
                if hold:
                    resv.add(i)
                return i
        raise RuntimeError("PSUM banks exhausted: %s" % sorted(resv))

    def psb(pi):
        return PS[pi][:].bitcast(BF16)

    def ACT(out, in_, func, reads, writes, **kw):
        return P.op("act", lambda e: e.activation(out=out, in_=in_, func=func, **kw), reads, writes)

    def CP(eng, out, in_, reads, writes):
        if eng == "act":
            return P.op("act", lambda e: e.copy(out=out, in_=in_), reads, writes)
        return P.op(eng, lambda e: e.tensor_copy(out=out, in_=in_), reads, writes)

    def TTo(eng, out, in0, in1, op, reads, writes):
        return P.op(eng, lambda e: e.tensor_tensor(out=out, in0=in0, in1=in1, op=op), reads, writes)

    def TS(eng, out, in0, s1, s2, op0, op1, reads, writes):
        if op1 is None:
            return P.op(eng, lambda e: e.tensor_scalar(out=out, in0=in0, scalar1=s1, scalar2=None, op0=op0), reads, writes)
        return P.op(eng, lambda e: e.tensor_scalar(out=out, in0=in0, scalar1=s1, scalar2=s2, op0=op0, op1=op1), reads, writes)

    def STT(out, in0, scalar, in1, op0, op1, reads, writes):
        return P.op("dve", lambda e: e.scalar_tensor_tensor(out=out, in0=in0, scalar=scalar, in1=in1, op0=op0, op1=op1), reads, writes)

    def MM(out, lhsT, rhs, start, stop, reads, writes):
        return P.op("pe", lambda e: e.matmul(out, lhsT=lhsT, rhs=rhs, start=start, stop=stop, skip_group_check=True), reads, writes)

    def TR(out, in_, idn, reads, writes):
        return P.op("pe", lambda e: e.transpose(out=out, in_=in_, identity=idn), reads, writes)

    class WStream:
        def __init__(self, name, bufs):
            self.name, self.bufs, self.n = name, bufs, len(bufs)
            self.reqs = []
            self.issued = 0
            self.cur = 0

        def add(self, fn):
            self.reqs.append(fn)

        def next(self):
            i = self.cur
            self.cur += 1
            upto = min(len(self.reqs), i + self.n - 1)
            while self.issued < upto:
                k = self.issued
                slot = k % self.n
                out_ap, in_ap = self.reqs[k](self.bufs[slot])
                P.dma("pool", out_ap, in_ap, writes=[(self.name, slot)])
                self.issued += 1
            return i % self.n

    WAS = WStream("WA", WA)
    WBS = WStream("WB", WB)
    FBLOCKS = [(0, 4), (4, 4), (8, 4), (12, 4), (16, 4), (20, 2)]

    def plan_cols(src, col):
        v = src.rearrange("(kc p) n -> p kc n", p=128)
        WAS.add(lambda buf, v=v, col=col: (buf[:], v[:, :, col:col + 128]))

    def plan_ffn(l, which):
        wup = dr["ffn%d_w_up" % which][l]
        wdn = dr["ffn%d_w_down" % which][l].rearrange("(c p) n -> p c n", p=128)
        for (c0, nch) in FBLOCKS:
            for j in range(nch):
                plan_cols(wup, (c0 + j) * 128)
                plan_cols(wup, DFF + (c0 + j) * 128)
            for oc in range(8):
                WBS.add(lambda buf, c0=c0, nch=nch, oc=oc, wdn=wdn: (buf[:, 0:nch, :], wdn[:, c0:c0 + nch, oc * 128:(oc + 1) * 128]))

    def plan_mixer(l):
        win = dr["w_in"][l]
        for ch in range(4):
            plan_cols(win, ch * 128)
            plan_cols(win, 2048 + ch * 128)
        for kind in range(3):
            for hh in range(4):
                plan_cols(win, 512 + kind * 512 + hh * 128)
        for hh in range(4):
            plan_cols(win, 2560 + hh * 128)
        for oc in range(8):
            plan_cols(dr["w_out"][l], oc * 128)

    for _g in cfg.groups:
        for l in range(DEPTH):
            plan_ffn(l, 1)
            plan_mixer(l)
            plan_ffn(l, 2)

    P.dma("sp", PPt[:], dr["pp"][:, :], writes=[("PP", None)])
    P.dma("sp", C32t[:], dr["c32"][:, :], writes=[("C32", None)])
    for i in range(0, N16, 768):
        n = min(768, N16 - i)
        P.dma("sp", TT[:, 0, 0:n], dr["c16"][:, i:i + n], writes=[("TT", 0)])
        CP("dve", C16t[:, i:i + n], TT[:, 0, 0:n], [("TT", 0)], [("C16", None)])
    CP("dve", ones_r, ones_f, [("C32", None)], [("ONESR", None)])
    EPSC = {}
    for i, (key, val) in enumerate([((D, 1.0), EPS), ((D, 0.5), 4.0 * EPS), ((128, 1.0), EPS), ((1, 1.0), EPS), ("q", 128.0 * EPS)]):
        EPSC[key] = EPST[:, i:i + 1]
        P.op("dve", lambda e, i=i, val=val: e.memset(EPST[:, i:i + 1], val), writes=[("EPST", i)])
    P.op("dve", lambda e: e.memset(ONE1[:, :], 1.0), writes=[("ONE1", None)])
    P.op("dve", lambda e: e.memset(CONVT[:].rearrange("p l c t -> p (l c t)"), 0.0), writes=[("CONVT", None)])
    P.op("dve", lambda e: e.memset(HRG[:].rearrange("p l c -> p (l c)"), 0.0), writes=[("HRG", None)])
    P.op("dve", lambda e: e.memset(SST[:].rearrange("p l h e -> p (l h e)"), 0.0), writes=[("SST", None)])

    out_ops = []

    def pcol(l, nm, j=0, n=1):
        c = l * PP_LAYER + PL[nm] + j
        return PPt[:, c:c + n]

    ngroups = len(cfg.groups)
    for gi, (p0, npr, smp) in enumerate(cfg.groups):
        T = npr + (128 if smp else 0)
        ntile = T // 128
        blks = blocks_of(T)
        last_group = (gi == ngroups - 1)

        def tiles_of(b0, bn):
            return list(range(b0 // 128, (b0 + bn + 127) // 128))

        def rk(name, kcs, b0, bn):
            if isinstance(kcs, int):
                kcs = [kcs]
            return [(name, (kc, t)) for kc in kcs for t in tiles_of(b0, bn)]

        def stg_view(si):
            return TT[:, 2 * si:2 * si + 2, :].rearrange("p a t -> p (a t)")[:, 0:D]

        def stg_keys(si):
            return [("TT", 2 * si), ("TT", 2 * si + 1)]

        for t in range(ntile):
            si = st["stg"]; st["stg"] ^= 1
            stg = stg_view(si)
            src = dr["xp"][p0 + t * 128: p0 + (t + 1) * 128, :] if t * 128 < npr else dr["xs"][:, :]
            P.dma("sp", stg, src, writes=stg_keys(si))
            for half in range(2):
                pi = psum()
                for j in range(4):
                    kc = half * 4 + j
                    TR(PS[pi][:, j * 128:(j + 1) * 128], stg[:, kc * 128:(kc + 1) * 128], ident,
                       stg_keys(si) + [("C32", None)], [("PS", pi)])
                CP("act" if half == 0 else "dve", X[:, half * 4:(half + 1) * 4, t * 128:(t + 1) * 128],
                   PS[pi][:].rearrange("p (j c) -> p j c", j=4), [("PS", pi)], [("X", (half * 4 + j, t)) for j in range(4)])

        def sumsq_rstd(SRC, srcname, b0, bn, nfeat, post_scale=1.0, nk=KC):
            pi = psum()
            for kc in range(nk):
                rd = rk(srcname, kc, b0, bn)
                ACT(SQ[:, kc % 4, 0:bn], SRC[:, kc, b0:b0 + bn], ACTF.Square, rd, [("SQ", kc % 4)])
                MM(PS[pi][:, 0:bn], ones_r, SQ[:, kc % 4, 0:bn], kc == 0, kc == nk - 1,
                   [("SQ", kc % 4), ("ONESR", None)], [("PS", pi)])
            return rstd_from_psum(pi, bn, nfeat, post_scale)

        def rstd_from_psum(pi, bn, nfeat, post_scale=1.0):
            ACT(PS[pi][:, 0:bn], PS[pi][:, 0:bn], ACTF.Ln, [("PS", pi), ("EPST", None)], [("PS", pi)],
                scale=1.0 / (nfeat * post_scale * post_scale), bias=EPSC[(nfeat, post_scale)])
            ACT(PS[pi][:, 0:bn], PS[pi][:, 0:bn], ACTF.Exp, [("PS", pi)], [("PS", pi)], scale=-0.5)
            return pi

        def norm_scale(DST, dstname, SRC, srcname, l, gname, b0, bn, pi):
            for kc in range(KC):
                STT(DST[:, kc, b0:b0 + bn], SRC[:, kc, b0:b0 + bn], pcol(l, gname, kc), PS[pi][:, 0:bn], ALU.mult, ALU.mult,
                    rk(srcname, kc, b0, bn) + [("PS", pi), ("PP", None)], rk(dstname, kc, b0, bn))

        def prenorm_to_H(l, gname):
            for (b0, bn) in blks:
                pi = sumsq_rstd(X, "X", b0, bn, D)
                norm_scale(H, "H", X, "X", l, gname, b0, bn, pi)

        def postnorm_residual(SRCw, SRC, srcname, l, gname, scale):
            for (b0, bn) in blks:
                pi = sumsq_rstd(SRC, srcname, b0, bn, D, post_scale=scale)
                norm_scale(SRCw, srcname, SRC, srcname, l, gname, b0, bn, pi)
                for kc in range(KC):
                    TTo("dve", X[:, kc, b0:b0 + bn], X[:, kc, b0:b0 + bn], SRC[:, kc, b0:b0 + bn], ALU.add,
                        rk(srcname, kc, b0, bn) + rk("X", kc, b0, bn), rk("X", kc, b0, bn))

        def ffn(l, which):
            prenorm_to_H(l, "ffn%d_norm_pre" % which)
            for fbi, (c0, nch) in enumerate(FBLOCKS):
                for j in range(nch):
                    wi_g = WAS.next()
                    pg = [psum() for _ in blks]
                    for kc in range(KC):
                        for bi, (b0, bn) in enumerate(blks):
                            MM(PS[pg[bi]][:, 0:bn], WA[wi_g][:, kc, :], H[:, kc, b0:b0 + bn],
                               kc == 0, kc == KC - 1, [("WA", wi_g)] + rk("H", kc, b0, bn), [("PS", pg[bi])])
                    sgs = []
                    for bi, (b0, bn) in enumerate(blks):
                        si = st["sg"]; st["sg"] = (st["sg"] + 1) % len(SG)
                        sgs.append(si)
                        ACT(SG[si][:, 0:bn], PS[pg[bi]][:, 0:bn], ACTF.Silu, [("PS", pg[bi])], [("TT", si)])
                    wi_u = WAS.next()
                    pu = [psum() for _ in blks]
                    for kc in range(KC):
                        for bi, (b0, bn) in enumerate(blks):
                            MM(PS[pu[bi]][:, 0:bn], WA[wi_u][:, kc, :], H[:, kc, b0:b0 + bn],
                               kc == 0, kc == KC - 1, [("WA", wi_u)] + rk("H", kc, b0, bn), [("PS", pu[bi])])
                    for bi, (b0, bn) in enumerate(blks):
                        TTo("dve", ACTB[:, j, b0:b0 + bn], PS[pu[bi]][:, 0:bn], SG[sgs[bi]][:, 0:bn], ALU.mult,
                            [("PS", pu[bi]), ("TT", sgs[bi])], [("ACTB", (j, b0))])
                for oc in range(8):
                    wi = WBS.next()
                    pd = [psum() for _ in blks]
                    for j in range(nch):
                        for bi, (b0, bn) in enumerate(blks):
                            MM(PS[pd[bi]][:, 0:bn], WB[wi][:, j, :], ACTB[:, j, b0:b0 + bn],
                               j == 0, j == nch - 1, [("WB", wi), ("ACTB", (j, b0))], [("PS", pd[bi])])
                    for bi, (b0, bn) in enumerate(blks):
                        if fbi == 0:
                            CP("act", Y[:, oc, b0:b0 + bn], PS[pd[bi]][:, 0:bn], [("PS", pd[bi])], rk("Y", oc, b0, bn))
                        else:
                            TTo("dve", Y[:, oc, b0:b0 + bn], PS[pd[bi]][:, 0:bn], YF[:, oc, b0:b0 + bn], ALU.add,
                                [("PS", pd[bi])] + rk("Y", oc, b0, bn), rk("Y", oc, b0, bn))
            postnorm_residual(Y, YF, "Y", l, "ffn%d_norm_post" % which, 0.5)

        MIXR = Y
        allH = [("H", (kc, t)) for kc in range(KC) for t in range(NTM)]
        allACTB = [("ACTB", None)]

        def mixer(l):
            win = dr["w_in"][l].rearrange("(kc p) n -> p kc n", p=128)
            prenorm_to_H(l, "mix_norm_pre")
            ACT(LAYC[:, 0:4], pcol(l, "rg_lambda", 0, 4), ACTF.Exp, [("PP", None)], [("LAYC", 0)], scale=-1.0)
            ACT(LAYC[:, 0:4], LAYC[:, 0:4], ACTF.Ln, [("LAYC", 0), ("ONE1", None)], [("LAYC", 0)], bias=ONE1[:, 0:1])
            TS("dve", LAYC[:, 4:8], LAYC[:, 0:4], -16.0, None, ALU.mult, None, [("LAYC", 0)], [("LAYC", 1)])
            TS("dve", LAYC[:, 0:4], LAYC[:, 0:4], -8.0, None, ALU.mult, None, [("LAYC", 0), ("LAYC", 1)], [("LAYC", 0)])
            ACT(LAYC[:, 8:12], pcol(l, "dn_a_log", 0, 4), ACTF.Exp, [("PP", None)], [("LAYC", 2)])
            TS("dve", LAYC[:, 8:12], LAYC[:, 8:12], -1.0, None, ALU.mult, None, [("LAYC", 2)], [("LAYC", 2)])
            P.dma("pool", RGWt[:], dr["rgw_r"][l].rearrange("g p c m -> p g c m"), writes=[("RGW", None)])
            P.dma("pool", WSC[:], win[:, :, 3072:3080], writes=[("WSC", None)])
            if smp:
                for q in range(4):
                    P.dma("sp", CSL, dr["st_conv"][l][:, q * 512:(q + 1) * 512], writes=[("TT", 3)])
                    pi = psum()
                    for j in range(4):
                        TR(PS[pi][:, j * 48:(j + 1) * 48], CSL[:, j * 128:(j + 1) * 128], ident[0:48, 0:48],
                           [("TT", 3), ("C32", None)], [("PS", pi)])
                    CP("dve", CSS[:, q * 4:(q + 1) * 4, :], PS[pi][:, 0:192].rearrange("p (j c) -> p j c", j=4),
                       [("PS", pi)], [("CSS", q * 4 + j) for j in range(4)])
                P.dma("sp", RSL, dr["st_rg"][l][:, :], writes=[("TT", 3)])
                pi = psum()
                for j in range(4):
                    TR(PS[pi][:, j * 16:(j + 1) * 16], RSL[:, j * 128:(j + 1) * 128], ident[0:16, 0:16],
                       [("TT", 3), ("C32", None)], [("PS", pi)])
                CP("dve", HS0[:, :, :], PS[pi][:, 0:64].rearrange("p (j c) -> p j c", j=4), [("PS", pi)], [("HS0", None)])

            wa_state = {}

            def proj():
                wi = WAS.next()
                pb = [psum() for _ in blks]
                for kc in range(KC):
                    for bi, (b0, bn) in enumerate(blks):
                        MM(PS[pb[bi]][:, 0:bn], WA[wi][:, kc, :], H[:, kc, b0:b0 + bn],
                           kc == 0, kc == KC - 1, [("WA", wi)] + rk("H", kc, b0, bn), [("PS", pb[bi])])
                return pb

            def conv_chunk(l, ch, pb):
                CP("act", XP[:, 0:3], CONVT[:, l, ch, :], [("CONVT", (l, ch))], [("XP", 0)])
                for bi, (b0, bn) in enumerate(blks):
                    n = min(bn, npr - b0)
                    if n > 0:
                        CP("dve" if bi == 0 else "act", XP[:, 3 + b0:3 + b0 + n], PS[pb[bi]][:, 0:n], [("PS", pb[bi])], [("XP", 1 + bi)])
                if smp:
                    b0, bn = blks[-1]
                    off = npr - b0
                    CP("dve", XPS[:, :, 0:3], CSS[:, ch, :].rearrange("p (s t) -> p s t", t=3), [("CSS", ch)], [("XPS", 0)])
                    CP("act", XPS[:, :, 3:11], PS[pb[-1]][:, off:off + 128].rearrange("p (s t) -> p s t", t=8),
                       [("PS", pb[-1])], [("XPS", 1)])
                for tap in range(4):
                    TS("dve", DG[:, tap, :], ident, pcol(l, "conv_w", ch * 4 + tap), None, ALU.mult, None,
                       [("C32", None), ("PP", None)], [("DG", tap)])
                xpk = [("XP", i) for i in range(1 + len(blks))]
                pc = [psum() for _ in blks]
                for bi, (b0, bn) in enumerate(blks):
                    n = min(bn, npr - b0)
                    for tap in range(4):
                        MM(PS[pc[bi]][:, 0:n], DG[:, tap, :], XP[:, b0 + tap:b0 + tap + n], tap == 0, tap == 3,
                           xpk + [("DG", tap)], [("PS", pc[bi])])
                CP("dve", CONVT[:, l, ch, :], XPf[:, npr:npr + 3], xpk, [("CONVT", (l, ch))])
                if smp:
                    b0, bn = blks[-1]
                    off = npr - b0
                    for tap in range(4):
                        MM(PS[pc[-1]][:, off:off + 128].rearrange("p (s t) -> p s t", t=8), DG[:, tap, :], XPS[:, :, tap:tap + 8],
                           False, tap == 3, [("XPS", 0), ("XPS", 1), ("DG", tap)], [("PS", pc[-1])])
                    CP("dve", CSS[:, ch, :].rearrange("p (s t) -> p s t", t=3), XPSf[:, :, 8:11], [("XPS", 0), ("XPS", 1)], [("CSS", ch)])
                return pc

            T0 = TT[:, 0, :]; T1 = TT[:, 1, :]; T2 = TT[:, 2, :]
            XRf = XR[:].bitcast(F32)
            k0, k1, k2, k3 = ("TT", 0), ("TT", 1), ("TT", 2), ("XR", None)

            nb_ = len(blks)
            allrow = [("TT", i) for i in range(4)] + [("XR", None)]
            allblk = [("TT", (i, bi)) for i in range(3) for bi in range(nb_)] + [("XR", bi) for bi in range(nb_)]
            P.op("dve", lambda e: e.memset(FEN[:, 0:1], 0.0), allrow + allblk, allrow + allblk)
            kq = lambda i, bi: ("TT", (i, bi))
            kx = lambda bi: ("XR", bi)
            for ch in range(4):
                pb = proj()
                pc = conv_chunk(l, ch, pb)
                for bi, (b0, bn) in enumerate(blks):
                    ACT(XR[:, b0:b0 + bn], PS[pc[bi]][:, 0:bn], ACTF.Identity, [("PS", pc[bi]), ("PP", None), kx(bi)], [kx(bi)],
                        bias=pcol(l, "conv_b_rg", ch))
                pa = [psum() for _ in blks]
                for bi, (b0, bn) in enumerate(blks):
                    MM(PS[pa[bi]][:, 0:bn], RGWt[:, 0, ch, :], XR[:, b0:b0 + bn], True, True, [("RGW", None), kx(bi)], [("PS", pa[bi])])
                px = [psum() for _ in blks]
                for bi, (b0, bn) in enumerate(blks):
                    MM(PS[px[bi]][:, 0:bn], RGWt[:, 1, ch, :], XR[:, b0:b0 + bn], True, True, [("RGW", None), kx(bi)], [("PS", px[bi])])
                pgt = proj()
                for p_ in pgt:
                    resv.add(p_)
                for bi, (b0, bn) in enumerate(blks):
                    ACT(T0[:, b0:b0 + bn], PS[pa[bi]][:, 0:bn], ACTF.Sigmoid, [("PS", pa[bi]), ("PP", None), kq(0, bi)], [kq(0, bi)],
                        bias=pcol(l, "rg_b_a", ch))
                for bi, (b0, bn) in enumerate(blks):
                    ACT(T1[:, b0:b0 + bn], PS[px[bi]][:, 0:bn], ACTF.Sigmoid, [("PS", px[bi]), ("PP", None), kq(1, bi)], [kq(1, bi)],
                        bias=pcol(l, "rg_b_x", ch))
                for bi, (b0, bn) in enumerate(blks):
                    sl = slice(b0, b0 + bn)
                    ACT(T2[:, sl], T0[:, sl], ACTF.Exp, [kq(0, bi), ("LAYC", 1), kq(2, bi)], [kq(2, bi)], scale=LAYC[:, 4 + ch:5 + ch])
                for bi, (b0, bn) in enumerate(blks):
                    sl = slice(b0, b0 + bn)
                    ACT(T0[:, sl], T0[:, sl], ACTF.Exp, [kq(0, bi), ("LAYC", 0)], [kq(0, bi)], scale=LAYC[:, ch:ch + 1])
                for bi, (b0, bn) in enumerate(blks):
                    sl = slice(b0, b0 + bn)
                    TS("dve", T2[:, sl], T2[:, sl], -1.0, 1.0, ALU.mult, ALU.add, [kq(2, bi)], [kq(2, bi)])
                    TS("dve", T2[:, sl], T2[:, sl], 1e-30, None, ALU.max, None, [kq(2, bi)], [kq(2, bi)])
                    TTo("dve", T1[:, sl], T1[:, sl], XRf[:, sl], ALU.mult, [kq(1, bi), kx(bi)], [kq(1, bi)])
                for bi, (b0, bn) in enumerate(blks):
                    sl = slice(b0, b0 + bn)
                    ACT(T2[:, sl], T2[:, sl], ACTF.Ln, [kq(2, bi)], [kq(2, bi)])
                    ACT(T2[:, sl], T2[:, sl], ACTF.Exp, [kq(2, bi)], [kq(2, bi)], scale=0.5)
                for bi, (b0, bn) in enumerate(blks):
                    sl = slice(b0, b0 + bn)
                    TTo("dve", T1[:, sl], T1[:, sl], T2[:, sl], ALU.mult, [kq(1, bi), kq(2, bi)], [kq(1, bi)])
                for bi, (b0, bn) in enumerate(blks):
                    n = min(bn, npr - b0)
                    if n <= 0:
                        continue
                    init = HRG[:, l, ch:ch + 1] if bi == 0 else T2[:, b0 - 1:b0]
                    extra = [("HRG", (l, ch))] if bi == 0 else [kq(2, bi - 1)]
                    P.op("dve", lambda e, b0=b0, n=n, init=init: e.tensor_tensor_scan(
                        out=T2[:, b0:b0 + n], data0=T0[:, b0:b0 + n], data1=T1[:, b0:b0 + n],
                        initial=init, op0=ALU.mult, op1=ALU.add),
                        [kq(0, bi), kq(1, bi), kq(2, bi)] + extra, [kq(2, bi)])
                lastb = max(bi for bi, (b0, bn) in enumerate(blks) if npr > b0)
                CP("dve", HRG[:, l, ch:ch + 1], T2[:, npr - 1:npr], [kq(2, lastb)], [("HRG", (l, ch))])
                if smp:
                    sb_ = nb_ - 1
                    a_first = T0[:, npr:npr + 128:8]
                    b_first = T1[:, npr:npr + 128:8]
                    TTo("dve", SM2[:, 0, 0:16], a_first, HS0[:, ch, :], ALU.mult, [kq(0, sb_), ("HS0", None)], [("SMr", None)])
                    TTo("dve", b_first, b_first, SM2[:, 0, 0:16], ALU.add, [kq(1, sb_), ("SMr", None)], [kq(1, sb_)])
                    TS("dve", a_first, a_first, 0.0, None, ALU.mult, None, [kq(0, sb_), ("SMr", None)], [kq(0, sb_)])
                    P.op("dve", lambda e: e.tensor_tensor_scan(out=T2[:, npr:npr + 128], data0=T0[:, npr:npr + 128],
                                                              data1=T1[:, npr:npr + 128], initial=0.0, op0=ALU.mult, op1=ALU.add),
                         [kq(0, sb_), kq(1, sb_), kq(2, sb_)], [kq(2, sb_)])
                    CP("dve", HS0[:, ch, :], T2[:, npr + 7:npr + 128:8], [kq(2, sb_), ("SMr", None)], [("HS0", None)])
                for bi, (b0, bn) in enumerate(blks):
                    sl = slice(b0, b0 + bn)
                    CP("act", T0[:, sl], PS[pgt[bi]][:, 0:bn], [("PS", pgt[bi]), kq(0, bi)], [kq(0, bi)])
                    ACT(T1[:, sl], PS[pgt[bi]][:, 0:bn], ACTF.Square, [("PS", pgt[bi]), kq(1, bi)], [kq(1, bi)])
                    resv.discard(pgt[bi])
                for bi, (b0, bn) in enumerate(blks):
                    sl = slice(b0, b0 + bn)
                    TS("dve", T1[:, sl], T1[:, sl], 0.044715, 1.0, ALU.mult, ALU.add, [kq(1, bi)], [kq(1, bi)])
                    TTo("dve", T1[:, sl], T1[:, sl], T0[:, sl], ALU.mult, [kq(0, bi), kq(1, bi)], [kq(1, bi)])
                for bi, (b0, bn) in enumerate(blks):
                    sl = slice(b0, b0 + bn)
                    ACT(T1[:, sl], T1[:, sl], ACTF.Sigmoid, [kq(1, bi)], [kq(1, bi)], scale=2.0 * 0.7978845608028654)
                for bi, (b0, bn) in enumerate(blks):
                    sl = slice(b0, b0 + bn)
                    TTo("dve", T0[:, sl], T0[:, sl], T1[:, sl], ALU.mult, [kq(0, bi), kq(1, bi)], [kq(0, bi)])
                    TTo("dve", MIXR[:, ch, sl], T2[:, sl], T0[:, sl], ALU.mult, [kq(0, bi), kq(2, bi)], rk("Y", ch, b0, bn))
            P.op("dve", lambda e: e.memset(FEN[:, 0:1], 0.0), allrow + allblk, allrow + allblk)

            TB = [(T0, k0), (T1, k1)]

            def qk_front(kind, hh, slot):
                ch = 4 + kind * 4 + hh
                Tb, kb = TB[slot]
                pb = proj()
                pc = conv_chunk(l, ch, pb)
                for bi, (b0, bn) in enumerate(blks):
                    ACT(Tb[:, b0:b0 + bn], PS[pc[bi]][:, 0:bn], ACTF.Silu, [("PS", pc[bi]), kb], [kb])
                for bi, (b0, bn) in enumerate(blks):
                    sq = slot * 2 + bi
                    TTo("dve", SQ[:, sq, 0:bn], Tb[:, b0:b0 + bn], Tb[:, b0:b0 + bn], ALU.mult, [kb], [("SQ", sq)])

            def qk_tail(kind, hh, slot):
                Tb, kb = TB[slot]
                for bi, (b0, bn) in enumerate(blks):
                    sq = slot * 2 + bi
                    pi = psum()
                    MM(PS[pi][:, 0:bn], ones_r, SQ[:, sq, 0:bn], True, True, [("SQ", sq), ("ONESR", None)], [("PS", pi)])
                    if kind == 0:
                        ACT(PS[pi][:, 0:bn], PS[pi][:, 0:bn], ACTF.Ln, [("PS", pi), ("EPST", None)], [("PS", pi)],
                            scale=128.0, bias=EPSC["q"])
                    else:
                        ACT(PS[pi][:, 0:bn], PS[pi][:, 0:bn], ACTF.Ln, [("PS", pi), ("EPST", None)], [("PS", pi)],
                            scale=1.0, bias=EPSC[(1, 1.0)])
                    ACT(PS[pi][:, 0:bn], PS[pi][:, 0:bn], ACTF.Exp, [("PS", pi)], [("PS", pi)], scale=-0.5)
                    TTo("dve", QKV[:, kind * 4 + hh, b0:b0 + bn], Tb[:, b0:b0 + bn], PS[pi][:, 0:bn], ALU.mult,
                        [kb, ("PS", pi)], [("QKV", None)])

            for kind in range(2):
                for hp in range(2):
                    qk_front(kind, 2 * hp, 0)
                    qk_front(kind, 2 * hp + 1, 1)
                    qk_tail(kind, 2 * hp, 0)
                    qk_tail(kind, 2 * hp + 1, 1)
            for hh in range(4):
                pb = proj()
                pc = conv_chunk(l, 12 + hh, pb)
                for bi, (b0, bn) in enumerate(blks):
                    ACT(QKV[:, 8 + hh, b0:b0 + bn], PS[pc[bi]][:, 0:bn], ACTF.Silu, [("PS", pc[bi])], [("QKV", None)])

            for cpair in range(2):
                for cc in range(2):
                    hh = cpair * 2 + cc
                    pb = proj()
                    for bi, (b0, bn) in enumerate(blks):
                        ACT(ZG[:, hh, b0:b0 + bn], PS[pb[bi]][:, 0:bn], ACTF.Silu, [("PS", pb[bi])], [("ZG", (hh, bi))])

            pi = psum()
            for t in range(ntile):
                for kc in range(KC):
                    MM(PS[pi][:, t * 8:(t + 1) * 8], H[:, kc, t * 128:(t + 1) * 128], WSC[:, kc, :], (t == 0 and kc == 0), kc == KC - 1,
                       [("WSC", None), ("H", (kc, t))], [("PS", pi)])
            CP("dve", SCT[:, 0:ntile, :], PS[pi][:, 0:ntile * 8].rearrange("p (t c) -> p t c", c=8), [("PS", pi)], [("SCT", None)])

            delta_phase(l)

            if getattr(cfg, "debug", False) and l == 0:
                out_ops.append(P.dma("sp", dr["dbg"][gi], YF, reads=[("Y", None)], writes=[("DBG", gi)]))
            for ocp in range(4):
                for o2 in range(2):
                    oc = ocp * 2 + o2
                    wi = WAS.next()
                    pd = [psum() for _ in blks]
                    for kc in range(KC):
                        for bi, (b0, bn) in enumerate(blks):
                            MM(PS[pd[bi]][:, 0:bn], WA[wi][:, kc, :], MIXR[:, kc, b0:b0 + bn],
                               kc == 0, kc == KC - 1, [("WA", wi)] + rk("Y", kc, b0, bn), [("PS", pd[bi])])
                    for bi, (b0, bn) in enumerate(blks):
                        CP("act", H[:, oc, b0:b0 + bn], PS[pd[bi]][:, 0:bn], [("PS", pd[bi])], rk("H", oc, b0, bn))
            postnorm_residual(H, HF, "H", l, "mix_norm_post", 1.0)

            if smp:
                for q in range(4):
                    pi = psum()
                    for j in range(4):
                        TR(PS[pi][0:48, j * 128:(j + 1) * 128], CSS[:, q * 4 + j, :], ident, [("CSS", q * 4 + j), ("C32", None)], [("PS", pi)])
                    CP("dve", CSL, PS[pi][0:48, :], [("PS", pi)], [("TT", 3)])
                    out_ops.append(P.dma("sp", dr["ncs"][l][:, q * 512:(q + 1) * 512], CSL, reads=[("TT", 3)], writes=[("CSLo", q)]))
                pi = psum()
                for j in range(4):
                    TR(PS[pi][0:16, j * 128:(j + 1) * 128], HS0[:, j, :], ident, [("HS0", None), ("C32", None)], [("PS", pi)])
                CP("dve", RSL, PS[pi][0:16, :], [("PS", pi)], [("TT", 3)])
                out_ops.append(P.dma("sp", dr["nrs"][l][:, :], RSL, reads=[("TT", 3)], writes=[("RSLo", 0)]))
            if last_group:
                for q in range(4):
                    pi = psum()
                    for j in range(4):
                        TR(PS[pi][0:3, j * 128:(j + 1) * 128], CONVT[:, l, q * 4 + j, :], ident, [("CONVT", (l, q * 4 + j)), ("C32", None)], [("PS", pi)])
                    CP("dve", CSL[0:3, :], PS[pi][0:3, :], [("PS", pi)], [("TT", 3)])
                    out_ops.append(P.dma("sp", dr["ncp"][l][:, q * 512:(q + 1) * 512], CSL[0:3, :], reads=[("TT", 3)], writes=[("CSLo", q)]))
                pi = psum()
                TR(PS[pi][0:4, 0:128], HRG[:, l, :], ident, [("HRG", None), ("C32", None)], [("PS", pi)])
                CP("dve", RSL[0:4, 0:128], PS[pi][0:4, 0:128], [("PS", pi)], [("TT", 3)])
                out_ops.append(P.dma("sp", dr["nrp"][l].rearrange("(c p) -> c p", p=128), RSL[0:4, 0:128], reads=[("TT", 3)], writes=[("RSLo", 0)]))
                out_ops.append(P.dma("sp", dr["ndp"][l].rearrange("h d e -> d h e"), SST[:, l, :, :], reads=[("SST", None)], writes=[("SSTo", l)]))

        def delta_prelude(l, t, is_s):
            sfx = "_s" if is_s else "_p"
            par = t % 2
            SM = SM2[:, par, :]
            EGR = EGR2[:, par]
            BETA = SM[:, 16:20]; GT = SM[:, 20:24]; GC = SM[:, 24:32]; EG = SM[:, 32:36]; BEG = SM[:, 36:40]; EKD = SM[:, 40:44]
            ACT(BETA, SCT[:, t, 0:4], ACTF.Sigmoid, [("SCT", None)], [("SM", (par, 1))])
            TTo("dve", GT, SCT[:, t, 4:8], pcol(l, "dn_dt_bias", 0, 4), ALU.add, [("SCT", None), ("PP", None)], [("SM", (par, 2))])
            ACT(GT, GT, ACTF.Exp, [("SM", (par, 2))], [("SM", (par, 2))])
            ACT(GT, GT, ACTF.Ln, [("SM", (par, 2)), ("ONE1", None)], [("SM", (par, 2))], bias=ONE1[:, 0:1])
            TTo("dve", GT, GT, LAYC[:, 8:12], ALU.mult, [("SM", (par, 2)), ("LAYC", 2)], [("SM", (par, 2))])
            pgc = psum()
            MM(PS[pgc][:, 0:4], c32("tri" + sfx), GT, True, True, [("SM", (par, 2)), ("C32", None)], [("PS", pgc)])
            MM(PS[pgc][:, 4:8], c32("up" + sfx), GT, False, True, [("SM", (par, 2)), ("C32", None)], [("PS", pgc)])
            bc4 = lambda ap: ap.unsqueeze(2).broadcast_to([128, NH, 128])
            bm4 = lambda ap: ap.unsqueeze(1).broadcast_to([128, NH, 128])
            TTo("dve", TRG[:, :, :], bm4(c32("tri" + sfx)), bc4(GT), ALU.mult, [("SM", (par, 2)), ("C32", None), ("TRG", 0), ("TRG", 1)], [("TRG", 0), ("TRG", 1)])
            pgr = psum()
            resv.add(pgr)
            st["pgr"] = pgr
            st["pgr_users"] = 2
            MM(PS[pgr][:, :], ones_f, TRG[:].rearrange("p h c -> p (h c)"), True, True, [("TRG", 0), ("TRG", 1), ("C32", None)], [("PS", pgr)])
            CP("dve", GC, PS[pgc][:, 0:8], [("PS", pgc), ("PS", pgr)], [("SM", (par, 3))])
            ACT(EG, GC[:, 0:4], ACTF.Exp, [("SM", (par, 3))], [("SM", (par, 4))])
            ACT(EKD, GC[:, 4:8], ACTF.Exp, [("SM", (par, 3))], [("SM", (par, 5))])
            TTo("dve", BEG, BETA, EG, ALU.mult, [("SM", (par, 1)), ("SM", (par, 4))], [("SM", (par, 6))])
            PGR4 = PS[pgr][:].rearrange("p (h c) -> p h c", h=NH)
            gk = [("GRW", 0), ("GRW", 1)]; ek = [("EGR", (par, 0)), ("EGR", (par, 1))]
            TTo("dve", GRW[:, :, :], PGR4, GC[:, 0:4].unsqueeze(2).broadcast_to([128, NH, 128]), ALU.subtract, [("PS", pgr), ("SM", (par, 3))] + gk, gk)
            GRWf = GRW[:].rearrange("p h c -> p (h c)")
            STT(GRWf, GRWf, -1.0, GRWf, ALU.mult, ALU.max, gk, gk)
            ACT(GRW[:], GRW[:], ACTF.Exp, gk, gk, scale=-1.0)
            ACT(EGR[:], PGR4, ACTF.Exp, [("PS", pgr)] + ek, ek)
            resv.discard(pgr)

        def delta_pair(l, t, is_s, hg, ds):
            c0 = t * 128
            par = t % 2
            SM = SM2[:, par, :]
            EGR = EGR2[:, par]
            sfx = "_s" if is_s else "_p"
            nlv = NLV_S if is_s else NLV_P
            HS = slice(2 * hg, 2 * hg + 2)
            heads = (2 * hg, 2 * hg + 1)
            bk = lambda n: [((n + str(ds)) if n in DBNAMES else n, hg)]
            KTOK, VTOK, AM, ATT, DTt, DINV, N1, BV, BKG, WT, VN, QG, KD, SBF = [DBS[ds][n][:, HS, :] for n in dt_names_bf]
            f2 = lambda ap: ap.rearrange("p h c -> p (h c)")
            qkv_r = [("QKV", None)]
            bc = lambda ap: ap.unsqueeze(2).broadcast_to([128, 2, 128])
            bm = lambda ap: ap.unsqueeze(1).broadcast_to([128, 2, 128])
            BETA = SM[:, 16 + 2 * hg:18 + 2 * hg]; GT = SM[:, 20 + 2 * hg:22 + 2 * hg]; GC0 = SM[:, 24 + 2 * hg:26 + 2 * hg]
            BEG = SM[:, 36 + 2 * hg:38 + 2 * hg]; EKD = SM[:, 40 + 2 * hg:42 + 2 * hg]
            GRWp = GRW[:, HS, :]; EGRp = EGR[:, HS, :]; TRGp = TRG[:, HS, :]; OTp = OT[:, HS, :]
            pgr = st["pgr"]
            ptk = psum(True)
            for i, h in enumerate(heads):
                TR(psb(ptk)[:, i * 128:(i + 1) * 128], QKV[:, 4 + h, c0:c0 + 128], identb, qkv_r + [("C16", None)], [("PS", ptk)])
            for i, h in enumerate(heads):
                TR(psb(ptk)[:, 256 + i * 128:256 + (i + 1) * 128], QKV[:, 8 + h, c0:c0 + 128], identb, qkv_r + [("C16", None)], [("PS", ptk)])
            pkk = psum(True)
            for i, h in enumerate(heads):
                MM(PS[pkk][:, i * 128:(i + 1) * 128], QKV[:, 4 + h, c0:c0 + 128], QKV[:, 4 + h, c0:c0 + 128], i == 0, True, qkv_r, [("PS", pkk)])
            for i, h in enumerate(heads):
                MM(PS[pkk][:, 256 + i * 128:256 + (i + 1) * 128], QKV[:, 4 + h, c0:c0 + 128], QKV[:, h, c0:c0 + 128], False, True, qkv_r,
                   [("PS", pkk), ("PGD", hg)])
            yield
            PGR3 = PS[pgr][:, hg * 256:(hg + 1) * 256].rearrange("p (h c) -> p h c", h=2)

            CP("act", f2(KTOK), psb(ptk)[:, 0:256], [("PS", ptk)] + bk("KTOK"), bk("KTOK"))
            CP("act", f2(VTOK), psb(ptk)[:, 256:512], [("PS", ptk)] + bk("VTOK"), bk("VTOK"))
            resv.discard(ptk)
            yield
            TTo("dve", BV, VTOK, bc(BETA), ALU.mult, bk("VTOK") + [("SM", (par, 1))] + bk("BV"), bk("BV"))
            TTo("dve", BKG, KTOK, bc(BEG), ALU.mult, bk("KTOK") + [("SM", (par, 6))] + bk("BKG"), bk("BKG"))
            TTo("dve", KD, KTOK, bc(EKD), ALU.mult, bk("KTOK") + [("SM", (par, 5))] + bk("KD"), bk("KD"))
            TTo("dve", QG, QKV[:, HS, c0:c0 + 128], EGRp, ALU.mult, qkv_r + [("EGR", (par, hg))] + bk("QG"), bk("QG"))
            CP("act", DTt, bm(identb), [("C16", None)] + bk("DT"), bk("DT"))
            CP("act", DINV, bm(identb), [("C16", None)] + bk("DINV"), bk("DINV"))
            yield
            TTo("dve", TRGp, GRWp, bm(c16("mstrict" + sfx)), ALU.mult, bk("GRW") + [("C16", None)] + bk("TRG"), bk("TRG"))
            TTo("dve", TRGp, TRGp, bc(BETA), ALU.mult, bk("TRG") + [("SM", (par, 1))], bk("TRG"))
            TTo("dve", OTp, GRWp, bm(c16("minclt" + sfx)), ALU.mult, bk("GRW") + [("C16", None)] + bk("OT"), bk("OT"))
            TTo("dve", f2(AM), PS[pkk][:, 0:256], f2(TRGp), ALU.mult, [("PS", pkk)] + bk("TRG") + bk("AM"), bk("AM"))
            TTo("dve", f2(ATT), PS[pkk][:, 256:512], f2(OTp), ALU.mult, [("PS", pkk)] + bk("OT") + bk("ATT"), bk("ATT"))
            resv.discard(pkk)
            yield
            for lv in range(nlv):
                p1 = psum(True)
                for i in range(2):
                    MM(PS[p1][:, i * 128:(i + 1) * 128], AM[:, i, :], DTt[:, i, :], i == 0, True, bk("AM") + bk("DT"), [("PS", p1)])
                yield
                TTo("dve", N1, PS[p1][:, 0:256].rearrange("p (h c) -> p h c", h=2), bm(c16("lv_p%d" % lv)), ALU.mult,
                    [("PS", p1), ("C16", None)] + bk("N1"), bk("N1"))
                resv.discard(p1)
                yield
                p2 = psum(True)
                for i in range(2):
                    MM(PS[p2][:, i * 128:(i + 1) * 128], DINV[:, i, :], N1[:, i, :], i == 0, True, bk("DINV") + bk("N1"), [("PS", p2)])
                yield
                TTo("dve", f2(DTt), f2(DTt), PS[p2][:, 0:256], ALU.subtract, [("PS", p2)] + bk("DT"), bk("DT"))
                resv.discard(p2)
                yield
                if lv < nlv - 1:
                    p3 = psum(True)
                    for i in range(2):
                        TR(psb(p3)[:, i * 128:(i + 1) * 128], DTt[:, i, :], identb, bk("DT") + [("C16", None)], [("PS", p3)])
                    yield
                    CP("act", f2(DINV), psb(p3)[:, 0:256], [("PS", p3)] + bk("DINV"), bk("DINV"))
                    resv.discard(p3)
                    yield
            pu = psum()
            resv.add(pu)
            if not is_s:
                for i in range(2):
                    MM(PS[pu][:, i * 128:(i + 1) * 128], DTt[:, i, :], BV[:, i, :], i == 0, False, bk("DT") + bk("BV"), [("PS", pu)])
            pw = psum(True)
            for i in range(2):
                MM(PS[pw][:, i * 128:(i + 1) * 128], BKG[:, i, :], DTt[:, i, :], i == 0, True, bk("DT") + bk("BKG"), [("PS", pw)])
            yield
            ACT(f2(WT), PS[pw][:, 0:256], ACTF.Copy, [("PS", pw)] + bk("AM"), bk("AM"), scale=-1.0)
            resv.discard(pw)
            po = psum()
            resv.add(po)
            if not is_s:
                CP("act", SBF, SST[:, l, HS, :], [("SST", hg)] + bk("DINV"), bk("DINV"))
                yield
                for i in range(2):
                    MM(PS[pu][:, i * 128:(i + 1) * 128], WT[:, i, :], SBF[:, i, :], False, True, bk("AM") + bk("DINV"), [("PS", pu)])
                yield
                CP("act", f2(VN), PS[pu][:, 0:256], [("PS", pu)] + bk("N1"), bk("N1"))
                resv.discard(pu)
                yield
                for i in range(2):
                    MM(PS[po][:, i * 128:(i + 1) * 128], SBF[:, i, :], QG[:, i, :], i == 0, False, bk("DINV") + bk("QG"), [("PS", po)])
                for i in range(2):
                    MM(PS[po][:, i * 128:(i + 1) * 128], VN[:, i, :], ATT[:, i, :], False, True, bk("N1") + bk("ATT"), [("PS", po)])
                psu = psum(True)
                for i in range(2):
                    MM(PS[psu][:, i * 128:(i + 1) * 128], KD[:, i, :], VN[:, i, :], i == 0, True, bk("KD") + bk("N1"), [("PS", psu)])
                yield
                for i, h in enumerate(heads):
                    STT(SST[:, l, h, :], SST[:, l, h, :], EGR[:, h, 127:128], PS[psu][:, i * 128:(i + 1) * 128], ALU.mult, ALU.add,
                        [("PS", psu), ("SST", hg)] + [("EGR", (par, hg))], [("SST", hg)])
                resv.discard(psu)
            else:
                resv.discard(pu)
                HSQ = NS // 2
                SS0 = TT[:, 0:2, :].rearrange("p a t -> p (a t)")[:, 0:HSQ * 128].rearrange("p (s e) -> p s e", s=HSQ)
                SS0B = TT[:, 2, :].bitcast(BF16)[:, 0:HSQ * 128].rearrange("p (s e) -> p s e", s=HSQ)
                WTX = TT[:, 3, 0:512].bitcast(BF16).rearrange("p (s c) -> p s c", s=HSQ)
                kS0 = [("TT", 0), ("TT", 1)]; kS0B = [("TT", 2)]; kWX = [("TT", 3)]
                segrow = c16("segrow", NS * 128).rearrange("p (s c) -> p s c", s=NS)
                segcol = c16("segcol", NS)
                first_po = True
                for i, h in enumerate(heads):
                    for hf in range(2):
                        s0 = hf * HSQ
                        P.dma("sp", SS0, dr["st_dn"][l][s0:s0 + HSQ, h, :, :].rearrange("s d e -> d s e"), reads=[], writes=kS0)
                        CP("act", SS0B, SS0, kS0 + kS0B, kS0B)
                        TTo("dve", WTX, WT[:, i, :].unsqueeze(1).broadcast_to([128, HSQ, 128]), segrow[:, s0:s0 + HSQ, :], ALU.mult,
                            bk("AM") + [("C16", None)] + kWX, kWX)
                        pu2 = psum(True)
                        MM(PS[pu2][:, 0:128], DTt[:, i, :], BV[:, i, :], True, False, bk("DT") + bk("BV"), [("PS", pu2)])
                        for s_ in range(HSQ):
                            MM(PS[pu2][:, 0:128], WTX[:, s_, :], SS0B[:, s_, :], False, True, kS0B + kWX, [("PS", pu2)])
                        CP("act", VN[:, i, :], PS[pu2][:, 0:128], [("PS", pu2)] + bk("N1"), bk("N1"))
                        resv.discard(pu2)
                        cb = i * 128 + hf * 64
                        MM(PS[po][:, cb:cb + 64], VN[:, i, :], ATT[:, i, hf * 64:hf * 64 + 64], first_po, False, bk("N1") + bk("ATT"), [("PS", po)])
                        first_po = False
                        for s_ in range(HSQ):
                            cs = i * 128 + (s0 + s_) * 8
                            MM(PS[po][:, cs:cs + 8], SS0B[:, s_, :], QG[:, i, (s0 + s_) * 8:(s0 + s_) * 8 + 8], False, True,
                               kS0B + bk("QG"), [("PS", po)])
                        TTo("dve", WTX, KD[:, i, :].unsqueeze(1).broadcast_to([128, HSQ, 128]),
                            segcol[:, s0:s0 + HSQ].unsqueeze(2).broadcast_to([128, HSQ, 128]), ALU.mult,
                            bk("KD") + [("C16", None)] + kWX, kWX)
                        for q in range(2):
                            psu = psum(True)
                            for j in range(4):
                                s_ = q * 4 + j
                                MM(PS[psu][:, j * 128:(j + 1) * 128], WTX[:, s_, :], VN[:, i, :], j == 0, True, kWX + bk("N1"), [("PS", psu)])
                            for j in range(4):
                                s_ = q * 4 + j
                                sg_ = s0 + s_
                                STT(SS0[:, s_, :], SS0[:, s_, :], EGR[:, h, sg_ * 8 + 7:sg_ * 8 + 8], PS[psu][:, j * 128:(j + 1) * 128],
                                    ALU.mult, ALU.add, [("PS", psu)] + kS0 + [("EGR", (par, hg))], kS0)
                            resv.discard(psu)
                        out_ops.append(P.dma("sp", dr["nds"][l][s0:s0 + HSQ, h, :, :].rearrange("s d e -> d s e"), SS0, reads=kS0, writes=[("NDSo", h)]))
                        yield
            SQW = SQ[:, 2 * hg, 0:256]
            ACT(SQW, PS[po][:, 0:256], ACTF.Square, [("PS", po)], [("SQ", 2 * hg)])
            yield
            pss = psum(True)
            MM(PS[pss][:, 0:256], ones_r, SQW, True, True, [("SQ", 2 * hg), ("ONESR", None)], [("PS", pss)])
            yield
            ACT(PS[pss][:, 0:256], PS[pss][:, 0:256], ACTF.Ln, [("PS", pss), ("EPST", None)], [("PS", pss)], scale=1.0 / 128.0, bias=EPSC[(128, 1.0)])
            ACT(PS[pss][:, 0:256], PS[pss][:, 0:256], ACTF.Exp, [("PS", pss)], [("PS", pss)], scale=-0.5)
            yield
            CP("act", f2(OTp), PS[pss][:, 0:256], [("PS", pss)] + bk("OT"), bk("OT"))
            resv.discard(pss)
            STT(f2(OTp), PS[po][:, 0:256], pcol(l, "dn_norm_w"), f2(OTp), ALU.mult, ALU.mult, [("PS", po), ("PP", None)] + bk("OT"), bk("OT"))
            resv.discard(po)
            TTo("dve", MIXR[:, 4 + 2 * hg:6 + 2 * hg, c0:c0 + 128], OTp, ZG[:, HS, c0:c0 + 128], ALU.mult,
                bk("OT") + [("ZG", None)], [("Y", (4 + h, t)) for h in heads])

        def delta_phase(l):
            db1keys = [(n + "1", hg) for n in DBNAMES for hg in range(2)]
            ttrows = [("TT", i) for i in range(4)]
            P.op("dve", lambda e: e.memset(FEN[:, 0:1], 0.0), ttrows + db1keys, ttrows + db1keys)
            tiles = list(range(ntile))
            active = []
            nxt_i = 0
            while nxt_i < len(tiles) or active:
                if nxt_i < len(tiles):
                    t = tiles[nxt_i]
                    is_s = smp and t == ntile - 1
                    can = False
                    if not active:
                        can = True
                    elif (not is_s) and len(active) == 1 and active[0][3] >= 27 and not active[0][4]:
                        can = True
                    if can:
                        ds = 0 if is_s else (t % 2)
                        if is_s:
                            P.op("dve", lambda e: e.memset(FEN[:, 0:1], 0.0), ttrows + db1keys, ttrows + db1keys)
                        delta_prelude(l, t, is_s)
                        gens = [delta_pair(l, t, is_s, 0, ds), delta_pair(l, t, is_s, 1, ds)]
                        active.append([t, gens, [True, True], 0, is_s])
                        nxt_i += 1
                for rec in active:
                    for gi_, g in enumerate(rec[1]):
                        if rec[2][gi_]:
                            try:
                                next(g)
                            except StopIteration:
                                rec[2][gi_] = False
                    rec[3] += 1
                active = [r for r in active if any(r[2])]
            P.op("dve", lambda e: e.memset(FEN[:, 0:1], 0.0), ttrows + db1keys, ttrows + db1keys)

        for l in range(DEPTH):
            ffn(l, 1)
            mixer(l)
            ffn(l, 2)

        for (b0, bn) in blks:
            pi = sumsq_rstd(X, "X", b0, bn, D)
            norm_scale(Y, "Y", X, "X", 0, "final_norm", b0, bn, pi)
        for t in range(ntile):
            si = st["stg"]; st["stg"] ^= 1
            stg = stg_view(si)
            for half in range(2):
                pi = psum()
                for j in range(4):
                    kc = half * 4 + j
                    TR(PS[pi][:, j * 128:(j + 1) * 128], YF[:, kc, t * 128:(t + 1) * 128], ident, [("Y", (kc, t)), ("C32", None)], [("PS", pi)])
                CP("act" if half == 0 else "dve", stg[:, half * 512:(half + 1) * 512], PS[pi][:], [("PS", pi)], [stg_keys(si)[half]])
            dst = dr["yp"][p0 + t * 128: p0 + (t + 1) * 128, :] if t * 128 < npr else dr["ys"][:, :]
            out_ops.append(P.dma("sp", dst, stg, reads=stg_keys(si), writes=[("STGo", si)]))

    P.emit(out_ops)
    es.close()
    return nc, P


def make_in_maps(inp):
    pp = _pack_params(inp)
    c32, c16 = _consts()
    rgw = _pack_rgw(inp)
    maps = []
    shared = {"pp": pp, "c32": c32, "c16": c16, "rgw_r": rgw}
    for nm in ("ffn1_w_up", "ffn2_w_up", "ffn1_w_down", "ffn2_w_down", "w_in", "w_out"):
        shared[nm] = np.ascontiguousarray(inp[nm])
    for core in range(NCORES):
        sl = slice(core * NS, (core + 1) * NS)
        m = dict(shared)
        m["xp"] = np.ascontiguousarray(inp["x_prompt"][core])
        m["xs"] = np.ascontiguousarray(inp["x_sample"][sl].reshape(NS * DS, D))
        m["st_conv"] = np.ascontiguousarray(inp["state_conv"][:, sl].reshape(DEPTH, NS * 3, CONVC))
        m["st_rg"] = np.ascontiguousarray(inp["state_rglru"][:, sl])
        m["st_dn"] = np.ascontiguousarray(inp["state_delta"][:, sl])
        maps.append(m)
    return maps


def gather(r):
    y_prompt = np.stack([r[c]["yp"] for c in range(NCORES)], axis=0)
    y_sample = np.concatenate([r[c]["ys"].reshape(NS, DS, D) for c in range(NCORES)], axis=0)
    ncp = np.stack([r[c]["ncp"] for c in range(NCORES)], axis=1)
    nrp = np.stack([r[c]["nrp"] for c in range(NCORES)], axis=1)
    ndp = np.stack([r[c]["ndp"] for c in range(NCORES)], axis=1)
    ncs = np.concatenate([r[c]["ncs"].reshape(DEPTH, NS, 3, CONVC) for c in range(NCORES)], axis=1)
    nrs = np.concatenate([r[c]["nrs"] for c in range(NCORES)], axis=1)
    nds = np.concatenate([r[c]["nds"] for c in range(NCORES)], axis=1)
    return (y_prompt, y_sample, ncp, nrp, ndp, ncs, nrs, nds)


def kernel(**inp):
    inp = {k: np.asarray(v) for k, v in inp.items()}
    cfg = Cfg()
    nc, P = build_program(cfg)
    maps = make_in_maps(inp)
    res = run_bass_kernel_spmd(nc, maps, core_ids=list(range(NCORES)))
    return gather(res.results)
```

```python
import numpy as np
from contextlib import ExitStack
import concourse.bass as bass
import concourse.mybir as mybir
from concourse.bass_utils import run_bass_kernel_spmd

F32 = mybir.dt.float32
F32R = mybir.dt.float32r
BF16 = mybir.dt.bfloat16
ACTF = mybir.ActivationFunctionType
ALU = mybir.AluOpType

D = 1024
KC = 8
DFF = 2816
NFF = 22
DEPTH = 2
SEQ = 2048
NS = 16
DS = 8
INC = 3080
CONVC = 2048
EPS = 1e-6
NCORES = 8


class Op:
    __slots__ = ("eng", "fn", "deps", "sig", "count", "sem", "is_dma", "idx")

    def __init__(self, eng, fn, is_dma=False):
        self.eng = eng
        self.fn = fn
        self.deps = []
        self.sig = False
        self.count = None
        self.sem = None
        self.is_dma = is_dma
        self.idx = None


class Prog:
    ENGS = ("pe", "act", "dve", "pool", "sp")
    NDMASEM = 8

    def __init__(self, nc, same_engine_sync=True):
        self.nc = nc
        self.ops = {e: [] for e in self.ENGS}
        self.last_w = {}
        self.readers = {}
        self.same_engine_sync = same_engine_sync
        self.dma_n = {"sp": 0, "pool": 0}
        self.dma_hist = {"sp": [], "pool": []}
        self.n_ops = 0

    def _collect(self, op, reads, writes):
        deps = []
        for (n, i) in reads:
            lw = self.last_w.get(n)
            if lw:
                if i is None:
                    deps.extend(lw.values())
                else:
                    if i in lw:
                        deps.append(lw[i])
                    if None in lw:
                        deps.append(lw[None])
        for (n, i) in writes:
            lw = self.last_w.get(n)
            rd = self.readers.get(n)
            if lw:
                if i is None:
                    deps.extend(lw.values())
                else:
                    if i in lw:
                        deps.append(lw[i])
                    if None in lw:
                        deps.append(lw[None])
            if rd:
                if i is None:
                    for v in rd.values():
                        deps.extend(v)
                else:
                    deps.extend(rd.get(i, ()))
                    deps.extend(rd.get(None, ()))
        for (n, i) in writes:
            lw = self.last_w.setdefault(n, {})
            rd = self.readers.setdefault(n, {})
            if i is None:
                lw.clear()
                rd.clear()
                lw[None] = op
            else:
                lw[i] = op
                rd.pop(i, None)
        for (n, i) in reads:
            self.readers.setdefault(n, {}).setdefault(i, []).append(op)
        seen = set()
        for d in deps:
            if d is op or id(d) in seen:
                continue
            seen.add(id(d))
            if d.eng == op.eng and not d.is_dma and not op.is_dma:
                if op.eng == "pe" or not self.same_engine_sync:
                    continue
            op.deps.append(d)
            d.sig = True

    def op(self, eng, fn, reads=(), writes=()):
        o = Op(eng, fn)
        self._collect(o, list(reads), list(writes))
        self.ops[eng].append(o)
        self.n_ops += 1
        return o

    def dma(self, q, out, in_, reads=(), writes=()):
        o = Op(q, lambda e: e.dma_start(out=out, in_=in_), is_dma=True)
        n = self.dma_n[q]
        self.dma_n[q] += 1
        o.idx = n
        self._collect(o, list(reads), list(writes))
        hist = self.dma_hist[q]
        if n >= self.NDMASEM:
            o.deps.append(hist[n - self.NDMASEM])
        hist.append(o)
        o.sig = True
        self.ops[q].append(o)
        self.n_ops += 1
        return o

    def emit(self, final_wait_ops):
        nc = self.nc
        with ExitStack() as es:
            esem = {e: es.enter_context(nc.semaphore("prog_" + e)) for e in self.ENGS}
            dsem = {q: [es.enter_context(nc.semaphore("dma_%s_%d" % (q, i))) for i in range(self.NDMASEM)]
                    for q in ("sp", "pool")}
            for e in self.ENGS:
                c = 0
                for o in self.ops[e]:
                    if o.is_dma:
                        slot = o.idx % self.NDMASEM
                        o.sem = dsem[e][slot]
                        o.count = 16 * (o.idx // self.NDMASEM + 1)
                    elif o.sig:
                        c += 1
                        o.sem = esem[e]
                        o.count = c
            block = es.enter_context(nc.Block())

            def run(ename, e, extra_final=None):
                known = {}
                for o in self.ops[ename]:
                    need = {}
                    for d in o.deps:
                        key = id(d.sem)
                        if known.get(key, 0) >= d.count:
                            continue
                        if key not in need or need[key][1] < d.count:
                            need[key] = (d.sem, d.count)
                    for key, (s, v) in need.items():
                        e.wait_ge(s, v)
                        known[key] = v
                    ins = o.fn(e)
                    if o.is_dma:
                        ins.then_inc(o.sem, 16)
                    elif o.sig:
                        ins.then_inc(o.sem, 1)
                if extra_final:
                    need = {}
                    for d in extra_final:
                        key = id(d.sem)
                        if key not in need or need[key][1] < d.count:
                            need[key] = (d.sem, d.count)
                    for key, (s, v) in need.items():
                        e.wait_ge(s, v)

            @block.tensor
            def _(e):
                run("pe", e)

            @block.scalar
            def _(e):
                run("act", e)

            @block.vector
            def _(e):
                run("dve", e)

            @block.gpsimd
            def _(e):
                run("pool", e)

            @block.sync
            def _(e):
                run("sp", e, extra_final=final_wait_ops)


RGW = 512
NH = 4
HD = 128
NLV_P = 7
NLV_S = 3

C32 = {"ident": 0, "ones": 128, "tri_p": 256, "up_p": 384, "tri_s": 512, "up_s": 640}
N32 = 768
C16 = {"identb": 0, "mstrict_p": 128, "minclt_p": 256, "mstrict_s": 384, "minclt_s": 512}
for _i in range(NLV_P):
    C16["lv_p%d" % _i] = 640 + 128 * _i
C16["segcol"] = 640 + 128 * NLV_P
C16["segrow"] = C16["segcol"] + 16
N16 = C16["segrow"] + 16 * 128


def _consts():
    i = np.arange(128)[:, None]
    j = np.arange(128)[None, :]
    c32 = np.zeros((128, N32), np.float32)
    c32[:, 0:128] = np.eye(128)
    c32[:, 128:256] = 1.0
    seg = 8
    same_s = (i // seg) == (j // seg)
    c32[:, 256:384] = (i <= j)
    c32[:, 384:512] = (i > j)
    c32[:, 512:640] = (i <= j) & same_s
    c32[:, 640:768] = (i > j) & same_s
    c16 = np.zeros((128, N16), np.float32)
    c16[:, 0:128] = np.eye(128)
    c16[:, 128:256] = (i > j)
    c16[:, 256:384] = (j >= i)
    c16[:, 384:512] = (i > j) & same_s
    c16[:, 512:640] = (j >= i) & same_s
    for lv in range(NLV_P):
        b = 1 << lv
        m = ((i // (2 * b)) == (j // (2 * b))) & (((j // b) % 2) == 1) & (((i // b) % 2) == 0)
        c16[:, C16["lv_p%d" % lv]:C16["lv_p%d" % lv] + 128] = m
    c16[:, C16["segcol"]:C16["segcol"] + 16] = (np.arange(128)[:, None] // seg) == np.arange(16)[None, :]
    sr = (np.arange(16)[:, None] == (np.arange(128)[None, :] // seg)).astype(np.float32).reshape(1, 16 * 128)
    c16[:, C16["segrow"]:] = np.repeat(sr, 128, axis=0)
    return c32, c16


PL = {}
_o = 0
for _nm, _n in (("ffn1_norm_pre", 8), ("ffn1_norm_post", 8), ("mix_norm_pre", 8), ("mix_norm_post", 8),
                ("ffn2_norm_pre", 8), ("ffn2_norm_post", 8), ("final_norm", 8),
                ("conv_w", 64), ("conv_b_rg", 4), ("rg_b_a", 4), ("rg_b_x", 4), ("rg_lambda", 4),
                ("dn_a_log", 4), ("dn_dt_bias", 4), ("dn_norm_w", 1)):
    PL[_nm] = _o
    _o += _n
PP_LAYER = _o


def _pack_params(inp):
    cols = []

    def vecn(v, n):
        return np.ascontiguousarray(np.asarray(v).reshape(n, 128).T)

    for l in range(DEPTH):
        for nm in ("ffn1_norm_pre", "ffn1_norm_post", "mix_norm_pre", "mix_norm_post",
                   "ffn2_norm_pre", "ffn2_norm_post"):
            cols.append(vecn(inp[nm][l], 8))
        cols.append(vecn(inp["final_norm"], 8))
        cw = np.asarray(inp["conv_w"][l])
        cols.append(np.ascontiguousarray(cw.reshape(4, 16, 128).transpose(2, 1, 0).reshape(128, 64)))
        for nm in ("conv_b_rg", "rg_b_a", "rg_b_x", "rg_lambda"):
            cols.append(vecn(inp[nm][l], 4))
        cols.append(np.repeat(np.asarray(inp["dn_a_log"][l]).reshape(1, 4), 128, axis=0))
        cols.append(np.repeat(np.asarray(inp["dn_dt_bias"][l]).reshape(1, 4), 128, axis=0))
        cols.append(np.asarray(inp["dn_norm_w"][l]).reshape(128, 1))
    return np.ascontiguousarray(np.concatenate(cols, axis=1).astype(np.float32))


def _pack_rgw(inp):
    out = np.zeros((DEPTH, 2, 128, 4, 128), np.float32)
    for l in range(DEPTH):
        for gi, nm in enumerate(("rg_w_a", "rg_w_x")):
            w = np.asarray(inp[nm][l])
            for c in range(4):
                for hh in range(2):
                    out[l, gi, hh * 64:(hh + 1) * 64, c, hh * 64:(hh + 1) * 64] = w[2 * c + hh]
    return out


class Cfg:
    def __init__(self, **kw):
        self.groups = [(0, 640, True), (640, 768, False), (1408, 640, False)]
        self.stages = "full"
        self.same_engine_sync = True
        self.__dict__.update(kw)


def blocks_of(T):
    if T == 768:
        return [(0, 384), (384, 384)]
    if T == 640:
        return [(0, 384), (384, 256)]
    raise ValueError(T)


def build_program(cfg):
    nc = bass.Bass("TRN2", target_bir_lowering=False)
    TM = 768
    NTM = TM // 128
    dr = {}

    def din(name, shape, dt=F32):
        dr[name] = nc.dram_tensor(name, shape, dt, kind="ExternalInput").ap()

    def dout(name, shape):
        dr[name] = nc.dram_tensor(name, shape, F32, kind="ExternalOutput").ap()

    din("xp", [SEQ, D]); din("xs", [NS * DS, D])
    din("pp", [128, DEPTH * PP_LAYER]); din("c32", [128, N32]); din("c16", [128, N16])
    din("rgw_r", [DEPTH, 2, 128, 4, 128], F32R)
    din("st_conv", [DEPTH, NS * 3, CONVC]); din("st_rg", [DEPTH, NS, RGW]); din("st_dn", [DEPTH, NS, NH, HD, HD])
    for nm in ("ffn1_w_up", "ffn2_w_up"):
        din(nm, [DEPTH, D, 2 * DFF], F32R)
    for nm in ("ffn1_w_down", "ffn2_w_down"):
        din(nm, [DEPTH, DFF, D], F32R)
    din("w_in", [DEPTH, D, INC], F32R); din("w_out", [DEPTH, D, D], F32R)
    dout("yp", [SEQ, D]); dout("ys", [NS * DS, D])
    dout("ncp", [DEPTH, 3, CONVC]); dout("nrp", [DEPTH, RGW]); dout("ndp", [DEPTH, NH, HD, HD])
    if getattr(cfg, "debug", False):
        dout("dbg", [3, 128, KC, TM])
    dout("ncs", [DEPTH, NS * 3, CONVC]); dout("nrs", [DEPTH, NS, RGW]); dout("nds", [DEPTH, NS, NH, HD, HD])

    P = Prog(nc, same_engine_sync=cfg.same_engine_sync)
    es = ExitStack()
    sb = lambda name, shape, dt: es.enter_context(nc.sbuf_tensor(name, shape, dt))
    X = sb("X", [128, KC, TM], F32)
    H = sb("H", [128, KC, TM], F32R)
    Y = sb("Y", [128, KC, TM], F32R)
    YF = Y[:].bitcast(F32)
    HF = H[:].bitcast(F32)
    NFB = 4
    ACTB = sb("ACTB", [128, NFB, TM], F32R)
    NWA = 4
    WA = [sb("WA%d" % i, [128, KC, 128], F32R) for i in range(NWA)]
    NWB = 4
    WB = [sb("WB%d" % i, [128, NFB, 128], F32R) for i in range(NWB)]
    PPt = sb("PPt", [128, DEPTH * PP_LAYER], F32)
    C32t = sb("C32t", [128, N32], F32)
    C16t = sb("C16t", [128, N16], BF16)
    ONESR = sb("ONESR", [128, 128], F32R)
    EPST = sb("EPST", [128, 8], F32)
    TT = sb("TT", [128, 4, TM], F32)
    SQ = sb("SQ", [128, 4, 384], F32R)
    RSTD = sb("RSTD", [128, 512], F32)
    SG = [TT[:, 0, 0:384], TT[:, 1, 0:384]]
    ZG = sb("ZG", [128, 4, TM], BF16)
    QKV = sb("QKV", [128, 12, TM], BF16)
    XR = sb("XR", [128, TM], F32R)
    XP = sb("XP", [128, 3 + TM], F32R)
    XPS = sb("XPS", [128, NS, 11], F32R)
    XPf = XP[:].bitcast(F32)
    XPSf = XPS[:].bitcast(F32)
    DG = sb("DG", [128, 4, 128], F32R)
    CSS = sb("CSS", [128, 16, NS * 3], F32)
    HS0 = sb("HS0", [128, 4, NS], F32)
    CONVT = sb("CONVT", [128, DEPTH, 16, 3], F32)
    HRG = sb("HRG", [128, DEPTH, 4], F32)
    SST = sb("SST", [128, DEPTH, NH, HD], F32)
    RGWt = sb("RGWt", [128, 2, 4, 128], F32R)
    WSC = sb("WSC", [128, KC, 8], F32R)
    SCT = sb("SCT", [128, NTM, 8], F32)
    LAYC = sb("LAYC", [128, 16], F32)
    ONE1 = sb("ONE1", [128, 1], F32)
    FEN = sb("FEN", [128, 2], F32)
    dt_names_bf = ["KTOK", "VTOK", "AM", "ATT", "DT", "DINV", "N1", "BV", "BKG", "WT", "VN", "QG", "KD", "SBF"]
    DB = {n: sb(n, [128, NH, 128], BF16) for n in dt_names_bf if n not in ("VN", "SBF", "WT")}
    DB["VN"] = DB["N1"]; DB["SBF"] = DB["DINV"]; DB["WT"] = DB["AM"]
    TTflat = TT[:].rearrange("p a t -> p (a t)")
    DB1 = {}
    for _k, _n in enumerate([n for n in dt_names_bf if n not in ("VN", "SBF", "WT")]):
        DB1[_n] = TTflat[:, _k * 256:(_k + 1) * 256].bitcast(BF16).rearrange("p (h c) -> p h c", h=NH)
    DB1["VN"] = DB1["N1"]; DB1["SBF"] = DB1["DINV"]; DB1["WT"] = DB1["AM"]
    DBS = [DB, DB1]
    DBNAMES = [n for n in dt_names_bf if n not in ("VN", "SBF", "WT")]
    GRW = sb("GRW", [128, NH, 128], F32)
    EGR2 = sb("EGR", [128, 2, NH, 128], F32)
    TRG = sb("TRG", [128, NH, 128], F32)
    OT = sb("OT", [128, NH, 128], F32)
    SM2 = sb("SM", [128, 2, 64], F32)
    CSL = TT[0:48, 3, 0:512]
    RSL = TT[0:16, 3, 0:512]
    PS = [es.enter_context(nc.psum_tensor("PS%d" % i, [128, 512], F32)) for i in range(8)]

    def c32(nm):
        return C32t[:, C32[nm]:C32[nm] + 128]

    def c16(nm, n=128):
        return C16t[:, C16[nm]:C16[nm] + n]

    ident = c32("ident")
    ones_f = c32("ones")
    ones_r = ONESR[:, :]
    identb = c16("identb")
    st = {"ps": 0, "wa": 0, "wb": 0, "stg": 0, "sg": 0}

    resv = set()

    def psum(hold=False):
        for _ in range(16):
            i = st["ps"]
            st["ps"] = (i + 1) % 8
            if i not in resv:
                if hold:
                    resv.add(i)
                return i
        raise RuntimeError("PSUM banks exhausted: %s" % sorted(resv))

    def psb(pi):
        return PS[pi][:].bitcast(BF16)

    def ACT(out, in_, func, reads, writes, **kw):
        return P.op("act", lambda e: e.activation(out=out, in_=in_, func=func, **kw), reads, writes)

    def CP(eng, out, in_, reads, writes):
        if eng == "act":
            return P.op("act", lambda e: e.copy(out=out, in_=in_), reads, writes)
        return P.op(eng, lambda e: e.tensor_copy(out=out, in_=in_), reads, writes)

    def TTo(eng, out, in0, in1, op, reads, writes):
        return P.op(eng, lambda e: e.tensor_tensor(out=out, in0=in0, in1=in1, op=op), reads, writes)

    def TS(eng, out, in0, s1, s2, op0, op1, reads, writes):
        if op1 is None:
            return P.op(eng, lambda e: e.tensor_scalar(out=out, in0=in0, scalar1=s1, scalar2=None, op0=op0), reads, writes)
        return P.op(eng, lambda e: e.tensor_scalar(out=out, in0=in0, scalar1=s1, scalar2=s2, op0=op0, op1=op1), reads, writes)

    def STT(out, in0, scalar, in1, op0, op1, reads, writes):
        return P.op("dve", lambda e: e.scalar_tensor_tensor(out=out, in0=in0, scalar=scalar, in1=in1, op0=op0, op1=op1), reads, writes)

    def MM(out, lhsT, rhs, start, stop, reads, writes):
        return P.op("pe", lambda e: e.matmul(out, lhsT=lhsT, rhs=rhs, start=start, stop=stop, skip_group_check=True), reads, writes)

    def TR(out, in_, idn, reads, writes):
        return P.op("pe", lambda e: e.transpose(out=out, in_=in_, identity=idn), reads, writes)

    class WStream:
        def __init__(self, name, bufs):
            self.name, self.bufs, self.n = name, bufs, len(bufs)
            self.reqs = []
            self.issued = 0
            self.cur = 0

        def add(self, fn):
            self.reqs.append(fn)

        def next(self):
            i = self.cur
            self.cur += 1
            upto = min(len(self.reqs), i + self.n - 1)
            while self.issued < upto:
                k = self.issued
                slot = k % self.n
                out_ap, in_ap = self.reqs[k](self.bufs[slot])
                P.dma("pool", out_ap, in_ap, writes=[(self.name, slot)])
                self.issued += 1
            return i % self.n

    WAS = WStream("WA", WA)
    WBS = WStream("WB", WB)
    FBLOCKS = [(0, 4), (4, 4), (8, 4), (12, 4), (16, 4), (20, 2)]

    def plan_cols(src, col):
        v = src.rearrange("(kc p) n -> p kc n", p=128)
        WAS.add(lambda buf, v=v, col=col: (buf[:], v[:, :, col:col + 128]))

    def plan_ffn(l, which):
        wup = dr["ffn%d_w_up" % which][l]
        wdn = dr["ffn%d_w_down" % which][l].rearrange("(c p) n -> p c n", p=128)
        for (c0, nch) in FBLOCKS:
            for j in range(nch):
                plan_cols(wup, (c0 + j) * 128)
                plan_cols(wup, DFF + (c0 + j) * 128)
            for oc in range(8):
                WBS.add(lambda buf, c0=c0, nch=nch, oc=oc, wdn=wdn: (buf[:, 0:nch, :], wdn[:, c0:c0 + nch, oc * 128:(oc + 1) * 128]))

    def plan_mixer(l):
        win = dr["w_in"][l]
        for ch in range(4):
            plan_cols(win, ch * 128)
            plan_cols(win, 2048 + ch * 128)
        for kind in range(3):
            for hh in range(4):
                plan_cols(win, 512 + kind * 512 + hh * 128)
        for hh in range(4):
            plan_cols(win, 2560 + hh * 128)
        for oc in range(8):
            plan_cols(dr["w_out"][l], oc * 128)

    for _g in cfg.groups:
        for l in range(DEPTH):
            plan_ffn(l, 1)
            plan_mixer(l)
            plan_ffn(l, 2)

    P.dma("sp", PPt[:], dr["pp"][:, :], writes=[("PP", None)])
    P.dma("sp", C32t[:], dr["c32"][:, :], writes=[("C32", None)])
    for i in range(0, N16, 768):
        n = min(768, N16 - i)
        P.dma("sp", TT[:, 0, 0:n], dr["c16"][:, i:i + n], writes=[("TT", 0)])
        CP("dve", C16t[:, i:i + n], TT[:, 0, 0:n], [("TT", 0)], [("C16", None)])
    CP("dve", ones_r, ones_f, [("C32", None)], [("ONESR", None)])
    EPSC = {}
    for i, (key, val) in enumerate([((D, 1.0), EPS), ((D, 0.5), 4.0 * EPS), ((128, 1.0), EPS), ((1, 1.0), EPS), ("q", 128.0 * EPS)]):
        EPSC[key] = EPST[:, i:i + 1]
        P.op("dve", lambda e, i=i, val=val: e.memset(EPST[:, i:i + 1], val), writes=[("EPST", i)])
    P.op("dve", lambda e: e.memset(ONE1[:, :], 1.0), writes=[("ONE1", None)])
    P.op("dve", lambda e: e.memset(CONVT[:].rearrange("p l c t -> p (l c t)"), 0.0), writes=[("CONVT", None)])
    P.op("dve", lambda e: e.memset(HRG[:].rearrange("p l c -> p (l c)"), 0.0), writes=[("HRG", None)])
    P.op("dve", lambda e: e.memset(SST[:].rearrange("p l h e -> p (l h e)"), 0.0), writes=[("SST", None)])

    out_ops = []

    def pcol(l, nm, j=0, n=1):
        c = l * PP_LAYER + PL[nm] + j
        return PPt[:, c:c + n]

    ngroups = len(cfg.groups)
    for gi, (p0, npr, smp) in enumerate(cfg.groups):
        T = npr + (128 if smp else 0)
        ntile = T // 128
        blks = blocks_of(T)
        last_group = (gi == ngroups - 1)

        def tiles_of(b0, bn):
            return list(range(b0 // 128, (b0 + bn + 127) // 128))

        def rk(name, kcs, b0, bn):
            if isinstance(kcs, int):
                kcs = [kcs]
            return [(name, (kc, t)) for kc in kcs for t in tiles_of(b0, bn)]

        def stg_view(si):
            return TT[:, 2 * si:2 * si + 2, :].rearrange("p a t -> p (a t)")[:, 0:D]

        def stg_keys(si):
            return [("TT", 2 * si), ("TT", 2 * si + 1)]

        for t in range(ntile):
            si = st["stg"]; st["stg"] ^= 1
            stg = stg_view(si)
            src = dr["xp"][p0 + t * 128: p0 + (t + 1) * 128, :] if t * 128 < npr else dr["xs"][:, :]
            P.dma("sp", stg, src, writes=stg_keys(si))
            for half in range(2):
                pi = psum()
                for j in range(4):
                    kc = half * 4 + j
                    TR(PS[pi][:, j * 128:(j + 1) * 128], stg[:, kc * 128:(kc + 1) * 128], ident,
                       stg_keys(si) + [("C32", None)], [("PS", pi)])
                CP("act" if half == 0 else "dve", X[:, half * 4:(half + 1) * 4, t * 128:(t + 1) * 128],
                   PS[pi][:].rearrange("p (j c) -> p j c", j=4), [("PS", pi)], [("X", (half * 4 + j, t)) for j in range(4)])

        def sumsq_rstd(SRC, srcname, b0, bn, nfeat, post_scale=1.0, nk=KC):
            pi = psum()
            for kc in range(nk):
                rd = rk(srcname, kc, b0, bn)
                ACT(SQ[:, kc % 4, 0:bn], SRC[:, kc, b0:b0 + bn], ACTF.Square, rd, [("SQ", kc % 4)])
                MM(PS[pi][:, 0:bn], ones_r, SQ[:, kc % 4, 0:bn], kc == 0, kc == nk - 1,
                   [("SQ", kc % 4), ("ONESR", None)], [("PS", pi)])
            return rstd_from_psum(pi, bn, nfeat, post_scale)

        def rstd_from_psum(pi, bn, nfeat, post_scale=1.0):
            ACT(PS[pi][:, 0:bn], PS[pi][:, 0:bn], ACTF.Ln, [("PS", pi), ("EPST", None)], [("PS", pi)],
                scale=1.0 / (nfeat * post_scale * post_scale), bias=EPSC[(nfeat, post_scale)])
            ACT(PS[pi][:, 0:bn], PS[pi][:, 0:bn], ACTF.Exp, [("PS", pi)], [("PS", pi)], scale=-0.5)
            return pi

        def norm_scale(DST, dstname, SRC, srcname, l, gname, b0, bn, pi):
            for kc in range(KC):
                STT(DST[:, kc, b0:b0 + bn], SRC[:, kc, b0:b0 + bn], pcol(l, gname, kc), PS[pi][:, 0:bn], ALU.mult, ALU.mult,
                    rk(srcname, kc, b0, bn) + [("PS", pi), ("PP", None)], rk(dstname, kc, b0, bn))

        def prenorm_to_H(l, gname):
            for (b0, bn) in blks:
                pi = sumsq_rstd(X, "X", b0, bn, D)
                norm_scale(H, "H", X, "X", l, gname, b0, bn, pi)

        def postnorm_residual(SRCw, SRC, srcname, l, gname, scale):
            for (b0, bn) in blks:
                pi = sumsq_rstd(SRC, srcname, b0, bn, D, post_scale=scale)
                norm_scale(SRCw, srcname, SRC, srcname, l, gname, b0, bn, pi)
                for kc in range(KC):
                    TTo("dve", X[:, kc, b0:b0 + bn], X[:, kc, b0:b0 + bn], SRC[:, kc, b0:b0 + bn], ALU.add,
                        rk(srcname, kc, b0, bn) + rk("X", kc, b0, bn), rk("X", kc, b0, bn))

        def ffn(l, which):
            prenorm_to_H(l, "ffn%d_norm_pre" % which)
            for fbi, (c0, nch) in enumerate(FBLOCKS):
                for j in range(nch):
                    wi_g = WAS.next()
                    pg = [psum() for _ in blks]
                    for kc in range(KC):
                        for bi, (b0, bn) in enumerate(blks):
                            MM(PS[pg[bi]][:, 0:bn], WA[wi_g][:, kc, :], H[:, kc, b0:b0 + bn],
                               kc == 0, kc == KC - 1, [("WA", wi_g)] + rk("H", kc, b0, bn), [("PS", pg[bi])])
                    sgs = []
                    for bi, (b0, bn) in enumerate(blks):
                        si = st["sg"]; st["sg"] = (st["sg"] + 1) % len(SG)
                        sgs.append(si)
                        ACT(SG[si][:, 0:bn], PS[pg[bi]][:, 0:bn], ACTF.Silu, [("PS", pg[bi])], [("TT", si)])
                    wi_u = WAS.next()
                    pu = [psum() for _ in blks]
                    for kc in range(KC):
                        for bi, (b0, bn) in enumerate(blks):
                            MM(PS[pu[bi]][:, 0:bn], WA[wi_u][:, kc, :], H[:, kc, b0:b0 + bn],
                               kc == 0, kc == KC - 1, [("WA", wi_u)] + rk("H", kc, b0, bn), [("PS", pu[bi])])
                    for bi, (b0, bn) in enumerate(blks):
                        TTo("dve", ACTB[:, j, b0:b0 + bn], PS[pu[bi]][:, 0:bn], SG[sgs[bi]][:, 0:bn], ALU.mult,
                            [("PS", pu[bi]), ("TT", sgs[bi])], [("ACTB", (j, b0))])
                for oc in range(8):
                    wi = WBS.next()
                    pd = [psum() for _ in blks]
                    for j in range(nch):
                        for bi, (b0, bn) in enumerate(blks):
                            MM(PS[pd[bi]][:, 0:bn], WB[wi][:, j, :], ACTB[:, j, b0:b0 + bn],
                               j == 0, j == nch - 1, [("WB", wi), ("ACTB", (j, b0))], [("PS", pd[bi])])
                    for bi, (b0, bn) in enumerate(blks):
                        if fbi == 0:
                            CP("act", Y[:, oc, b0:b0 + bn], PS[pd[bi]][:, 0:bn], [("PS", pd[bi])], rk("Y", oc, b0, bn))
                        else:
                            TTo("dve", Y[:, oc, b0:b0 + bn], PS[pd[bi]][:, 0:bn], YF[:, oc, b0:b0 + bn], ALU.add,
                                [("PS", pd[bi])] + rk("Y", oc, b0, bn), rk("Y", oc, b0, bn))
            postnorm_residual(Y, YF, "Y", l, "ffn%d_norm_post" % which, 0.5)

        MIXR = Y
        allH = [("H", (kc, t)) for kc in range(KC) for t in range(NTM)]
        allACTB = [("ACTB", None)]

        def mixer(l):
            win = dr["w_in"][l].rearrange("(kc p) n -> p kc n", p=128)
            prenorm_to_H(l, "mix_norm_pre")
            ACT(LAYC[:, 0:4], pcol(l, "rg_lambda", 0, 4), ACTF.Exp, [("PP", None)], [("LAYC", 0)], scale=-1.0)
            ACT(LAYC[:, 0:4], LAYC[:, 0:4], ACTF.Ln, [("LAYC", 0), ("ONE1", None)], [("LAYC", 0)], bias=ONE1[:, 0:1])
            TS("dve", LAYC[:, 4:8], LAYC[:, 0:4], -16.0, None, ALU.mult, None, [("LAYC", 0)], [("LAYC", 1)])
            TS("dve", LAYC[:, 0:4], LAYC[:, 0:4], -8.0, None, ALU.mult, None, [("LAYC", 0), ("LAYC", 1)], [("LAYC", 0)])
            ACT(LAYC[:, 8:12], pcol(l, "dn_a_log", 0, 4), ACTF.Exp, [("PP", None)], [("LAYC", 2)])
            TS("dve", LAYC[:, 8:12], LAYC[:, 8:12], -1.0, None, ALU.mult, None, [("LAYC", 2)], [("LAYC", 2)])
            P.dma("pool", RGWt[:], dr["rgw_r"][l].rearrange("g p c m -> p g c m"), writes=[("RGW", None)])
            P.dma("pool", WSC[:], win[:, :, 3072:3080], writes=[("WSC", None)])
            if smp:
                for q in range(4):
                    P.dma("sp", CSL, dr["st_conv"][l][:, q * 512:(q + 1) * 512], writes=[("TT", 3)])
                    pi = psum()
                    for j in range(4):
                        TR(PS[pi][:, j * 48:(j + 1) * 48], CSL[:, j * 128:(j + 1) * 128], ident[0:48, 0:48],
                           [("TT", 3), ("C32", None)], [("PS", pi)])
                    CP("dve", CSS[:, q * 4:(q + 1) * 4, :], PS[pi][:, 0:192].rearrange("p (j c) -> p j c", j=4),
                       [("PS", pi)], [("CSS", q * 4 + j) for j in range(4)])
                P.dma("sp", RSL, dr["st_rg"][l][:, :], writes=[("TT", 3)])
                pi = psum()
                for j in range(4):
                    TR(PS[pi][:, j * 16:(j + 1) * 16], RSL[:, j * 128:(j + 1) * 128], ident[0:16, 0:16],
                       [("TT", 3), ("C32", None)], [("PS", pi)])
                CP("dve", HS0[:, :, :], PS[pi][:, 0:64].rearrange("p (j c) -> p j c", j=4), [("PS", pi)], [("HS0", None)])

            wa_state = {}

            def proj():
                wi = WAS.next()
                pb = [psum() for _ in blks]
                for kc in range(KC):
                    for bi, (b0, bn) in enumerate(blks):
                        MM(PS[pb[bi]][:, 0:bn], WA[wi][:, kc, :], H[:, kc, b0:b0 + bn],
                           kc == 0, kc == KC - 1, [("WA", wi)] + rk("H", kc, b0, bn), [("PS", pb[bi])])
                return pb

            def conv_chunk(l, ch, pb):
                CP("act", XP[:, 0:3], CONVT[:, l, ch, :], [("CONVT", (l, ch))], [("XP", 0)])
                for bi, (b0, bn) in enumerate(blks):
                    n = min(bn, npr - b0)
                    if n > 0:
                        CP("dve" if bi == 0 else "act", XP[:, 3 + b0:3 + b0 + n], PS[pb[bi]][:, 0:n], [("PS", pb[bi])], [("XP", 1 + bi)])
                if smp:
                    b0, bn = blks[-1]
                    off = npr - b0
                    CP("dve", XPS[:, :, 0:3], CSS[:, ch, :].rearrange("p (s t) -> p s t", t=3), [("CSS", ch)], [("XPS", 0)])
                    CP("act", XPS[:, :, 3:11], PS[pb[-1]][:, off:off + 128].rearrange("p (s t) -> p s t", t=8),
                       [("PS", pb[-1])], [("XPS", 1)])
                for tap in range(4):
                    TS("dve", DG[:, tap, :], ident, pcol(l, "conv_w", ch * 4 + tap), None, ALU.mult, None,
                       [("C32", None), ("PP", None)], [("DG", tap)])
                xpk = [("XP", i) for i in range(1 + len(blks))]
                pc = [psum() for _ in blks]
                for bi, (b0, bn) in enumerate(blks):
                    n = min(bn, npr - b0)
                    for tap in range(4):
                        MM(PS[pc[bi]][:, 0:n], DG[:, tap, :], XP[:, b0 + tap:b0 + tap + n], tap == 0, tap == 3,
                           xpk + [("DG", tap)], [("PS", pc[bi])])
                CP("dve", CONVT[:, l, ch, :], XPf[:, npr:npr + 3], xpk, [("CONVT", (l, ch))])
                if smp:
                    b0, bn = blks[-1]
                    off = npr - b0
                    for tap in range(4):
                        MM(PS[pc[-1]][:, off:off + 128].rearrange("p (s t) -> p s t", t=8), DG[:, tap, :], XPS[:, :, tap:tap + 8],
                           False, tap == 3, [("XPS", 0), ("XPS", 1), ("DG", tap)], [("PS", pc[-1])])
                    CP("dve", CSS[:, ch, :].rearrange("p (s t) -> p s t", t=3), XPSf[:, :, 8:11], [("XPS", 0), ("XPS", 1)], [("CSS", ch)])
                return pc

            T0 = TT[:, 0, :]; T1 = TT[:, 1, :]; T2 = TT[:, 2, :]
            XRf = XR[:].bitcast(F32)
            k0, k1, k2, k3 = ("TT", 0), ("TT", 1), ("TT", 2), ("XR", None)

            nb_ = len(blks)
            allrow = [("TT", i) for i in range(4)] + [("XR", None)]
            allblk = [("TT", (i, bi)) for i in range(3) for bi in range(nb_)] + [("XR", bi) for bi in range(nb_)]
            P.op("dve", lambda e: e.memset(FEN[:, 0:1], 0.0), allrow + allblk, allrow + allblk)
            kq = lambda i, bi: ("TT", (i, bi))
            kx = lambda bi: ("XR", bi)
            for ch in range(4):
                pb = proj()
                pc = conv_chunk(l, ch, pb)
                for bi, (b0, bn) in enumerate(blks):
                    ACT(XR[:, b0:b0 + bn], PS[pc[bi]][:, 0:bn], ACTF.Identity, [("PS", pc[bi]), ("PP", None), kx(bi)], [kx(bi)],
                        bias=pcol(l, "conv_b_rg", ch))
                pa = [psum() for _ in blks]
                for bi, (b0, bn) in enumerate(blks):
                    MM(PS[pa[bi]][:, 0:bn], RGWt[:, 0, ch, :], XR[:, b0:b0 + bn], True, True, [("RGW", None), kx(bi)], [("PS", pa[bi])])
                px = [psum() for _ in blks]
                for bi, (b0, bn) in enumerate(blks):
                    MM(PS[px[bi]][:, 0:bn], RGWt[:, 1, ch, :], XR[:, b0:b0 + bn], True, True, [("RGW", None), kx(bi)], [("PS", px[bi])])
                pgt = proj()
                for p_ in pgt:
                    resv.add(p_)
                for bi, (b0, bn) in enumerate(blks):
                    ACT(T0[:, b0:b0 + bn], PS[pa[bi]][:, 0:bn], ACTF.Sigmoid, [("PS", pa[bi]), ("PP", None), kq(0, bi)], [kq(0, bi)],
                        bias=pcol(l, "rg_b_a", ch))
                for bi, (b0, bn) in enumerate(blks):
                    ACT(T1[:, b0:b0 + bn], PS[px[bi]][:, 0:bn], ACTF.Sigmoid, [("PS", px[bi]), ("PP", None), kq(1, bi)], [kq(1, bi)],
                        bias=pcol(l, "rg_b_x", ch))
                for bi, (b0, bn) in enumerate(blks):
                    sl = slice(b0, b0 + bn)
                    ACT(T2[:, sl], T0[:, sl], ACTF.Exp, [kq(0, bi), ("LAYC", 1), kq(2, bi)], [kq(2, bi)], scale=LAYC[:, 4 + ch:5 + ch])
                for bi, (b0, bn) in enumerate(blks):
                    sl = slice(b0, b0 + bn)
                    ACT(T0[:, sl], T0[:, sl], ACTF.Exp, [kq(0, bi), ("LAYC", 0)], [kq(0, bi)], scale=LAYC[:, ch:ch + 1])
                for bi, (b0, bn) in enumerate(blks):
                    sl = slice(b0, b0 + bn)
                    TS("dve", T2[:, sl], T2[:, sl], -1.0, 1.0, ALU.mult, ALU.add, [kq(2, bi)], [kq(2, bi)])
                    TS("dve", T2[:, sl], T2[:, sl], 1e-30, None, ALU.max, None, [kq(2, bi)], [kq(2, bi)])
                    TTo("dve", T1[:, sl], T1[:, sl], XRf[:, sl], ALU.mult, [kq(1, bi), kx(bi)], [kq(1, bi)])
                for bi, (b0, bn) in enumerate(blks):
                    sl = slice(b0, b0 + bn)
                    ACT(T2[:, sl], T2[:, sl], ACTF.Ln, [kq(2, bi)], [kq(2, bi)])
                    ACT(T2[:, sl], T2[:, sl], ACTF.Exp, [kq(2, bi)], [kq(2, bi)], scale=0.5)
                for bi, (b0, bn) in enumerate(blks):
                    sl = slice(b0, b0 + bn)
                    TTo("dve", T1[:, sl], T1[:, sl], T2[:, sl], ALU.mult, [kq(1, bi), kq(2, bi)], [kq(1, bi)])
                for bi, (b0, bn) in enumerate(blks):
                    n = min(bn, npr - b0)
                    if n <= 0:
                        continue
                    init = HRG[:, l, ch:ch + 1] if bi == 0 else T2[:, b0 - 1:b0]
                    extra = [("HRG", (l, ch))] if bi == 0 else [kq(2, bi - 1)]
                    P.op("dve", lambda e, b0=b0, n=n, init=init: e.tensor_tensor_scan(
                        out=T2[:, b0:b0 + n], data0=T0[:, b0:b0 + n], data1=T1[:, b0:b0 + n],
                        initial=init, op0=ALU.mult, op1=ALU.add),
                        [kq(0, bi), kq(1, bi), kq(2, bi)] + extra, [kq(2, bi)])
                lastb = max(bi for bi, (b0, bn) in enumerate(blks) if npr > b0)
                CP("dve", HRG[:, l, ch:ch + 1], T2[:, npr - 1:npr], [kq(2, lastb)], [("HRG", (l, ch))])
                if smp:
                    sb_ = nb_ - 1
                    a_first = T0[:, npr:npr + 128:8]
                    b_first = T1[:, npr:npr + 128:8]
                    TTo("dve", SM2[:, 0, 0:16], a_first, HS0[:, ch, :], ALU.mult, [kq(0, sb_), ("HS0", None)], [("SMr", None)])
                    TTo("dve", b_first, b_first, SM2[:, 0, 0:16], ALU.add, [kq(1, sb_), ("SMr", None)], [kq(1, sb_)])
                    TS("dve", a_first, a_first, 0.0, None, ALU.mult, None, [kq(0, sb_), ("SMr", None)], [kq(0, sb_)])
                    P.op("dve", lambda e: e.tensor_tensor_scan(out=T2[:, npr:npr + 128], data0=T0[:, npr:npr + 128],
                                                              data1=T1[:, npr:npr + 128], initial=0.0, op0=ALU.mult, op1=ALU.add),
                         [kq(0, sb_), kq(1, sb_), kq(2, sb_)], [kq(2, sb_)])
                    CP("dve", HS0[:, ch, :], T2[:, npr + 7:npr + 128:8], [kq(2, sb_), ("SMr", None)], [("HS0", None)])
                for bi, (b0, bn) in enumerate(blks):
                    sl = slice(b0, b0 + bn)
                    CP("act", T0[:, sl], PS[pgt[bi]][:, 0:bn], [("PS", pgt[bi]), kq(0, bi)], [kq(0, bi)])
                    ACT(T1[:, sl], PS[pgt[bi]][:, 0:bn], ACTF.Square, [("PS", pgt[bi]), kq(1, bi)], [kq(1, bi)])
                    resv.discard(pgt[bi])
                for bi, (b0, bn) in enumerate(blks):
                    sl = slice(b0, b0 + bn)
                    TS("dve", T1[:, sl], T1[:, sl], 0.044715, 1.0, ALU.mult, ALU.add, [kq(1, bi)], [kq(1, bi)])
                    TTo("dve", T1[:, sl], T1[:, sl], T0[:, sl], ALU.mult, [kq(0, bi), kq(1, bi)], [kq(1, bi)])
                for bi, (b0, bn) in enumerate(blks):
                    sl = slice(b0, b0 + bn)
                    ACT(T1[:, sl], T1[:, sl], ACTF.Sigmoid, [kq(1, bi)], [kq(1, bi)], scale=2.0 * 0.7978845608028654)
                for bi, (b0, bn) in enumerate(blks):
                    sl = slice(b0, b0 + bn)
                    TTo("dve", T0[:, sl], T0[:, sl], T1[:, sl], ALU.mult, [kq(0, bi), kq(1, bi)], [kq(0, bi)])
                    TTo("dve", MIXR[:, ch, sl], T2[:, sl], T0[:, sl], ALU.mult, [kq(0, bi), kq(2, bi)], rk("Y", ch, b0, bn))
            P.op("dve", lambda e: e.memset(FEN[:, 0:1], 0.0), allrow + allblk, allrow + allblk)

            TB = [(T0, k0), (T1, k1)]

            def qk_front(kind, hh, slot):
                ch = 4 + kind * 4 + hh
                Tb, kb = TB[slot]
                pb = proj()
                pc = conv_chunk(l, ch, pb)
                for bi, (b0, bn) in enumerate(blks):
                    ACT(Tb[:, b0:b0 + bn], PS[pc[bi]][:, 0:bn], ACTF.Silu, [("PS", pc[bi]), kb], [kb])
                for bi, (b0, bn) in enumerate(blks):
                    sq = slot * 2 + bi
                    TTo("dve", SQ[:, sq, 0:bn], Tb[:, b0:b0 + bn], Tb[:, b0:b0 + bn], ALU.mult, [kb], [("SQ", sq)])

            def qk_tail(kind, hh, slot):
                Tb, kb = TB[slot]
                for bi, (b0, bn) in enumerate(blks):
                    sq = slot * 2 + bi
                    pi = psum()
                    MM(PS[pi][:, 0:bn], ones_r, SQ[:, sq, 0:bn], True, True, [("SQ", sq), ("ONESR", None)], [("PS", pi)])
                    if kind == 0:
                        ACT(PS[pi][:, 0:bn], PS[pi][:, 0:bn], ACTF.Ln, [("PS", pi), ("EPST", None)], [("PS", pi)],
                            scale=128.0, bias=EPSC["q"])
                    else:
                        ACT(PS[pi][:, 0:bn], PS[pi][:, 0:bn], ACTF.Ln, [("PS", pi), ("EPST", None)], [("PS", pi)],
                            scale=1.0, bias=EPSC[(1, 1.0)])
                    ACT(PS[pi][:, 0:bn], PS[pi][:, 0:bn], ACTF.Exp, [("PS", pi)], [("PS", pi)], scale=-0.5)
                    TTo("dve", QKV[:, kind * 4 + hh, b0:b0 + bn], Tb[:, b0:b0 + bn], PS[pi][:, 0:bn], ALU.mult,
                        [kb, ("PS", pi)], [("QKV", None)])

            for kind in range(2):
                for hp in range(2):
                    qk_front(kind, 2 * hp, 0)
                    qk_front(kind, 2 * hp + 1, 1)
                    qk_tail(kind, 2 * hp, 0)
                    qk_tail(kind, 2 * hp + 1, 1)
            for hh in range(4):
                pb = proj()
                pc = conv_chunk(l, 12 + hh, pb)
                for bi, (b0, bn) in enumerate(blks):
                    ACT(QKV[:, 8 + hh, b0:b0 + bn], PS[pc[bi]][:, 0:bn], ACTF.Silu, [("PS", pc[bi])], [("QKV", None)])

            for cpair in range(2):
                for cc in range(2):
                    hh = cpair * 2 + cc
                    pb = proj()
                    for bi, (b0, bn) in enumerate(blks):
                        ACT(ZG[:, hh, b0:b0 + bn], PS[pb[bi]][:, 0:bn], ACTF.Silu, [("PS", pb[bi])], [("ZG", (hh, bi))])

            pi = psum()
            for t in range(ntile):
                for kc in range(KC):
                    MM(PS[pi][:, t * 8:(t + 1) * 8], H[:, kc, t * 128:(t + 1) * 128], WSC[:, kc, :], (t == 0 and kc == 0), kc == KC - 1,
                       [("WSC", None), ("H", (kc, t))], [("PS", pi)])
            CP("dve", SCT[:, 0:ntile, :], PS[pi][:, 0:ntile * 8].rearrange("p (t c) -> p t c", c=8), [("PS", pi)], [("SCT", None)])

            delta_phase(l)

            if getattr(cfg, "debug", False) and l == 0:
                out_ops.append(P.dma("sp", dr["dbg"][gi], YF, reads=[("Y", None)], writes=[("DBG", gi)]))
            for ocp in range(4):
                for o2 in range(2):
                    oc = ocp * 2 + o2
                    wi = WAS.next()
                    pd = [psum() for _ in blks]
                    for kc in range(KC):
                        for bi, (b0, bn) in enumerate(blks):
                            MM(PS[pd[bi]][:, 0:bn], WA[wi][:, kc, :], MIXR[:, kc, b0:b0 + bn],
                               kc == 0, kc == KC - 1, [("WA", wi)] + rk("Y", kc, b0, bn), [("PS", pd[bi])])
                    for bi, (b0, bn) in enumerate(blks):
                        CP("act", H[:, oc, b0:b0 + bn], PS[pd[bi]][:, 0:bn], [("PS", pd[bi])], rk("H", oc, b0, bn))
            postnorm_residual(H, HF, "H", l, "mix_norm_post", 1.0)

            if smp:
                for q in range(4):
                    pi = psum()
                    for j in range(4):
                        TR(PS[pi][0:48, j * 128:(j + 1) * 128], CSS[:, q * 4 + j, :], ident, [("CSS", q * 4 + j), ("C32", None)], [("PS", pi)])
                    CP("dve", CSL, PS[pi][0:48, :], [("PS", pi)], [("TT", 3)])
                    out_ops.append(P.dma("sp", dr["ncs"][l][:, q * 512:(q + 1) * 512], CSL, reads=[("TT", 3)], writes=[("CSLo", q)]))
                pi = psum()
                for j in range(4):
                    TR(PS[pi][0:16, j * 128:(j + 1) * 128], HS0[:, j, :], ident, [("HS0", None), ("C32", None)], [("PS", pi)])
                CP("dve", RSL, PS[pi][0:16, :], [("PS", pi)], [("TT", 3)])
                out_ops.append(P.dma("sp", dr["nrs"][l][:, :], RSL, reads=[("TT", 3)], writes=[("RSLo", 0)]))
            if last_group:
                for q in range(4):
                    pi = psum()
                    for j in range(4):
                        TR(PS[pi][0:3, j * 128:(j + 1) * 128], CONVT[:, l, q * 4 + j, :], ident, [("CONVT", (l, q * 4 + j)), ("C32", None)], [("PS", pi)])
                    CP("dve", CSL[0:3, :], PS[pi][0:3, :], [("PS", pi)], [("TT", 3)])
                    out_ops.append(P.dma("sp", dr["ncp"][l][:, q * 512:(q + 1) * 512], CSL[0:3, :], reads=[("TT", 3)], writes=[("CSLo", q)]))
                pi = psum()
                TR(PS[pi][0:4, 0:128], HRG[:, l, :], ident, [("HRG", None), ("C32", None)], [("PS", pi)])
                CP("dve", RSL[0:4, 0:128], PS[pi][0:4, 0:128], [("PS", pi)], [("TT", 3)])
                out_ops.append(P.dma("sp", dr["nrp"][l].rearrange("(c p) -> c p", p=128), RSL[0:4, 0:128], reads=[("TT", 3)], writes=[("RSLo", 0)]))
                out_ops.append(P.dma("sp", dr["ndp"][l].rearrange("h d e -> d h e"), SST[:, l, :, :], reads=[("SST", None)], writes=[("SSTo", l)]))

        def delta_prelude(l, t, is_s):
            sfx = "_s" if is_s else "_p"
            par = t % 2
            SM = SM2[:, par, :]
            EGR = EGR2[:, par]
            BETA = SM[:, 16:20]; GT = SM[:, 20:24]; GC = SM[:, 24:32]; EG = SM[:, 32:36]; BEG = SM[:, 36:40]; EKD = SM[:, 40:44]
            ACT(BETA, SCT[:, t, 0:4], ACTF.Sigmoid, [("SCT", None)], [("SM", (par, 1))])
            TTo("dve", GT, SCT[:, t, 4:8], pcol(l, "dn_dt_bias", 0, 4), ALU.add, [("SCT", None), ("PP", None)], [("SM", (par, 2))])
            ACT(GT, GT, ACTF.Exp, [("SM", (par, 2))], [("SM", (par, 2))])
            ACT(GT, GT, ACTF.Ln, [("SM", (par, 2)), ("ONE1", None)], [("SM", (par, 2))], bias=ONE1[:, 0:1])
            TTo("dve", GT, GT, LAYC[:, 8:12], ALU.mult, [("SM", (par, 2)), ("LAYC", 2)], [("SM", (par, 2))])
            pgc = psum()
            MM(PS[pgc][:, 0:4], c32("tri" + sfx), GT, True, True, [("SM", (par, 2)), ("C32", None)], [("PS", pgc)])
            MM(PS[pgc][:, 4:8], c32("up" + sfx), GT, False, True, [("SM", (par, 2)), ("C32", None)], [("PS", pgc)])
            bc4 = lambda ap: ap.unsqueeze(2).broadcast_to([128, NH, 128])
            bm4 = lambda ap: ap.unsqueeze(1).broadcast_to([128, NH, 128])
            TTo("dve", TRG[:, :, :], bm4(c32("tri" + sfx)), bc4(GT), ALU.mult, [("SM", (par, 2)), ("C32", None), ("TRG", 0), ("TRG", 1)], [("TRG", 0), ("TRG", 1)])
            pgr = psum()
            resv.add(pgr)
            st["pgr"] = pgr
            st["pgr_users"] = 2
            MM(PS[pgr][:, :], ones_f, TRG[:].rearrange("p h c -> p (h c)"), True, True, [("TRG", 0), ("TRG", 1), ("C32", None)], [("PS", pgr)])
            CP("dve", GC, PS[pgc][:, 0:8], [("PS", pgc), ("PS", pgr)], [("SM", (par, 3))])
            ACT(EG, GC[:, 0:4], ACTF.Exp, [("SM", (par, 3))], [("SM", (par, 4))])
            ACT(EKD, GC[:, 4:8], ACTF.Exp, [("SM", (par, 3))], [("SM", (par, 5))])
            TTo("dve", BEG, BETA, EG, ALU.mult, [("SM", (par, 1)), ("SM", (par, 4))], [("SM", (par, 6))])
            PGR4 = PS[pgr][:].rearrange("p (h c) -> p h c", h=NH)
            gk = [("GRW", 0), ("GRW", 1)]; ek = [("EGR", (par, 0)), ("EGR", (par, 1))]
            TTo("dve", GRW[:, :, :], PGR4, GC[:, 0:4].unsqueeze(2).broadcast_to([128, NH, 128]), ALU.subtract, [("PS", pgr), ("SM", (par, 3))] + gk, gk)
            GRWf = GRW[:].rearrange("p h c -> p (h c)")
            STT(GRWf, GRWf, -1.0, GRWf, ALU.mult, ALU.max, gk, gk)
            ACT(GRW[:], GRW[:], ACTF.Exp, gk, gk, scale=-1.0)
            ACT(EGR[:], PGR4, ACTF.Exp, [("PS", pgr)] + ek, ek)
            resv.discard(pgr)

        def delta_pair(l, t, is_s, hg, ds):
            c0 = t * 128
            par = t % 2
            SM = SM2[:, par, :]
            EGR = EGR2[:, par]
            sfx = "_s" if is_s else "_p"
            nlv = NLV_S if is_s else NLV_P
            HS = slice(2 * hg, 2 * hg + 2)
            heads = (2 * hg, 2 * hg + 1)
            bk = lambda n: [((n + str(ds)) if n in DBNAMES else n, hg)]
            KTOK, VTOK, AM, ATT, DTt, DINV, N1, BV, BKG, WT, VN, QG, KD, SBF = [DBS[ds][n][:, HS, :] for n in dt_names_bf]
            f2 = lambda ap: ap.rearrange("p h c -> p (h c)")
            qkv_r = [("QKV", None)]
            bc = lambda ap: ap.unsqueeze(2).broadcast_to([128, 2, 128])
            bm = lambda ap: ap.unsqueeze(1).broadcast_to([128, 2, 128])
            BETA = SM[:, 16 + 2 * hg:18 + 2 * hg]; GT = SM[:, 20 + 2 * hg:22 + 2 * hg]; GC0 = SM[:, 24 + 2 * hg:26 + 2 * hg]
            BEG = SM[:, 36 + 2 * hg:38 + 2 * hg]; EKD = SM[:, 40 + 2 * hg:42 + 2 * hg]
            GRWp = GRW[:, HS, :]; EGRp = EGR[:, HS, :]; TRGp = TRG[:, HS, :]; OTp = OT[:, HS, :]
            pgr = st["pgr"]
            ptk = psum(True)
            for i, h in enumerate(heads):
                TR(psb(ptk)[:, i * 128:(i + 1) * 128], QKV[:, 4 + h, c0:c0 + 128], identb, qkv_r + [("C16", None)], [("PS", ptk)])
            for i, h in enumerate(heads):
                TR(psb(ptk)[:, 256 + i * 128:256 + (i + 1) * 128], QKV[:, 8 + h, c0:c0 + 128], identb, qkv_r + [("C16", None)], [("PS", ptk)])
            pkk = psum(True)
            for i, h in enumerate(heads):
                MM(PS[pkk][:, i * 128:(i + 1) * 128], QKV[:, 4 + h, c0:c0 + 128], QKV[:, 4 + h, c0:c0 + 128], i == 0, True, qkv_r, [("PS", pkk)])
            for i, h in enumerate(heads):
                MM(PS[pkk][:, 256 + i * 128:256 + (i + 1) * 128], QKV[:, 4 + h, c0:c0 + 128], QKV[:, h, c0:c0 + 128], False, True, qkv_r,
                   [("PS", pkk), ("PGD", hg)])
            yield
            PGR3 = PS[pgr][:, hg * 256:(hg + 1) * 256].rearrange("p (h c) -> p h c", h=2)

            CP("act", f2(KTOK), psb(ptk)[:, 0:256], [("PS", ptk)] + bk("KTOK"), bk("KTOK"))
            CP("act", f2(VTOK), psb(ptk)[:, 256:512], [("PS", ptk)] + bk("VTOK"), bk("VTOK"))
            resv.discard(ptk)
            yield
            TTo("dve", BV, VTOK, bc(BETA), ALU.mult, bk("VTOK") + [("SM", (par, 1))] + bk("BV"), bk("BV"))
            TTo("dve", BKG, KTOK, bc(BEG), ALU.mult, bk("KTOK") + [("SM", (par, 6))] + bk("BKG"), bk("BKG"))
            TTo("dve", KD, KTOK, bc(EKD), ALU.mult, bk("KTOK") + [("SM", (par, 5))] + bk("KD"), bk("KD"))
            TTo("dve", QG, QKV[:, HS, c0:c0 + 128], EGRp, ALU.mult, qkv_r + [("EGR", (par, hg))] + bk("QG"), bk("QG"))
            CP("act", DTt, bm(identb), [("C16", None)] + bk("DT"), bk("DT"))
            CP("act", DINV, bm(identb), [("C16", None)] + bk("DINV"), bk("DINV"))
            yield
            TTo("dve", TRGp, GRWp, bm(c16("mstrict" + sfx)), ALU.mult, bk("GRW") + [("C16", None)] + bk("TRG"), bk("TRG"))
            TTo("dve", TRGp, TRGp, bc(BETA), ALU.mult, bk("TRG") + [("SM", (par, 1))], bk("TRG"))
            TTo("dve", OTp, GRWp, bm(c16("minclt" + sfx)), ALU.mult, bk("GRW") + [("C16", None)] + bk("OT"), bk("OT"))
            TTo("dve", f2(AM), PS[pkk][:, 0:256], f2(TRGp), ALU.mult, [("PS", pkk)] + bk("TRG") + bk("AM"), bk("AM"))
            TTo("dve", f2(ATT), PS[pkk][:, 256:512], f2(OTp), ALU.mult, [("PS", pkk)] + bk("OT") + bk("ATT"), bk("ATT"))
            resv.discard(pkk)
            yield
            for lv in range(nlv):
                p1 = psum(True)
                for i in range(2):
                    MM(PS[p1][:, i * 128:(i + 1) * 128], AM[:, i, :], DTt[:, i, :], i == 0, True, bk("AM") + bk("DT"), [("PS", p1)])
                yield
                TTo("dve", N1, PS[p1][:, 0:256].rearrange("p (h c) -> p h c", h=2), bm(c16("lv_p%d" % lv)), ALU.mult,
                    [("PS", p1), ("C16", None)] + bk("N1"), bk("N1"))
                resv.discard(p1)
                yield
                p2 = psum(True)
                for i in range(2):
                    MM(PS[p2][:, i * 128:(i + 1) * 128], DINV[:, i, :], N1[:, i, :], i == 0, True, bk("DINV") + bk("N1"), [("PS", p2)])
                yield
                TTo("dve", f2(DTt), f2(DTt), PS[p2][:, 0:256], ALU.subtract, [("PS", p2)] + bk("DT"), bk("DT"))
                resv.discard(p2)
                yield
                if lv < nlv - 1:
                    p3 = psum(True)
                    for i in range(2):
                        TR(psb(p3)[:, i * 128:(i + 1) * 128], DTt[:, i, :], identb, bk("DT") + [("C16", None)], [("PS", p3)])
                    yield
                    CP("act", f2(DINV), psb(p3)[:, 0:256], [("PS", p3)] + bk("DINV"), bk("DINV"))
                    resv.discard(p3)
                    yield
            pu = psum()
            resv.add(pu)
            if not is_s:
                for i in range(2):
                    MM(PS[pu][:, i * 128:(i + 1) * 128], DTt[:, i, :], BV[:, i, :], i == 0, False, bk("DT") + bk("BV"), [("PS", pu)])
            pw = psum(True)
            for i in range(2):
                MM(PS[pw][:, i * 128:(i + 1) * 128], BKG[:, i, :], DTt[:, i, :], i == 0, True, bk("DT") + bk("BKG"), [("PS", pw)])
            yield
            ACT(f2(WT), PS[pw][:, 0:256], ACTF.Copy, [("PS", pw)] + bk("AM"), bk("AM"), scale=-1.0)
            resv.discard(pw)
            po = psum()
            resv.add(po)
            if not is_s:
                CP("act", SBF, SST[:, l, HS, :], [("SST", hg)] + bk("DINV"), bk("DINV"))
                yield
                for i in range(2):
                    MM(PS[pu][:, i * 128:(i + 1) * 128], WT[:, i, :], SBF[:, i, :], False, True, bk("AM") + bk("DINV"), [("PS", pu)])
                yield
                CP("act", f2(VN), PS[pu][:, 0:256], [("PS", pu)] + bk("N1"), bk("N1"))
                resv.discard(pu)
                yield
                for i in range(2):
                    MM(PS[po][:, i * 128:(i + 1) * 128], SBF[:, i, :], QG[:, i, :], i == 0, False, bk("DINV") + bk("QG"), [("PS", po)])
                for i in range(2):
                    MM(PS[po][:, i * 128:(i + 1) * 128], VN[:, i, :], ATT[:, i, :], False, True, bk("N1") + bk("ATT"), [("PS", po)])
                psu = psum(True)
                for i in range(2):
                    MM(PS[psu][:, i * 128:(i + 1) * 128], KD[:, i, :], VN[:, i, :], i == 0, True, bk("KD") + bk("N1"), [("PS", psu)])
                yield
                for i, h in enumerate(heads):
                    STT(SST[:, l, h, :], SST[:, l, h, :], EGR[:, h, 127:128], PS[psu][:, i * 128:(i + 1) * 128], ALU.mult, ALU.add,
                        [("PS", psu), ("SST", hg)] + [("EGR", (par, hg))], [("SST", hg)])
                resv.discard(psu)
            else:
                resv.discard(pu)
                HSQ = NS // 2
                SS0 = TT[:, 0:2, :].rearrange("p a t -> p (a t)")[:, 0:HSQ * 128].rearrange("p (s e) -> p s e", s=HSQ)
                SS0B = TT[:, 2, :].bitcast(BF16)[:, 0:HSQ * 128].rearrange("p (s e) -> p s e", s=HSQ)
                WTX = TT[:, 3, 0:512].bitcast(BF16).rearrange("p (s c) -> p s c", s=HSQ)
                kS0 = [("TT", 0), ("TT", 1)]; kS0B = [("TT", 2)]; kWX = [("TT", 3)]
                segrow = c16("segrow", NS * 128).rearrange("p (s c) -> p s c", s=NS)
                segcol = c16("segcol", NS)
                first_po = True
                for i, h in enumerate(heads):
                    for hf in range(2):
                        s0 = hf * HSQ
                        P.dma("sp", SS0, dr["st_dn"][l][s0:s0 + HSQ, h, :, :].rearrange("s d e -> d s e"), reads=[], writes=kS0)
                        CP("act", SS0B, SS0, kS0 + kS0B, kS0B)
                        TTo("dve", WTX, WT[:, i, :].unsqueeze(1).broadcast_to([128, HSQ, 128]), segrow[:, s0:s0 + HSQ, :], ALU.mult,
                            bk("AM") + [("C16", None)] + kWX, kWX)
                        pu2 = psum(True)
                        MM(PS[pu2][:, 0:128], DTt[:, i, :], BV[:, i, :], True, False, bk("DT") + bk("BV"), [("PS", pu2)])
                        for s_ in range(HSQ):
                            MM(PS[pu2][:, 0:128], WTX[:, s_, :], SS0B[:, s_, :], False, True, kS0B + kWX, [("PS", pu2)])
                        CP("act", VN[:, i, :], PS[pu2][:, 0:128], [("PS", pu2)] + bk("N1"), bk("N1"))
                        resv.discard(pu2)
                        cb = i * 128 + hf * 64
                        MM(PS[po][:, cb:cb + 64], VN[:, i, :], ATT[:, i, hf * 64:hf * 64 + 64], first_po, False, bk("N1") + bk("ATT"), [("PS", po)])
                        first_po = False
                        for s_ in range(HSQ):
                            cs = i * 128 + (s0 + s_) * 8
                            MM(PS[po][:, cs:cs + 8], SS0B[:, s_, :], QG[:, i, (s0 + s_) * 8:(s0 + s_) * 8 + 8], False, True,
                               kS0B + bk("QG"), [("PS", po)])
                        TTo("dve", WTX, KD[:, i, :].unsqueeze(1).broadcast_to([128, HSQ, 128]),
                            segcol[:, s0:s0 + HSQ].unsqueeze(2).broadcast_to([128, HSQ, 128]), ALU.mult,
                            bk("KD") + [("C16", None)] + kWX, kWX)
                        for q in range(2):
                            psu = psum(True)
                            for j in range(4):
                                s_ = q * 4 + j
                                MM(PS[psu][:, j * 128:(j + 1) * 128], WTX[:, s_, :], VN[:, i, :], j == 0, True, kWX + bk("N1"), [("PS", psu)])
                            for j in range(4):
                                s_ = q * 4 + j
                                sg_ = s0 + s_
                                STT(SS0[:, s_, :], SS0[:, s_, :], EGR[:, h, sg_ * 8 + 7:sg_ * 8 + 8], PS[psu][:, j * 128:(j + 1) * 128],
                                    ALU.mult, ALU.add, [("PS", psu)] + kS0 + [("EGR", (par, hg))], kS0)
                            resv.discard(psu)
                        out_ops.append(P.dma("sp", dr["nds"][l][s0:s0 + HSQ, h, :, :].rearrange("s d e -> d s e"), SS0, reads=kS0, writes=[("NDSo", h)]))
                        yield
            SQW = SQ[:, 2 * hg, 0:256]
            ACT(SQW, PS[po][:, 0:256], ACTF.Square, [("PS", po)], [("SQ", 2 * hg)])
            yield
            pss = psum(True)
            MM(PS[pss][:, 0:256], ones_r, SQW, True, True, [("SQ", 2 * hg), ("ONESR", None)], [("PS", pss)])
            yield
            ACT(PS[pss][:, 0:256], PS[pss][:, 0:256], ACTF.Ln, [("PS", pss), ("EPST", None)], [("PS", pss)], scale=1.0 / 128.0, bias=EPSC[(128, 1.0)])
            ACT(PS[pss][:, 0:256], PS[pss][:, 0:256], ACTF.Exp, [("PS", pss)], [("PS", pss)], scale=-0.5)
            yield
            CP("act", f2(OTp), PS[pss][:, 0:256], [("PS", pss)] + bk("OT"), bk("OT"))
            resv.discard(pss)
            STT(f2(OTp), PS[po][:, 0:256], pcol(l, "dn_norm_w"), f2(OTp), ALU.mult, ALU.mult, [("PS", po), ("PP", None)] + bk("OT"), bk("OT"))
            resv.discard(po)
            TTo("dve", MIXR[:, 4 + 2 * hg:6 + 2 * hg, c0:c0 + 128], OTp, ZG[:, HS, c0:c0 + 128], ALU.mult,
                bk("OT") + [("ZG", None)], [("Y", (4 + h, t)) for h in heads])

        def delta_phase(l):
            db1keys = [(n + "1", hg) for n in DBNAMES for hg in range(2)]
            ttrows = [("TT", i) for i in range(4)]
            P.op("dve", lambda e: e.memset(FEN[:, 0:1], 0.0), ttrows + db1keys, ttrows + db1keys)
            tiles = list(range(ntile))
            active = []
            nxt_i = 0
            while nxt_i < len(tiles) or active:
                if nxt_i < len(tiles):
                    t = tiles[nxt_i]
                    is_s = smp and t == ntile - 1
                    can = False
                    if not active:
                        can = True
                    elif (not is_s) and len(active) == 1 and active[0][3] >= getattr(cfg, "dthr", 19) and not active[0][4]:
                        can = True
                    if can:
                        ds = 0 if is_s else (t % 2)
                        if is_s:
                            P.op("dve", lambda e: e.memset(FEN[:, 0:1], 0.0), ttrows + db1keys, ttrows + db1keys)
                        delta_prelude(l, t, is_s)
                        gens = [delta_pair(l, t, is_s, 0, ds), delta_pair(l, t, is_s, 1, ds)]
                        active.append([t, gens, [True, True], 0, is_s])
                        nxt_i += 1
                for rec in active:
                    for gi_, g in enumerate(rec[1]):
                        if rec[2][gi_]:
                            try:
                                next(g)
                            except StopIteration:
                                rec[2][gi_] = False
                    rec[3] += 1
                active = [r for r in active if any(r[2])]
            P.op("dve", lambda e: e.memset(FEN[:, 0:1], 0.0), ttrows + db1keys, ttrows + db1keys)

        for l in range(DEPTH):
            ffn(l, 1)
            mixer(l)
            ffn(l, 2)

        for (b0, bn) in blks:
            pi = sumsq_rstd(X, "X", b0, bn, D)
            norm_scale(Y, "Y", X, "X", 0, "final_norm", b0, bn, pi)
        for t in range(ntile):
            si = st["stg"]; st["stg"] ^= 1
            stg = stg_view(si)
            for half in range(2):
                pi = psum()
                for j in range(4):
                    kc = half * 4 + j
                    TR(PS[pi][:, j * 128:(j + 1) * 128], YF[:, kc, t * 128:(t + 1) * 128], ident, [("Y", (kc, t)), ("C32", None)], [("PS", pi)])
                CP("act" if half == 0 else "dve", stg[:, half * 512:(half + 1) * 512], PS[pi][:], [("PS", pi)], [stg_keys(si)[half]])
            dst = dr["yp"][p0 + t * 128: p0 + (t + 1) * 128, :] if t * 128 < npr else dr["ys"][:, :]
            out_ops.append(P.dma("sp", dst, stg, reads=stg_keys(si), writes=[("STGo", si)]))

    P.emit(out_ops)
    es.close()
    return nc, P


def make_in_maps(inp):
    pp = _pack_params(inp)
    c32, c16 = _consts()
    rgw = _pack_rgw(inp)
    maps = []
    shared = {"pp": pp, "c32": c32, "c16": c16, "rgw_r": rgw}
    for nm in ("ffn1_w_up", "ffn2_w_up", "ffn1_w_down", "ffn2_w_down", "w_in", "w_out"):
        shared[nm] = np.ascontiguousarray(inp[nm])
    for core in range(NCORES):
        sl = slice(core * NS, (core + 1) * NS)
        m = dict(shared)
        m["xp"] = np.ascontiguousarray(inp["x_prompt"][core])
        m["xs"] = np.ascontiguousarray(inp["x_sample"][sl].reshape(NS * DS, D))
        m["st_conv"] = np.ascontiguousarray(inp["state_conv"][:, sl].reshape(DEPTH, NS * 3, CONVC))
        m["st_rg"] = np.ascontiguousarray(inp["state_rglru"][:, sl])
        m["st_dn"] = np.ascontiguousarray(inp["state_delta"][:, sl])
        maps.append(m)
    return maps


def gather(r):
    y_prompt = np.stack([r[c]["yp"] for c in range(NCORES)], axis=0)
    y_sample = np.concatenate([r[c]["ys"].reshape(NS, DS, D) for c in range(NCORES)], axis=0)
    ncp = np.stack([r[c]["ncp"] for c in range(NCORES)], axis=1)
    nrp = np.stack([r[c]["nrp"] for c in range(NCORES)], axis=1)
    ndp = np.stack([r[c]["ndp"] for c in range(NCORES)], axis=1)
    ncs = np.concatenate([r[c]["ncs"].reshape(DEPTH, NS, 3, CONVC) for c in range(NCORES)], axis=1)
    nrs = np.concatenate([r[c]["nrs"] for c in range(NCORES)], axis=1)
    nds = np.concatenate([r[c]["nds"] for c in range(NCORES)], axis=1)
    return (y_prompt, y_sample, ncp, nrp, ndp, ncs, nrs, nds)


def kernel(**inp):
    inp = {k: np.asarray(v) for k, v in inp.items()}
    cfg = Cfg()
    nc, P = build_program(cfg)
    maps = make_in_maps(inp)
    res = run_bass_kernel_spmd(nc, maps, core_ids=list(range(NCORES)))
    return gather(res.results)
```
